# Optimizing a Trainium2 kernel written in Bass

```python
import numpy as np
import jax, jax.numpy as jnp
from jax import lax

D_MODEL = 1024
BATCH = 16
SEQ = 2048
DEPTH = 1

RET_HEADS = 4
RET_DK = 128
RET_DV = 256
RET_CHUNK = 128
NSA_HEADS = 8
NSA_KV_GROUPS = 2
NSA_GROUP_HEADS = NSA_HEADS // NSA_KV_GROUPS
NSA_DK = 64
CMP_BLOCK = 32
CMP_STRIDE = 16
CMP_HIDDEN = 256
SEL_BLOCK = 64
SEL_TOPK = 8
WINDOW = 512
D_FF = 2816
CONV_W = 3

EPS = 1e-6
NEG = -1e30
FORCE = 1e9

IN_SIZES = (RET_HEADS * RET_DK, RET_HEADS * RET_DK, RET_HEADS * RET_DV, RET_HEADS * RET_DV,
            NSA_HEADS * NSA_DK,
            NSA_KV_GROUPS * NSA_DK, NSA_KV_GROUPS * NSA_DK, NSA_KV_GROUPS * NSA_DK,
            NSA_KV_GROUPS * NSA_DK, NSA_KV_GROUPS * NSA_DK, NSA_KV_GROUPS * NSA_DK,
            3 * NSA_HEADS, D_MODEL, D_MODEL)
N_IN = sum(IN_SIZES)

kernel_name = "hybrid_retention_nsa_convffn"


def rmsnorm(x, g):
    xf = x.astype(jnp.float32)
    y = xf * lax.rsqrt(jnp.mean(xf * xf, axis=-1, keepdims=True) + EPS) * g.astype(jnp.float32)
    return y.astype(x.dtype)


def retention(q, k, v, g, gn_g):
    B, S = q.shape[0], q.shape[1]
    H, C = RET_HEADS, RET_CHUNK
    NCH = S // C
    lg = jnp.log1p(-jnp.exp2(-5.0 - jnp.arange(H, dtype=jnp.float32)))
    pos = jnp.arange(C, dtype=jnp.float32)
    diff = pos[:, None] - pos[None, :]
    dmask = jnp.where(diff >= 0, jnp.exp(lg[:, None, None] * jnp.maximum(diff, 0.0)), 0.0)
    zeta = jnp.exp(lg[:, None] * (C - 1 - pos))
    xi = jnp.exp(lg[:, None] * (pos + 1.0))

    def to_chunks(t):
        return t.reshape(B, NCH, C, H, t.shape[-1]).transpose(0, 3, 1, 2, 4)

    qc = to_chunks(q)
    kc = to_chunks(k) * (RET_DK ** -0.5)
    vc = to_chunks(v)
    inner = jnp.einsum('bhncd,bhnmd->bhncm', qc, kc) * dmask[:, None]
    o_inner = jnp.einsum('bhncm,bhnme->bhnce', inner, vc)
    kv = jnp.einsum('bhnmd,bhnme->nbhde', kc * zeta[:, None, :, None], vc)
    chunk_decay = jnp.exp(lg * C).astype(kv.dtype)[:, None, None]

    def step(R, kv_n):
        return chunk_decay * R + kv_n, R

    R0 = jnp.zeros(kv.shape[1:], kv.dtype)
    _, R_prev = lax.scan(step, R0, kv)
    o_cross = jnp.einsum('bhncd,nbhde->bhnce', qc * xi[:, None, :, None], R_prev)
    o = (o_inner + o_cross).transpose(0, 2, 3, 1, 4).reshape(B, S, H, RET_DV)
    of = o.astype(jnp.float32)
    mu = jnp.mean(of, axis=-1, keepdims=True)
    var = jnp.mean(jnp.square(of - mu), axis=-1, keepdims=True)
    on = ((of - mu) * lax.rsqrt(var + EPS)).reshape(B, S, H * RET_DV) * gn_g.astype(jnp.float32)
    return (jax.nn.silu(g.astype(jnp.float32)) * on).astype(q.dtype)


def compress(t, pos_emb, w1, b1, w2):
    B, G, S, dk = t.shape
    npc = CMP_BLOCK // CMP_STRIDE
    NP = S // CMP_STRIDE
    pieces = t.reshape(B, G, NP, CMP_STRIDE, dk)
    blocks = jnp.concatenate([pieces[:, :, i:NP - (npc - 1) + i] for i in range(npc)], axis=3)
    blocks = (blocks + pos_emb).reshape(B, G, NP - npc + 1, CMP_BLOCK * dk)
    return jax.nn.gelu(blocks @ w1 + b1) @ w2


def nsa(q, kcr, vcr, ks, vs, kw, vw, gates,
        cmp_pos_k, cmp_w1_k, cmp_b1_k, cmp_w2_k, cmp_pos_v, cmp_w1_v, cmp_b1_v, cmp_w2_v):
    B, S = q.shape[0], q.shape[1]
    G, R, dk, T = NSA_KV_GROUPS, NSA_GROUP_HEADS, NSA_DK, SEL_BLOCK
    NS = S // SEL_BLOCK
    NQ = S // T
    k_eff = min(SEL_TOPK, NS)
    scale = dk ** -0.5
    qh = q.reshape(B, S, G, R, dk).transpose(0, 2, 3, 1, 4)

    def kvh(t):
        return t.reshape(B, S, G, dk).transpose(0, 2, 1, 3)

    kcr, vcr, ks, vs, kw, vw = [kvh(t) for t in (kcr, vcr, ks, vs, kw, vw)]
    slopes = jnp.exp2(-(jnp.arange(NSA_HEADS, dtype=jnp.float32) + 1.0)).reshape(G, R)
    tpos = jnp.arange(S, dtype=jnp.float32)

    kcmp = compress(kcr, cmp_pos_k, cmp_w1_k, cmp_b1_k, cmp_w2_k)
    vcmp = compress(vcr, cmp_pos_v, cmp_w1_v, cmp_b1_v, cmp_w2_v)
    NC = kcmp.shape[2]
    c_start = jnp.arange(NC) * CMP_STRIDE
    c_end = (c_start + CMP_BLOCK - 1).astype(jnp.float32)
    dcmp = tpos[:, None] - c_end[None, :]
    cmask = dcmp >= 0
    s = jnp.einsum('bgrtd,bgcd->bgrtc', qh, kcmp).astype(jnp.float32) * scale
    s = jnp.where(cmask, s - slopes[None, :, :, None, None] * dcmp, NEG)
    p_cmp = jax.nn.softmax(s, axis=-1) * cmask
    o_cmp = jnp.einsum('bgrtc,bgcd->bgrtd', p_cmp.astype(vcmp.dtype), vcmp)

    j_start = jnp.arange(NS) * SEL_BLOCK
    overlap = ((c_start[:, None] < j_start[None, :] + SEL_BLOCK) &
               (c_start[:, None] + CMP_BLOCK > j_start[None, :])).astype(jnp.float32)
    imp = jnp.einsum('bgrtc,cj->bgtj', p_cmp, overlap)
    cur = (jnp.arange(S) // SEL_BLOCK)[:, None]
    jj = jnp.arange(NS)[None, :]
    imp = jnp.where((jj == 0) | (jj == cur) | (jj == cur - 1), FORCE, imp)
    imp = jnp.where(jj > cur, -FORCE, imp)
    _, sel_idx = lax.top_k(imp, k_eff)

    ks_blk = ks.reshape(B, G, NS, SEL_BLOCK, dk)
    vs_blk = vs.reshape(B, G, NS, SEL_BLOCK, dk)
    kw_pad = jnp.pad(kw, ((0, 0), (0, 0), (WINDOW, 0), (0, 0)))
    vw_pad = jnp.pad(vw, ((0, 0), (0, 0), (WINDOW, 0), (0, 0)))
    bi = jnp.arange(B)[:, None, None, None]
    gi = jnp.arange(G)[None, :, None, None]
    sl = slopes[None, :, :, None]

    def block_fn(n):
        t0 = n * T
        qb = lax.dynamic_slice_in_dim(qh, t0, T, axis=3)
        tb = t0 + jnp.arange(T)
        idb = lax.dynamic_slice_in_dim(sel_idx, t0, T, axis=2)
        kb = ks_blk[bi, gi, idb]
        vb = vs_blk[bi, gi, idb]
        kpos = idb[..., None] * SEL_BLOCK + jnp.arange(SEL_BLOCK)
        dsel = (tb[None, None, :, None, None] - kpos).astype(jnp.float32)[:, :, None]
        ss = jnp.einsum('bgrtd,bgtnsd->bgrtns', qb, kb).astype(jnp.float32) * scale
        ss = jnp.where(dsel >= 0, ss - sl[..., None, None] * dsel, NEG)
        ps = jax.nn.softmax(ss.reshape(B, G, R, T, -1), axis=-1).reshape(ss.shape)
        o_sel = jnp.einsum('bgrtns,bgtnsd->bgrtd', ps.astype(vb.dtype), vb)
        kwb = lax.dynamic_slice_in_dim(kw_pad, t0, WINDOW + T, axis=2)
        vwb = lax.dynamic_slice_in_dim(vw_pad, t0, WINDOW + T, axis=2)
        wpos = t0 - WINDOW + jnp.arange(WINDOW + T)
        dw = tb[:, None] - wpos[None, :]
        wmask = (dw >= 0) & (dw < WINDOW) & (wpos[None, :] >= 0)
        sw = jnp.einsum('bgrtd,bgsd->bgrts', qb, kwb).astype(jnp.float32) * scale
        sw = jnp.where(wmask, sw - sl[..., None] * dw.astype(jnp.float32), NEG)
        pw = jax.nn.softmax(sw, axis=-1)
        o_win = jnp.einsum('bgrts,bgsd->bgrtd', pw.astype(vwb.dtype), vwb)
        return o_sel, o_win

    o_sel, o_win = lax.map(block_fn, jnp.arange(NQ))
    o_sel = o_sel.transpose(1, 2, 3, 0, 4, 5).reshape(B, G, R, S, dk)
    o_win = o_win.transpose(1, 2, 3, 0, 4, 5).reshape(B, G, R, S, dk)

    gt = jax.nn.sigmoid(gates.reshape(B, S, 3, G, R).transpose(2, 0, 3, 4, 1))[..., None]
    o = gt[0] * o_cmp + gt[1] * o_sel + gt[2] * o_win
    return o.transpose(0, 3, 1, 2, 4).reshape(B, S, NSA_HEADS * dk).astype(q.dtype)


def conv_ffn(h, w_up, conv_w, conv_b, w_down):
    S = h.shape[1]
    a, b = jnp.split(h @ w_up, 2, axis=-1)
    a_pad = jnp.pad(a, ((0, 0), (CONV_W - 1, 0), (0, 0)))
    ac = conv_b + sum(conv_w[i] * a_pad[:, i:i + S] for i in range(CONV_W))
    return (jax.nn.gelu(ac) * b) @ w_down


def setup_inputs(seed: int = 0) -> dict:
    key = jax.random.key(seed)
    ks = jax.random.split(key, 24)
    f32 = jnp.float32

    def nrm(k, shape, fan_in):
        return jax.random.normal(k, shape, f32) * (fan_in ** -0.5)

    def gain(k, shape):
        return 1.0 + 0.01 * jax.random.normal(k, shape, f32)

    L = DEPTH
    return {
        "x": jax.random.normal(ks[0], (BATCH, SEQ, D_MODEL), f32),
        "norm_mix": gain(ks[1], (L, D_MODEL)),
        "w_in": nrm(ks[2], (L, D_MODEL, N_IN), D_MODEL),
        "ret_gn_g": gain(ks[3], (L, RET_HEADS * RET_DV)),
        "cmp_pos_k": 0.02 * jax.random.normal(ks[4], (L, CMP_BLOCK, NSA_DK), f32),
        "cmp_w1_k": nrm(ks[5], (L, CMP_BLOCK * NSA_DK, CMP_HIDDEN), CMP_BLOCK * NSA_DK),
        "cmp_b1_k": 0.01 * jax.random.normal(ks[6], (L, CMP_HIDDEN), f32),
        "cmp_w2_k": nrm(ks[7], (L, CMP_HIDDEN, NSA_DK), CMP_HIDDEN),
        "cmp_pos_v": 0.02 * jax.random.normal(ks[8], (L, CMP_BLOCK, NSA_DK), f32),
        "cmp_w1_v": nrm(ks[9], (L, CMP_BLOCK * NSA_DK, CMP_HIDDEN), CMP_BLOCK * NSA_DK),
        "cmp_b1_v": 0.01 * jax.random.normal(ks[10], (L, CMP_HIDDEN), f32),
        "cmp_w2_v": nrm(ks[11], (L, CMP_HIDDEN, NSA_DK), CMP_HIDDEN),
        "w_ret_o": nrm(ks[12], (L, RET_HEADS * RET_DV, D_MODEL), RET_HEADS * RET_DV),
        "w_nsa_o": nrm(ks[13], (L, NSA_HEADS * NSA_DK, D_MODEL), NSA_HEADS * NSA_DK),
        "w_out": nrm(ks[14], (L, D_MODEL, D_MODEL), D_MODEL),
        "norm_ffn": gain(ks[15], (L, D_MODEL)),
        "w_up": nrm(ks[16], (L, D_MODEL, 2 * D_FF), D_MODEL),
        "conv_w": nrm(ks[17], (L, CONV_W, D_FF), CONV_W),
        "conv_b": 0.01 * jax.random.normal(ks[18], (L, D_FF), f32),
        "w_down": nrm(ks[19], (L, D_FF, D_MODEL), D_FF),
        "norm_final": gain(ks[20], (D_MODEL,)),
    }


def reference(x, norm_mix, w_in, ret_gn_g, cmp_pos_k, cmp_w1_k, cmp_b1_k, cmp_w2_k,
              cmp_pos_v, cmp_w1_v, cmp_b1_v, cmp_w2_v, w_ret_o, w_nsa_o, w_out,
              norm_ffn, w_up, conv_w, conv_b, w_down, norm_final):
    B, S, _ = x.shape
    split_at = list(np.cumsum(IN_SIZES)[:-1])
    for l in range(DEPTH):
        h = rmsnorm(x, norm_mix[l])
        proj = h @ w_in[l]
        (rq, rk, rv, rg, nq, kcr, vcr, ksl, vsl, kwn, vwn, ngate,
         ga, gb) = jnp.split(proj, split_at, axis=-1)
        y_a = retention(rq.reshape(B, S, RET_HEADS, RET_DK), rk.reshape(B, S, RET_HEADS, RET_DK),
                        rv.reshape(B, S, RET_HEADS, RET_DV), rg, ret_gn_g[l]) @ w_ret_o[l]
        y_b = nsa(nq, kcr, vcr, ksl, vsl, kwn, vwn, ngate,
                  cmp_pos_k[l], cmp_w1_k[l], cmp_b1_k[l], cmp_w2_k[l],
                  cmp_pos_v[l], cmp_w1_v[l], cmp_b1_v[l], cmp_w2_v[l]) @ w_nsa_o[l]
        merged = jax.nn.sigmoid(ga) * y_a + jax.nn.sigmoid(gb) * y_b
        x = x + merged @ w_out[l]
        x = x + conv_ffn(rmsnorm(x, norm_ffn[l]), w_up[l], conv_w[l], conv_b[l], w_down[l])
    return rmsnorm(x, norm_final)
```

```python
import contextlib
import numpy as np
STAGE = 9
import ml_dtypes
import concourse.bass as bass
import concourse.mybir as mybir
from concourse.bass_utils import run_bass_kernel_spmd

F32 = mybir.dt.float32
BF16 = mybir.dt.bfloat16
ALU = mybir.AluOpType
AF = mybir.ActivationFunctionType

PE, ACT, DVE, POOL, SP = "tensor", "scalar", "vector", "gpsimd", "sync"
ENGS = (PE, ACT, DVE, POOL, SP)
NDMASEM = 8

S_LEN = 2048
D = 1024
DFF = 2816
NFC = DFF // 128
EPS = 1e-6
N_IN = 6424
O_RQ, O_RK, O_RV, O_RG, O_NQ, O_KCR, O_VCR, O_KS, O_VS, O_KW, O_VW, O_NG, O_GA, O_GB = (
    0, 512, 1024, 2048, 3072, 3584, 3712, 3840, 3968, 4096, 4224, 4352, 4376, 5400)


class Op:
    __slots__ = ("eng", "fn", "reads", "writes", "is_dma", "waits", "inc", "ticket", "dsem", "dval")

    def __init__(self, eng, fn, reads, writes, is_dma):
        self.eng = eng
        self.fn = fn
        self.reads = reads
        self.writes = writes
        self.is_dma = is_dma
        self.waits = []
        self.inc = False
        self.ticket = None
        self.dsem = None
        self.dval = None


class Sched:
    def __init__(self, nc):
        self.nc = nc
        self.ops = []
        self.last_writer = {}
        self.readers = {}
        self.dma_count = {e: 0 for e in ENGS}
        self.dma_hist = {e: [] for e in ENGS}
        self.last_op = {e: None for e in ENGS}

    def op(self, eng, fn, reads=(), writes=()):
        o = Op(eng, fn, tuple(reads), tuple(writes), False)
        self._add(o)
        return o

    def dma(self, eng, fn, reads=(), writes=()):
        o = Op(eng, fn, tuple(reads), tuple(writes), True)
        i = self.dma_count[eng]
        self.dma_count[eng] += 1
        o.dsem = (eng, i % NDMASEM)
        o.dval = 16 * (i // NDMASEM + 1)
        self.dma_hist[eng].append(o)
        if i >= NDMASEM:
            o.waits.append(self.dma_hist[eng][i - NDMASEM])
        self._add(o)
        return o

    def barrier(self):
        tails = []
        for e in ENGS:
            if self.last_op[e] is not None and not self.last_op[e].is_dma:
                tails.append(self.last_op[e])
            tails.extend(self.dma_hist[e][-NDMASEM:])
        for e in ENGS:
            o = Op(e, None, (), (), False)
            o.waits = [t for t in tails]
            self.ops.append(o)
            self.last_op[e] = o
        self.last_writer = {}
        self.readers = {}

    def _add(self, o):
        self.ops.append(o)
        deps = o.waits
        for k in o.reads:
            w = self.last_writer.get(k)
            if w is not None:
                deps.append(w)
            if k.startswith("ps"):
                for r in self.readers.get(k, ()):
                    if r.eng != o.eng:
                        deps.append(r)
        for k in o.writes:
            w = self.last_writer.get(k)
            if w is not None and (w.is_dma or o.is_dma or w.eng != o.eng):
                deps.append(w)
            for r in self.readers.get(k, ()):
                if r.is_dma or o.is_dma or r.eng != o.eng:
                    deps.append(r)
        for k in o.reads:
            self.readers.setdefault(k, []).append(o)
        for k in o.writes:
            self.last_writer[k] = o
            self.readers[k] = []
        if not o.is_dma:
            self.last_op[o.eng] = o

    def run(self):
        nc = self.nc
        for o in self.ops:
            for d in o.waits:
                if not d.is_dma:
                    d.inc = True
        cnt = {e: 0 for e in ENGS}
        for o in self.ops:
            if o.fn is None:
                o.inc = False
            if not o.is_dma and o.inc:
                cnt[o.eng] += 1
                o.ticket = cnt[o.eng]
        seen = {e: {} for e in ENGS}
        plans = {e: [] for e in ENGS}
        for o in self.ops:
            e = o.eng
            wl = {}
            for d in o.waits:
                if d.is_dma:
                    key = ("d",) + d.dsem
                    val = d.dval
                else:
                    if d.ticket is None:
                        continue
                    key = ("c", d.eng)
                    val = d.ticket
                if seen[e].get(key, 0) >= val:
                    continue
                if wl.get(key, 0) < val:
                    wl[key] = val
            for key, val in wl.items():
                seen[e][key] = val
            plans[e].append((o, list(wl.items())))
        with contextlib.ExitStack() as st:
            sems = {}
            for e in ENGS:
                sems[("c", e)] = st.enter_context(nc.semaphore(f"c_{e}"))
                for j in range(min(NDMASEM, self.dma_count[e])):
                    sems[("d", e, j)] = st.enter_context(nc.semaphore(f"d_{e}_{j}"))
            block = st.enter_context(nc.Block())

            def mk(e):
                plan = plans[e]

                def body(eng):
                    for o, wl in plan:
                        for key, val in wl:
                            eng.wait_ge(sems[key], val)
                        if o.fn is None:
                            continue
                        ins = o.fn(eng)
                        if o.is_dma:
                            ins.then_inc(sems[("d",) + o.dsem], 16)
                        elif o.inc:
                            ins.then_inc(sems[("c", e)], 1)
                    n = self.dma_count[e]
                    for j in range(min(NDMASEM, n)):
                        eng.wait_ge(sems[("d", e, j)], 16 * ((n - 1 - j) // NDMASEM + 1))
                return body

            block.tensor(mk(PE))
            block.scalar(mk(ACT))
            block.vector(mk(DVE))
            block.gpsimd(mk(POOL))
            block.sync(mk(SP))


class Ring:
    def __init__(self, items):
        self.items = items
        self.i = 0

    def get(self):
        it = self.items[self.i % len(self.items)]
        self.i += 1
        return it


def make_consts():
    c = {}
    bf = ml_dtypes.bfloat16
    f = np.float32
    c["ident"] = np.eye(128, dtype=f).astype(bf)
    lg = np.log1p(-np.exp2(-5.0 - np.arange(4, dtype=np.float64)))
    pos = np.arange(128, dtype=np.float64)
    diff = pos[None, :] - pos[:, None]
    dmt = np.where(diff >= 0, np.exp(lg[:, None, None] * np.maximum(diff, 0.0)), 0.0)
    c["c_dmt"] = np.ascontiguousarray(dmt.transpose(1, 0, 2).reshape(128, 512)).astype(f)
    xi = np.exp(lg[:, None] * (pos + 1.0))
    xi2 = np.tile(xi, (1, 2))
    c["c_xi"] = np.ascontiguousarray(np.broadcast_to(xi2.reshape(1, 4 * 256), (128, 4 * 256))).astype(f)
    zeta = np.exp(lg[:, None] * (127 - pos)) * (128 ** -0.5)
    c["c_zeta"] = np.ascontiguousarray(np.repeat(zeta.T[:, :, None], 128, axis=2).reshape(128, 512)).astype(f)
    slopes = np.exp2(-(np.arange(8, dtype=np.float64) + 1.0)).reshape(2, 4)
    t = np.arange(S_LEN)
    qaug = np.zeros((2, 4, 4, S_LEN))
    for g in range(2):
        for r in range(4):
            sl = slopes[g, r]
            qaug[g, 0, r] = 8 * sl * 128
            qaug[g, 1, r] = 8 * sl
            qaug[g, 2, r] = -8 * sl * 128 * (t // 128)
            qaug[g, 3, r] = -8 * sl * (t % 128)
    c["c_qaug"] = qaug.reshape(2 * 4, 4 * S_LEN).astype(bf)
    kaug = np.stack([t // 128, t % 128, np.ones(S_LEN), np.ones(S_LEN)]).astype(np.float64)
    c["c_kaug"] = kaug.astype(bf)
    pc = 16 * np.arange(127) + 31
    caug = np.stack([pc // 128, pc % 128, np.ones(127), np.ones(127)]).astype(np.float64)
    c["c_caug"] = caug.astype(bf)
    c["c_ehot"] = (np.arange(32)[:, None] == (t[None, :] // 64)).astype(f).astype(bf)
    tt = (128 * np.arange(16)[:, None] + np.arange(128)[None, :])
    cm = (pc[:, None, None] <= tt[None, :, :]).astype(f)
    c["c_cmask"] = np.ascontiguousarray(cm.reshape(127, 16 * 128)).astype(bf)
    kb = np.arange(128)
    c["c_caus"] = np.ascontiguousarray(np.tile((kb[:, None] <= kb[None, :]).astype(f), (1, 4))).astype(bf)
    c["c_far"] = np.ascontiguousarray(np.tile((kb[:, None] > kb[None, :]).astype(f), (1, 4))).astype(bf)
    cur = tt // 64
    jj = np.arange(32)
    force = (jj[None, None, :] == 0) | (jj[None, None, :] == cur[:, :, None]) | (jj[None, None, :] == cur[:, :, None] - 1)
    fut = jj[None, None, :] > cur[:, :, None]
    A = 1.0 - force - fut
    Bm = 1e9 * force - 1e9 * fut
    NF = 1.0 - fut
    c["c_impA"] = np.ascontiguousarray(A.transpose(1, 0, 2).reshape(128, 16 * 32)).astype(f)
    c["c_impB"] = np.ascontiguousarray(Bm.transpose(1, 0, 2).reshape(128, 16 * 32)).astype(f)
    c["c_impNF"] = np.ascontiguousarray(NF.transpose(1, 0, 2).reshape(128, 16 * 32)).astype(f)
    cs = 16 * np.arange(127)
    js = 64 * np.arange(32)
    ovl = ((cs[:, None] < js[None, :] + 64) & (cs[:, None] + 32 > js[None, :])).astype(f)
    c["c_ovl"] = np.concatenate([np.ones((127, 1), f), ovl], axis=1).astype(bf)
    return c


class Prog:
    def __init__(self, nseq=2, passes="RNF", dbg=False):
        self.nseq = nseq
        self.ntok = nseq * S_LEN
        self.passes = passes
        self.dbg = dbg
        nc = self.nc = bass.Bass("TRN2", target_bir_lowering=False)
        self.S = Sched(nc)
        self.in_names = []

    def din(self, name, shape, dt=F32):
        self.in_names.append(name)
        return self.nc.dram_tensor(name, list(shape), dt, kind="ExternalInput").ap()

    def build(self):
        nc, S = self.nc, self.S
        NT = self.ntok
        self.x = self.din("x", [NT, D])
        self.w_in = self.din("w_in", [D, N_IN])
        self.w_up = self.din("w_up", [D, 2 * DFF])
        self.w_down = self.din("w_down", [DFF, D])
        self.g_ffn = self.din("g_ffn", [128, 8 * 128])
        self.g_mix = self.din("g_mix", [128, 8 * 128])
        self.g_fin = self.din("g_fin", [128, D])
        self.convw = self.din("convw", [128, NFC * 3])
        self.convb = self.din("convb", [128, NFC])
        self.ident_d = self.din("ident", [128, 128], BF16)
        self.w_ret_o = self.din("w_ret_o", [D, D])
        self.w_nsa_o = self.din("w_nsa_o", [512, D])
        self.w_out = self.din("w_out", [D, D])
        self.gng = self.din("gng", [128, 8 * 128])
        self.w1 = [self.din("w1k", [2048, 256]), self.din("w1v", [2048, 256])]
        self.w2 = [self.din("w2k", [256, 64]), self.din("w2v", [256, 64])]
        self.b1 = [self.din("b1k", [128, 2]), self.din("b1v", [128, 2])]
        self.posT = [self.din("posTk", [64, 64]), self.din("posTv", [64, 64])]
        self.cd = {}
        for k, v in make_consts().items():
            if k != "ident":
                self.cd[k] = self.din(k, v.shape, BF16 if v.dtype == ml_dtypes.bfloat16 else F32)
        self.out = nc.dram_tensor("out", [NT, D], F32, kind="ExternalOutput").ap()
        self.x1_scr = nc.dram_tensor("x1_scr", [NT, D], F32, kind="Internal").ap()
        self.ma_scr = nc.dram_tensor("ma_scr", [128, 8 * NT], BF16,
                                     kind="ExternalOutput" if self.dbg else "Internal").ap()

        with contextlib.ExitStack() as st:
            self.st = st
            banks = []
            for i in range(7):
                t = st.enter_context(nc.psum_tensor(f"psf{i}", [128, 512], F32))
                banks.append((t, f"psf{i}"))
            self.banks = Ring(banks[0:5])
            self.accb = Ring(banks[5:7])
            self.psb = st.enter_context(nc.psum_tensor("psb", [128, 1024], BF16))
            self.ident = self.sb("ident", [128, 128], BF16)
            self.epsb = self.sb("epsb", [128, 1], F32)
            S.dma(SP, lambda e: e.dma_start(out=self.ident[:], in_=self.ident_d), writes=["ident"])
            S.op(DVE, lambda e: e.memset(self.epsb[:], EPS), writes=["epsb"])
            self.junk = self.sb("junk", [128, D], BF16)
            self.xt = Ring([(self.sb(f"xt{i}", [128, D], F32), f"xt{i}") for i in range(4)])
            self.xn = Ring([(self.sb(f"xn{i}", [128, D], BF16), f"xn{i}") for i in range(3)])
            self.ss = Ring([(self.sb(f"ss{i}", [128, 1], F32), f"ss{i}") for i in range(4)])
            if "R" in self.passes:
                with contextlib.ExitStack() as st2:
                    self.st = st2
                    self.pass_R()
                    S.barrier()
            if "N" in self.passes:
                with contextlib.ExitStack() as st2:
                    self.st = st2
                    self.pass_N_alloc()
                    with contextlib.ExitStack() as st3:
                        self.st = st3
                        self.pass_N1()
                        S.barrier()
                    with contextlib.ExitStack() as st3:
                        self.st = st3
                        self.pass_N2()
                        S.barrier()
            if "F" in self.passes:
                with contextlib.ExitStack() as st2:
                    self.st = st2
                    self.pass_F(self.x1_scr if "N" in self.passes else self.x)
                    S.barrier()
            S.run()
        return nc

    def sb(self, name, shape, dt):
        self._uid = getattr(self, "_uid", 0) + 1
        return self.st.enter_context(self.nc.sbuf_tensor(f"s{self._uid}_{name}", list(shape), dt))

    def load_w(self, dst, dkey, src_rows, c0, ncols, kcs, dcol0=0, rows=128):
        S = self.S
        for kc in range(kcs):
            for cc in range(0, ncols, 2048):
                n = min(2048, ncols - cc)
                S.dma(POOL, lambda e, kc=kc, cc=cc, n=n: e.dma_start(
                    out=dst[0:rows, kc, dcol0 + cc:dcol0 + cc + n],
                    in_=src_rows[kc * rows:(kc + 1) * rows, c0 + cc:c0 + cc + n]), writes=[dkey])

    def cload(self, name, shape, dt, src, eng=SP):
        t = self.sb(name, shape, dt)
        self.S.dma(eng, lambda e: e.dma_start(out=t[:], in_=src), writes=[name])
        return t

    def mm8(self, out, lhs_fn, rhs_fn, n=8):
        def f(e):
            ins = None
            for kc in range(n):
                ins = e.matmul(out, lhsT=lhs_fn(kc), rhs=rhs_fn(kc), start=(kc == 0), stop=(kc == n - 1))
            return ins
        return f

    def pass_R(self):
        nc, S = self.nc, self.S
        TT = 256
        NS = TT // 128
        lg = np.log1p(-np.exp2(-5.0 - np.arange(4, dtype=np.float64)))
        decay = [float(np.exp(lg[h] * 128)) for h in range(4)]
        allb = self.banks.items + self.accb.items
        gen = Ring(allb[0:2])
        (pin, pink), (po0, po0k), (po1, po1k), (pk0, pk0k), (pk1, pk1k) = allb[2:7]
        pos_ = [(po0, po0k), (po1, po1k)]
        pks_ = [(pk0, pk0k), (pk1, pk1k)]
        wr = self.sb("wr", [128, 8, 3072], BF16)
        wga = self.sb("wga", [128, 8, D], BF16)
        wro = self.sb("wro", [128, 8, D], BF16)
        gcb = self.cload("gcb", [128, 1024], F32, self.g_mix)
        gngb = self.cload("gngb", [128, 1024], F32, self.gng)
        dmt = self.cload("dmt", [128, 512], F32, self.cd["c_dmt"])
        xi = self.cload("xi", [128, 4 * 256], F32, self.cd["c_xi"])
        zeta = self.cload("zeta", [128, 512], F32, self.cd["c_zeta"])
        self.load_w(wr, "wr", self.w_in, O_RQ, 3072, 8)
        self.load_w(wga, "wga", self.w_in, O_GA, D, 8)
        self.load_w(wro, "wro", self.w_ret_o, 0, D, 8)
        hTs = [self.sb(f"hTR{b}", [128, 8, TT], BF16) for b in range(2)]
        qTs = [self.sb(f"qT{b}", [128, 4, TT], BF16) for b in range(2)]
        qxTs = [self.sb(f"qxT{b}", [128, 4, TT], BF16) for b in range(2)]
        kTs = [self.sb(f"kT{b}", [128, 4, TT], BF16) for b in range(2)]
        kzs = [self.sb(f"kz{b}", [128, NS, 512], BF16) for b in range(2)]
        vs = [self.sb(f"v{b}", [128, NS, D], BF16) for b in range(2)]
        sgs = [self.sb(f"sg{b}", [128, NS, D], BF16) for b in range(2)]
        sga = self.sb("sga", [128, 8, TT], BF16)
        roT = self.sb("roT", [128, 8, TT], BF16)
        maT = self.sb("maT", [128, 8, TT], BF16)
        R = self.sb("R", [128, 4, 256], F32)
        Rb = self.sb("Rb", [128, 4, 256], BF16)
        inT = self.sb("inT4", [128, 512], BF16)
        on = self.sb("on4", [128, D], F32)
        stt = self.sb("stt", [128, 4, 6], F32)
        mv = self.sb("mv", [128, 4, 4], F32)
        NTL = self.ntok // TT

        def proj(t):
            b = t % 2
            hT, qT, qxT, kT, kz, v, sg = hTs[b], qTs[b], qxTs[b], kTs[b], kzs[b], vs[b], sgs[b]
            hk = f"hTR{b}"
            ops = []
            for h in range(4):
                def fq(h=h):
                    pq, pqk = gen.get()
                    S.op(PE, self.mm8(pq[:, 0:TT], lambda kc: wr[:, kc, O_RQ + h * 128:O_RQ + (h + 1) * 128], lambda kc: hT[:, kc, :]),
                         reads=["wr", hk], writes=[pqk])
                    S.op(ACT, lambda e: e.copy(out=qT[:, h, :], in_=pq[:, 0:TT]), reads=[pqk], writes=[f"qT{b}"])
                    S.op(DVE, lambda e: e.tensor_tensor(out=qxT[:, h, :], in0=pq[:, 0:TT], in1=xi[:, h * 256:(h + 1) * 256], op=ALU.mult),
                         reads=[pqk, "xi"], writes=[f"qxT{b}"])

                def fk(h=h):
                    pk, pkk = gen.get()
                    S.op(PE, self.mm8(pk[:, 0:TT], lambda kc: wr[:, kc, 512 + h * 128:512 + (h + 1) * 128], lambda kc: hT[:, kc, :]),
                         reads=["wr", hk], writes=[pkk])
                    S.op(ACT, lambda e: e.mul(out=kT[:, h, :], in_=pk[:, 0:TT], mul=128 ** -0.5), reads=[pkk], writes=[f"kT{b}"])
                ops += [fq, fk]
            for c in range(NS):
                cs = slice(c * 128, (c + 1) * 128)

                def fz(c=c, cs=cs):
                    pk, pkk = gen.get()
                    S.op(PE, self.mm8(pk[:, :], lambda kc: hT[:, kc, cs], lambda kc: wr[:, kc, 512:1024]), reads=["wr", hk], writes=[pkk])
                    S.op(DVE, lambda e: e.tensor_tensor(out=kz[:, c, :], in0=pk[:, :], in1=zeta[:], op=ALU.mult),
                         reads=[pkk, "zeta"], writes=[f"kz{b}"])
                ops.append(fz)
                for half in range(2):
                    def fv(c=c, cs=cs, half=half):
                        pv, pvk = gen.get()
                        S.op(PE, self.mm8(pv[:, :], lambda kc: hT[:, kc, cs], lambda kc: wr[:, kc, 1024 + half * 512:1024 + (half + 1) * 512]),
                             reads=["wr", hk], writes=[pvk])
                        S.op(ACT, lambda e: e.copy(out=v[:, c, half * 512:(half + 1) * 512], in_=pv[:, :]), reads=[pvk], writes=[f"v{b}"])

                    def fg(c=c, cs=cs, half=half):
                        pg, pgk = gen.get()
                        S.op(PE, self.mm8(pg[:, :], lambda kc: hT[:, kc, cs], lambda kc: wr[:, kc, 2048 + half * 512:2048 + (half + 1) * 512]),
                             reads=["wr", hk], writes=[pgk])
                        S.op(ACT, lambda e: e.activation(out=sg[:, c, half * 512:(half + 1) * 512], in_=pg[:, :], func=AF.Silu),
                             reads=[pgk], writes=[f"sg{b}"])
                    ops += [fv, fg]
            return ops

        def gates(t):
            b = t % 2
            hT = hTs[b]
            ops = []
            for cc in range(8):
                def f(cc=cc):
                    pga, pgak = gen.get()
                    S.op(PE, self.mm8(pga[:, 0:TT], lambda kc: wga[:, kc, cc * 128:(cc + 1) * 128], lambda kc: hT[:, kc, :]),
                         reads=["wga", f"hTR{b}"], writes=[pgak])
                    S.op(ACT, lambda e: e.activation(out=sga[:, cc, :], in_=pga[:, 0:TT], func=AF.Sigmoid), reads=[pgak], writes=["sga"])
                ops.append(f)
            return ops

        def chunk_stages(t):
            b = t % 2
            qT, qxT, kT, kz, v, sg = qTs[b], qxTs[b], kTs[b], kzs[b], vs[b], sgs[b]
            tok0 = t * TT
            stages = []
            for c in range(NS):
                cs = slice(c * 128, (c + 1) * 128)
                first = ((tok0 + c * 128) % S_LEN) == 0

                def st0(cs=cs):
                    def mmi(e):
                        ins = None
                        for h in range(4):
                            ins = e.matmul(pin[:, h * 128:(h + 1) * 128], lhsT=kT[:, h, cs], rhs=qT[:, h, cs], start=True, stop=True,
                                           skip_group_check=True)
                        return ins
                    S.op(PE, mmi, reads=[f"kT{b}", f"qT{b}"], writes=[pink])
                    S.op(DVE, lambda e: e.tensor_tensor(out=inT[:], in0=pin[:, :], in1=dmt[:], op=ALU.mult), reads=[pink, "dmt"], writes=["inT4"])

                def st1(c=c, cs=cs, first=first):
                    for hp in range(2):
                        po, pok = pos_[hp]

                        def mmo(e, po=po, hp=hp):
                            ins = None
                            for hh in range(2):
                                h = hp * 2 + hh
                                ins = e.matmul(po[:, hh * 256:(hh + 1) * 256], lhsT=inT[:, h * 128:(h + 1) * 128], rhs=v[:, c, h * 256:(h + 1) * 256],
                                               start=True, stop=first, skip_group_check=True)
                                if not first:
                                    ins = e.matmul(po[:, hh * 256:(hh + 1) * 256], lhsT=qxT[:, h, cs], rhs=Rb[:, h, :], start=False, stop=True,
                                                   skip_group_check=True)
                            return ins
                        S.op(PE, mmo, reads=["inT4", f"v{b}", f"qxT{b}", "Rb"], writes=[pok])
                    for hp in range(2):
                        pk, pkk = pks_[hp]

                        def mmk(e, pk=pk, hp=hp):
                            ins = None
                            for hh in range(2):
                                h = hp * 2 + hh
                                ins = e.matmul(pk[:, hh * 256:(hh + 1) * 256], lhsT=kz[:, c, h * 128:(h + 1) * 128], rhs=v[:, c, h * 256:(h + 1) * 256],
                                               start=True, stop=True, skip_group_check=True)
                            return ins
                        S.op(PE, mmk, reads=[f"kz{b}", f"v{b}"], writes=[pkk])

                def st2(first=first):
                    for h in range(4):
                        pk, pkk = pks_[h // 2]
                        src = pk[:, (h % 2) * 256:(h % 2 + 1) * 256]
                        if first:
                            S.op(DVE, lambda e, h=h, src=src: e.tensor_copy(out=R[:, h, :], in_=src), reads=[pkk], writes=["R"])
                        else:
                            S.op(DVE, lambda e, h=h, src=src: e.scalar_tensor_tensor(out=R[:, h, :], in0=R[:, h, :], scalar=decay[h], in1=src,
                                                                                     op0=ALU.mult, op1=ALU.add), reads=[pkk, "R"], writes=["R"])
                    S.op(ACT, lambda e: e.copy(out=Rb[:].rearrange("p h e -> p (h e)"), in_=R[:].rearrange("p h e -> p (h e)")),
                         reads=["R"], writes=["Rb"])
                    for h in range(4):
                        po, pok = pos_[h // 2]
                        S.op(DVE, lambda e, h=h, po=po: e.bn_stats(out=stt[:, h, :], in_=po[:, (h % 2) * 256:(h % 2 + 1) * 256]),
                             reads=[pok], writes=["stt"])
                    for h in range(4):
                        S.op(DVE, lambda e, h=h: e.bn_aggr(out=mv[:, h, 0:2], in_=stt[:, h, :]), reads=["stt"], writes=["mv"])
                    S.op(ACT, lambda e: e.activation(out=mv[:, :, 2], in_=mv[:, :, 1], func=AF.Sqrt, bias=self.epsb[:]),
                         reads=["mv", "epsb"], writes=["mv"])
                    S.op(DVE, lambda e: e.reciprocal(out=mv[:, :, 2], in_=mv[:, :, 2]), reads=["mv"], writes=["mv"])
                    S.op(DVE, lambda e: e.scalar_tensor_tensor(out=mv[:, :, 3], in0=mv[:, :, 0], scalar=-1.0, in1=mv[:, :, 2],
                                                               op0=ALU.mult, op1=ALU.mult), reads=["mv"], writes=["mv"])

                def st3(c=c):
                    for h in range(4):
                        po, pok = pos_[h // 2]
                        S.op(ACT, lambda e, h=h, po=po: e.activation(out=on[:, h * 256:(h + 1) * 256], in_=po[:, (h % 2) * 256:(h % 2 + 1) * 256],
                                                                     func=AF.Identity, scale=mv[:, h, 2:3], bias=mv[:, h, 3:4]),
                             reads=[pok, "mv"], writes=["on4"])
                    S.op(DVE, lambda e: e.tensor_tensor(out=sg[:, c, :], in0=on[:], in1=sg[:, c, :], op=ALU.mult),
                         reads=["on4", f"sg{b}"], writes=[f"sg{b}"])
                stages += [st0, st1, st2, st3]
            return stages

        def tail(t):
            b = t % 2
            sg = sgs[b]
            tok0 = t * TT
            for c in range(NS):
                cs = slice(c * 128, (c + 1) * 128)

                def tr(e, c=c):
                    ins = None
                    for kc in range(8):
                        ins = e.transpose(out=self.psb[:, kc * 128:(kc + 1) * 128], in_=sg[:, c, kc * 128:(kc + 1) * 128],
                                          identity=self.ident[:])
                    return ins
                S.op(PE, tr, reads=[f"sg{b}", "ident"], writes=["psb"])
                S.op(DVE, lambda e, cs=cs: e.tensor_tensor(out=roT[:, :, cs], in0=self.psb[:].rearrange("p (k j) -> p k j", k=8),
                                                           in1=gngb[:].rearrange("p (k j) -> p k j", k=8), op=ALU.mult),
                     reads=["psb", "gngb"], writes=["roT"])
            for cc in range(8):
                ccs = slice(cc * 128, (cc + 1) * 128)
                pya, pyak = gen.get()
                S.op(PE, self.mm8(pya[:, 0:TT], lambda kc, ccs=ccs: wro[:, kc, ccs], lambda kc: roT[:, kc, :]),
                     reads=["wro", "roT"], writes=[pyak])
                S.op(DVE, lambda e, pya=pya, cc=cc: e.tensor_tensor(out=maT[:, cc, :], in0=sga[:, cc, :], in1=pya[:, 0:TT], op=ALU.mult),
                     reads=[pyak, "sga"], writes=["maT"])
            S.dma(SP, lambda e: e.dma_start(out=self.ma_scr.rearrange("p (c n) -> p c n", c=8)[:, :, tok0:tok0 + TT], in_=maT[:]),
                  reads=["maT"])

        for sub in range(NS):
            self.rmsnorm_hT(self.x, sub * 128, gcb, hTs[0], "hTR0", sub * 128)
        for f in proj(0):
            f()
        for t in range(NTL):
            stages = chunk_stages(t)
            fill = gates(t)
            if t + 1 < NTL:
                nb_ = (t + 1) % 2
                pend = {}
                for sub in range(NS):
                    def fa(sub=sub, t=t):
                        pend[sub] = self.rms_a(self.x, (t + 1) * TT + sub * 128)
                    fill.append(fa)
                for sub in range(NS):
                    def fb(sub=sub, nb_=nb_):
                        self.rms_b(pend[sub], gcb, hTs[nb_], f"hTR{nb_}", sub * 128)
                    fill.append(fb)
                fill += proj(t + 1)
            fi = 0
            for k, stg_ in enumerate(stages):
                stg_()
                tgt = ((k + 1) * len(fill) + len(stages) - 1) // len(stages)
                while fi < min(tgt, len(fill)):
                    fill[fi]()
                    fi += 1
            while fi < len(fill):
                fill[fi]()
                fi += 1
            tail(t)

    def rms_a(self, src, r0):
        S = self.S
        xt, xk = self.xt.get()
        xn, nk = self.xn.get()
        ss, sk = self.ss.get()
        S.dma(SP, lambda e: e.dma_start(out=xt[:], in_=src[r0:r0 + 128, :]), writes=[xk])
        S.op(ACT, lambda e: e.activation(out=self.junk[:], in_=xt[:], func=AF.Square, accum_out=ss[:]),
             reads=[xk], writes=[sk])
        S.op(ACT, lambda e: e.activation(out=ss[:], in_=ss[:], func=AF.Sqrt, scale=1.0 / D, bias=self.epsb[:]),
             reads=[sk, "epsb"], writes=[sk])
        S.op(DVE, lambda e: e.reciprocal(out=ss[:], in_=ss[:]), reads=[sk], writes=[sk])
        S.op(DVE, lambda e: e.tensor_scalar(out=xn[:], in0=xt[:], scalar1=ss[:, 0:1], scalar2=None, op0=ALU.mult),
             reads=[xk, sk], writes=[nk])
        return (xt, xk, xn, nk)

    def rms_b(self, state, gcb, hT, hkey, col0):
        S = self.S
        xt, xk, xn, nk = state

        def tr(e):
            ins = None
            for kc in range(8):
                ins = e.transpose(out=self.psb[:, kc * 128:(kc + 1) * 128], in_=xn[:, kc * 128:(kc + 1) * 128],
                                  identity=self.ident[:])
            return ins
        S.op(PE, tr, reads=[nk, "ident"], writes=["psb"])
        S.op(DVE, lambda e: e.tensor_tensor(
            out=hT[:, :, col0:col0 + 128], in0=self.psb[:].rearrange("p (k j) -> p k j", k=8),
            in1=gcb[:].rearrange("p (k j) -> p k j", k=8), op=ALU.mult),
            reads=["psb", "gcb"], writes=[hkey])
        return xt, xk

    def rmsnorm_hT(self, src, r0, gcb, hT, hkey, col0, keep=None):
        return self.rms_b(self.rms_a(src, r0), gcb, hT, hkey, col0)

    def pass_N_alloc(self):
        S = self.S
        ns = self.nseq
        self.KST = [[self.sb(f"KST{s}{g}", [100, S_LEN], BF16) for g in range(2)] for s in range(ns)]
        self.KWT = [[self.sb(f"KWT{s}{g}", [100, S_LEN], BF16) for g in range(2)] for s in range(ns)]
        self.VS1 = [self.sb(f"VS1{s}", [128, 16, 2, 65], BF16) for s in range(ns)]
        self.VW1 = [self.sb(f"VW1{s}", [128, 16, 2, 65], BF16) for s in range(ns)]
        self.KCM = [[self.sb(f"KCM{s}{g}", [100, 128], BF16) for g in range(2)] for s in range(ns)]
        self.VCO = [[self.sb(f"VCO{s}{g}", [128, 97], BF16) for g in range(2)] for s in range(ns)]
        for s in range(ns):
            S.op(DVE, lambda e, s=s: e.memset(self.VS1[s][:], 1.0), writes=[f"VS1{s}"])
            S.op(DVE, lambda e, s=s: e.memset(self.VW1[s][:], 1.0), writes=[f"VW1{s}"])
            for g in range(2):
                S.dma(SP, lambda e, s=s, g=g: e.dma_start(out=self.KST[s][g][64:96, :], in_=self.cd["c_ehot"]), writes=[f"KST{s}{g}"])
                S.dma(SP, lambda e, s=s, g=g: e.dma_start(out=self.KST[s][g][96:100, :], in_=self.cd["c_kaug"]), writes=[f"KST{s}{g}"])
                S.op(DVE, lambda e, s=s, g=g: e.memset(self.KWT[s][g][64:96, :], 0.0), writes=[f"KWT{s}{g}"])
                S.dma(SP, lambda e, s=s, g=g: e.dma_start(out=self.KWT[s][g][96:100, :], in_=self.cd["c_kaug"]), writes=[f"KWT{s}{g}"])
                S.op(DVE, lambda e, s=s, g=g: e.memset(self.KCM[s][g][64:96, :], 0.0), writes=[f"KCM{s}{g}"])
                S.dma(SP, lambda e, s=s, g=g: e.dma_start(out=self.KCM[s][g][96:100, 0:127], in_=self.cd["c_caug"]), writes=[f"KCM{s}{g}"])
                S.dma(SP, lambda e, s=s, g=g: e.dma_start(out=self.VCO[s][g][0:127, 64:97], in_=self.cd["c_ovl"]), writes=[f"VCO{s}{g}"])

    def pass_N1(self):
        nc, S = self.nc, self.S
        TT = 256
        wkv = self.sb("wkv", [128, 8, 768], BF16)
        gcb = self.cload("gcb", [128, 1024], F32, self.g_mix)
        self.load_w(wkv, "wkv", self.w_in, O_KCR, 768, 8)
        w1 = [self.sb(f"w1_{k}", [64, 32, 256], BF16) for k in range(2)]
        w2 = [self.sb(f"w2_{k}", [128, 2, 64], BF16) for k in range(2)]
        cb1 = [self.sb(f"cb1_{k}", [128, 2], F32) for k in range(2)]
        for k in range(2):
            src = self.w1[k].rearrange("(p d) n -> d p n", d=64)
            for p0 in range(0, 32, 8):
                S.dma(POOL, lambda e, k=k, src=src, p0=p0: e.dma_start(out=w1[k][:, p0:p0 + 8, :], in_=src[:, p0:p0 + 8, :]),
                      writes=[f"w1_{k}"])
            self.load_w(w2[k], f"w2_{k}", self.w2[k], 0, 64, 2)
            b1 = self.cload(f"b1_{k}", [128, 2], F32, self.b1[k])
            pf = self.cload(f"posf_{k}", [64, 64], F32, self.posT[k])
            pb16 = self.sb(f"posb_{k}", [64, 64], BF16)
            S.op(DVE, lambda e, pf=pf, pb16=pb16: e.tensor_copy(out=pb16[:], in_=pf[:]), reads=[f"posf_{k}"], writes=[f"posb_{k}"])
            for nch in range(2):
                pb, pbk = self.banks.get()

                def mmc(e, pb=pb, k=k, nch=nch, pb16=pb16):
                    ins = None
                    for p in range(32):
                        ins = e.matmul(pb[:, 0:2], lhsT=w1[k][0:64, p, nch * 128:(nch + 1) * 128], rhs=pb16[0:64, 2 * p:2 * p + 2],
                                       start=(p == 0), stop=(p == 31))
                    return ins
                S.op(PE, mmc, reads=[f"w1_{k}", f"posb_{k}"], writes=[pbk])
                S.op(DVE, lambda e, pb=pb, k=k, nch=nch, b1=b1: e.tensor_tensor(
                    out=cb1[k][:, nch:nch + 1], in0=pb[:, 0:1], in1=b1[:, nch:nch + 1], op=ALU.add),
                    reads=[pbk, f"b1_{k}"], writes=[f"cb1_{k}"])
        CRT = [[self.sb(f"CRT{k}{g}", [64, S_LEN], BF16) for g in range(2)] for k in range(2)]
        hT = self.sb("hTN1", [128, 8, TT], BF16)
        hidT = self.sb("hidT", [128, 2, 128], BF16)
        for s in range(self.nseq):
            for t in range(S_LEN // TT):
                tok0 = s * S_LEN + t * TT
                pos0 = t * TT
                for sub in range(TT // 128):
                    self.rmsnorm_hT(self.x, tok0 + sub * 128, gcb, hT, "hTN1", sub * 128)
                dests = [(0, CRT[0], "CRT0"), (128, CRT[1], "CRT1"), (256, self.KST[s], f"KST{s}"), (512, self.KWT[s], f"KWT{s}")]
                for off, dst, dk in dests:
                    for g in range(2):
                        pb, pbk = self.banks.get()
                        S.op(PE, self.mm8(pb[0:64, 0:TT], lambda kc, off=off, g=g: wkv[:, kc, off + g * 64:off + (g + 1) * 64],
                                          lambda kc: hT[:, kc, :]), reads=["wkv", "hTN1"], writes=[pbk])
                        S.op(ACT, lambda e, pb=pb, dst=dst, g=g, pos0=pos0: e.copy(out=dst[g][0:64, pos0:pos0 + TT], in_=pb[0:64, 0:TT]),
                             reads=[pbk], writes=[f"{dk}{g}"])
                for sub in range(TT // 128):
                    cs = slice(sub * 128, (sub + 1) * 128)
                    kt = pos0 // 128 + sub
                    pb, pbk = self.banks.get()
                    S.op(PE, self.mm8(pb[:, 0:128], lambda kc, cs=cs: hT[:, kc, cs], lambda kc: wkv[:, kc, 384:512]),
                         reads=["wkv", "hTN1"], writes=[pbk])
                    S.op(PE, self.mm8(pb[:, 128:256], lambda kc, cs=cs: hT[:, kc, cs], lambda kc: wkv[:, kc, 640:768]),
                         reads=["wkv", "hTN1"], writes=[pbk])
                    S.op(ACT, lambda e, pb=pb, s=s, kt=kt: e.copy(out=self.VS1[s][:, kt, :, 0:64],
                                                                  in_=pb[:, 0:128].rearrange("p (g d) -> p g d", g=2)),
                         reads=[pbk], writes=[f"VS1{s}"])
                    S.op(ACT, lambda e, pb=pb, s=s, kt=kt: e.copy(out=self.VW1[s][:, kt, :, 0:64],
                                                                  in_=pb[:, 128:256].rearrange("p (g d) -> p g d", g=2)),
                         reads=[pbk], writes=[f"VW1{s}"])
            for k in range(2):
                for g in range(2):
                    for nch in range(2):
                        pb, pbk = self.banks.get()

                        def mmh(e, pb=pb, k=k, g=g, nch=nch):
                            ins = None
                            for p in range(32):
                                ins = e.matmul(pb[:, 0:127], lhsT=w1[k][0:64, p, nch * 128:(nch + 1) * 128],
                                               rhs=CRT[k][g][0:64, p:p + 2017:16], start=(p == 0), stop=(p == 31))
                            return ins
                        S.op(PE, mmh, reads=[f"w1_{k}", f"CRT{k}{g}"], writes=[pbk])
                        S.op(ACT, lambda e, pb=pb, k=k, nch=nch: e.activation(
                            out=hidT[:, nch, 0:127], in_=pb[:, 0:127], func=AF.Gelu_apprx_tanh, bias=cb1[k][:, nch:nch + 1]),
                            reads=[pbk, f"cb1_{k}"], writes=["hidT"])
                    pb, pbk = self.banks.get()
                    if k == 0:
                        S.op(PE, self.mm8(pb[0:64, 0:127], lambda nch: w2[0][:, nch, :], lambda nch: hidT[:, nch, 0:127], n=2),
                             reads=["w2_0", "hidT"], writes=[pbk])
                        S.op(ACT, lambda e, pb=pb, s=s, g=g: e.copy(out=self.KCM[s][g][0:64, 0:127], in_=pb[0:64, 0:127]),
                             reads=[pbk], writes=[f"KCM{s}{g}"])
                    else:
                        S.op(PE, self.mm8(pb[0:127, 0:64], lambda nch: hidT[:, nch, 0:127], lambda nch: w2[1][:, nch, :], n=2),
                             reads=["w2_1", "hidT"], writes=[pbk])
                        S.op(ACT, lambda e, pb=pb, s=s, g=g: e.copy(out=self.VCO[s][g][0:127, 0:64], in_=pb[0:127, 0:64]),
                             reads=[pbk], writes=[f"VCO{s}{g}"])

    def pass_N2(self):
        nc, S = self.nc, self.S
        TT = 256
        NS = TT // 128
        LA = 3
        allb = self.banks.items + self.accb.items
        gen = Ring(allb[0:2])
        scr = Ring(allb[2:5])
        accr = Ring(allb[5:7])
        wnq = self.sb("wnq", [128, 8, 512], BF16)
        wng = self.sb("wng", [128, 8, 24], BF16)
        wgb = self.sb("wgb", [128, 8, D], BF16)
        wno = self.sb("wno", [128, 4, D], BF16)
        wout = self.sb("wout", [128, 8, D], BF16)
        gcb = self.cload("gcb", [128, 1024], F32, self.g_mix)
        cmask = self.sb("cmask", [128, 2048], BF16)
        S.dma(SP, lambda e: e.dma_start(out=cmask[0:127, :], in_=self.cd["c_cmask"]), writes=["cmask"])
        caus = self.cload("caus", [128, 512], BF16, self.cd["c_caus"])
        far = self.cload("far", [128, 512], BF16, self.cd["c_far"])
        impA = self.cload("impA", [128, 512], F32, self.cd["c_impA"])
        impB = self.cload("impB", [128, 512], F32, self.cd["c_impB"])
        impNF = self.cload("impNF", [128, 512], F32, self.cd["c_impNF"])
        self.load_w(wnq, "wnq", self.w_in, O_NQ, 512, 8)
        self.load_w(wng, "wng", self.w_in, O_NG, 24, 8)
        self.load_w(wgb, "wgb", self.w_in, O_GB, D, 8)
        self.load_w(wno, "wno", self.w_nsa_o, 0, D, 4)
        self.load_w(wout, "wout", self.w_out, 0, D, 8)
        hTs = [self.sb(f"hTN2_{b}", [128, 8, TT], BF16) for b in range(2)]
        QTs = [[self.sb(f"QT{b}{g}", [100, 4, TT], BF16) for g in range(2)] for b in range(2)]
        for b in range(2):
            for g in range(2):
                S.op(DVE, lambda e, b=b, g=g: e.memset(QTs[b][g][64:96, :, :], 0.0), writes=[f"QT{b}{g}0", f"QT{b}{g}1"])
        SGs = [self.sb(f"SG{b}", [128, NS, 24], F32) for b in range(2)]
        sgbs = [self.sb(f"sgb{b}", [128, 8, TT], BF16) for b in range(2)]
        maTs = [self.sb(f"maTN{b}", [128, 8, TT], BF16) for b in range(2)]
        nso = self.sb("nso", [128, NS, 512], BF16)
        noT = self.sb("noT", [128, 4, TT], BF16)
        selr = Ring([(self.sb(f"SELB{i}", [128, 96], BF16), f"SELB{i}") for i in range(4)])
        cper = Ring([(self.sb(f"cpe{i}", [128, 512], BF16), f"cpe{i}") for i in range(4)])
        for t_, k_ in selr.items:
            S.op(DVE, lambda e, t_=t_: e.memset(t_[:], 0.0), writes=[k_])
        ONSs = [[self.sb(f"ONS{b}{sub}", [128, 8, 64], F32) for sub in range(NS)] for b in range(2)]
        per = Ring([(self.sb(f"pe{i}", [128, 512], BF16), f"pe{i}") for i in range(6)])
        rdr = Ring([(self.sb(f"rd{i}", [128, 8], F32), f"rd{i}") for i in range(8)])
        impr = Ring([(self.sb(f"imp{i}", [128, 40], F32), f"imp{i}") for i in range(4)])
        tmr = Ring([(self.sb(f"tmb{i}", [128, TT], F32), f"tmb{i}") for i in range(2)])
        qaug = self.cd["c_qaug"].rearrange("a (r t) -> a r t", r=4)
        mav = self.ma_scr.rearrange("p (c n) -> p c n", c=8)
        tiles = [(s, t) for s in range(self.nseq) for t in range(S_LEN // TT)]
        xts_of = {}

        def v3(ap):
            return ap.rearrange("p (r t) -> p r t", r=4)

        def prologue(n):
            s, t = tiles[n]
            b = n % 2
            tok0 = s * S_LEN + t * TT
            pos0 = t * TT
            hT, QT, SG, sgb, maT = hTs[b], QTs[b], SGs[b], sgbs[b], maTs[b]
            hk = f"hTN2_{b}"
            ops = []
            xts_of[n] = [None] * NS
            for sub in range(NS):
                def f(sub=sub):
                    xts_of[n][sub] = self.rmsnorm_hT(self.x, tok0 + sub * 128, gcb, hT, hk, sub * 128)
                ops.append(f)
            ops.append(lambda: S.dma(SP, lambda e: e.dma_start(out=maT[:], in_=mav[:, :, tok0:tok0 + TT]), writes=[f"maTN{b}"]))
            for g in range(2):
                ops.append(lambda g=g: S.dma(SP, lambda e: e.dma_start(out=QT[g][96:100, :, :], in_=qaug[g * 4:(g + 1) * 4, :, pos0:pos0 + TT]),
                                             writes=[f"QT{b}{g}0", f"QT{b}{g}1"]))
                for r in range(4):
                    def f(g=g, r=r):
                        pb, pbk = gen.get()
                        hh = g * 4 + r
                        S.op(PE, self.mm8(pb[0:64, 0:TT], lambda kc: wnq[:, kc, hh * 64:(hh + 1) * 64], lambda kc: hT[:, kc, :]),
                             reads=["wnq", hk], writes=[pbk])
                        S.op(DVE, lambda e: e.tensor_copy(out=QT[g][0:64, r, :], in_=pb[0:64, 0:TT]), reads=[pbk],
                             writes=[f"QT{b}{g}0", f"QT{b}{g}1"])
                    ops.append(f)
            for sub in range(NS):
                def f(sub=sub):
                    cs = slice(sub * 128, (sub + 1) * 128)
                    pb, pbk = gen.get()
                    S.op(PE, self.mm8(pb[:, 0:24], lambda kc: hT[:, kc, cs], lambda kc: wng[:, kc, :]), reads=["wng", hk], writes=[pbk])
                    S.op(ACT, lambda e: e.activation(out=SG[:, sub, :], in_=pb[:, 0:24], func=AF.Sigmoid), reads=[pbk], writes=[f"SG{b}"])
                ops.append(f)
            chains = []
            for sub in range(NS):
                for g in range(2):
                    chains.append(cmp_chain(n, s, t, b, sub, g))
            nst = max(len(c) for c in chains)
            for st_ in range(nst):
                for c in chains:
                    if st_ < len(c):
                        ops.append(c[st_])
            for cc in range(8):
                def f(cc=cc):
                    pb, pbk = gen.get()
                    S.op(PE, self.mm8(pb[:, 0:TT], lambda kc: wgb[:, kc, cc * 128:(cc + 1) * 128], lambda kc: hT[:, kc, :]),
                         reads=["wgb", hk], writes=[pbk])
                    S.op(ACT, lambda e: e.activation(out=sgb[:, cc, :], in_=pb[:, 0:TT], func=AF.Sigmoid), reads=[pbk], writes=[f"sgb{b}"])
                ops.append(f)
            return ops

        def cmp_chain(n, s, t, b, sub, g):
            i = t * NS + sub
            qs = slice(sub * 128, (sub + 1) * 128)
            QT, SG = QTs[b], SGs[b]
            ONS = ONSs[b][sub]
            onk = f"ONS{b}{sub}{g}"
            qk = f"QT{b}{g}{sub}"
            rhsQ = QT[g][0:100, :, qs]
            nb = min(127, 8 * i + 8)
            isl = slice(i * 32, (i + 1) * 32)
            st = {}

            def s0():
                st["psc"], st["psck"] = gen.get()
                st["pe"], st["pek"] = cper.get()
                psc, pe = st["psc"], st["pe"]
                S.op(PE, lambda e: e.matmul(v3(psc[0:nb, :]), lhsT=self.KCM[s][g][0:100, 0:nb], rhs=rhsQ, start=True, stop=True),
                     reads=[f"KCM{s}{g}", qk], writes=[st["psck"]])
                S.op(ACT, lambda e: e.activation(out=pe[0:nb, :], in_=psc[0:nb, :], func=AF.Exp, scale=0.125),
                     reads=[st["psck"]], writes=[st["pek"]])

            def s1():
                pe, pek = st["pe"], st["pek"]
                for r in range(4):
                    S.op(DVE, lambda e, r=r: e.tensor_tensor(out=pe[0:nb, r * 128:(r + 1) * 128], in0=pe[0:nb, r * 128:(r + 1) * 128],
                                                             in1=cmask[0:nb, i * 128:(i + 1) * 128], op=ALU.mult),
                         reads=[pek, "cmask"], writes=[pek])

            def s2():
                pe, pek = st["pe"], st["pek"]
                st["pcv"], st["pcvk"] = gen.get()
                pcv = st["pcv"]

                def pvc(e):
                    ins = None
                    for r in range(4):
                        ins = e.matmul(pcv[:, r * 97:(r + 1) * 97], lhsT=pe[0:nb, r * 128:(r + 1) * 128],
                                       rhs=self.VCO[s][g][0:nb, 0:97], start=True, stop=True, skip_group_check=True)
                    return ins
                S.op(PE, pvc, reads=[pek, f"VCO{s}{g}"], writes=[st["pcvk"]])
                pcv, pcvk = st["pcv"], st["pcvk"]
                rd, rdk = rdr.get()
                imp, impk = impr.get()
                st["imp"], st["impk"] = imp, impk
                S.op(DVE, lambda e: e.tensor_scalar(out=rd[:, 0:4], in0=pcv[:, 0:388].rearrange("p (r c) -> p r c", r=4)[:, :, 64],
                                                    scalar1=1e-30, scalar2=None, op0=ALU.add), reads=[pcvk], writes=[rdk])
                S.op(DVE, lambda e: e.reciprocal(out=rd[:, 0:4], in_=rd[:, 0:4]), reads=[rdk], writes=[rdk])
                S.op(DVE, lambda e: e.tensor_tensor(out=rd[:, 4:8], in0=rd[:, 0:4], in1=SG[:, sub, g * 4:g * 4 + 4], op=ALU.mult),
                     reads=[rdk, f"SG{b}"], writes=[rdk])
                for r in range(4):
                    S.op(DVE, lambda e, r=r: e.tensor_scalar(out=ONS[:, g * 4 + r, :], in0=pcv[:, r * 97:r * 97 + 64],
                                                             scalar1=rd[:, 4 + r:5 + r], scalar2=None, op0=ALU.mult),
                         reads=[pcvk, rdk], writes=[onk])
                    if r == 0:
                        S.op(DVE, lambda e: e.tensor_scalar(out=imp[:, 0:32], in0=pcv[:, 65:97], scalar1=rd[:, 0:1], scalar2=None,
                                                            op0=ALU.mult), reads=[pcvk, rdk], writes=[impk])
                    else:
                        S.op(DVE, lambda e, r=r: e.scalar_tensor_tensor(out=imp[:, 0:32], in0=pcv[:, r * 97 + 65:r * 97 + 97],
                                                                        scalar=rd[:, r:r + 1], in1=imp[:, 0:32], op0=ALU.mult, op1=ALU.add),
                             reads=[pcvk, rdk, impk], writes=[impk])

            def s3():
                imp, impk = st["imp"], st["impk"]
                S.op(DVE, lambda e: e.tensor_tensor(out=imp[:, 0:32], in0=imp[:, 0:32], in1=impA[:, isl], op=ALU.mult),
                     reads=[impk, "impA"], writes=[impk])
                S.op(DVE, lambda e: e.tensor_tensor(out=imp[:, 0:32], in0=imp[:, 0:32], in1=impB[:, isl], op=ALU.add),
                     reads=[impk, "impB"], writes=[impk])
                S.op(DVE, lambda e: e.max(out=imp[:, 32:40], in_=imp[:, 0:32]), reads=[impk], writes=[impk])
                S.op(DVE, lambda e: e.tensor_scalar(out=imp[:, 0:32], in0=imp[:, 0:32], scalar1=imp[:, 39:40], scalar2=None, op0=ALU.is_ge),
                     reads=[impk], writes=[impk])
                S.op(DVE, lambda e: e.tensor_tensor(out=imp[:, 0:32], in0=imp[:, 0:32], in1=impNF[:, isl], op=ALU.mult),
                     reads=[impk, "impNF"], writes=[impk])
                SELB, selk = selr.get()
                S.op(DVE, lambda e: e.tensor_scalar(out=SELB[:, 64:96], in0=imp[:, 0:32], scalar1=-1.0, scalar2=30000.0,
                                                    op0=ALU.add, op1=ALU.mult), reads=[impk], writes=[selk])
                st["SELB"], st["selk"] = SELB, selk

            def s4():
                SELB, selk = st["SELB"], st["selk"]
                pst, pstk = gen.get()
                S.op(PE, lambda e: e.matmul(pst[0:96, 0:128], lhsT=SELB[:, 0:96], rhs=self.ident[:, :], start=True, stop=True),
                     reads=[selk, "ident"], writes=[pstk])
                for r in range(4):
                    S.op(DVE, lambda e, r=r: e.tensor_copy(out=QT[g][64:96, r, qs], in_=pst[64:96, 0:128]), reads=[pstk], writes=[qk])
            return [s0, s1, s2, s3, s4]

        def pair_tasks(n):
            s, t = tiles[n]
            b = n % 2
            QT, SG = QTs[b], SGs[b]
            tasks = []
            for sub in range(NS):
                i = t * NS + sub
                qs = slice(sub * 128, (sub + 1) * 128)
                ONS = ONSs[b][sub]
                for g in range(2):
                    onk = f"ONS{b}{sub}{g}"
                    qk = f"QT{b}{g}{sub}"
                    rhsQ = QT[g][0:100, :, qs]
                    for (KT, kkey, V1, vkey, j0, goff, isw) in (
                            (self.KWT[s][g], f"KWT{s}{g}", self.VW1[s], f"VW1{s}", max(0, i - 4), 16, True),
                            (self.KST[s][g], f"KST{s}{g}", self.VS1[s], f"VS1{s}", 0, 8, False)):
                        acc = {}
                        for j in range(j0, i + 1):
                            tk = {}

                            def A(tk=tk, j=j, KT=KT, kkey=kkey, rhsQ=rhsQ, qk=qk):
                                tk["pss"], tk["pssk"] = scr.get()
                                pss = tk["pss"]
                                S.op(PE, lambda e: e.matmul(v3(pss[:, :]), lhsT=KT[0:100, j * 128:(j + 1) * 128], rhs=rhsQ,
                                                            start=True, stop=True), reads=[kkey, qk], writes=[tk["pssk"]])

                            def B(tk=tk, j=j, i=i, isw=isw):
                                tk["pe"], tk["pek"] = per.get()
                                pe, pek, pss = tk["pe"], tk["pek"], tk["pss"]
                                S.op(ACT, lambda e: e.activation(out=pe[:], in_=pss[:, :], func=AF.Exp, scale=0.125),
                                     reads=[tk["pssk"]], writes=[pek])
                                if j == i:
                                    S.op(DVE, lambda e: e.tensor_tensor(out=pe[:], in0=pe[:], in1=caus[:], op=ALU.mult),
                                         reads=[pek, "caus"], writes=[pek])
                                if isw and i >= 4 and j == i - 4:
                                    S.op(DVE, lambda e: e.tensor_tensor(out=pe[:], in0=pe[:], in1=far[:], op=ALU.mult),
                                         reads=[pek, "far"], writes=[pek])

                            def C(tk=tk, j=j, i=i, j0=j0, acc=acc, V1=V1, vkey=vkey, g=g, goff=goff, ONS=ONS, onk=onk, SG=SG, sub=sub, b=b):
                                if j == j0:
                                    acc["psv"], acc["psvk"] = accr.get()
                                psv, psvk, pe = acc["psv"], acc["psvk"], tk["pe"]

                                def pv(e):
                                    ins = None
                                    for r in range(4):
                                        ins = e.matmul(psv[:, r * 65:(r + 1) * 65], lhsT=pe[:, r * 128:(r + 1) * 128], rhs=V1[:, j, g, :],
                                                       start=(j == j0 and r == 0), stop=(j == i), skip_group_check=True)
                                    return ins
                                S.op(PE, pv, reads=[tk["pek"], vkey], writes=[psvk])
                                if j == i:
                                    rd, rdk = rdr.get()
                                    S.op(DVE, lambda e: e.reciprocal(out=rd[:, 0:4],
                                                                     in_=psv[:, 0:260].rearrange("p (r c) -> p r c", r=4)[:, :, 64]),
                                         reads=[psvk], writes=[rdk])
                                    S.op(DVE, lambda e: e.tensor_tensor(out=rd[:, 4:8], in0=rd[:, 0:4],
                                                                        in1=SG[:, sub, goff + g * 4:goff + g * 4 + 4], op=ALU.mult),
                                         reads=[rdk, f"SG{b}"], writes=[rdk])
                                    for r in range(4):
                                        S.op(DVE, lambda e, r=r: e.scalar_tensor_tensor(
                                            out=ONS[:, g * 4 + r, :], in0=psv[:, r * 65:r * 65 + 64], scalar=rd[:, 4 + r:5 + r],
                                            in1=ONS[:, g * 4 + r, :], op0=ALU.mult, op1=ALU.add), reads=[psvk, rdk, onk], writes=[onk])
                            tasks.append((A, B, C))
            return tasks

        def epilogue(n):
            s, t = tiles[n]
            b = n % 2
            tok0 = s * S_LEN + t * TT
            sgb, maT = sgbs[b], maTs[b]
            for sub in range(NS):
                qs = slice(sub * 128, (sub + 1) * 128)
                ONS = ONSs[b][sub]
                S.op(POOL, lambda e, ONS=ONS, sub=sub: e.tensor_copy(out=nso[:, sub, :], in_=ONS[:].rearrange("p h d -> p (h d)")),
                     reads=[f"ONS{b}{sub}0", f"ONS{b}{sub}1"], writes=["nso"])

                def tr(e, sub=sub):
                    ins = None
                    for k4 in range(4):
                        ins = e.transpose(out=self.psb[:, k4 * 128:(k4 + 1) * 128], in_=nso[:, sub, k4 * 128:(k4 + 1) * 128],
                                          identity=self.ident[:])
                    return ins
                S.op(PE, tr, reads=["nso", "ident"], writes=["psb"])
                S.op(DVE, lambda e, qs=qs: e.tensor_copy(out=noT[:, :, qs], in_=self.psb[:, 0:512].rearrange("p (k j) -> p k j", k=4)),
                     reads=["psb"], writes=["noT"])
            for cc in range(8):
                pb, pbk = gen.get()
                S.op(PE, self.mm8(pb[:, 0:TT], lambda k4, cc=cc: wno[:, k4, cc * 128:(cc + 1) * 128], lambda k4: noT[:, k4, :], n=4),
                     reads=["wno", "noT"], writes=[pbk])
                tm, tmk = tmr.get()
                S.op(DVE, lambda e, pb=pb, tm=tm, cc=cc: e.tensor_tensor(out=tm[:], in0=pb[:, 0:TT], in1=sgb[:, cc, :], op=ALU.mult),
                     reads=[pbk, f"sgb{b}"], writes=[tmk])
                S.op(DVE, lambda e, tm=tm, cc=cc: e.tensor_tensor(out=maT[:, cc, :], in0=tm[:], in1=maT[:, cc, :], op=ALU.add),
                     reads=[tmk, f"maTN{b}"], writes=[f"maTN{b}"])
            for sub in range(NS):
                xt, xk = xts_of[n][sub]
                cs = slice(sub * 128, (sub + 1) * 128)
                for half in range(2):
                    pb, pbk = gen.get()
                    S.op(PE, self.mm8(pb[:, :], lambda cc, cs=cs: maT[:, cc, cs], lambda cc, half=half: wout[:, cc, half * 512:(half + 1) * 512]),
                         reads=[f"maTN{b}", "wout"], writes=[pbk])
                    S.op(DVE, lambda e, pb=pb, xt=xt, half=half: e.tensor_tensor(
                        out=xt[:, half * 512:(half + 1) * 512], in0=xt[:, half * 512:(half + 1) * 512], in1=pb[:, :], op=ALU.add),
                        reads=[pbk, xk], writes=[xk])
                r0 = tok0 + sub * 128
                S.dma(SP, lambda e, xt=xt, r0=r0: e.dma_start(out=self.x1_scr[r0:r0 + 128, :], in_=xt[:]), reads=[xk])

        for f in prologue(0):
            f()
        for n in range(len(tiles)):
            pro = prologue(n + 1) if n + 1 < len(tiles) else []
            tasks = pair_tasks(n)
            steps = len(tasks) + LA
            pi = 0
            for k in range(steps):
                if k < len(tasks):
                    tasks[k][0]()
                    tasks[k][1]()
                if k - LA >= 0:
                    tasks[k - LA][2]()
                tgt = ((k + 1) * len(pro) + steps - 1) // steps
                while pi < min(tgt, len(pro)):
                    pro[pi]()
                    pi += 1
            while pi < len(pro):
                pro[pi]()
                pi += 1
            epilogue(n)

    def pass_F(self, src):
        nc, S = self.nc, self.S
        TT = 256
        wup = self.sb("wup", [128, 8, 2 * DFF], BF16)
        wdn = self.sb("wdn", [128, NFC, D], BF16)
        gcb = self.sb("gcbF", [128, 8 * 128], F32)
        gfin = self.sb("gfin", [128, D], F32)
        cw = self.sb("cw", [128, NFC * 3], F32)
        cb = self.sb("cb", [128, NFC], F32)
        halo = self.sb("halo", [128, NFC, 2], F32)
        hTs = [self.sb(f"hTF{b}", [128, 8, TT], BF16) for b in range(2)]
        uT = self.sb("uT", [128, NFC, TT], BF16)
        t1r = Ring([(self.sb(f"t1_{i}", [128, TT], F32), f"t1_{i}") for i in range(3)])
        ger = Ring([(self.sb(f"ge_{i}", [128, TT], F32), f"ge_{i}") for i in range(2)])
        osb = Ring([(self.sb(f"osb{i}", [128, D], F32), f"osb{i}") for i in range(2)])
        S.dma(SP, lambda e: e.dma_start(out=gcb[:], in_=self.g_ffn), writes=["gcb"])
        S.dma(SP, lambda e: e.dma_start(out=gfin[:], in_=self.g_fin), writes=["gfin"])
        S.dma(SP, lambda e: e.dma_start(out=cw[:], in_=self.convw), writes=["cw"])
        S.dma(SP, lambda e: e.dma_start(out=cb[:], in_=self.convb), writes=["cb"])
        self.load_w(wup, "wup", self.w_up, 0, 2 * DFF, 8)
        self.load_w(wdn, "wdn", self.w_down, 0, D, NFC)
        NTL = self.ntok // TT
        NSB = TT // 128
        xts_next = [self.rmsnorm_hT(src, sub * 128, gcb, hTs[0], "hTF0", sub * 128) for sub in range(NSB)]
        for t in range(NTL):
            tok0 = t * TT
            first = (tok0 % S_LEN) == 0
            hT = hTs[t % 2]
            hk = f"hTF{t % 2}"
            xts = xts_next
            xts_next = []
            pend = []
            deferred = None
            for fc in range(NFC):
                pa, pak = self.banks.get()
                pb, pbk = self.banks.get()

                def mm_a(e, pa=pa, fc=fc, hT=hT):
                    ins = None
                    for kc in range(8):
                        ins = e.matmul(pa[:, 2:2 + TT], lhsT=wup[:, kc, fc * 128:(fc + 1) * 128], rhs=hT[:, kc, :],
                                       start=(kc == 0), stop=(kc == 7))
                    return ins

                def mm_b(e, pb=pb, fc=fc, hT=hT):
                    ins = None
                    for kc in range(8):
                        ins = e.matmul(pb[:, 0:TT], lhsT=wup[:, kc, DFF + fc * 128:DFF + (fc + 1) * 128],
                                       rhs=hT[:, kc, :], start=(kc == 0), stop=(kc == 7))
                    return ins
                S.op(PE, mm_a, reads=["wup", hk], writes=[pak])
                S.op(PE, mm_b, reads=["wup", hk], writes=[pbk])
                if t + 1 < NTL:
                    nb_ = (t + 1) % 2
                    if fc in (2, 6):
                        pend.append(self.rms_a(src, tok0 + TT + (fc // 4) * 128))
                    if fc in (10, 14):
                        sub_ = (fc - 10) // 4
                        xts_next.append(self.rms_b(pend[sub_], gcb, hTs[nb_], f"hTF{nb_}", sub_ * 128))
                if first:
                    S.op(ACT, lambda e, pa=pa: e.memzero(pa[:, 0:2]), reads=[pak], writes=[pak])
                else:
                    S.op(ACT, lambda e, pa=pa, fc=fc: e.copy(out=pa[:, 0:2], in_=halo[:, fc, :]),
                         reads=[pak, "halo"], writes=[pak])
                S.op(ACT, lambda e, pa=pa, fc=fc: e.copy(out=halo[:, fc, :], in_=pa[:, TT:TT + 2]),
                     reads=[pak], writes=["halo"])
                t1, t1k = t1r.get()
                S.op(ACT, lambda e, pa=pa, fc=fc, t1=t1: e.activation(
                    out=t1[:], in_=pa[:, 2:2 + TT], func=AF.Copy, scale=cw[:, fc * 3 + 2:fc * 3 + 3]),
                    reads=[pak, "cw"], writes=[t1k])
                S.op(DVE, lambda e, pa=pa, fc=fc, t1=t1: e.scalar_tensor_tensor(
                    out=t1[:], in0=pa[:, 1:1 + TT], scalar=cw[:, fc * 3 + 1:fc * 3 + 2], in1=t1[:],
                    op0=ALU.mult, op1=ALU.add), reads=[pak, "cw", t1k], writes=[t1k])
                S.op(DVE, lambda e, pa=pa, fc=fc, t1=t1: e.scalar_tensor_tensor(
                    out=t1[:], in0=pa[:, 0:TT], scalar=cw[:, fc * 3:fc * 3 + 1], in1=t1[:],
                    op0=ALU.mult, op1=ALU.add), reads=[pak, "cw", t1k], writes=[t1k])

                def fin(fc=fc, t1=t1, t1k=t1k, pb=pb, pbk=pbk):
                    ge, gek = ger.get()
                    S.op(ACT, lambda e: e.activation(out=ge[:], in_=t1[:], func=AF.Gelu_apprx_tanh, bias=cb[:, fc:fc + 1]),
                         reads=[t1k, "cb"], writes=[gek])
                    S.op(DVE, lambda e: e.tensor_tensor(out=uT[:, fc, :], in0=ge[:], in1=pb[:, 0:TT], op=ALU.mult),
                         reads=[gek, pbk], writes=["uT"])
                if deferred is not None:
                    deferred()
                deferred = fin
            deferred()
            deferred = None
            for sub in range(TT // 128):
                xt, xk = xts[sub]
                ss, sk = self.ss.get()
                for half in range(2):
                    po, pok = self.banks.get()

                    def mm_o(e, po=po, sub=sub, half=half):
                        ins = None
                        for fc in range(NFC):
                            ins = e.matmul(po[:, :], lhsT=uT[:, fc, sub * 128:(sub + 1) * 128],
                                           rhs=wdn[:, fc, half * 512:(half + 1) * 512],
                                           start=(fc == 0), stop=(fc == NFC - 1))
                        return ins
                    S.op(PE, mm_o, reads=["uT", "wdn"], writes=[pok])
                    S.op(DVE, lambda e, po=po, xt=xt, half=half: e.tensor_tensor(
                        out=xt[:, half * 512:(half + 1) * 512], in0=xt[:, half * 512:(half + 1) * 512],
                        in1=po[:, :], op=ALU.add), reads=[pok, xk], writes=[xk])
                ob, obk = osb.get()
                S.op(ACT, lambda e, xt=xt, ss=ss: e.activation(out=self.junk[:], in_=xt[:], func=AF.Square,
                                                               accum_out=ss[:]), reads=[xk], writes=[sk])
                S.op(ACT, lambda e, ss=ss: e.activation(out=ss[:], in_=ss[:], func=AF.Sqrt, scale=1.0 / D,
                                                        bias=self.epsb[:]), reads=[sk, "epsb"], writes=[sk])
                S.op(DVE, lambda e, ss=ss: e.reciprocal(out=ss[:], in_=ss[:]), reads=[sk], writes=[sk])
                S.op(DVE, lambda e, xt=xt, ss=ss, ob=ob: e.scalar_tensor_tensor(
                    out=ob[:], in0=xt[:], scalar=ss[:, 0:1], in1=gfin[:], op0=ALU.mult, op1=ALU.mult),
                    reads=[xk, sk, "gfin"], writes=[obk])
                r0 = tok0 + sub * 128
                S.dma(SP, lambda e, ob=ob, r0=r0: e.dma_start(out=self.out[r0:r0 + 128, :], in_=ob[:]),
                      reads=[obk])


def host_inputs(inp, nseq, core):
    f = np.float32
    x = np.ascontiguousarray(inp["x"][core * nseq:(core + 1) * nseq].reshape(nseq * S_LEN, D))

    def gcol(g):
        return np.ascontiguousarray(np.broadcast_to(g.reshape(8, 128).T[:, :, None], (128, 8, 128)).reshape(128, 1024))
    m = {
        "x": x,
        "w_in": np.ascontiguousarray(inp["w_in"][0]),
        "w_up": np.ascontiguousarray(inp["w_up"][0]),
        "w_down": np.ascontiguousarray(inp["w_down"][0]),
        "g_ffn": gcol(inp["norm_ffn"][0]),
        "g_mix": gcol(inp["norm_mix"][0]),
        "g_fin": np.ascontiguousarray(np.broadcast_to(inp["norm_final"][None, :], (128, D))),
        "convw": np.ascontiguousarray(inp["conv_w"][0].reshape(3, NFC, 128).transpose(2, 1, 0).reshape(128, NFC * 3)),
        "convb": np.ascontiguousarray(inp["conv_b"][0].reshape(NFC, 128).T),
    }
    m.update({
        "w_ret_o": np.ascontiguousarray(inp["w_ret_o"][0]),
        "w_nsa_o": np.ascontiguousarray(inp["w_nsa_o"][0]),
        "w_out": np.ascontiguousarray(inp["w_out"][0]),
        "gng": gcol(inp["ret_gn_g"][0]),
        "w1k": np.ascontiguousarray(inp["cmp_w1_k"][0]),
        "w1v": np.ascontiguousarray(inp["cmp_w1_v"][0]),
        "w2k": np.ascontiguousarray(inp["cmp_w2_k"][0]),
        "w2v": np.ascontiguousarray(inp["cmp_w2_v"][0]),
        "b1k": np.ascontiguousarray(inp["cmp_b1_k"][0].reshape(2, 128).T),
        "b1v": np.ascontiguousarray(inp["cmp_b1_v"][0].reshape(2, 128).T),
        "posTk": np.ascontiguousarray(np.repeat(inp["cmp_pos_k"][0].T[:, :, None], 2, axis=2).reshape(64, 64)),
        "posTv": np.ascontiguousarray(np.repeat(inp["cmp_pos_v"][0].T[:, :, None], 2, axis=2).reshape(64, 64)),
    })
    m.update(make_consts())
    return m


_CACHE = {}


def kernel(**inputs):
    inp = {k: np.asarray(v) for k, v in inputs.items()}
    ncores, nseq = 8, 2
    if "prog" not in _CACHE:
        p = Prog(nseq=nseq)
        p.build()
        _CACHE["prog"] = p
    p = _CACHE["prog"]
    in_maps = []
    for c in range(ncores):
        m = host_inputs(inp, nseq, c)
        in_maps.append({k: m[k] for k in p.in_names})
    res = run_bass_kernel_spmd(p.nc, in_maps, core_ids=list(range(ncores)))
    out = np.concatenate([r["out"] for r in res.results], axis=0)
    return out.reshape(16, S_LEN, D).astype(np.float32)
```

```python
import contextlib
import numpy as np
STAGE = 9
import ml_dtypes
import concourse.bass as bass
import concourse.mybir as mybir
from concourse.bass_utils import run_bass_kernel_spmd

F32 = mybir.dt.float32
BF16 = mybir.dt.bfloat16
ALU = mybir.AluOpType
AF = mybir.ActivationFunctionType

PE, ACT, DVE, POOL, SP = "tensor", "scalar", "vector", "gpsimd", "sync"
ENGS = (PE, ACT, DVE, POOL, SP)
NDMASEM = 8

S_LEN = 2048
D = 1024
DFF = 2816
NFC = DFF // 128
EPS = 1e-6
N_IN = 6424
O_RQ, O_RK, O_RV, O_RG, O_NQ, O_KCR, O_VCR, O_KS, O_VS, O_KW, O_VW, O_NG, O_GA, O_GB = (
    0, 512, 1024, 2048, 3072, 3584, 3712, 3840, 3968, 4096, 4224, 4352, 4376, 5400)


class Op:
    __slots__ = ("eng", "fn", "reads", "writes", "is_dma", "waits", "inc", "ticket", "dsem", "dval")

    def __init__(self, eng, fn, reads, writes, is_dma):
        self.eng = eng
        self.fn = fn
        self.reads = reads
        self.writes = writes
        self.is_dma = is_dma
        self.waits = []
        self.inc = False
        self.ticket = None
        self.dsem = None
        self.dval = None


class Sched:
    def __init__(self, nc):
        self.nc = nc
        self.ops = []
        self.last_writer = {}
        self.readers = {}
        self.dma_count = {e: 0 for e in ENGS}
        self.dma_hist = {e: [] for e in ENGS}
        self.last_op = {e: None for e in ENGS}

    def op(self, eng, fn, reads=(), writes=()):
        o = Op(eng, fn, tuple(reads), tuple(writes), False)
        self._add(o)
        return o

    def dma(self, eng, fn, reads=(), writes=()):
        o = Op(eng, fn, tuple(reads), tuple(writes), True)
        i = self.dma_count[eng]
        self.dma_count[eng] += 1
        o.dsem = (eng, i % NDMASEM)
        o.dval = 16 * (i // NDMASEM + 1)
        self.dma_hist[eng].append(o)
        if i >= NDMASEM:
            o.waits.append(self.dma_hist[eng][i - NDMASEM])
        self._add(o)
        return o

    def barrier(self):
        tails = []
        for e in ENGS:
            if self.last_op[e] is not None and not self.last_op[e].is_dma:
                tails.append(self.last_op[e])
            tails.extend(self.dma_hist[e][-NDMASEM:])
        for e in ENGS:
            o = Op(e, None, (), (), False)
            o.waits = [t for t in tails]
            self.ops.append(o)
            self.last_op[e] = o
        self.last_writer = {}
        self.readers = {}

    def _add(self, o):
        self.ops.append(o)
        deps = o.waits
        for k in o.reads:
            w = self.last_writer.get(k)
            if w is not None:
                deps.append(w)
            if k.startswith("ps"):
                for r in self.readers.get(k, ()):
                    if r.eng != o.eng:
                        deps.append(r)
        for k in o.writes:
            w = self.last_writer.get(k)
            if w is not None and (w.is_dma or o.is_dma or w.eng != o.eng):
                deps.append(w)
            for r in self.readers.get(k, ()):
                if r.is_dma or o.is_dma or r.eng != o.eng:
                    deps.append(r)
        for k in o.reads:
            self.readers.setdefault(k, []).append(o)
        for k in o.writes:
            self.last_writer[k] = o
            self.readers[k] = []
        if not o.is_dma:
            self.last_op[o.eng] = o

    def run(self):
        nc = self.nc
        for o in self.ops:
            for d in o.waits:
                if not d.is_dma:
                    d.inc = True
        cnt = {e: 0 for e in ENGS}
        for o in self.ops:
            if o.fn is None:
                o.inc = False
            if not o.is_dma and o.inc:
                cnt[o.eng] += 1
                o.ticket = cnt[o.eng]
        seen = {e: {} for e in ENGS}
        plans = {e: [] for e in ENGS}
        for o in self.ops:
            e = o.eng
            wl = {}
            for d in o.waits:
                if d.is_dma:
                    key = ("d",) + d.dsem
                    val = d.dval
                else:
                    if d.ticket is None:
                        continue
                    key = ("c", d.eng)
                    val = d.ticket
                if seen[e].get(key, 0) >= val:
                    continue
                if wl.get(key, 0) < val:
                    wl[key] = val
            for key, val in wl.items():
                seen[e][key] = val
            plans[e].append((o, list(wl.items())))
        with contextlib.ExitStack() as st:
            sems = {}
            for e in ENGS:
                sems[("c", e)] = st.enter_context(nc.semaphore(f"c_{e}"))
                for j in range(min(NDMASEM, self.dma_count[e])):
                    sems[("d", e, j)] = st.enter_context(nc.semaphore(f"d_{e}_{j}"))
            block = st.enter_context(nc.Block())

            def mk(e):
                plan = plans[e]

                def body(eng):
                    for o, wl in plan:
                        for key, val in wl:
                            eng.wait_ge(sems[key], val)
                        if o.fn is None:
                            continue
                        ins = o.fn(eng)
                        if o.is_dma:
                            ins.then_inc(sems[("d",) + o.dsem], 16)
                        elif o.inc:
                            ins.then_inc(sems[("c", e)], 1)
                    n = self.dma_count[e]
                    for j in range(min(NDMASEM, n)):
                        eng.wait_ge(sems[("d", e, j)], 16 * ((n - 1 - j) // NDMASEM + 1))
                return body

            block.tensor(mk(PE))
            block.scalar(mk(ACT))
            block.vector(mk(DVE))
            block.gpsimd(mk(POOL))
            block.sync(mk(SP))


class Ring:
    def __init__(self, items):
        self.items = items
        self.i = 0

    def get(self):
        it = self.items[self.i % len(self.items)]
        self.i += 1
        return it


def make_consts():
    c = {}
    bf = ml_dtypes.bfloat16
    f = np.float32
    c["ident"] = np.eye(128, dtype=f).astype(bf)
    lg = np.log1p(-np.exp2(-5.0 - np.arange(4, dtype=np.float64)))
    pos = np.arange(128, dtype=np.float64)
    diff = pos[None, :] - pos[:, None]
    dmt = np.where(diff >= 0, np.exp(lg[:, None, None] * np.maximum(diff, 0.0)), 0.0)
    c["c_dmt"] = np.ascontiguousarray(dmt.transpose(1, 0, 2).reshape(128, 512)).astype(f)
    xi = np.exp(lg[:, None] * (pos + 1.0))
    xi2 = np.tile(xi, (1, 2))
    c["c_xi"] = np.ascontiguousarray(np.broadcast_to(xi2.reshape(1, 4 * 256), (128, 4 * 256))).astype(f)
    zeta = np.exp(lg[:, None] * (127 - pos)) * (128 ** -0.5)
    c["c_zeta"] = np.ascontiguousarray(np.repeat(zeta.T[:, :, None], 128, axis=2).reshape(128, 512)).astype(f)
    slopes = np.exp2(-(np.arange(8, dtype=np.float64) + 1.0)).reshape(2, 4)
    t = np.arange(S_LEN)
    qaug = np.zeros((2, 4, 4, S_LEN))
    for g in range(2):
        for r in range(4):
            sl = slopes[g, r]
            qaug[g, 0, r] = 8 * sl * 128
            qaug[g, 1, r] = 8 * sl
            qaug[g, 2, r] = -8 * sl * 128 * (t // 128)
            qaug[g, 3, r] = -8 * sl * (t % 128)
    c["c_qaug"] = qaug.reshape(2 * 4, 4 * S_LEN).astype(bf)
    kaug = np.stack([t // 128, t % 128, np.ones(S_LEN), np.ones(S_LEN)]).astype(np.float64)
    c["c_kaug"] = kaug.astype(bf)
    pc = 16 * np.arange(127) + 31
    caug = np.stack([pc // 128, pc % 128, np.ones(127), np.ones(127)]).astype(np.float64)
    c["c_caug"] = caug.astype(bf)
    c["c_ehot"] = (np.arange(32)[:, None] == (t[None, :] // 64)).astype(f).astype(bf)
    tt = (128 * np.arange(16)[:, None] + np.arange(128)[None, :])
    cm = (pc[:, None, None] <= tt[None, :, :]).astype(f)
    c["c_cmask"] = np.ascontiguousarray(cm.reshape(127, 16 * 128)).astype(bf)
    kb = np.arange(128)
    c["c_caus"] = np.ascontiguousarray(np.tile((kb[:, None] <= kb[None, :]).astype(f), (1, 4))).astype(bf)
    c["c_far"] = np.ascontiguousarray(np.tile((kb[:, None] > kb[None, :]).astype(f), (1, 4))).astype(bf)
    cur = tt // 64
    jj = np.arange(32)
    force = (jj[None, None, :] == 0) | (jj[None, None, :] == cur[:, :, None]) | (jj[None, None, :] == cur[:, :, None] - 1)
    fut = jj[None, None, :] > cur[:, :, None]
    A = 1.0 - force - fut
    Bm = 1e9 * force - 1e9 * fut
    NF = 1.0 - fut
    c["c_impA"] = np.ascontiguousarray(A.transpose(1, 0, 2).reshape(128, 16 * 32)).astype(f)
    c["c_impB"] = np.ascontiguousarray(Bm.transpose(1, 0, 2).reshape(128, 16 * 32)).astype(f)
    c["c_impNF"] = np.ascontiguousarray(NF.transpose(1, 0, 2).reshape(128, 16 * 32)).astype(f)
    cs = 16 * np.arange(127)
    js = 64 * np.arange(32)
    ovl = ((cs[:, None] < js[None, :] + 64) & (cs[:, None] + 32 > js[None, :])).astype(f)
    c["c_ovl"] = np.concatenate([np.ones((127, 1), f), ovl], axis=1).astype(bf)
    return c


class Prog:
    def __init__(self, nseq=2, passes="RNF", dbg=False):
        self.nseq = nseq
        self.ntok = nseq * S_LEN
        self.passes = passes
        self.dbg = dbg
        nc = self.nc = bass.Bass("TRN2", target_bir_lowering=False)
        self.S = Sched(nc)
        self.in_names = []

    def din(self, name, shape, dt=F32):
        self.in_names.append(name)
        return self.nc.dram_tensor(name, list(shape), dt, kind="ExternalInput").ap()

    def build(self):
        nc, S = self.nc, self.S
        NT = self.ntok
        self.x = self.din("x", [NT, D])
        self.w_in = self.din("w_in", [D, N_IN])
        self.w_up = self.din("w_up", [D, 2 * DFF])
        self.w_down = self.din("w_down", [DFF, D])
        self.g_ffn = self.din("g_ffn", [128, 8 * 128])
        self.g_mix = self.din("g_mix", [128, 8 * 128])
        self.g_fin = self.din("g_fin", [128, D])
        self.convw = self.din("convw", [128, NFC * 3])
        self.convb = self.din("convb", [128, NFC])
        self.ident_d = self.din("ident", [128, 128], BF16)
        self.w_ret_o = self.din("w_ret_o", [D, D])
        self.w_nsa_o = self.din("w_nsa_o", [512, D])
        self.w_out = self.din("w_out", [D, D])
        self.gng = self.din("gng", [128, 8 * 128])
        self.w1 = [self.din("w1k", [2048, 256]), self.din("w1v", [2048, 256])]
        self.w2 = [self.din("w2k", [256, 64]), self.din("w2v", [256, 64])]
        self.b1 = [self.din("b1k", [128, 2]), self.din("b1v", [128, 2])]
        self.posT = [self.din("posTk", [64, 64]), self.din("posTv", [64, 64])]
        self.cd = {}
        for k, v in make_consts().items():
            if k != "ident":
                self.cd[k] = self.din(k, v.shape, BF16 if v.dtype == ml_dtypes.bfloat16 else F32)
        self.out = nc.dram_tensor("out", [NT, D], F32, kind="ExternalOutput").ap()
        self.x1_scr = nc.dram_tensor("x1_scr", [NT, D], F32, kind="Internal").ap()
        self.ma_scr = nc.dram_tensor("ma_scr", [128, 8 * NT], BF16,
                                     kind="ExternalOutput" if self.dbg else "Internal").ap()

        with contextlib.ExitStack() as st:
            self.st = st
            banks = []
            for i in range(7):
                t = st.enter_context(nc.psum_tensor(f"psf{i}", [128, 512], F32))
                banks.append((t, f"psf{i}"))
            self.banks = Ring(banks[0:5])
            self.accb = Ring(banks[5:7])
            self.psb = st.enter_context(nc.psum_tensor("psb", [128, 1024], BF16))
            self.ident = self.sb("ident", [128, 128], BF16)
            self.epsb = self.sb("epsb", [128, 1], F32)
            S.dma(SP, lambda e: e.dma_start(out=self.ident[:], in_=self.ident_d), writes=["ident"])
            S.op(DVE, lambda e: e.memset(self.epsb[:], EPS), writes=["epsb"])
            self.junk = self.sb("junk", [128, D], BF16)
            self.xt = Ring([(self.sb(f"xt{i}", [128, D], F32), f"xt{i}") for i in range(4)])
            self.xn = Ring([(self.sb(f"xn{i}", [128, D], BF16), f"xn{i}") for i in range(3)])
            self.ss = Ring([(self.sb(f"ss{i}", [128, 1], F32), f"ss{i}") for i in range(4)])
            if "R" in self.passes:
                with contextlib.ExitStack() as st2:
                    self.st = st2
                    self.pass_R()
                    S.barrier()
            if "N" in self.passes:
                with contextlib.ExitStack() as st2:
                    self.st = st2
                    self.pass_N_alloc()
                    with contextlib.ExitStack() as st3:
                        self.st = st3
                        self.pass_N1()
                        S.barrier()
                    with contextlib.ExitStack() as st3:
                        self.st = st3
                        self.pass_N2()
                        S.barrier()
            if "F" in self.passes:
                with contextlib.ExitStack() as st2:
                    self.st = st2
                    self.pass_F(self.x1_scr if "N" in self.passes else self.x)
                    S.barrier()
            S.run()
        return nc

    def sb(self, name, shape, dt):
        self._uid = getattr(self, "_uid", 0) + 1
        return self.st.enter_context(self.nc.sbuf_tensor(f"s{self._uid}_{name}", list(shape), dt))

    def load_w(self, dst, dkey, src_rows, c0, ncols, kcs, dcol0=0, rows=128):
        S = self.S
        for kc in range(kcs):
            for cc in range(0, ncols, 2048):
                n = min(2048, ncols - cc)
                S.dma(POOL, lambda e, kc=kc, cc=cc, n=n: e.dma_start(
                    out=dst[0:rows, kc, dcol0 + cc:dcol0 + cc + n],
                    in_=src_rows[kc * rows:(kc + 1) * rows, c0 + cc:c0 + cc + n]), writes=[dkey])

    def cload(self, name, shape, dt, src, eng=SP):
        t = self.sb(name, shape, dt)
        self.S.dma(eng, lambda e: e.dma_start(out=t[:], in_=src), writes=[name])
        return t

    def mm8(self, out, lhs_fn, rhs_fn, n=8):
        def f(e):
            ins = None
            for kc in range(n):
                ins = e.matmul(out, lhsT=lhs_fn(kc), rhs=rhs_fn(kc), start=(kc == 0), stop=(kc == n - 1))
            return ins
        return f

    def pass_R(self):
        nc, S = self.nc, self.S
        TT = 256
        NS = TT // 128
        lg = np.log1p(-np.exp2(-5.0 - np.arange(4, dtype=np.float64)))
        decay = [float(np.exp(lg[h] * 128)) for h in range(4)]
        allb = self.banks.items + self.accb.items
        gen = Ring(allb[0:2])
        (pin, pink), (po0, po0k), (po1, po1k), (pk0, pk0k), (pk1, pk1k) = allb[2:7]
        pos_ = [(po0, po0k), (po1, po1k)]
        pks_ = [(pk0, pk0k), (pk1, pk1k)]
        wr = self.sb("wr", [128, 8, 3072], BF16)
        wga = self.sb("wga", [128, 8, D], BF16)
        wro = self.sb("wro", [128, 8, D], BF16)
        gcb = self.cload("gcb", [128, 1024], F32, self.g_mix)
        gngb = self.cload("gngb", [128, 1024], F32, self.gng)
        dmt = self.cload("dmt", [128, 512], F32, self.cd["c_dmt"])
        xi = self.cload("xi", [128, 4 * 256], F32, self.cd["c_xi"])
        zeta = self.cload("zeta", [128, 512], F32, self.cd["c_zeta"])
        self.load_w(wr, "wr", self.w_in, O_RQ, 3072, 8)
        self.load_w(wga, "wga", self.w_in, O_GA, D, 8)
        self.load_w(wro, "wro", self.w_ret_o, 0, D, 8)
        hTs = [self.sb(f"hTR{b}", [128, 8, TT], BF16) for b in range(2)]
        qTs = [self.sb(f"qT{b}", [128, 4, TT], BF16) for b in range(2)]
        qxTs = [self.sb(f"qxT{b}", [128, 4, TT], BF16) for b in range(2)]
        kTs = [self.sb(f"kT{b}", [128, 4, TT], BF16) for b in range(2)]
        kzs = [self.sb(f"kz{b}", [128, NS, 512], BF16) for b in range(2)]
        vs = [self.sb(f"v{b}", [128, NS, D], BF16) for b in range(2)]
        sgs = [self.sb(f"sg{b}", [128, NS, D], BF16) for b in range(2)]
        sga = self.sb("sga", [128, 8, TT], BF16)
        roT = self.sb("roT", [128, 8, TT], BF16)
        maT = self.sb("maT", [128, 8, TT], BF16)
        R = self.sb("R", [128, 4, 256], F32)
        Rb = self.sb("Rb", [128, 4, 256], BF16)
        inT = self.sb("inT4", [128, 512], BF16)
        on = self.sb("on4", [128, D], F32)
        stt = self.sb("stt", [128, 4, 6], F32)
        mv = self.sb("mv", [128, 4, 4], F32)
        NTL = self.ntok // TT

        def proj(t):
            b = t % 2
            hT, qT, qxT, kT, kz, v, sg = hTs[b], qTs[b], qxTs[b], kTs[b], kzs[b], vs[b], sgs[b]
            hk = f"hTR{b}"
            ops = []
            for h in range(4):
                def fq(h=h):
                    pq, pqk = gen.get()
                    S.op(PE, self.mm8(pq[:, 0:TT], lambda kc: wr[:, kc, O_RQ + h * 128:O_RQ + (h + 1) * 128], lambda kc: hT[:, kc, :]),
                         reads=["wr", hk], writes=[pqk])
                    S.op(ACT, lambda e: e.copy(out=qT[:, h, :], in_=pq[:, 0:TT]), reads=[pqk], writes=[f"qT{b}"])
                    S.op(DVE, lambda e: e.tensor_tensor(out=qxT[:, h, :], in0=pq[:, 0:TT], in1=xi[:, h * 256:(h + 1) * 256], op=ALU.mult),
                         reads=[pqk, "xi"], writes=[f"qxT{b}"])

                def fk(h=h):
                    pk, pkk = gen.get()
                    S.op(PE, self.mm8(pk[:, 0:TT], lambda kc: wr[:, kc, 512 + h * 128:512 + (h + 1) * 128], lambda kc: hT[:, kc, :]),
                         reads=["wr", hk], writes=[pkk])
                    S.op(ACT, lambda e: e.mul(out=kT[:, h, :], in_=pk[:, 0:TT], mul=128 ** -0.5), reads=[pkk], writes=[f"kT{b}"])
                ops += [fq, fk]
            for c in range(NS):
                cs = slice(c * 128, (c + 1) * 128)

                def fz(c=c, cs=cs):
                    pk, pkk = gen.get()
                    S.op(PE, self.mm8(pk[:, :], lambda kc: hT[:, kc, cs], lambda kc: wr[:, kc, 512:1024]), reads=["wr", hk], writes=[pkk])
                    S.op(DVE, lambda e: e.tensor_tensor(out=kz[:, c, :], in0=pk[:, :], in1=zeta[:], op=ALU.mult),
                         reads=[pkk, "zeta"], writes=[f"kz{b}"])
                ops.append(fz)
                for half in range(2):
                    def fv(c=c, cs=cs, half=half):
                        pv, pvk = gen.get()
                        S.op(PE, self.mm8(pv[:, :], lambda kc: hT[:, kc, cs], lambda kc: wr[:, kc, 1024 + half * 512:1024 + (half + 1) * 512]),
                             reads=["wr", hk], writes=[pvk])
                        S.op(ACT, lambda e: e.copy(out=v[:, c, half * 512:(half + 1) * 512], in_=pv[:, :]), reads=[pvk], writes=[f"v{b}"])

                    def fg(c=c, cs=cs, half=half):
                        pg, pgk = gen.get()
                        S.op(PE, self.mm8(pg[:, :], lambda kc: hT[:, kc, cs], lambda kc: wr[:, kc, 2048 + half * 512:2048 + (half + 1) * 512]),
                             reads=["wr", hk], writes=[pgk])
                        S.op(ACT, lambda e: e.activation(out=sg[:, c, half * 512:(half + 1) * 512], in_=pg[:, :], func=AF.Silu),
                             reads=[pgk], writes=[f"sg{b}"])
                    ops += [fv, fg]
            return ops

        def gates(t):
            b = t % 2
            hT = hTs[b]
            ops = []
            for cc in range(8):
                def f(cc=cc):
                    pga, pgak = gen.get()
                    S.op(PE, self.mm8(pga[:, 0:TT], lambda kc: wga[:, kc, cc * 128:(cc + 1) * 128], lambda kc: hT[:, kc, :]),
                         reads=["wga", f"hTR{b}"], writes=[pgak])
                    S.op(ACT, lambda e: e.activation(out=sga[:, cc, :], in_=pga[:, 0:TT], func=AF.Sigmoid), reads=[pgak], writes=["sga"])
                ops.append(f)
            return ops

        def chunk_stages(t):
            b = t % 2
            qT, qxT, kT, kz, v, sg = qTs[b], qxTs[b], kTs[b], kzs[b], vs[b], sgs[b]
            tok0 = t * TT
            stages = []
            for c in range(NS):
                cs = slice(c * 128, (c + 1) * 128)
                first = ((tok0 + c * 128) % S_LEN) == 0

                def st0(cs=cs):
                    def mmi(e):
                        ins = None
                        for h in range(4):
                            ins = e.matmul(pin[:, h * 128:(h + 1) * 128], lhsT=kT[:, h, cs], rhs=qT[:, h, cs], start=True, stop=True,
                                           skip_group_check=True)
                        return ins
                    S.op(PE, mmi, reads=[f"kT{b}", f"qT{b}"], writes=[pink])
                    S.op(DVE, lambda e: e.tensor_tensor(out=inT[:], in0=pin[:, :], in1=dmt[:], op=ALU.mult), reads=[pink, "dmt"], writes=["inT4"])

                def st1(c=c, cs=cs, first=first):
                    for hp in range(2):
                        po, pok = pos_[hp]

                        def mmo(e, po=po, hp=hp):
                            ins = None
                            for hh in range(2):
                                h = hp * 2 + hh
                                ins = e.matmul(po[:, hh * 256:(hh + 1) * 256], lhsT=inT[:, h * 128:(h + 1) * 128], rhs=v[:, c, h * 256:(h + 1) * 256],
                                               start=True, stop=first, skip_group_check=True)
                                if not first:
                                    ins = e.matmul(po[:, hh * 256:(hh + 1) * 256], lhsT=qxT[:, h, cs], rhs=Rb[:, h, :], start=False, stop=True,
                                                   skip_group_check=True)
                            return ins
                        S.op(PE, mmo, reads=["inT4", f"v{b}", f"qxT{b}", "Rb"], writes=[pok])
                    for hp in range(2):
                        pk, pkk = pks_[hp]

                        def mmk(e, pk=pk, hp=hp):
                            ins = None
                            for hh in range(2):
                                h = hp * 2 + hh
                                ins = e.matmul(pk[:, hh * 256:(hh + 1) * 256], lhsT=kz[:, c, h * 128:(h + 1) * 128], rhs=v[:, c, h * 256:(h + 1) * 256],
                                               start=True, stop=True, skip_group_check=True)
                            return ins
                        S.op(PE, mmk, reads=[f"kz{b}", f"v{b}"], writes=[pkk])

                def st2(first=first):
                    for h in range(4):
                        pk, pkk = pks_[h // 2]
                        src = pk[:, (h % 2) * 256:(h % 2 + 1) * 256]
                        if first:
                            S.op(DVE, lambda e, h=h, src=src: e.tensor_copy(out=R[:, h, :], in_=src), reads=[pkk], writes=["R"])
                        else:
                            S.op(DVE, lambda e, h=h, src=src: e.scalar_tensor_tensor(out=R[:, h, :], in0=R[:, h, :], scalar=decay[h], in1=src,
                                                                                     op0=ALU.mult, op1=ALU.add), reads=[pkk, "R"], writes=["R"])
                    S.op(ACT, lambda e: e.copy(out=Rb[:].rearrange("p h e -> p (h e)"), in_=R[:].rearrange("p h e -> p (h e)")),
                         reads=["R"], writes=["Rb"])
                    for h in range(4):
                        po, pok = pos_[h // 2]
                        S.op(DVE, lambda e, h=h, po=po: e.bn_stats(out=stt[:, h, :], in_=po[:, (h % 2) * 256:(h % 2 + 1) * 256]),
                             reads=[pok], writes=["stt"])
                    for h in range(4):
                        S.op(DVE, lambda e, h=h: e.bn_aggr(out=mv[:, h, 0:2], in_=stt[:, h, :]), reads=["stt"], writes=["mv"])
                    S.op(ACT, lambda e: e.activation(out=mv[:, :, 2], in_=mv[:, :, 1], func=AF.Sqrt, bias=self.epsb[:]),
                         reads=["mv", "epsb"], writes=["mv"])
                    S.op(DVE, lambda e: e.reciprocal(out=mv[:, :, 2], in_=mv[:, :, 2]), reads=["mv"], writes=["mv"])
                    S.op(DVE, lambda e: e.scalar_tensor_tensor(out=mv[:, :, 3], in0=mv[:, :, 0], scalar=-1.0, in1=mv[:, :, 2],
                                                               op0=ALU.mult, op1=ALU.mult), reads=["mv"], writes=["mv"])

                def st3(c=c):
                    for h in range(4):
                        po, pok = pos_[h // 2]
                        S.op(ACT, lambda e, h=h, po=po: e.activation(out=on[:, h * 256:(h + 1) * 256], in_=po[:, (h % 2) * 256:(h % 2 + 1) * 256],
                                                                     func=AF.Identity, scale=mv[:, h, 2:3], bias=mv[:, h, 3:4]),
                             reads=[pok, "mv"], writes=["on4"])
                    S.op(DVE, lambda e: e.tensor_tensor(out=sg[:, c, :], in0=on[:], in1=sg[:, c, :], op=ALU.mult),
                         reads=["on4", f"sg{b}"], writes=[f"sg{b}"])
                stages += [st0, st1, st2, st3]
            return stages

        def tail(t):
            b = t % 2
            sg = sgs[b]
            tok0 = t * TT
            for c in range(NS):
                cs = slice(c * 128, (c + 1) * 128)

                def tr(e, c=c):
                    ins = None
                    for kc in range(8):
                        ins = e.transpose(out=self.psb[:, kc * 128:(kc + 1) * 128], in_=sg[:, c, kc * 128:(kc + 1) * 128],
                                          identity=self.ident[:])
                    return ins
                S.op(PE, tr, reads=[f"sg{b}", "ident"], writes=["psb"])
                S.op(DVE, lambda e, cs=cs: e.tensor_tensor(out=roT[:, :, cs], in0=self.psb[:].rearrange("p (k j) -> p k j", k=8),
                                                           in1=gngb[:].rearrange("p (k j) -> p k j", k=8), op=ALU.mult),
                     reads=["psb", "gngb"], writes=["roT"])
            for cc in range(8):
                ccs = slice(cc * 128, (cc + 1) * 128)
                pya, pyak = gen.get()
                S.op(PE, self.mm8(pya[:, 0:TT], lambda kc, ccs=ccs: wro[:, kc, ccs], lambda kc: roT[:, kc, :]),
                     reads=["wro", "roT"], writes=[pyak])
                S.op(DVE, lambda e, pya=pya, cc=cc: e.tensor_tensor(out=maT[:, cc, :], in0=sga[:, cc, :], in1=pya[:, 0:TT], op=ALU.mult),
                     reads=[pyak, "sga"], writes=["maT"])
            S.dma(SP, lambda e: e.dma_start(out=self.ma_scr.rearrange("p (c n) -> p c n", c=8)[:, :, tok0:tok0 + TT], in_=maT[:]),
                  reads=["maT"])

        for sub in range(NS):
            self.rmsnorm_hT(self.x, sub * 128, gcb, hTs[0], "hTR0", sub * 128)
        for f in proj(0):
            f()
        for t in range(NTL):
            stages = chunk_stages(t)
            fill = gates(t)
            if t + 1 < NTL:
                nb_ = (t + 1) % 2
                pend = {}
                for sub in range(NS):
                    def fa(sub=sub, t=t):
                        pend[sub] = self.rms_a(self.x, (t + 1) * TT + sub * 128)
                    fill.append(fa)
                for sub in range(NS):
                    def fb(sub=sub, nb_=nb_):
                        self.rms_b(pend[sub], gcb, hTs[nb_], f"hTR{nb_}", sub * 128)
                    fill.append(fb)
                fill += proj(t + 1)
            fi = 0
            for k, stg_ in enumerate(stages):
                stg_()
                tgt = ((k + 1) * len(fill) + len(stages) - 1) // len(stages)
                while fi < min(tgt, len(fill)):
                    fill[fi]()
                    fi += 1
            while fi < len(fill):
                fill[fi]()
                fi += 1
            tail(t)

    def rms_a(self, src, r0):
        S = self.S
        xt, xk = self.xt.get()
        xn, nk = self.xn.get()
        ss, sk = self.ss.get()
        S.dma(SP, lambda e: e.dma_start(out=xt[:], in_=src[r0:r0 + 128, :]), writes=[xk])
        S.op(ACT, lambda e: e.activation(out=self.junk[:], in_=xt[:], func=AF.Square, accum_out=ss[:]),
             reads=[xk], writes=[sk])
        S.op(ACT, lambda e: e.activation(out=ss[:], in_=ss[:], func=AF.Sqrt, scale=1.0 / D, bias=self.epsb[:]),
             reads=[sk, "epsb"], writes=[sk])
        S.op(DVE, lambda e: e.reciprocal(out=ss[:], in_=ss[:]), reads=[sk], writes=[sk])
        S.op(DVE, lambda e: e.tensor_scalar(out=xn[:], in0=xt[:], scalar1=ss[:, 0:1], scalar2=None, op0=ALU.mult),
             reads=[xk, sk], writes=[nk])
        return (xt, xk, xn, nk)

    def rms_b(self, state, gcb, hT, hkey, col0):
        S = self.S
        xt, xk, xn, nk = state

        def tr(e):
            ins = None
            for kc in range(8):
                ins = e.transpose(out=self.psb[:, kc * 128:(kc + 1) * 128], in_=xn[:, kc * 128:(kc + 1) * 128],
                                  identity=self.ident[:])
            return ins
        S.op(PE, tr, reads=[nk, "ident"], writes=["psb"])
        S.op(DVE, lambda e: e.tensor_tensor(
            out=hT[:, :, col0:col0 + 128], in0=self.psb[:].rearrange("p (k j) -> p k j", k=8),
            in1=gcb[:].rearrange("p (k j) -> p k j", k=8), op=ALU.mult),
            reads=["psb", "gcb"], writes=[hkey])
        return xt, xk

    def rmsnorm_hT(self, src, r0, gcb, hT, hkey, col0, keep=None):
        return self.rms_b(self.rms_a(src, r0), gcb, hT, hkey, col0)

    def pass_N_alloc(self):
        S = self.S
        ns = self.nseq
        self.KST = [[self.sb(f"KST{s}{g}", [100, S_LEN], BF16) for g in range(2)] for s in range(ns)]
        self.KWT = [[self.sb(f"KWT{s}{g}", [100, S_LEN], BF16) for g in range(2)] for s in range(ns)]
        self.VS1 = [self.sb(f"VS1{s}", [128, 16, 2, 65], BF16) for s in range(ns)]
        self.VW1 = [self.sb(f"VW1{s}", [128, 16, 2, 65], BF16) for s in range(ns)]
        self.KCM = [[self.sb(f"KCM{s}{g}", [100, 128], BF16) for g in range(2)] for s in range(ns)]
        self.VCO = [[self.sb(f"VCO{s}{g}", [128, 97], BF16) for g in range(2)] for s in range(ns)]
        for s in range(ns):
            S.op(DVE, lambda e, s=s: e.memset(self.VS1[s][:], 1.0), writes=[f"VS1{s}"])
            S.op(DVE, lambda e, s=s: e.memset(self.VW1[s][:], 1.0), writes=[f"VW1{s}"])
            for g in range(2):
                S.dma(SP, lambda e, s=s, g=g: e.dma_start(out=self.KST[s][g][64:96, :], in_=self.cd["c_ehot"]), writes=[f"KST{s}{g}"])
                S.dma(SP, lambda e, s=s, g=g: e.dma_start(out=self.KST[s][g][96:100, :], in_=self.cd["c_kaug"]), writes=[f"KST{s}{g}"])
                S.op(DVE, lambda e, s=s, g=g: e.memset(self.KWT[s][g][64:96, :], 0.0), writes=[f"KWT{s}{g}"])
                S.dma(SP, lambda e, s=s, g=g: e.dma_start(out=self.KWT[s][g][96:100, :], in_=self.cd["c_kaug"]), writes=[f"KWT{s}{g}"])
                S.op(DVE, lambda e, s=s, g=g: e.memset(self.KCM[s][g][64:96, :], 0.0), writes=[f"KCM{s}{g}"])
                S.dma(SP, lambda e, s=s, g=g: e.dma_start(out=self.KCM[s][g][96:100, 0:127], in_=self.cd["c_caug"]), writes=[f"KCM{s}{g}"])
                S.dma(SP, lambda e, s=s, g=g: e.dma_start(out=self.VCO[s][g][0:127, 64:97], in_=self.cd["c_ovl"]), writes=[f"VCO{s}{g}"])

    def pass_N1(self):
        nc, S = self.nc, self.S
        TT = 256
        wkv = self.sb("wkv", [128, 8, 768], BF16)
        gcb = self.cload("gcb", [128, 1024], F32, self.g_mix)
        self.load_w(wkv, "wkv", self.w_in, O_KCR, 768, 8)
        w1 = [self.sb(f"w1_{k}", [64, 32, 256], BF16) for k in range(2)]
        w2 = [self.sb(f"w2_{k}", [128, 2, 64], BF16) for k in range(2)]
        cb1 = [self.sb(f"cb1_{k}", [128, 2], F32) for k in range(2)]
        for k in range(2):
            src = self.w1[k].rearrange("(p d) n -> d p n", d=64)
            for p0 in range(0, 32, 8):
                S.dma(POOL, lambda e, k=k, src=src, p0=p0: e.dma_start(out=w1[k][:, p0:p0 + 8, :], in_=src[:, p0:p0 + 8, :]),
                      writes=[f"w1_{k}"])
            self.load_w(w2[k], f"w2_{k}", self.w2[k], 0, 64, 2)
            b1 = self.cload(f"b1_{k}", [128, 2], F32, self.b1[k])
            pf = self.cload(f"posf_{k}", [64, 64], F32, self.posT[k])
            pb16 = self.sb(f"posb_{k}", [64, 64], BF16)
            S.op(DVE, lambda e, pf=pf, pb16=pb16: e.tensor_copy(out=pb16[:], in_=pf[:]), reads=[f"posf_{k}"], writes=[f"posb_{k}"])
            for nch in range(2):
                pb, pbk = self.banks.get()

                def mmc(e, pb=pb, k=k, nch=nch, pb16=pb16):
                    ins = None
                    for p in range(32):
                        ins = e.matmul(pb[:, 0:2], lhsT=w1[k][0:64, p, nch * 128:(nch + 1) * 128], rhs=pb16[0:64, 2 * p:2 * p + 2],
                                       start=(p == 0), stop=(p == 31))
                    return ins
                S.op(PE, mmc, reads=[f"w1_{k}", f"posb_{k}"], writes=[pbk])
                S.op(DVE, lambda e, pb=pb, k=k, nch=nch, b1=b1: e.tensor_tensor(
                    out=cb1[k][:, nch:nch + 1], in0=pb[:, 0:1], in1=b1[:, nch:nch + 1], op=ALU.add),
                    reads=[pbk, f"b1_{k}"], writes=[f"cb1_{k}"])
        CRT = [[self.sb(f"CRT{k}{g}", [64, S_LEN], BF16) for g in range(2)] for k in range(2)]
        hT = self.sb("hTN1", [128, 8, TT], BF16)
        hidT = self.sb("hidT", [128, 2, 128], BF16)
        for s in range(self.nseq):
            for t in range(S_LEN // TT):
                tok0 = s * S_LEN + t * TT
                pos0 = t * TT
                for sub in range(TT // 128):
                    self.rmsnorm_hT(self.x, tok0 + sub * 128, gcb, hT, "hTN1", sub * 128)
                dests = [(0, CRT[0], "CRT0"), (128, CRT[1], "CRT1"), (256, self.KST[s], f"KST{s}"), (512, self.KWT[s], f"KWT{s}")]
                for off, dst, dk in dests:
                    for g in range(2):
                        pb, pbk = self.banks.get()
                        S.op(PE, self.mm8(pb[0:64, 0:TT], lambda kc, off=off, g=g: wkv[:, kc, off + g * 64:off + (g + 1) * 64],
                                          lambda kc: hT[:, kc, :]), reads=["wkv", "hTN1"], writes=[pbk])
                        S.op(ACT, lambda e, pb=pb, dst=dst, g=g, pos0=pos0: e.copy(out=dst[g][0:64, pos0:pos0 + TT], in_=pb[0:64, 0:TT]),
                             reads=[pbk], writes=[f"{dk}{g}"])
                for sub in range(TT // 128):
                    cs = slice(sub * 128, (sub + 1) * 128)
                    kt = pos0 // 128 + sub
                    pb, pbk = self.banks.get()
                    S.op(PE, self.mm8(pb[:, 0:128], lambda kc, cs=cs: hT[:, kc, cs], lambda kc: wkv[:, kc, 384:512]),
                         reads=["wkv", "hTN1"], writes=[pbk])
                    S.op(PE, self.mm8(pb[:, 128:256], lambda kc, cs=cs: hT[:, kc, cs], lambda kc: wkv[:, kc, 640:768]),
                         reads=["wkv", "hTN1"], writes=[pbk])
                    S.op(ACT, lambda e, pb=pb, s=s, kt=kt: e.copy(out=self.VS1[s][:, kt, :, 0:64],
                                                                  in_=pb[:, 0:128].rearrange("p (g d) -> p g d", g=2)),
                         reads=[pbk], writes=[f"VS1{s}"])
                    S.op(ACT, lambda e, pb=pb, s=s, kt=kt: e.copy(out=self.VW1[s][:, kt, :, 0:64],
                                                                  in_=pb[:, 128:256].rearrange("p (g d) -> p g d", g=2)),
                         reads=[pbk], writes=[f"VW1{s}"])
            for k in range(2):
                for g in range(2):
                    for nch in range(2):
                        pb, pbk = self.banks.get()

                        def mmh(e, pb=pb, k=k, g=g, nch=nch):
                            ins = None
                            for p in range(32):
                                ins = e.matmul(pb[:, 0:127], lhsT=w1[k][0:64, p, nch * 128:(nch + 1) * 128],
                                               rhs=CRT[k][g][0:64, p:p + 2017:16], start=(p == 0), stop=(p == 31))
                            return ins
                        S.op(PE, mmh, reads=[f"w1_{k}", f"CRT{k}{g}"], writes=[pbk])
                        S.op(ACT, lambda e, pb=pb, k=k, nch=nch: e.activation(
                            out=hidT[:, nch, 0:127], in_=pb[:, 0:127], func=AF.Gelu_apprx_tanh, bias=cb1[k][:, nch:nch + 1]),
                            reads=[pbk, f"cb1_{k}"], writes=["hidT"])
                    pb, pbk = self.banks.get()
                    if k == 0:
                        S.op(PE, self.mm8(pb[0:64, 0:127], lambda nch: w2[0][:, nch, :], lambda nch: hidT[:, nch, 0:127], n=2),
                             reads=["w2_0", "hidT"], writes=[pbk])
                        S.op(ACT, lambda e, pb=pb, s=s, g=g: e.copy(out=self.KCM[s][g][0:64, 0:127], in_=pb[0:64, 0:127]),
                             reads=[pbk], writes=[f"KCM{s}{g}"])
                    else:
                        S.op(PE, self.mm8(pb[0:127, 0:64], lambda nch: hidT[:, nch, 0:127], lambda nch: w2[1][:, nch, :], n=2),
                             reads=["w2_1", "hidT"], writes=[pbk])
                        S.op(ACT, lambda e, pb=pb, s=s, g=g: e.copy(out=self.VCO[s][g][0:127, 0:64], in_=pb[0:127, 0:64]),
                             reads=[pbk], writes=[f"VCO{s}{g}"])

    def pass_N2(self):
        nc, S = self.nc, self.S
        TT = 256
        NS = TT // 128
        LA = 3
        allb = self.banks.items + self.accb.items
        gen = Ring(allb[0:2])
        scr = Ring(allb[2:5])
        accr = Ring(allb[5:7])
        wnq = self.sb("wnq", [128, 8, 512], BF16)
        wng = self.sb("wng", [128, 8, 24], BF16)
        wgb = self.sb("wgb", [128, 8, D], BF16)
        wno = self.sb("wno", [128, 4, D], BF16)
        wout = self.sb("wout", [128, 8, D], BF16)
        gcb = self.cload("gcb", [128, 1024], F32, self.g_mix)
        cmask = self.sb("cmask", [128, 2048], BF16)
        S.dma(SP, lambda e: e.dma_start(out=cmask[0:127, :], in_=self.cd["c_cmask"]), writes=["cmask"])
        caus = self.cload("caus", [128, 512], BF16, self.cd["c_caus"])
        far = self.cload("far", [128, 512], BF16, self.cd["c_far"])
        impA = self.cload("impA", [128, 512], F32, self.cd["c_impA"])
        impB = self.cload("impB", [128, 512], F32, self.cd["c_impB"])
        impNF = self.cload("impNF", [128, 512], F32, self.cd["c_impNF"])
        self.load_w(wnq, "wnq", self.w_in, O_NQ, 512, 8)
        self.load_w(wng, "wng", self.w_in, O_NG, 24, 8)
        self.load_w(wgb, "wgb", self.w_in, O_GB, D, 8)
        self.load_w(wno, "wno", self.w_nsa_o, 0, D, 4)
        self.load_w(wout, "wout", self.w_out, 0, D, 8)
        hTs = [self.sb(f"hTN2_{b}", [128, 8, TT], BF16) for b in range(2)]
        QTs = [[self.sb(f"QT{b}{g}", [100, 4, TT], BF16) for g in range(2)] for b in range(2)]
        for b in range(2):
            for g in range(2):
                S.op(DVE, lambda e, b=b, g=g: e.memset(QTs[b][g][64:96, :, :], 0.0), writes=[f"QT{b}{g}0", f"QT{b}{g}1"])
        SGs = [self.sb(f"SG{b}", [128, NS, 24], F32) for b in range(2)]
        sgbs = [self.sb(f"sgb{b}", [128, 8, TT], BF16) for b in range(2)]
        maTs = [self.sb(f"maTN{b}", [128, 8, TT], BF16) for b in range(2)]
        nso = self.sb("nso", [128, NS, 512], BF16)
        noT = self.sb("noT", [128, 4, TT], BF16)
        selr = Ring([(self.sb(f"SELB{i}", [128, 96], BF16), f"SELB{i}") for i in range(4)])
        cper = Ring([(self.sb(f"cpe{i}", [128, 512], BF16), f"cpe{i}") for i in range(4)])
        for t_, k_ in selr.items:
            S.op(DVE, lambda e, t_=t_: e.memset(t_[:], 0.0), writes=[k_])
        ONSs = [[self.sb(f"ONS{b}{sub}", [128, 8, 64], F32) for sub in range(NS)] for b in range(2)]
        per = Ring([(self.sb(f"pe{i}", [128, 512], BF16), f"pe{i}") for i in range(6)])
        rdr = Ring([(self.sb(f"rd{i}", [128, 8], F32), f"rd{i}") for i in range(8)])
        impr = Ring([(self.sb(f"imp{i}", [128, 40], F32), f"imp{i}") for i in range(4)])
        it4r = Ring([(self.sb(f"it4_{i}", [128, 4, 32], F32), f"it4_{i}") for i in range(2)])
        ftr = Ring([(self.sb(f"ft{i}", [128, 4, 64], F32), f"ft{i}") for i in range(2)])
        tmr = Ring([(self.sb(f"tmb{i}", [128, TT], F32), f"tmb{i}") for i in range(4)])
        qaug = self.cd["c_qaug"].rearrange("a (r t) -> a r t", r=4)
        mav = self.ma_scr.rearrange("p (c n) -> p c n", c=8)
        tiles = [(s, t) for s in range(self.nseq) for t in range(S_LEN // TT)]
        xts_of = {}

        def v3(ap):
            return ap.rearrange("p (r t) -> p r t", r=4)

        def prologue(n):
            s, t = tiles[n]
            b = n % 2
            tok0 = s * S_LEN + t * TT
            pos0 = t * TT
            hT, QT, SG, sgb, maT = hTs[b], QTs[b], SGs[b], sgbs[b], maTs[b]
            hk = f"hTN2_{b}"
            ops = []
            xts_of[n] = [None] * NS
            for sub in range(NS):
                def f(sub=sub):
                    xts_of[n][sub] = self.rmsnorm_hT(self.x, tok0 + sub * 128, gcb, hT, hk, sub * 128)
                ops.append(f)
            ops.append(lambda: S.dma(SP, lambda e: e.dma_start(out=maT[:], in_=mav[:, :, tok0:tok0 + TT]), writes=[f"maTN{b}"]))
            for g in range(2):
                ops.append(lambda g=g: S.dma(SP, lambda e: e.dma_start(out=QT[g][96:100, :, :], in_=qaug[g * 4:(g + 1) * 4, :, pos0:pos0 + TT]),
                                             writes=[f"QT{b}{g}0", f"QT{b}{g}1"]))
                for r in range(4):
                    def f(g=g, r=r):
                        pb, pbk = gen.get()
                        hh = g * 4 + r
                        S.op(PE, self.mm8(pb[0:64, 0:TT], lambda kc: wnq[:, kc, hh * 64:(hh + 1) * 64], lambda kc: hT[:, kc, :]),
                             reads=["wnq", hk], writes=[pbk])
                        S.op(DVE, lambda e: e.tensor_copy(out=QT[g][0:64, r, :], in_=pb[0:64, 0:TT]), reads=[pbk],
                             writes=[f"QT{b}{g}0", f"QT{b}{g}1"])
                    ops.append(f)
            for sub in range(NS):
                def f(sub=sub):
                    cs = slice(sub * 128, (sub + 1) * 128)
                    pb, pbk = gen.get()
                    S.op(PE, self.mm8(pb[:, 0:24], lambda kc: hT[:, kc, cs], lambda kc: wng[:, kc, :]), reads=["wng", hk], writes=[pbk])
                    S.op(ACT, lambda e: e.activation(out=SG[:, sub, :], in_=pb[:, 0:24], func=AF.Sigmoid), reads=[pbk], writes=[f"SG{b}"])
                ops.append(f)
            chains = []
            for sub in range(NS):
                for g in range(2):
                    chains.append(cmp_chain(n, s, t, b, sub, g))
            nst = max(len(c) for c in chains)
            for st_ in range(nst):
                for c in chains:
                    if st_ < len(c):
                        ops.append(c[st_])
            for cc in range(8):
                def f(cc=cc):
                    pb, pbk = gen.get()
                    S.op(PE, self.mm8(pb[:, 0:TT], lambda kc: wgb[:, kc, cc * 128:(cc + 1) * 128], lambda kc: hT[:, kc, :]),
                         reads=["wgb", hk], writes=[pbk])
                    S.op(ACT, lambda e: e.activation(out=sgb[:, cc, :], in_=pb[:, 0:TT], func=AF.Sigmoid), reads=[pbk], writes=[f"sgb{b}"])
                ops.append(f)
            return ops

        def cmp_chain(n, s, t, b, sub, g):
            i = t * NS + sub
            qs = slice(sub * 128, (sub + 1) * 128)
            QT, SG = QTs[b], SGs[b]
            ONS = ONSs[b][sub]
            onk = f"ONS{b}{sub}{g}"
            qk = f"QT{b}{g}{sub}"
            rhsQ = QT[g][0:100, :, qs]
            nb = min(127, 8 * i + 8)
            isl = slice(i * 32, (i + 1) * 32)
            st = {}

            def s0():
                st["psc"], st["psck"] = gen.get()
                st["pe"], st["pek"] = cper.get()
                psc, pe = st["psc"], st["pe"]
                S.op(PE, lambda e: e.matmul(v3(psc[0:nb, :]), lhsT=self.KCM[s][g][0:100, 0:nb], rhs=rhsQ, start=True, stop=True),
                     reads=[f"KCM{s}{g}", qk], writes=[st["psck"]])
                S.op(ACT, lambda e: e.activation(out=pe[0:nb, :], in_=psc[0:nb, :], func=AF.Exp, scale=0.125),
                     reads=[st["psck"]], writes=[st["pek"]])

            def s1():
                pe, pek = st["pe"], st["pek"]
                S.op(DVE, lambda e: e.tensor_tensor(out=v3(pe[0:nb, :]), in0=v3(pe[0:nb, :]),
                                                    in1=cmask[0:nb, i * 128:(i + 1) * 128].unsqueeze(1).to_broadcast([nb, 4, 128]), op=ALU.mult),
                     reads=[pek, "cmask"], writes=[pek])

            def s2():
                pe, pek = st["pe"], st["pek"]
                st["pcv"], st["pcvk"] = gen.get()
                pcv = st["pcv"]

                def pvc(e):
                    ins = None
                    for r in range(4):
                        ins = e.matmul(pcv[:, r * 97:(r + 1) * 97], lhsT=pe[0:nb, r * 128:(r + 1) * 128],
                                       rhs=self.VCO[s][g][0:nb, 0:97], start=True, stop=True, skip_group_check=True)
                    return ins
                S.op(PE, pvc, reads=[pek, f"VCO{s}{g}"], writes=[st["pcvk"]])
                pcv, pcvk = st["pcv"], st["pcvk"]
                rd, rdk = rdr.get()
                imp, impk = impr.get()
                st["imp"], st["impk"] = imp, impk
                S.op(DVE, lambda e: e.tensor_scalar(out=rd[:, 0:4], in0=pcv[:, 0:388].rearrange("p (r c) -> p r c", r=4)[:, :, 64],
                                                    scalar1=1e-30, scalar2=None, op0=ALU.add), reads=[pcvk], writes=[rdk])
                S.op(DVE, lambda e: e.reciprocal(out=rd[:, 0:4], in_=rd[:, 0:4]), reads=[rdk], writes=[rdk])
                S.op(DVE, lambda e: e.tensor_tensor(out=rd[:, 4:8], in0=rd[:, 0:4], in1=SG[:, sub, g * 4:g * 4 + 4], op=ALU.mult),
                     reads=[rdk, f"SG{b}"], writes=[rdk])
                pcv3 = pcv[:, 0:388].rearrange("p (r c) -> p r c", r=4)
                S.op(DVE, lambda e: e.tensor_tensor(out=ONS[:, g * 4:(g + 1) * 4, :], in0=pcv3[:, :, 0:64],
                                                    in1=rd[:, 4:8].unsqueeze(2).to_broadcast([128, 4, 64]), op=ALU.mult),
                     reads=[pcvk, rdk], writes=[onk])
                it4, it4k = it4r.get()
                S.op(DVE, lambda e: e.tensor_tensor(out=it4[:], in0=pcv3[:, :, 65:97],
                                                    in1=rd[:, 0:4].unsqueeze(2).to_broadcast([128, 4, 32]), op=ALU.mult),
                     reads=[pcvk, rdk], writes=[it4k])
                S.op(DVE, lambda e: e.tensor_reduce(out=imp[:, 0:32], in_=it4[:].rearrange("p r j -> p j r"), axis=mybir.AxisListType.X,
                                                    op=ALU.add), reads=[it4k], writes=[impk])

            def s3():
                imp, impk = st["imp"], st["impk"]
                S.op(DVE, lambda e: e.tensor_tensor(out=imp[:, 0:32], in0=imp[:, 0:32], in1=impA[:, isl], op=ALU.mult),
                     reads=[impk, "impA"], writes=[impk])
                S.op(DVE, lambda e: e.tensor_tensor(out=imp[:, 0:32], in0=imp[:, 0:32], in1=impB[:, isl], op=ALU.add),
                     reads=[impk, "impB"], writes=[impk])
                S.op(DVE, lambda e: e.max(out=imp[:, 32:40], in_=imp[:, 0:32]), reads=[impk], writes=[impk])
                S.op(DVE, lambda e: e.scalar_tensor_tensor(out=imp[:, 0:32], in0=imp[:, 0:32], scalar=imp[:, 39:40], in1=impNF[:, isl],
                                                           op0=ALU.is_ge, op1=ALU.mult), reads=[impk, "impNF"], writes=[impk])
                SELB, selk = selr.get()
                S.op(DVE, lambda e: e.tensor_scalar(out=SELB[:, 64:96], in0=imp[:, 0:32], scalar1=-1.0, scalar2=30000.0,
                                                    op0=ALU.add, op1=ALU.mult), reads=[impk], writes=[selk])
                st["SELB"], st["selk"] = SELB, selk

            def s4():
                SELB, selk = st["SELB"], st["selk"]
                pst, pstk = gen.get()
                S.op(PE, lambda e: e.matmul(pst[0:96, 0:128], lhsT=SELB[:, 0:96], rhs=self.ident[:, :], start=True, stop=True),
                     reads=[selk, "ident"], writes=[pstk])
                S.op(DVE, lambda e: e.tensor_copy(out=QT[g][64:96, :, qs], in_=pst[64:96, 0:128].unsqueeze(1).to_broadcast([32, 4, 128])),
                     reads=[pstk], writes=[qk])
            return [s0, s1, s2, s3, s4]

        def pair_tasks(n):
            s, t = tiles[n]
            b = n % 2
            QT, SG = QTs[b], SGs[b]
            tasks = []
            for sub in range(NS):
                i = t * NS + sub
                qs = slice(sub * 128, (sub + 1) * 128)
                ONS = ONSs[b][sub]
                for g in range(2):
                    onk = f"ONS{b}{sub}{g}"
                    qk = f"QT{b}{g}{sub}"
                    rhsQ = QT[g][0:100, :, qs]
                    for (KT, kkey, V1, vkey, j0, goff, isw) in (
                            (self.KWT[s][g], f"KWT{s}{g}", self.VW1[s], f"VW1{s}", max(0, i - 4), 16, True),
                            (self.KST[s][g], f"KST{s}{g}", self.VS1[s], f"VS1{s}", 0, 8, False)):
                        acc = {}
                        for j in range(j0, i + 1):
                            tk = {}

                            def A(tk=tk, j=j, KT=KT, kkey=kkey, rhsQ=rhsQ, qk=qk):
                                tk["pss"], tk["pssk"] = scr.get()
                                pss = tk["pss"]
                                S.op(PE, lambda e: e.matmul(v3(pss[:, :]), lhsT=KT[0:100, j * 128:(j + 1) * 128], rhs=rhsQ,
                                                            start=True, stop=True), reads=[kkey, qk], writes=[tk["pssk"]])

                            def B(tk=tk, j=j, i=i, isw=isw):
                                tk["pe"], tk["pek"] = per.get()
                                pe, pek, pss = tk["pe"], tk["pek"], tk["pss"]
                                S.op(ACT, lambda e: e.activation(out=pe[:], in_=pss[:, :], func=AF.Exp, scale=0.125),
                                     reads=[tk["pssk"]], writes=[pek])
                                if j == i:
                                    S.op(DVE, lambda e: e.tensor_tensor(out=pe[:], in0=pe[:], in1=caus[:], op=ALU.mult),
                                         reads=[pek, "caus"], writes=[pek])
                                if isw and i >= 4 and j == i - 4:
                                    S.op(DVE, lambda e: e.tensor_tensor(out=pe[:], in0=pe[:], in1=far[:], op=ALU.mult),
                                         reads=[pek, "far"], writes=[pek])

                            def C(tk=tk, j=j, i=i, j0=j0, acc=acc, V1=V1, vkey=vkey, g=g, goff=goff, ONS=ONS, onk=onk, SG=SG, sub=sub, b=b):
                                if j == j0:
                                    acc["psv"], acc["psvk"] = accr.get()
                                psv, psvk, pe = acc["psv"], acc["psvk"], tk["pe"]

                                def pv(e):
                                    ins = None
                                    for r in range(4):
                                        ins = e.matmul(psv[:, r * 65:(r + 1) * 65], lhsT=pe[:, r * 128:(r + 1) * 128], rhs=V1[:, j, g, :],
                                                       start=(j == j0 and r == 0), stop=(j == i), skip_group_check=True)
                                    return ins
                                S.op(PE, pv, reads=[tk["pek"], vkey], writes=[psvk])
                                if j == i:
                                    rd, rdk = rdr.get()
                                    S.op(DVE, lambda e: e.reciprocal(out=rd[:, 0:4],
                                                                     in_=psv[:, 0:260].rearrange("p (r c) -> p r c", r=4)[:, :, 64]),
                                         reads=[psvk], writes=[rdk])
                                    S.op(DVE, lambda e: e.tensor_tensor(out=rd[:, 4:8], in0=rd[:, 0:4],
                                                                        in1=SG[:, sub, goff + g * 4:goff + g * 4 + 4], op=ALU.mult),
                                         reads=[rdk, f"SG{b}"], writes=[rdk])
                                    ft, ftk = ftr.get()
                                    S.op(DVE, lambda e: e.tensor_tensor(
                                        out=ft[:], in0=psv[:, 0:260].rearrange("p (r c) -> p r c", r=4)[:, :, 0:64],
                                        in1=rd[:, 4:8].unsqueeze(2).to_broadcast([128, 4, 64]), op=ALU.mult), reads=[psvk, rdk], writes=[ftk])
                                    S.op(POOL, lambda e: e.tensor_tensor(out=ONS[:, g * 4:(g + 1) * 4, :], in0=ONS[:, g * 4:(g + 1) * 4, :],
                                                                         in1=ft[:], op=ALU.add), reads=[ftk, onk], writes=[onk])
                            tasks.append((A, B, C))
            return tasks

        def epilogue(n):
            s, t = tiles[n]
            b = n % 2
            tok0 = s * S_LEN + t * TT
            sgb, maT = sgbs[b], maTs[b]
            for sub in range(NS):
                qs = slice(sub * 128, (sub + 1) * 128)
                ONS = ONSs[b][sub]
                S.op(POOL, lambda e, ONS=ONS, sub=sub: e.tensor_copy(out=nso[:, sub, :], in_=ONS[:].rearrange("p h d -> p (h d)")),
                     reads=[f"ONS{b}{sub}0", f"ONS{b}{sub}1"], writes=["nso"])

                def tr(e, sub=sub):
                    ins = None
                    for k4 in range(4):
                        ins = e.transpose(out=self.psb[:, k4 * 128:(k4 + 1) * 128], in_=nso[:, sub, k4 * 128:(k4 + 1) * 128],
                                          identity=self.ident[:])
                    return ins
                S.op(PE, tr, reads=["nso", "ident"], writes=["psb"])
                S.op(DVE, lambda e, qs=qs: e.tensor_copy(out=noT[:, :, qs], in_=self.psb[:, 0:512].rearrange("p (k j) -> p k j", k=4)),
                     reads=["psb"], writes=["noT"])
            for cc in range(8):
                pb, pbk = gen.get()
                S.op(PE, self.mm8(pb[:, 0:TT], lambda k4, cc=cc: wno[:, k4, cc * 128:(cc + 1) * 128], lambda k4: noT[:, k4, :], n=4),
                     reads=["wno", "noT"], writes=[pbk])
                tm, tmk = tmr.get()
                S.op(DVE, lambda e, pb=pb, tm=tm, cc=cc: e.tensor_tensor(out=tm[:], in0=pb[:, 0:TT], in1=sgb[:, cc, :], op=ALU.mult),
                     reads=[pbk, f"sgb{b}"], writes=[tmk])
                S.op(POOL, lambda e, tm=tm, cc=cc: e.tensor_tensor(out=maT[:, cc, :], in0=tm[:], in1=maT[:, cc, :], op=ALU.add),
                     reads=[tmk, f"maTN{b}"], writes=[f"maTN{b}"])
            for sub in range(NS):
                xt, xk = xts_of[n][sub]
                cs = slice(sub * 128, (sub + 1) * 128)
                for half in range(2):
                    pb, pbk = gen.get()
                    S.op(PE, self.mm8(pb[:, :], lambda cc, cs=cs: maT[:, cc, cs], lambda cc, half=half: wout[:, cc, half * 512:(half + 1) * 512]),
                         reads=[f"maTN{b}", "wout"], writes=[pbk])
                    S.op(DVE, lambda e, pb=pb, xt=xt, half=half: e.tensor_tensor(
                        out=xt[:, half * 512:(half + 1) * 512], in0=xt[:, half * 512:(half + 1) * 512], in1=pb[:, :], op=ALU.add),
                        reads=[pbk, xk], writes=[xk])
                r0 = tok0 + sub * 128
                S.dma(SP, lambda e, xt=xt, r0=r0: e.dma_start(out=self.x1_scr[r0:r0 + 128, :], in_=xt[:]), reads=[xk])

        for f in prologue(0):
            f()
        for n in range(len(tiles)):
            pro = prologue(n + 1) if n + 1 < len(tiles) else []
            tasks = pair_tasks(n)
            steps = len(tasks) + LA
            pi = 0
            for k in range(steps):
                if k < len(tasks):
                    tasks[k][0]()
                    tasks[k][1]()
                if k - LA >= 0:
                    tasks[k - LA][2]()
                tgt = ((k + 1) * len(pro) + steps - 1) // steps
                while pi < min(tgt, len(pro)):
                    pro[pi]()
                    pi += 1
            while pi < len(pro):
                pro[pi]()
                pi += 1
            epilogue(n)

    def pass_F(self, src):
        nc, S = self.nc, self.S
        TT = 256
        wup = self.sb("wup", [128, 8, 2 * DFF], BF16)
        wdn = self.sb("wdn", [128, NFC, D], BF16)
        gcb = self.sb("gcbF", [128, 8 * 128], F32)
        gfin = self.sb("gfin", [128, D], F32)
        cw = self.sb("cw", [128, NFC * 3], F32)
        cb = self.sb("cb", [128, NFC], F32)
        halo = self.sb("halo", [128, NFC, 2], F32)
        hTs = [self.sb(f"hTF{b}", [128, 8, TT], BF16) for b in range(2)]
        uT = self.sb("uT", [128, NFC, TT], BF16)
        t1r = Ring([(self.sb(f"t1_{i}", [128, TT], F32), f"t1_{i}") for i in range(3)])
        ger = Ring([(self.sb(f"ge_{i}", [128, TT], F32), f"ge_{i}") for i in range(2)])
        osb = Ring([(self.sb(f"osb{i}", [128, D], F32), f"osb{i}") for i in range(2)])
        S.dma(SP, lambda e: e.dma_start(out=gcb[:], in_=self.g_ffn), writes=["gcb"])
        S.dma(SP, lambda e: e.dma_start(out=gfin[:], in_=self.g_fin), writes=["gfin"])
        S.dma(SP, lambda e: e.dma_start(out=cw[:], in_=self.convw), writes=["cw"])
        S.dma(SP, lambda e: e.dma_start(out=cb[:], in_=self.convb), writes=["cb"])
        self.load_w(wup, "wup", self.w_up, 0, 2 * DFF, 8)
        self.load_w(wdn, "wdn", self.w_down, 0, D, NFC)
        NTL = self.ntok // TT
        NSB = TT // 128
        xts_next = [self.rmsnorm_hT(src, sub * 128, gcb, hTs[0], "hTF0", sub * 128) for sub in range(NSB)]
        for t in range(NTL):
            tok0 = t * TT
            first = (tok0 % S_LEN) == 0
            hT = hTs[t % 2]
            hk = f"hTF{t % 2}"
            xts = xts_next
            xts_next = []
            pend = []
            deferred = None
            for fc in range(NFC):
                pa, pak = self.banks.get()
                pb, pbk = self.banks.get()

                def mm_a(e, pa=pa, fc=fc, hT=hT):
                    ins = None
                    for kc in range(8):
                        ins = e.matmul(pa[:, 2:2 + TT], lhsT=wup[:, kc, fc * 128:(fc + 1) * 128], rhs=hT[:, kc, :],
                                       start=(kc == 0), stop=(kc == 7))
                    return ins

                def mm_b(e, pb=pb, fc=fc, hT=hT):
                    ins = None
                    for kc in range(8):
                        ins = e.matmul(pb[:, 0:TT], lhsT=wup[:, kc, DFF + fc * 128:DFF + (fc + 1) * 128],
                                       rhs=hT[:, kc, :], start=(kc == 0), stop=(kc == 7))
                    return ins
                S.op(PE, mm_a, reads=["wup", hk], writes=[pak])
                S.op(PE, mm_b, reads=["wup", hk], writes=[pbk])
                if t + 1 < NTL:
                    nb_ = (t + 1) % 2
                    if fc in (2, 6):
                        pend.append(self.rms_a(src, tok0 + TT + (fc // 4) * 128))
                    if fc in (10, 14):
                        sub_ = (fc - 10) // 4
                        xts_next.append(self.rms_b(pend[sub_], gcb, hTs[nb_], f"hTF{nb_}", sub_ * 128))
                if first:
                    S.op(ACT, lambda e, pa=pa: e.memzero(pa[:, 0:2]), reads=[pak], writes=[pak])
                else:
                    S.op(ACT, lambda e, pa=pa, fc=fc: e.copy(out=pa[:, 0:2], in_=halo[:, fc, :]),
                         reads=[pak, "halo"], writes=[pak])
                S.op(ACT, lambda e, pa=pa, fc=fc: e.copy(out=halo[:, fc, :], in_=pa[:, TT:TT + 2]),
                     reads=[pak], writes=["halo"])
                t1, t1k = t1r.get()
                S.op(ACT, lambda e, pa=pa, fc=fc, t1=t1: e.activation(
                    out=t1[:], in_=pa[:, 2:2 + TT], func=AF.Copy, scale=cw[:, fc * 3 + 2:fc * 3 + 3]),
                    reads=[pak, "cw"], writes=[t1k])
                S.op(DVE, lambda e, pa=pa, fc=fc, t1=t1: e.scalar_tensor_tensor(
                    out=t1[:], in0=pa[:, 1:1 + TT], scalar=cw[:, fc * 3 + 1:fc * 3 + 2], in1=t1[:],
                    op0=ALU.mult, op1=ALU.add), reads=[pak, "cw", t1k], writes=[t1k])
                S.op(DVE, lambda e, pa=pa, fc=fc, t1=t1: e.scalar_tensor_tensor(
                    out=t1[:], in0=pa[:, 0:TT], scalar=cw[:, fc * 3:fc * 3 + 1], in1=t1[:],
                    op0=ALU.mult, op1=ALU.add), reads=[pak, "cw", t1k], writes=[t1k])

                def fin(fc=fc, t1=t1, t1k=t1k, pb=pb, pbk=pbk):
                    ge, gek = ger.get()
                    S.op(ACT, lambda e: e.activation(out=ge[:], in_=t1[:], func=AF.Gelu_apprx_tanh, bias=cb[:, fc:fc + 1]),
                         reads=[t1k, "cb"], writes=[gek])
                    S.op(DVE, lambda e: e.tensor_tensor(out=uT[:, fc, :], in0=ge[:], in1=pb[:, 0:TT], op=ALU.mult),
                         reads=[gek, pbk], writes=["uT"])
                if deferred is not None:
                    deferred()
                deferred = fin
            deferred()
            deferred = None
            for sub in range(TT // 128):
                xt, xk = xts[sub]
                ss, sk = self.ss.get()
                for half in range(2):
                    po, pok = self.banks.get()

                    def mm_o(e, po=po, sub=sub, half=half):
                        ins = None
                        for fc in range(NFC):
                            ins = e.matmul(po[:, :], lhsT=uT[:, fc, sub * 128:(sub + 1) * 128],
                                           rhs=wdn[:, fc, half * 512:(half + 1) * 512],
                                           start=(fc == 0), stop=(fc == NFC - 1))
                        return ins
                    S.op(PE, mm_o, reads=["uT", "wdn"], writes=[pok])
                    S.op(DVE, lambda e, po=po, xt=xt, half=half: e.tensor_tensor(
                        out=xt[:, half * 512:(half + 1) * 512], in0=xt[:, half * 512:(half + 1) * 512],
                        in1=po[:, :], op=ALU.add), reads=[pok, xk], writes=[xk])
                ob, obk = osb.get()
                S.op(ACT, lambda e, xt=xt, ss=ss: e.activation(out=self.junk[:], in_=xt[:], func=AF.Square,
                                                               accum_out=ss[:]), reads=[xk], writes=[sk])
                S.op(ACT, lambda e, ss=ss: e.activation(out=ss[:], in_=ss[:], func=AF.Sqrt, scale=1.0 / D,
                                                        bias=self.epsb[:]), reads=[sk, "epsb"], writes=[sk])
                S.op(DVE, lambda e, ss=ss: e.reciprocal(out=ss[:], in_=ss[:]), reads=[sk], writes=[sk])
                S.op(DVE, lambda e, xt=xt, ss=ss, ob=ob: e.scalar_tensor_tensor(
                    out=ob[:], in0=xt[:], scalar=ss[:, 0:1], in1=gfin[:], op0=ALU.mult, op1=ALU.mult),
                    reads=[xk, sk, "gfin"], writes=[obk])
                r0 = tok0 + sub * 128
                S.dma(SP, lambda e, ob=ob, r0=r0: e.dma_start(out=self.out[r0:r0 + 128, :], in_=ob[:]),
                      reads=[obk])


def host_inputs(inp, nseq, core):
    f = np.float32
    x = np.ascontiguousarray(inp["x"][core * nseq:(core + 1) * nseq].reshape(nseq * S_LEN, D))

    def gcol(g):
        return np.ascontiguousarray(np.broadcast_to(g.reshape(8, 128).T[:, :, None], (128, 8, 128)).reshape(128, 1024))
    m = {
        "x": x,
        "w_in": np.ascontiguousarray(inp["w_in"][0]),
        "w_up": np.ascontiguousarray(inp["w_up"][0]),
        "w_down": np.ascontiguousarray(inp["w_down"][0]),
        "g_ffn": gcol(inp["norm_ffn"][0]),
        "g_mix": gcol(inp["norm_mix"][0]),
        "g_fin": np.ascontiguousarray(np.broadcast_to(inp["norm_final"][None, :], (128, D))),
        "convw": np.ascontiguousarray(inp["conv_w"][0].reshape(3, NFC, 128).transpose(2, 1, 0).reshape(128, NFC * 3)),
        "convb": np.ascontiguousarray(inp["conv_b"][0].reshape(NFC, 128).T),
    }
    m.update({
        "w_ret_o": np.ascontiguousarray(inp["w_ret_o"][0]),
        "w_nsa_o": np.ascontiguousarray(inp["w_nsa_o"][0]),
        "w_out": np.ascontiguousarray(inp["w_out"][0]),
        "gng": gcol(inp["ret_gn_g"][0]),
        "w1k": np.ascontiguousarray(inp["cmp_w1_k"][0]),
        "w1v": np.ascontiguousarray(inp["cmp_w1_v"][0]),
        "w2k": np.ascontiguousarray(inp["cmp_w2_k"][0]),
        "w2v": np.ascontiguousarray(inp["cmp_w2_v"][0]),
        "b1k": np.ascontiguousarray(inp["cmp_b1_k"][0].reshape(2, 128).T),
        "b1v": np.ascontiguousarray(inp["cmp_b1_v"][0].reshape(2, 128).T),
        "posTk": np.ascontiguousarray(np.repeat(inp["cmp_pos_k"][0].T[:, :, None], 2, axis=2).reshape(64, 64)),
        "posTv": np.ascontiguousarray(np.repeat(inp["cmp_pos_v"][0].T[:, :, None], 2, axis=2).reshape(64, 64)),
    })
    m.update(make_consts())
    return m


_CACHE = {}


def kernel(**inputs):
    inp = {k: np.asarray(v) for k, v in inputs.items()}
    ncores, nseq = 8, 2
    if "prog" not in _CACHE:
        p = Prog(nseq=nseq)
        p.build()
        _CACHE["prog"] = p
    p = _CACHE["prog"]
    in_maps = []
    for c in range(ncores):
        m = host_inputs(inp, nseq, c)
        in_maps.append({k: m[k] for k in p.in_names})
    res = run_bass_kernel_spmd(p.nc, in_maps, core_ids=list(range(ncores)))
    out = np.concatenate([r["out"] for r in res.results], axis=0)
    return out.reshape(16, S_LEN, D).astype(np.float32)
```

```python
import contextlib
import numpy as np
STAGE = 9
import ml_dtypes
import concourse.bass as bass
import concourse.mybir as mybir
from concourse.bass_utils import run_bass_kernel_spmd

F32 = mybir.dt.float32
BF16 = mybir.dt.bfloat16
ALU = mybir.AluOpType
AF = mybir.ActivationFunctionType

PE, ACT, DVE, POOL, SP = "tensor", "scalar", "vector", "gpsimd", "sync"
ENGS = (PE, ACT, DVE, POOL, SP)
NDMASEM = 8

S_LEN = 2048
D = 1024
DFF = 2816
NFC = DFF // 128
EPS = 1e-6
N_IN = 6424
O_RQ, O_RK, O_RV, O_RG, O_NQ, O_KCR, O_VCR, O_KS, O_VS, O_KW, O_VW, O_NG, O_GA, O_GB = (
    0, 512, 1024, 2048, 3072, 3584, 3712, 3840, 3968, 4096, 4224, 4352, 4376, 5400)


class Op:
    __slots__ = ("eng", "fn", "reads", "writes", "is_dma", "waits", "inc", "ticket", "dsem", "dval")

    def __init__(self, eng, fn, reads, writes, is_dma):
        self.eng = eng
        self.fn = fn
        self.reads = reads
        self.writes = writes
        self.is_dma = is_dma
        self.waits = []
        self.inc = False
        self.ticket = None
        self.dsem = None
        self.dval = None


class Sched:
    def __init__(self, nc):
        self.nc = nc
        self.ops = []
        self.last_writer = {}
        self.readers = {}
        self.dma_count = {e: 0 for e in ENGS}
        self.dma_hist = {e: [] for e in ENGS}
        self.last_op = {e: None for e in ENGS}

    def op(self, eng, fn, reads=(), writes=()):
        o = Op(eng, fn, tuple(reads), tuple(writes), False)
        self._add(o)
        return o

    def dma(self, eng, fn, reads=(), writes=()):
        o = Op(eng, fn, tuple(reads), tuple(writes), True)
        i = self.dma_count[eng]
        self.dma_count[eng] += 1
        o.dsem = (eng, i % NDMASEM)
        o.dval = 16 * (i // NDMASEM + 1)
        self.dma_hist[eng].append(o)
        if i >= NDMASEM:
            o.waits.append(self.dma_hist[eng][i - NDMASEM])
        self._add(o)
        return o

    def barrier(self):
        tails = []
        for e in ENGS:
            if self.last_op[e] is not None and not self.last_op[e].is_dma:
                tails.append(self.last_op[e])
            tails.extend(self.dma_hist[e][-NDMASEM:])
        for e in ENGS:
            o = Op(e, None, (), (), False)
            o.waits = [t for t in tails]
            self.ops.append(o)
            self.last_op[e] = o
        self.last_writer = {}
        self.readers = {}

    def _add(self, o):
        self.ops.append(o)
        deps = o.waits
        for k in o.reads:
            w = self.last_writer.get(k)
            if w is not None:
                deps.append(w)
            if k.startswith("ps"):
                for r in self.readers.get(k, ()):
                    if r.eng != o.eng:
                        deps.append(r)
        for k in o.writes:
            w = self.last_writer.get(k)
            if w is not None and (w.is_dma or o.is_dma or w.eng != o.eng):
                deps.append(w)
            for r in self.readers.get(k, ()):
                if r.is_dma or o.is_dma or r.eng != o.eng:
                    deps.append(r)
        for k in o.reads:
            self.readers.setdefault(k, []).append(o)
        for k in o.writes:
            self.last_writer[k] = o
            self.readers[k] = []
        if not o.is_dma:
            self.last_op[o.eng] = o

    def run(self):
        nc = self.nc
        for o in self.ops:
            for d in o.waits:
                if not d.is_dma:
                    d.inc = True
        cnt = {e: 0 for e in ENGS}
        for o in self.ops:
            if o.fn is None:
                o.inc = False
            if not o.is_dma and o.inc:
                cnt[o.eng] += 1
                o.ticket = cnt[o.eng]
        seen = {e: {} for e in ENGS}
        plans = {e: [] for e in ENGS}
        for o in self.ops:
            e = o.eng
            wl = {}
            for d in o.waits:
                if d.is_dma:
                    key = ("d",) + d.dsem
                    val = d.dval
                else:
                    if d.ticket is None:
                        continue
                    key = ("c", d.eng)
                    val = d.ticket
                if seen[e].get(key, 0) >= val:
                    continue
                if wl.get(key, 0) < val:
                    wl[key] = val
            for key, val in wl.items():
                seen[e][key] = val
            plans[e].append((o, list(wl.items())))
        with contextlib.ExitStack() as st:
            sems = {}
            for e in ENGS:
                sems[("c", e)] = st.enter_context(nc.semaphore(f"c_{e}"))
                for j in range(min(NDMASEM, self.dma_count[e])):
                    sems[("d", e, j)] = st.enter_context(nc.semaphore(f"d_{e}_{j}"))
            block = st.enter_context(nc.Block())

            def mk(e):
                plan = plans[e]

                def body(eng):
                    for o, wl in plan:
                        for key, val in wl:
                            eng.wait_ge(sems[key], val)
                        if o.fn is None:
                            continue
                        ins = o.fn(eng)
                        if o.is_dma:
                            ins.then_inc(sems[("d",) + o.dsem], 16)
                        elif o.inc:
                            ins.then_inc(sems[("c", e)], 1)
                    n = self.dma_count[e]
                    for j in range(min(NDMASEM, n)):
                        eng.wait_ge(sems[("d", e, j)], 16 * ((n - 1 - j) // NDMASEM + 1))
                return body

            block.tensor(mk(PE))
            block.scalar(mk(ACT))
            block.vector(mk(DVE))
            block.gpsimd(mk(POOL))
            block.sync(mk(SP))


class Ring:
    def __init__(self, items):
        self.items = items
        self.i = 0

    def get(self):
        it = self.items[self.i % len(self.items)]
        self.i += 1
        return it


def make_consts():
    c = {}
    bf = ml_dtypes.bfloat16
    f = np.float32
    c["ident"] = np.eye(128, dtype=f).astype(bf)
    lg = np.log1p(-np.exp2(-5.0 - np.arange(4, dtype=np.float64)))
    pos = np.arange(128, dtype=np.float64)
    diff = pos[None, :] - pos[:, None]
    dmt = np.where(diff >= 0, np.exp(lg[:, None, None] * np.maximum(diff, 0.0)), 0.0)
    c["c_dmt"] = np.ascontiguousarray(dmt.transpose(1, 0, 2).reshape(128, 512)).astype(f)
    xi = np.exp(lg[:, None] * (pos + 1.0))
    xi2 = np.tile(xi, (1, 2))
    c["c_xi"] = np.ascontiguousarray(np.broadcast_to(xi2.reshape(1, 4 * 256), (128, 4 * 256))).astype(f)
    zeta = np.exp(lg[:, None] * (127 - pos)) * (128 ** -0.5)
    c["c_zeta"] = np.ascontiguousarray(np.repeat(zeta.T[:, :, None], 128, axis=2).reshape(128, 512)).astype(f)
    slopes = np.exp2(-(np.arange(8, dtype=np.float64) + 1.0)).reshape(2, 4)
    t = np.arange(S_LEN)
    qaug = np.zeros((2, 4, 4, S_LEN))
    for g in range(2):
        for r in range(4):
            sl = slopes[g, r]
            qaug[g, 0, r] = 8 * sl * 128
            qaug[g, 1, r] = 8 * sl
            qaug[g, 2, r] = -8 * sl * 128 * (t // 128)
            qaug[g, 3, r] = -8 * sl * (t % 128)
    c["c_qaug"] = qaug.reshape(2 * 4, 4 * S_LEN).astype(bf)
    kaug = np.stack([t // 128, t % 128, np.ones(S_LEN), np.ones(S_LEN)]).astype(np.float64)
    c["c_kaug"] = kaug.astype(bf)
    pc = 16 * np.arange(127) + 31
    caug = np.stack([pc // 128, pc % 128, np.ones(127), np.ones(127)]).astype(np.float64)
    c["c_caug"] = caug.astype(bf)
    c["c_ehot"] = (np.arange(32)[:, None] == (t[None, :] // 64)).astype(f).astype(bf)
    tt = (128 * np.arange(16)[:, None] + np.arange(128)[None, :])
    cm = (pc[:, None, None] <= tt[None, :, :]).astype(f)
    c["c_cmask"] = np.ascontiguousarray(cm.reshape(127, 16 * 128)).astype(bf)
    kb = np.arange(128)
    c["c_caus"] = np.ascontiguousarray(np.tile((kb[:, None] <= kb[None, :]).astype(f), (1, 4))).astype(bf)
    c["c_far"] = np.ascontiguousarray(np.tile((kb[:, None] > kb[None, :]).astype(f), (1, 4))).astype(bf)
    cur = tt // 64
    jj = np.arange(32)
    force = (jj[None, None, :] == 0) | (jj[None, None, :] == cur[:, :, None]) | (jj[None, None, :] == cur[:, :, None] - 1)
    fut = jj[None, None, :] > cur[:, :, None]
    A = 1.0 - force - fut
    Bm = 1e9 * force - 1e9 * fut
    NF = 1.0 - fut
    c["c_impA"] = np.ascontiguousarray(A.transpose(1, 0, 2).reshape(128, 16 * 32)).astype(f)
    c["c_impB"] = np.ascontiguousarray(Bm.transpose(1, 0, 2).reshape(128, 16 * 32)).astype(f)
    c["c_impNF"] = np.ascontiguousarray(NF.transpose(1, 0, 2).reshape(128, 16 * 32)).astype(f)
    cs = 16 * np.arange(127)
    js = 64 * np.arange(32)
    ovl = ((cs[:, None] < js[None, :] + 64) & (cs[:, None] + 32 > js[None, :])).astype(f)
    c["c_ovl"] = np.concatenate([np.ones((127, 1), f), ovl], axis=1).astype(bf)
    return c


class Prog:
    def __init__(self, nseq=2, passes="RNF", dbg=False):
        self.nseq = nseq
        self.ntok = nseq * S_LEN
        self.passes = passes
        self.dbg = dbg
        nc = self.nc = bass.Bass("TRN2", target_bir_lowering=False)
        self.S = Sched(nc)
        self.in_names = []

    def din(self, name, shape, dt=F32):
        self.in_names.append(name)
        return self.nc.dram_tensor(name, list(shape), dt, kind="ExternalInput").ap()

    def build(self):
        nc, S = self.nc, self.S
        NT = self.ntok
        self.x = self.din("x", [NT, D])
        self.w_in = self.din("w_in", [D, N_IN])
        self.w_up = self.din("w_up", [D, 2 * DFF])
        self.w_down = self.din("w_down", [DFF, D])
        self.g_ffn = self.din("g_ffn", [128, 8 * 128])
        self.g_mix = self.din("g_mix", [128, 8 * 128])
        self.g_fin = self.din("g_fin", [128, D])
        self.convw = self.din("convw", [128, NFC * 3])
        self.convb = self.din("convb", [128, NFC])
        self.ident_d = self.din("ident", [128, 128], BF16)
        self.w_ret_o = self.din("w_ret_o", [D, D])
        self.w_nsa_o = self.din("w_nsa_o", [512, D])
        self.w_out = self.din("w_out", [D, D])
        self.gng = self.din("gng", [128, 8 * 128])
        self.w1 = [self.din("w1k", [2048, 256]), self.din("w1v", [2048, 256])]
        self.w2 = [self.din("w2k", [256, 64]), self.din("w2v", [256, 64])]
        self.b1 = [self.din("b1k", [128, 2]), self.din("b1v", [128, 2])]
        self.posT = [self.din("posTk", [64, 64]), self.din("posTv", [64, 64])]
        self.cd = {}
        for k, v in make_consts().items():
            if k != "ident":
                self.cd[k] = self.din(k, v.shape, BF16 if v.dtype == ml_dtypes.bfloat16 else F32)
        self.out = nc.dram_tensor("out", [NT, D], F32, kind="ExternalOutput").ap()
        self.x1_scr = nc.dram_tensor("x1_scr", [NT, D], F32, kind="Internal").ap()
        self.ma_scr = nc.dram_tensor("ma_scr", [128, 8 * NT], BF16,
                                     kind="ExternalOutput" if self.dbg else "Internal").ap()

        with contextlib.ExitStack() as st:
            self.st = st
            banks = []
            for i in range(7):
                t = st.enter_context(nc.psum_tensor(f"psf{i}", [128, 512], F32))
                banks.append((t, f"psf{i}"))
            self.banks = Ring(banks[0:5])
            self.accb = Ring(banks[5:7])
            self.psb = st.enter_context(nc.psum_tensor("psb", [128, 1024], BF16))
            self.ident = self.sb("ident", [128, 128], BF16)
            self.epsb = self.sb("epsb", [128, 1], F32)
            S.dma(SP, lambda e: e.dma_start(out=self.ident[:], in_=self.ident_d), writes=["ident"])
            S.op(DVE, lambda e: e.memset(self.epsb[:], EPS), writes=["epsb"])
            self.junk = self.sb("junk", [128, D], BF16)
            self.xt = Ring([(self.sb(f"xt{i}", [128, D], F32), f"xt{i}") for i in range(4)])
            self.xn = Ring([(self.sb(f"xn{i}", [128, D], BF16), f"xn{i}") for i in range(3)])
            self.ss = Ring([(self.sb(f"ss{i}", [128, 1], F32), f"ss{i}") for i in range(4)])
            if "R" in self.passes:
                with contextlib.ExitStack() as st2:
                    self.st = st2
                    self.pass_R()
                    S.barrier()
            if "N" in self.passes:
                with contextlib.ExitStack() as st2:
                    self.st = st2
                    self.pass_N_alloc()
                    with contextlib.ExitStack() as st3:
                        self.st = st3
                        self.pass_N1()
                        S.barrier()
                    with contextlib.ExitStack() as st3:
                        self.st = st3
                        self.pass_N2()
                        S.barrier()
            if "F" in self.passes:
                with contextlib.ExitStack() as st2:
                    self.st = st2
                    self.pass_F(self.x1_scr if "N" in self.passes else self.x)
                    S.barrier()
            S.run()
        return nc

    def sb(self, name, shape, dt):
        self._uid = getattr(self, "_uid", 0) + 1
        return self.st.enter_context(self.nc.sbuf_tensor(f"s{self._uid}_{name}", list(shape), dt))

    def load_w(self, dst, dkey, src_rows, c0, ncols, kcs, dcol0=0, rows=128):
        S = self.S
        for kc in range(kcs):
            for cc in range(0, ncols, 2048):
                n = min(2048, ncols - cc)
                S.dma(POOL, lambda e, kc=kc, cc=cc, n=n: e.dma_start(
                    out=dst[0:rows, kc, dcol0 + cc:dcol0 + cc + n],
                    in_=src_rows[kc * rows:(kc + 1) * rows, c0 + cc:c0 + cc + n]), writes=[dkey])

    def cload(self, name, shape, dt, src, eng=SP):
        t = self.sb(name, shape, dt)
        self.S.dma(eng, lambda e: e.dma_start(out=t[:], in_=src), writes=[name])
        return t

    def mm8(self, out, lhs_fn, rhs_fn, n=8):
        def f(e):
            ins = None
            for kc in range(n):
                ins = e.matmul(out, lhsT=lhs_fn(kc), rhs=rhs_fn(kc), start=(kc == 0), stop=(kc == n - 1))
            return ins
        return f

    def pass_R(self):
        nc, S = self.nc, self.S
        TT = 256
        NS = TT // 128
        lg = np.log1p(-np.exp2(-5.0 - np.arange(4, dtype=np.float64)))
        decay = [float(np.exp(lg[h] * 128)) for h in range(4)]
        allb = self.banks.items + self.accb.items
        gen = Ring(allb[0:2])
        (pin, pink), (po0, po0k), (po1, po1k), (pk0, pk0k), (pk1, pk1k) = allb[2:7]
        pos_ = [(po0, po0k), (po1, po1k)]
        pks_ = [(pk0, pk0k), (pk1, pk1k)]
        wr = self.sb("wr", [128, 8, 3072], BF16)
        wga = self.sb("wga", [128, 8, D], BF16)
        wro = self.sb("wro", [128, 8, D], BF16)
        gcb = self.cload("gcb", [128, 1024], F32, self.g_mix)
        gngb = self.cload("gngb", [128, 1024], F32, self.gng)
        dmt = self.cload("dmt", [128, 512], F32, self.cd["c_dmt"])
        xi = self.cload("xi", [128, 4 * 256], F32, self.cd["c_xi"])
        zeta = self.cload("zeta", [128, 512], F32, self.cd["c_zeta"])
        self.load_w(wr, "wr", self.w_in, O_RQ, 3072, 8)
        self.load_w(wga, "wga", self.w_in, O_GA, D, 8)
        self.load_w(wro, "wro", self.w_ret_o, 0, D, 8)
        hTs = [self.sb(f"hTR{b}", [128, 8, TT], BF16) for b in range(2)]
        qTs = [self.sb(f"qT{b}", [128, 4, TT], BF16) for b in range(2)]
        qxTs = [self.sb(f"qxT{b}", [128, 4, TT], BF16) for b in range(2)]
        kTs = [self.sb(f"kT{b}", [128, 4, TT], BF16) for b in range(2)]
        kzs = [self.sb(f"kz{b}", [128, NS, 512], BF16) for b in range(2)]
        vs = [self.sb(f"v{b}", [128, NS, D], BF16) for b in range(2)]
        sgs = [self.sb(f"sg{b}", [128, NS, D], BF16) for b in range(2)]
        sga = self.sb("sga", [128, 8, TT], BF16)
        roT = self.sb("roT", [128, 8, TT], BF16)
        maT = self.sb("maT", [128, 8, TT], BF16)
        R = self.sb("R", [128, 4, 256], F32)
        Rb = self.sb("Rb", [128, 4, 256], BF16)
        inT = self.sb("inT4", [128, 512], BF16)
        on = self.sb("on4", [128, D], F32)
        stt = self.sb("stt", [128, 4, 6], F32)
        mv = self.sb("mv", [128, 4, 4], F32)
        NTL = self.ntok // TT

        def proj(t):
            b = t % 2
            hT, qT, qxT, kT, kz, v, sg = hTs[b], qTs[b], qxTs[b], kTs[b], kzs[b], vs[b], sgs[b]
            hk = f"hTR{b}"
            ops = []
            for h in range(4):
                def fq(h=h):
                    pq, pqk = gen.get()
                    S.op(PE, self.mm8(pq[:, 0:TT], lambda kc: wr[:, kc, O_RQ + h * 128:O_RQ + (h + 1) * 128], lambda kc: hT[:, kc, :]),
                         reads=["wr", hk], writes=[pqk])
                    S.op(ACT, lambda e: e.copy(out=qT[:, h, :], in_=pq[:, 0:TT]), reads=[pqk], writes=[f"qT{b}"])
                    S.op(DVE, lambda e: e.tensor_tensor(out=qxT[:, h, :], in0=pq[:, 0:TT], in1=xi[:, h * 256:(h + 1) * 256], op=ALU.mult),
                         reads=[pqk, "xi"], writes=[f"qxT{b}"])

                def fk(h=h):
                    pk, pkk = gen.get()
                    S.op(PE, self.mm8(pk[:, 0:TT], lambda kc: wr[:, kc, 512 + h * 128:512 + (h + 1) * 128], lambda kc: hT[:, kc, :]),
                         reads=["wr", hk], writes=[pkk])
                    S.op(ACT, lambda e: e.mul(out=kT[:, h, :], in_=pk[:, 0:TT], mul=128 ** -0.5), reads=[pkk], writes=[f"kT{b}"])
                ops += [fq, fk]
            for c in range(NS):
                cs = slice(c * 128, (c + 1) * 128)

                def fz(c=c, cs=cs):
                    pk, pkk = gen.get()
                    S.op(PE, self.mm8(pk[:, :], lambda kc: hT[:, kc, cs], lambda kc: wr[:, kc, 512:1024]), reads=["wr", hk], writes=[pkk])
                    S.op(DVE, lambda e: e.tensor_tensor(out=kz[:, c, :], in0=pk[:, :], in1=zeta[:], op=ALU.mult),
                         reads=[pkk, "zeta"], writes=[f"kz{b}"])
                ops.append(fz)
                for half in range(2):
                    def fv(c=c, cs=cs, half=half):
                        pv, pvk = gen.get()
                        S.op(PE, self.mm8(pv[:, :], lambda kc: hT[:, kc, cs], lambda kc: wr[:, kc, 1024 + half * 512:1024 + (half + 1) * 512]),
                             reads=["wr", hk], writes=[pvk])
                        S.op(ACT, lambda e: e.copy(out=v[:, c, half * 512:(half + 1) * 512], in_=pv[:, :]), reads=[pvk], writes=[f"v{b}"])

                    def fg(c=c, cs=cs, half=half):
                        pg, pgk = gen.get()
                        S.op(PE, self.mm8(pg[:, :], lambda kc: hT[:, kc, cs], lambda kc: wr[:, kc, 2048 + half * 512:2048 + (half + 1) * 512]),
                             reads=["wr", hk], writes=[pgk])
                        S.op(ACT, lambda e: e.activation(out=sg[:, c, half * 512:(half + 1) * 512], in_=pg[:, :], func=AF.Silu),
                             reads=[pgk], writes=[f"sg{b}"])
                    ops += [fv, fg]
            return ops

        def gates(t):
            b = t % 2
            hT = hTs[b]
            ops = []
            for cc in range(8):
                def f(cc=cc):
                    pga, pgak = gen.get()
                    S.op(PE, self.mm8(pga[:, 0:TT], lambda kc: wga[:, kc, cc * 128:(cc + 1) * 128], lambda kc: hT[:, kc, :]),
                         reads=["wga", f"hTR{b}"], writes=[pgak])
                    S.op(ACT, lambda e: e.activation(out=sga[:, cc, :], in_=pga[:, 0:TT], func=AF.Sigmoid), reads=[pgak], writes=["sga"])
                ops.append(f)
            return ops

        def chunk_stages(t):
            b = t % 2
            qT, qxT, kT, kz, v, sg = qTs[b], qxTs[b], kTs[b], kzs[b], vs[b], sgs[b]
            tok0 = t * TT
            stages = []
            for c in range(NS):
                cs = slice(c * 128, (c + 1) * 128)
                first = ((tok0 + c * 128) % S_LEN) == 0

                def st0(cs=cs):
                    def mmi(e):
                        ins = None
                        for h in range(4):
                            ins = e.matmul(pin[:, h * 128:(h + 1) * 128], lhsT=kT[:, h, cs], rhs=qT[:, h, cs], start=True, stop=True,
                                           skip_group_check=True)
                        return ins
                    S.op(PE, mmi, reads=[f"kT{b}", f"qT{b}"], writes=[pink])
                    S.op(DVE, lambda e: e.tensor_tensor(out=inT[:], in0=pin[:, :], in1=dmt[:], op=ALU.mult), reads=[pink, "dmt"], writes=["inT4"])

                def st1(c=c, cs=cs, first=first):
                    for hp in range(2):
                        po, pok = pos_[hp]

                        def mmo(e, po=po, hp=hp):
                            ins = None
                            for hh in range(2):
                                h = hp * 2 + hh
                                ins = e.matmul(po[:, hh * 256:(hh + 1) * 256], lhsT=inT[:, h * 128:(h + 1) * 128], rhs=v[:, c, h * 256:(h + 1) * 256],
                                               start=True, stop=first, skip_group_check=True)
                                if not first:
                                    ins = e.matmul(po[:, hh * 256:(hh + 1) * 256], lhsT=qxT[:, h, cs], rhs=Rb[:, h, :], start=False, stop=True,
                                                   skip_group_check=True)
                            return ins
                        S.op(PE, mmo, reads=["inT4", f"v{b}", f"qxT{b}", "Rb"], writes=[pok])
                    for hp in range(2):
                        pk, pkk = pks_[hp]

                        def mmk(e, pk=pk, hp=hp):
                            ins = None
                            for hh in range(2):
                                h = hp * 2 + hh
                                ins = e.matmul(pk[:, hh * 256:(hh + 1) * 256], lhsT=kz[:, c, h * 128:(h + 1) * 128], rhs=v[:, c, h * 256:(h + 1) * 256],
                                               start=True, stop=True, skip_group_check=True)
                            return ins
                        S.op(PE, mmk, reads=[f"kz{b}", f"v{b}"], writes=[pkk])

                def st2(first=first):
                    for h in range(4):
                        pk, pkk = pks_[h // 2]
                        src = pk[:, (h % 2) * 256:(h % 2 + 1) * 256]
                        if first:
                            S.op(DVE, lambda e, h=h, src=src: e.tensor_copy(out=R[:, h, :], in_=src), reads=[pkk], writes=["R"])
                        else:
                            S.op(DVE, lambda e, h=h, src=src: e.scalar_tensor_tensor(out=R[:, h, :], in0=R[:, h, :], scalar=decay[h], in1=src,
                                                                                     op0=ALU.mult, op1=ALU.add), reads=[pkk, "R"], writes=["R"])
                    S.op(ACT, lambda e: e.copy(out=Rb[:].rearrange("p h e -> p (h e)"), in_=R[:].rearrange("p h e -> p (h e)")),
                         reads=["R"], writes=["Rb"])
                    for h in range(4):
                        po, pok = pos_[h // 2]
                        S.op(DVE, lambda e, h=h, po=po: e.bn_stats(out=stt[:, h, :], in_=po[:, (h % 2) * 256:(h % 2 + 1) * 256]),
                             reads=[pok], writes=["stt"])
                    for h in range(4):
                        S.op(DVE, lambda e, h=h: e.bn_aggr(out=mv[:, h, 0:2], in_=stt[:, h, :]), reads=["stt"], writes=["mv"])
                    S.op(ACT, lambda e: e.activation(out=mv[:, :, 2], in_=mv[:, :, 1], func=AF.Sqrt, bias=self.epsb[:]),
                         reads=["mv", "epsb"], writes=["mv"])
                    S.op(DVE, lambda e: e.reciprocal(out=mv[:, :, 2], in_=mv[:, :, 2]), reads=["mv"], writes=["mv"])
                    S.op(DVE, lambda e: e.scalar_tensor_tensor(out=mv[:, :, 3], in0=mv[:, :, 0], scalar=-1.0, in1=mv[:, :, 2],
                                                               op0=ALU.mult, op1=ALU.mult), reads=["mv"], writes=["mv"])

                def st3(c=c):
                    for h in range(4):
                        po, pok = pos_[h // 2]
                        S.op(ACT, lambda e, h=h, po=po: e.activation(out=on[:, h * 256:(h + 1) * 256], in_=po[:, (h % 2) * 256:(h % 2 + 1) * 256],
                                                                     func=AF.Identity, scale=mv[:, h, 2:3], bias=mv[:, h, 3:4]),
                             reads=[pok, "mv"], writes=["on4"])
                    S.op(DVE, lambda e: e.tensor_tensor(out=sg[:, c, :], in0=on[:], in1=sg[:, c, :], op=ALU.mult),
                         reads=["on4", f"sg{b}"], writes=[f"sg{b}"])
                stages += [st0, st1, st2, st3]
            return stages

        def tail(t):
            b = t % 2
            sg = sgs[b]
            tok0 = t * TT
            for c in range(NS):
                cs = slice(c * 128, (c + 1) * 128)

                def tr(e, c=c):
                    ins = None
                    for kc in range(8):
                        ins = e.transpose(out=self.psb[:, kc * 128:(kc + 1) * 128], in_=sg[:, c, kc * 128:(kc + 1) * 128],
                                          identity=self.ident[:])
                    return ins
                S.op(PE, tr, reads=[f"sg{b}", "ident"], writes=["psb"])
                S.op(DVE, lambda e, cs=cs: e.tensor_tensor(out=roT[:, :, cs], in0=self.psb[:].rearrange("p (k j) -> p k j", k=8),
                                                           in1=gngb[:].rearrange("p (k j) -> p k j", k=8), op=ALU.mult),
                     reads=["psb", "gngb"], writes=["roT"])
            for cc in range(8):
                ccs = slice(cc * 128, (cc + 1) * 128)
                pya, pyak = gen.get()
                S.op(PE, self.mm8(pya[:, 0:TT], lambda kc, ccs=ccs: wro[:, kc, ccs], lambda kc: roT[:, kc, :]),
                     reads=["wro", "roT"], writes=[pyak])
                S.op(DVE, lambda e, pya=pya, cc=cc: e.tensor_tensor(out=maT[:, cc, :], in0=sga[:, cc, :], in1=pya[:, 0:TT], op=ALU.mult),
                     reads=[pyak, "sga"], writes=["maT"])
            S.dma(SP, lambda e: e.dma_start(out=self.ma_scr.rearrange("p (c n) -> p c n", c=8)[:, :, tok0:tok0 + TT], in_=maT[:]),
                  reads=["maT"])

        for sub in range(NS):
            self.rmsnorm_hT(self.x, sub * 128, gcb, hTs[0], "hTR0", sub * 128)
        for f in proj(0):
            f()
        for t in range(NTL):
            stages = chunk_stages(t)
            fill = gates(t)
            if t + 1 < NTL:
                nb_ = (t + 1) % 2
                pend = {}
                for sub in range(NS):
                    def fa(sub=sub, t=t):
                        pend[sub] = self.rms_a(self.x, (t + 1) * TT + sub * 128)
                    fill.append(fa)
                for sub in range(NS):
                    def fb(sub=sub, nb_=nb_):
                        self.rms_b(pend[sub], gcb, hTs[nb_], f"hTR{nb_}", sub * 128)
                    fill.append(fb)
                fill += proj(t + 1)
            fi = 0
            for k, stg_ in enumerate(stages):
                stg_()
                tgt = ((k + 1) * len(fill) + len(stages) - 1) // len(stages)
                while fi < min(tgt, len(fill)):
                    fill[fi]()
                    fi += 1
            while fi < len(fill):
                fill[fi]()
                fi += 1
            tail(t)

    def rms_a(self, src, r0):
        S = self.S
        xt, xk = self.xt.get()
        xn, nk = self.xn.get()
        ss, sk = self.ss.get()
        S.dma(SP, lambda e: e.dma_start(out=xt[:], in_=src[r0:r0 + 128, :]), writes=[xk])
        S.op(ACT, lambda e: e.activation(out=self.junk[:], in_=xt[:], func=AF.Square, accum_out=ss[:]),
             reads=[xk], writes=[sk])
        S.op(ACT, lambda e: e.activation(out=ss[:], in_=ss[:], func=AF.Sqrt, scale=1.0 / D, bias=self.epsb[:]),
             reads=[sk, "epsb"], writes=[sk])
        S.op(DVE, lambda e: e.reciprocal(out=ss[:], in_=ss[:]), reads=[sk], writes=[sk])
        S.op(DVE, lambda e: e.tensor_scalar(out=xn[:], in0=xt[:], scalar1=ss[:, 0:1], scalar2=None, op0=ALU.mult),
             reads=[xk, sk], writes=[nk])
        return (xt, xk, xn, nk)

    def rms_b(self, state, gcb, hT, hkey, col0):
        S = self.S
        xt, xk, xn, nk = state

        def tr(e):
            ins = None
            for kc in range(8):
                ins = e.transpose(out=self.psb[:, kc * 128:(kc + 1) * 128], in_=xn[:, kc * 128:(kc + 1) * 128],
                                  identity=self.ident[:])
            return ins
        S.op(PE, tr, reads=[nk, "ident"], writes=["psb"])
        S.op(DVE, lambda e: e.tensor_tensor(
            out=hT[:, :, col0:col0 + 128], in0=self.psb[:].rearrange("p (k j) -> p k j", k=8),
            in1=gcb[:].rearrange("p (k j) -> p k j", k=8), op=ALU.mult),
            reads=["psb", "gcb"], writes=[hkey])
        return xt, xk

    def rmsnorm_hT(self, src, r0, gcb, hT, hkey, col0, keep=None):
        return self.rms_b(self.rms_a(src, r0), gcb, hT, hkey, col0)

    def pass_N_alloc(self):
        S = self.S
        ns = self.nseq
        self.KST = [[self.sb(f"KST{s}{g}", [100, S_LEN], BF16) for g in range(2)] for s in range(ns)]
        self.KWT = [[self.sb(f"KWT{s}{g}", [100, S_LEN], BF16) for g in range(2)] for s in range(ns)]
        self.VS1 = [self.sb(f"VS1{s}", [128, 16, 2, 65], BF16) for s in range(ns)]
        self.VW1 = [self.sb(f"VW1{s}", [128, 16, 2, 65], BF16) for s in range(ns)]
        self.KCM = [[self.sb(f"KCM{s}{g}", [100, 128], BF16) for g in range(2)] for s in range(ns)]
        self.VCO = [[self.sb(f"VCO{s}{g}", [128, 97], BF16) for g in range(2)] for s in range(ns)]
        for s in range(ns):
            S.op(DVE, lambda e, s=s: e.memset(self.VS1[s][:], 1.0), writes=[f"VS1{s}"])
            S.op(DVE, lambda e, s=s: e.memset(self.VW1[s][:], 1.0), writes=[f"VW1{s}"])
            for g in range(2):
                S.dma(SP, lambda e, s=s, g=g: e.dma_start(out=self.KST[s][g][64:96, :], in_=self.cd["c_ehot"]), writes=[f"KST{s}{g}"])
                S.dma(SP, lambda e, s=s, g=g: e.dma_start(out=self.KST[s][g][96:100, :], in_=self.cd["c_kaug"]), writes=[f"KST{s}{g}"])
                S.op(DVE, lambda e, s=s, g=g: e.memset(self.KWT[s][g][64:96, :], 0.0), writes=[f"KWT{s}{g}"])
                S.dma(SP, lambda e, s=s, g=g: e.dma_start(out=self.KWT[s][g][96:100, :], in_=self.cd["c_kaug"]), writes=[f"KWT{s}{g}"])
                S.op(DVE, lambda e, s=s, g=g: e.memset(self.KCM[s][g][64:96, :], 0.0), writes=[f"KCM{s}{g}"])
                S.dma(SP, lambda e, s=s, g=g: e.dma_start(out=self.KCM[s][g][96:100, 0:127], in_=self.cd["c_caug"]), writes=[f"KCM{s}{g}"])
                S.dma(SP, lambda e, s=s, g=g: e.dma_start(out=self.VCO[s][g][0:127, 64:97], in_=self.cd["c_ovl"]), writes=[f"VCO{s}{g}"])

    def pass_N1(self):
        nc, S = self.nc, self.S
        TT = 256
        wkv = self.sb("wkv", [128, 8, 768], BF16)
        gcb = self.cload("gcb", [128, 1024], F32, self.g_mix)
        self.load_w(wkv, "wkv", self.w_in, O_KCR, 768, 8)
        w1 = [self.sb(f"w1_{k}", [64, 32, 256], BF16) for k in range(2)]
        w2 = [self.sb(f"w2_{k}", [128, 2, 64], BF16) for k in range(2)]
        cb1 = [self.sb(f"cb1_{k}", [128, 2], F32) for k in range(2)]
        for k in range(2):
            src = self.w1[k].rearrange("(p d) n -> d p n", d=64)
            for p0 in range(0, 32, 8):
                S.dma(POOL, lambda e, k=k, src=src, p0=p0: e.dma_start(out=w1[k][:, p0:p0 + 8, :], in_=src[:, p0:p0 + 8, :]),
                      writes=[f"w1_{k}"])
            self.load_w(w2[k], f"w2_{k}", self.w2[k], 0, 64, 2)
            b1 = self.cload(f"b1_{k}", [128, 2], F32, self.b1[k])
            pf = self.cload(f"posf_{k}", [64, 64], F32, self.posT[k])
            pb16 = self.sb(f"posb_{k}", [64, 64], BF16)
            S.op(DVE, lambda e, pf=pf, pb16=pb16: e.tensor_copy(out=pb16[:], in_=pf[:]), reads=[f"posf_{k}"], writes=[f"posb_{k}"])
            for nch in range(2):
                pb, pbk = self.banks.get()

                def mmc(e, pb=pb, k=k, nch=nch, pb16=pb16):
                    ins = None
                    for p in range(32):
                        ins = e.matmul(pb[:, 0:2], lhsT=w1[k][0:64, p, nch * 128:(nch + 1) * 128], rhs=pb16[0:64, 2 * p:2 * p + 2],
                                       start=(p == 0), stop=(p == 31))
                    return ins
                S.op(PE, mmc, reads=[f"w1_{k}", f"posb_{k}"], writes=[pbk])
                S.op(DVE, lambda e, pb=pb, k=k, nch=nch, b1=b1: e.tensor_tensor(
                    out=cb1[k][:, nch:nch + 1], in0=pb[:, 0:1], in1=b1[:, nch:nch + 1], op=ALU.add),
                    reads=[pbk, f"b1_{k}"], writes=[f"cb1_{k}"])
        CRT = [[self.sb(f"CRT{k}{g}", [64, S_LEN], BF16) for g in range(2)] for k in range(2)]
        hTs = [self.sb(f"hTN1_{b}", [128, 8, TT], BF16) for b in range(2)]
        hidT = self.sb("hidT", [128, 2, 128], BF16)
        NSB = TT // 128
        ntl = self.nseq * (S_LEN // TT)
        for sub in range(NSB):
            self.rmsnorm_hT(self.x, sub * 128, gcb, hTs[0], "hTN1_0", sub * 128)
        for s in range(self.nseq):
            for t in range(S_LEN // TT):
                tok0 = s * S_LEN + t * TT
                pos0 = t * TT
                tix = s * (S_LEN // TT) + t
                hT = hTs[tix % 2]
                hk = f"hTN1_{tix % 2}"
                pend = []
                if tix + 1 < ntl:
                    for sub in range(NSB):
                        pend.append(self.rms_a(self.x, tok0 + TT + sub * 128))
                dests = [(0, CRT[0], "CRT0"), (128, CRT[1], "CRT1"), (256, self.KST[s], f"KST{s}"), (512, self.KWT[s], f"KWT{s}")]
                for off, dst, dk in dests:
                    for g in range(2):
                        pb, pbk = self.banks.get()
                        S.op(PE, self.mm8(pb[0:64, 0:TT], lambda kc, off=off, g=g: wkv[:, kc, off + g * 64:off + (g + 1) * 64],
                                          lambda kc, hT=hT: hT[:, kc, :]), reads=["wkv", hk], writes=[pbk])
                        S.op(ACT, lambda e, pb=pb, dst=dst, g=g, pos0=pos0: e.copy(out=dst[g][0:64, pos0:pos0 + TT], in_=pb[0:64, 0:TT]),
                             reads=[pbk], writes=[f"{dk}{g}"])
                for sub in range(TT // 128):
                    cs = slice(sub * 128, (sub + 1) * 128)
                    kt = pos0 // 128 + sub
                    pb, pbk = self.banks.get()
                    S.op(PE, self.mm8(pb[:, 0:128], lambda kc, cs=cs, hT=hT: hT[:, kc, cs], lambda kc: wkv[:, kc, 384:512]),
                         reads=["wkv", hk], writes=[pbk])
                    S.op(PE, self.mm8(pb[:, 128:256], lambda kc, cs=cs, hT=hT: hT[:, kc, cs], lambda kc: wkv[:, kc, 640:768]),
                         reads=["wkv", hk], writes=[pbk])
                    S.op(ACT, lambda e, pb=pb, s=s, kt=kt: e.copy(out=self.VS1[s][:, kt, :, 0:64],
                                                                  in_=pb[:, 0:128].rearrange("p (g d) -> p g d", g=2)),
                         reads=[pbk], writes=[f"VS1{s}"])
                    S.op(ACT, lambda e, pb=pb, s=s, kt=kt: e.copy(out=self.VW1[s][:, kt, :, 0:64],
                                                                  in_=pb[:, 128:256].rearrange("p (g d) -> p g d", g=2)),
                         reads=[pbk], writes=[f"VW1{s}"])
                for sub, st_ in enumerate(pend):
                    nb_ = (tix + 1) % 2
                    self.rms_b(st_, gcb, hTs[nb_], f"hTN1_{nb_}", sub * 128)
            for k in range(2):
                for g in range(2):
                    for nch in range(2):
                        pb, pbk = self.banks.get()

                        def mmh(e, pb=pb, k=k, g=g, nch=nch):
                            ins = None
                            for p in range(32):
                                ins = e.matmul(pb[:, 0:127], lhsT=w1[k][0:64, p, nch * 128:(nch + 1) * 128],
                                               rhs=CRT[k][g][0:64, p:p + 2017:16], start=(p == 0), stop=(p == 31))
                            return ins
                        S.op(PE, mmh, reads=[f"w1_{k}", f"CRT{k}{g}"], writes=[pbk])
                        S.op(ACT, lambda e, pb=pb, k=k, nch=nch: e.activation(
                            out=hidT[:, nch, 0:127], in_=pb[:, 0:127], func=AF.Gelu_apprx_tanh, bias=cb1[k][:, nch:nch + 1]),
                            reads=[pbk, f"cb1_{k}"], writes=["hidT"])
                    pb, pbk = self.banks.get()
                    if k == 0:
                        S.op(PE, self.mm8(pb[0:64, 0:127], lambda nch: w2[0][:, nch, :], lambda nch: hidT[:, nch, 0:127], n=2),
                             reads=["w2_0", "hidT"], writes=[pbk])
                        S.op(ACT, lambda e, pb=pb, s=s, g=g: e.copy(out=self.KCM[s][g][0:64, 0:127], in_=pb[0:64, 0:127]),
                             reads=[pbk], writes=[f"KCM{s}{g}"])
                    else:
                        S.op(PE, self.mm8(pb[0:127, 0:64], lambda nch: hidT[:, nch, 0:127], lambda nch: w2[1][:, nch, :], n=2),
                             reads=["w2_1", "hidT"], writes=[pbk])
                        S.op(ACT, lambda e, pb=pb, s=s, g=g: e.copy(out=self.VCO[s][g][0:127, 0:64], in_=pb[0:127, 0:64]),
                             reads=[pbk], writes=[f"VCO{s}{g}"])

    def pass_N2(self):
        nc, S = self.nc, self.S
        TT = 256
        NS = TT // 128
        LA = 3
        allb = self.banks.items + self.accb.items
        gen = Ring(allb[0:2])
        scr = Ring(allb[2:5])
        accr = Ring(allb[5:7])
        wnq = self.sb("wnq", [128, 8, 512], BF16)
        wng = self.sb("wng", [128, 8, 24], BF16)
        wgb = self.sb("wgb", [128, 8, D], BF16)
        wno = self.sb("wno", [128, 4, D], BF16)
        wout = self.sb("wout", [128, 8, D], BF16)
        gcb = self.cload("gcb", [128, 1024], F32, self.g_mix)
        cmask = self.sb("cmask", [128, 2048], BF16)
        S.dma(SP, lambda e: e.dma_start(out=cmask[0:127, :], in_=self.cd["c_cmask"]), writes=["cmask"])
        caus = self.cload("caus", [128, 512], BF16, self.cd["c_caus"])
        far = self.cload("far", [128, 512], BF16, self.cd["c_far"])
        impA = self.cload("impA", [128, 512], F32, self.cd["c_impA"])
        impB = self.cload("impB", [128, 512], F32, self.cd["c_impB"])
        impNF = self.cload("impNF", [128, 512], F32, self.cd["c_impNF"])
        self.load_w(wnq, "wnq", self.w_in, O_NQ, 512, 8)
        self.load_w(wng, "wng", self.w_in, O_NG, 24, 8)
        self.load_w(wgb, "wgb", self.w_in, O_GB, D, 8)
        self.load_w(wno, "wno", self.w_nsa_o, 0, D, 4)
        self.load_w(wout, "wout", self.w_out, 0, D, 8)
        hTs = [self.sb(f"hTN2_{b}", [128, 8, TT], BF16) for b in range(2)]
        QTs = [[self.sb(f"QT{b}{g}", [100, 4, TT], BF16) for g in range(2)] for b in range(2)]
        for b in range(2):
            for g in range(2):
                S.op(DVE, lambda e, b=b, g=g: e.memset(QTs[b][g][64:96, :, :], 0.0), writes=[f"QT{b}{g}0", f"QT{b}{g}1"])
        SGs = [self.sb(f"SG{b}", [128, NS, 24], F32) for b in range(2)]
        sgbs = [self.sb(f"sgb{b}", [128, 8, TT], BF16) for b in range(2)]
        maTs = [self.sb(f"maTN{b}", [128, 8, TT], BF16) for b in range(2)]
        nso = self.sb("nso", [128, NS, 512], BF16)
        noT = self.sb("noT", [128, 4, TT], BF16)
        selr = Ring([(self.sb(f"SELB{i}", [128, 96], BF16), f"SELB{i}") for i in range(4)])
        cper = Ring([(self.sb(f"cpe{i}", [128, 512], BF16), f"cpe{i}") for i in range(4)])
        for t_, k_ in selr.items:
            S.op(DVE, lambda e, t_=t_: e.memset(t_[:], 0.0), writes=[k_])
        ONSs = [[self.sb(f"ONS{b}{sub}", [128, 8, 64], F32) for sub in range(NS)] for b in range(2)]
        per = Ring([(self.sb(f"pe{i}", [128, 512], BF16), f"pe{i}") for i in range(6)])
        rdr = Ring([(self.sb(f"rd{i}", [128, 8], F32), f"rd{i}") for i in range(8)])
        impr = Ring([(self.sb(f"imp{i}", [128, 40], F32), f"imp{i}") for i in range(4)])
        it4r = Ring([(self.sb(f"it4_{i}", [128, 4, 32], F32), f"it4_{i}") for i in range(2)])
        ftr = Ring([(self.sb(f"ft{i}", [128, 4, 64], F32), f"ft{i}") for i in range(2)])
        tmr = Ring([(self.sb(f"tmb{i}", [128, TT], F32), f"tmb{i}") for i in range(4)])
        qaug = self.cd["c_qaug"].rearrange("a (r t) -> a r t", r=4)
        mav = self.ma_scr.rearrange("p (c n) -> p c n", c=8)
        tiles = [(s, t) for s in range(self.nseq) for t in range(S_LEN // TT)]
        xts_of = {}

        def v3(ap):
            return ap.rearrange("p (r t) -> p r t", r=4)

        def prologue(n):
            s, t = tiles[n]
            b = n % 2
            tok0 = s * S_LEN + t * TT
            pos0 = t * TT
            hT, QT, SG, sgb, maT = hTs[b], QTs[b], SGs[b], sgbs[b], maTs[b]
            hk = f"hTN2_{b}"
            ops = []
            xts_of[n] = [None] * NS
            for sub in range(NS):
                def f(sub=sub):
                    xts_of[n][sub] = self.rmsnorm_hT(self.x, tok0 + sub * 128, gcb, hT, hk, sub * 128)
                ops.append(f)
            ops.append(lambda: S.dma(SP, lambda e: e.dma_start(out=maT[:], in_=mav[:, :, tok0:tok0 + TT]), writes=[f"maTN{b}"]))
            for g in range(2):
                ops.append(lambda g=g: S.dma(SP, lambda e: e.dma_start(out=QT[g][96:100, :, :], in_=qaug[g * 4:(g + 1) * 4, :, pos0:pos0 + TT]),
                                             writes=[f"QT{b}{g}0", f"QT{b}{g}1"]))
                for r in range(4):
                    def f(g=g, r=r):
                        pb, pbk = gen.get()
                        hh = g * 4 + r
                        S.op(PE, self.mm8(pb[0:64, 0:TT], lambda kc: wnq[:, kc, hh * 64:(hh + 1) * 64], lambda kc: hT[:, kc, :]),
                             reads=["wnq", hk], writes=[pbk])
                        S.op(DVE, lambda e: e.tensor_copy(out=QT[g][0:64, r, :], in_=pb[0:64, 0:TT]), reads=[pbk],
                             writes=[f"QT{b}{g}0", f"QT{b}{g}1"])
                    ops.append(f)
            for sub in range(NS):
                def f(sub=sub):
                    cs = slice(sub * 128, (sub + 1) * 128)
                    pb, pbk = gen.get()
                    S.op(PE, self.mm8(pb[:, 0:24], lambda kc: hT[:, kc, cs], lambda kc: wng[:, kc, :]), reads=["wng", hk], writes=[pbk])
                    S.op(ACT, lambda e: e.activation(out=SG[:, sub, :], in_=pb[:, 0:24], func=AF.Sigmoid), reads=[pbk], writes=[f"SG{b}"])
                ops.append(f)
            chains = []
            for sub in range(NS):
                for g in range(2):
                    chains.append(cmp_chain(n, s, t, b, sub, g))
            nst = max(len(c) for c in chains)
            for st_ in range(nst):
                for c in chains:
                    if st_ < len(c):
                        ops.append(c[st_])
            for cc in range(8):
                def f(cc=cc):
                    pb, pbk = gen.get()
                    S.op(PE, self.mm8(pb[:, 0:TT], lambda kc: wgb[:, kc, cc * 128:(cc + 1) * 128], lambda kc: hT[:, kc, :]),
                         reads=["wgb", hk], writes=[pbk])
                    S.op(ACT, lambda e: e.activation(out=sgb[:, cc, :], in_=pb[:, 0:TT], func=AF.Sigmoid), reads=[pbk], writes=[f"sgb{b}"])
                ops.append(f)
            return ops

        def cmp_chain(n, s, t, b, sub, g):
            i = t * NS + sub
            qs = slice(sub * 128, (sub + 1) * 128)
            QT, SG = QTs[b], SGs[b]
            ONS = ONSs[b][sub]
            onk = f"ONS{b}{sub}{g}"
            qk = f"QT{b}{g}{sub}"
            rhsQ = QT[g][0:100, :, qs]
            nb = min(127, 8 * i + 8)
            isl = slice(i * 32, (i + 1) * 32)
            st = {}

            def s0():
                st["psc"], st["psck"] = gen.get()
                st["pe"], st["pek"] = cper.get()
                psc, pe = st["psc"], st["pe"]
                S.op(PE, lambda e: e.matmul(v3(psc[0:nb, :]), lhsT=self.KCM[s][g][0:100, 0:nb], rhs=rhsQ, start=True, stop=True),
                     reads=[f"KCM{s}{g}", qk], writes=[st["psck"]])
                S.op(ACT, lambda e: e.activation(out=pe[0:nb, :], in_=psc[0:nb, :], func=AF.Exp, scale=0.125),
                     reads=[st["psck"]], writes=[st["pek"]])

            def s1():
                pe, pek = st["pe"], st["pek"]
                S.op(DVE, lambda e: e.tensor_tensor(out=v3(pe[0:nb, :]), in0=v3(pe[0:nb, :]),
                                                    in1=cmask[0:nb, i * 128:(i + 1) * 128].unsqueeze(1).to_broadcast([nb, 4, 128]), op=ALU.mult),
                     reads=[pek, "cmask"], writes=[pek])

            def s2():
                pe, pek = st["pe"], st["pek"]
                st["pcv"], st["pcvk"] = gen.get()
                pcv = st["pcv"]

                def pvc(e):
                    ins = None
                    for r in range(4):
                        ins = e.matmul(pcv[:, r * 97:(r + 1) * 97], lhsT=pe[0:nb, r * 128:(r + 1) * 128],
                                       rhs=self.VCO[s][g][0:nb, 0:97], start=True, stop=True, skip_group_check=True)
                    return ins
                S.op(PE, pvc, reads=[pek, f"VCO{s}{g}"], writes=[st["pcvk"]])
                pcv, pcvk = st["pcv"], st["pcvk"]
                rd, rdk = rdr.get()
                imp, impk = impr.get()
                st["imp"], st["impk"] = imp, impk
                S.op(DVE, lambda e: e.tensor_scalar(out=rd[:, 0:4], in0=pcv[:, 0:388].rearrange("p (r c) -> p r c", r=4)[:, :, 64],
                                                    scalar1=1e-30, scalar2=None, op0=ALU.add), reads=[pcvk], writes=[rdk])
                S.op(DVE, lambda e: e.reciprocal(out=rd[:, 0:4], in_=rd[:, 0:4]), reads=[rdk], writes=[rdk])
                S.op(DVE, lambda e: e.tensor_tensor(out=rd[:, 4:8], in0=rd[:, 0:4], in1=SG[:, sub, g * 4:g * 4 + 4], op=ALU.mult),
                     reads=[rdk, f"SG{b}"], writes=[rdk])
                pcv3 = pcv[:, 0:388].rearrange("p (r c) -> p r c", r=4)
                S.op(DVE, lambda e: e.tensor_tensor(out=ONS[:, g * 4:(g + 1) * 4, :], in0=pcv3[:, :, 0:64],
                                                    in1=rd[:, 4:8].unsqueeze(2).to_broadcast([128, 4, 64]), op=ALU.mult),
                     reads=[pcvk, rdk], writes=[onk])
                it4, it4k = it4r.get()
                S.op(DVE, lambda e: e.tensor_tensor(out=it4[:], in0=pcv3[:, :, 65:97],
                                                    in1=rd[:, 0:4].unsqueeze(2).to_broadcast([128, 4, 32]), op=ALU.mult),
                     reads=[pcvk, rdk], writes=[it4k])
                S.op(DVE, lambda e: e.tensor_reduce(out=imp[:, 0:32], in_=it4[:].rearrange("p r j -> p j r"), axis=mybir.AxisListType.X,
                                                    op=ALU.add), reads=[it4k], writes=[impk])

            def s3():
                imp, impk = st["imp"], st["impk"]
                S.op(DVE, lambda e: e.tensor_tensor(out=imp[:, 0:32], in0=imp[:, 0:32], in1=impA[:, isl], op=ALU.mult),
                     reads=[impk, "impA"], writes=[impk])
                S.op(DVE, lambda e: e.tensor_tensor(out=imp[:, 0:32], in0=imp[:, 0:32], in1=impB[:, isl], op=ALU.add),
                     reads=[impk, "impB"], writes=[impk])
                S.op(DVE, lambda e: e.max(out=imp[:, 32:40], in_=imp[:, 0:32]), reads=[impk], writes=[impk])
                S.op(DVE, lambda e: e.scalar_tensor_tensor(out=imp[:, 0:32], in0=imp[:, 0:32], scalar=imp[:, 39:40], in1=impNF[:, isl],
                                                           op0=ALU.is_ge, op1=ALU.mult), reads=[impk, "impNF"], writes=[impk])
                SELB, selk = selr.get()
                S.op(DVE, lambda e: e.tensor_scalar(out=SELB[:, 64:96], in0=imp[:, 0:32], scalar1=-1.0, scalar2=30000.0,
                                                    op0=ALU.add, op1=ALU.mult), reads=[impk], writes=[selk])
                st["SELB"], st["selk"] = SELB, selk

            def s4():
                SELB, selk = st["SELB"], st["selk"]
                pst, pstk = gen.get()
                S.op(PE, lambda e: e.matmul(pst[0:96, 0:128], lhsT=SELB[:, 0:96], rhs=self.ident[:, :], start=True, stop=True),
                     reads=[selk, "ident"], writes=[pstk])
                S.op(DVE, lambda e: e.tensor_copy(out=QT[g][64:96, :, qs], in_=pst[64:96, 0:128].unsqueeze(1).to_broadcast([32, 4, 128])),
                     reads=[pstk], writes=[qk])
            return [s0, s1, s2, s3, s4]

        def pair_tasks(n):
            s, t = tiles[n]
            b = n % 2
            QT, SG = QTs[b], SGs[b]
            tasks = []
            for sub in range(NS):
                i = t * NS + sub
                qs = slice(sub * 128, (sub + 1) * 128)
                ONS = ONSs[b][sub]
                for g in range(2):
                    onk = f"ONS{b}{sub}{g}"
                    qk = f"QT{b}{g}{sub}"
                    rhsQ = QT[g][0:100, :, qs]
                    for (KT, kkey, V1, vkey, j0, goff, isw) in (
                            (self.KWT[s][g], f"KWT{s}{g}", self.VW1[s], f"VW1{s}", max(0, i - 4), 16, True),
                            (self.KST[s][g], f"KST{s}{g}", self.VS1[s], f"VS1{s}", 0, 8, False)):
                        acc = {}
                        for j in range(j0, i + 1):
                            tk = {}

                            def A(tk=tk, j=j, KT=KT, kkey=kkey, rhsQ=rhsQ, qk=qk):
                                tk["pss"], tk["pssk"] = scr.get()
                                pss = tk["pss"]
                                S.op(PE, lambda e: e.matmul(v3(pss[:, :]), lhsT=KT[0:100, j * 128:(j + 1) * 128], rhs=rhsQ,
                                                            start=True, stop=True), reads=[kkey, qk], writes=[tk["pssk"]])

                            def B(tk=tk, j=j, i=i, isw=isw):
                                tk["pe"], tk["pek"] = per.get()
                                pe, pek, pss = tk["pe"], tk["pek"], tk["pss"]
                                S.op(ACT, lambda e: e.activation(out=pe[:], in_=pss[:, :], func=AF.Exp, scale=0.125),
                                     reads=[tk["pssk"]], writes=[pek])
                                if j == i:
                                    S.op(DVE, lambda e: e.tensor_tensor(out=pe[:], in0=pe[:], in1=caus[:], op=ALU.mult),
                                         reads=[pek, "caus"], writes=[pek])
                                if isw and i >= 4 and j == i - 4:
                                    S.op(DVE, lambda e: e.tensor_tensor(out=pe[:], in0=pe[:], in1=far[:], op=ALU.mult),
                                         reads=[pek, "far"], writes=[pek])

                            def C(tk=tk, j=j, i=i, j0=j0, acc=acc, V1=V1, vkey=vkey, g=g, goff=goff, ONS=ONS, onk=onk, SG=SG, sub=sub, b=b):
                                if j == j0:
                                    acc["psv"], acc["psvk"] = accr.get()
                                psv, psvk, pe = acc["psv"], acc["psvk"], tk["pe"]

                                def pv(e):
                                    ins = None
                                    for r in range(4):
                                        ins = e.matmul(psv[:, r * 65:(r + 1) * 65], lhsT=pe[:, r * 128:(r + 1) * 128], rhs=V1[:, j, g, :],
                                                       start=(j == j0 and r == 0), stop=(j == i), skip_group_check=True)
                                    return ins
                                S.op(PE, pv, reads=[tk["pek"], vkey], writes=[psvk])
                                if j == i:
                                    rd, rdk = rdr.get()
                                    S.op(DVE, lambda e: e.reciprocal(out=rd[:, 0:4],
                                                                     in_=psv[:, 0:260].rearrange("p (r c) -> p r c", r=4)[:, :, 64]),
                                         reads=[psvk], writes=[rdk])
                                    S.op(DVE, lambda e: e.tensor_tensor(out=rd[:, 4:8], in0=rd[:, 0:4],
                                                                        in1=SG[:, sub, goff + g * 4:goff + g * 4 + 4], op=ALU.mult),
                                         reads=[rdk, f"SG{b}"], writes=[rdk])
                                    ft, ftk = ftr.get()
                                    S.op(DVE, lambda e: e.tensor_tensor(
                                        out=ft[:], in0=psv[:, 0:260].rearrange("p (r c) -> p r c", r=4)[:, :, 0:64],
                                        in1=rd[:, 4:8].unsqueeze(2).to_broadcast([128, 4, 64]), op=ALU.mult), reads=[psvk, rdk], writes=[ftk])
                                    S.op(POOL, lambda e: e.tensor_tensor(out=ONS[:, g * 4:(g + 1) * 4, :], in0=ONS[:, g * 4:(g + 1) * 4, :],
                                                                         in1=ft[:], op=ALU.add), reads=[ftk, onk], writes=[onk])
                            tasks.append((A, B, C))
            return tasks

        def epilogue(n):
            s, t = tiles[n]
            b = n % 2
            tok0 = s * S_LEN + t * TT
            sgb, maT = sgbs[b], maTs[b]
            for sub in range(NS):
                qs = slice(sub * 128, (sub + 1) * 128)
                ONS = ONSs[b][sub]
                S.op(POOL, lambda e, ONS=ONS, sub=sub: e.tensor_copy(out=nso[:, sub, :], in_=ONS[:].rearrange("p h d -> p (h d)")),
                     reads=[f"ONS{b}{sub}0", f"ONS{b}{sub}1"], writes=["nso"])

                def tr(e, sub=sub):
                    ins = None
                    for k4 in range(4):
                        ins = e.transpose(out=self.psb[:, k4 * 128:(k4 + 1) * 128], in_=nso[:, sub, k4 * 128:(k4 + 1) * 128],
                                          identity=self.ident[:])
                    return ins
                S.op(PE, tr, reads=["nso", "ident"], writes=["psb"])
                S.op(DVE, lambda e, qs=qs: e.tensor_copy(out=noT[:, :, qs], in_=self.psb[:, 0:512].rearrange("p (k j) -> p k j", k=4)),
                     reads=["psb"], writes=["noT"])
            for cc in range(8):
                pb, pbk = gen.get()
                S.op(PE, self.mm8(pb[:, 0:TT], lambda k4, cc=cc: wno[:, k4, cc * 128:(cc + 1) * 128], lambda k4: noT[:, k4, :], n=4),
                     reads=["wno", "noT"], writes=[pbk])
                tm, tmk = tmr.get()
                S.op(DVE, lambda e, pb=pb, tm=tm, cc=cc: e.tensor_tensor(out=tm[:], in0=pb[:, 0:TT], in1=sgb[:, cc, :], op=ALU.mult),
                     reads=[pbk, f"sgb{b}"], writes=[tmk])
                S.op(POOL, lambda e, tm=tm, cc=cc: e.tensor_tensor(out=maT[:, cc, :], in0=tm[:], in1=maT[:, cc, :], op=ALU.add),
                     reads=[tmk, f"maTN{b}"], writes=[f"maTN{b}"])
            for sub in range(NS):
                xt, xk = xts_of[n][sub]
                cs = slice(sub * 128, (sub + 1) * 128)
                for half in range(2):
                    pb, pbk = gen.get()
                    S.op(PE, self.mm8(pb[:, :], lambda cc, cs=cs: maT[:, cc, cs], lambda cc, half=half: wout[:, cc, half * 512:(half + 1) * 512]),
                         reads=[f"maTN{b}", "wout"], writes=[pbk])
                    S.op(DVE, lambda e, pb=pb, xt=xt, half=half: e.tensor_tensor(
                        out=xt[:, half * 512:(half + 1) * 512], in0=xt[:, half * 512:(half + 1) * 512], in1=pb[:, :], op=ALU.add),
                        reads=[pbk, xk], writes=[xk])
                r0 = tok0 + sub * 128
                S.dma(SP, lambda e, xt=xt, r0=r0: e.dma_start(out=self.x1_scr[r0:r0 + 128, :], in_=xt[:]), reads=[xk])

        for f in prologue(0):
            f()
        for n in range(len(tiles)):
            pro = prologue(n + 1) if n + 1 < len(tiles) else []
            tasks = pair_tasks(n)
            steps = len(tasks) + LA
            pi = 0
            for k in range(steps):
                if k < len(tasks):
                    tasks[k][0]()
                    tasks[k][1]()
                if k - LA >= 0:
                    tasks[k - LA][2]()
                tgt = ((k + 1) * len(pro) + steps - 1) // steps
                while pi < min(tgt, len(pro)):
                    pro[pi]()
                    pi += 1
            while pi < len(pro):
                pro[pi]()
                pi += 1
            epilogue(n)

    def pass_F(self, src):
        nc, S = self.nc, self.S
        TT = 256
        self.banks = Ring(self.banks.items + self.accb.items)
        wup = self.sb("wup", [128, 8, 2 * DFF], BF16)
        wdn = self.sb("wdn", [128, NFC, D], BF16)
        gcb = self.sb("gcbF", [128, 8 * 128], F32)
        gfin = self.sb("gfin", [128, D], F32)
        cw = self.sb("cw", [128, NFC * 3], F32)
        cb = self.sb("cb", [128, NFC], F32)
        halo = self.sb("halo", [128, NFC, 2], F32)
        hTs = [self.sb(f"hTF{b}", [128, 8, TT], BF16) for b in range(2)]
        uT = self.sb("uT", [128, NFC, TT], BF16)
        t1r = Ring([(self.sb(f"t1_{i}", [128, TT], F32), f"t1_{i}") for i in range(3)])
        ger = Ring([(self.sb(f"ge_{i}", [128, TT], F32), f"ge_{i}") for i in range(2)])
        osb = Ring([(self.sb(f"osb{i}", [128, D], F32), f"osb{i}") for i in range(2)])
        S.dma(SP, lambda e: e.dma_start(out=gcb[:], in_=self.g_ffn), writes=["gcb"])
        S.dma(SP, lambda e: e.dma_start(out=gfin[:], in_=self.g_fin), writes=["gfin"])
        S.dma(SP, lambda e: e.dma_start(out=cw[:], in_=self.convw), writes=["cw"])
        S.dma(SP, lambda e: e.dma_start(out=cb[:], in_=self.convb), writes=["cb"])
        self.load_w(wup, "wup", self.w_up, 0, 2 * DFF, 8)
        self.load_w(wdn, "wdn", self.w_down, 0, D, NFC)
        NTL = self.ntok // TT
        NSB = TT // 128
        xts_next = [self.rmsnorm_hT(src, sub * 128, gcb, hTs[0], "hTF0", sub * 128) for sub in range(NSB)]
        for t in range(NTL):
            tok0 = t * TT
            first = (tok0 % S_LEN) == 0
            hT = hTs[t % 2]
            hk = f"hTF{t % 2}"
            xts = xts_next
            xts_next = []
            pend = []
            deferred = None
            for fc in range(NFC):
                pa, pak = self.banks.get()
                pb, pbk = self.banks.get()

                def mm_a(e, pa=pa, fc=fc, hT=hT):
                    ins = None
                    for kc in range(8):
                        ins = e.matmul(pa[:, 2:2 + TT], lhsT=wup[:, kc, fc * 128:(fc + 1) * 128], rhs=hT[:, kc, :],
                                       start=(kc == 0), stop=(kc == 7))
                    return ins

                def mm_b(e, pb=pb, fc=fc, hT=hT):
                    ins = None
                    for kc in range(8):
                        ins = e.matmul(pb[:, 0:TT], lhsT=wup[:, kc, DFF + fc * 128:DFF + (fc + 1) * 128],
                                       rhs=hT[:, kc, :], start=(kc == 0), stop=(kc == 7))
                    return ins
                S.op(PE, mm_a, reads=["wup", hk], writes=[pak])
                S.op(PE, mm_b, reads=["wup", hk], writes=[pbk])
                if t + 1 < NTL:
                    nb_ = (t + 1) % 2
                    if fc in (2, 6):
                        pend.append(self.rms_a(src, tok0 + TT + (fc // 4) * 128))
                    if fc in (10, 14):
                        sub_ = (fc - 10) // 4
                        xts_next.append(self.rms_b(pend[sub_], gcb, hTs[nb_], f"hTF{nb_}", sub_ * 128))
                if first:
                    S.op(ACT, lambda e, pa=pa: e.memzero(pa[:, 0:2]), reads=[pak], writes=[pak])
                else:
                    S.op(ACT, lambda e, pa=pa, fc=fc: e.copy(out=pa[:, 0:2], in_=halo[:, fc, :]),
                         reads=[pak, "halo"], writes=[pak])
                S.op(ACT, lambda e, pa=pa, fc=fc: e.copy(out=halo[:, fc, :], in_=pa[:, TT:TT + 2]),
                     reads=[pak], writes=["halo"])
                t1, t1k = t1r.get()
                S.op(ACT, lambda e, pa=pa, fc=fc, t1=t1: e.activation(
                    out=t1[:], in_=pa[:, 2:2 + TT], func=AF.Copy, scale=cw[:, fc * 3 + 2:fc * 3 + 3]),
                    reads=[pak, "cw"], writes=[t1k])
                S.op(DVE, lambda e, pa=pa, fc=fc, t1=t1: e.scalar_tensor_tensor(
                    out=t1[:], in0=pa[:, 1:1 + TT], scalar=cw[:, fc * 3 + 1:fc * 3 + 2], in1=t1[:],
                    op0=ALU.mult, op1=ALU.add), reads=[pak, "cw", t1k], writes=[t1k])
                S.op(DVE, lambda e, pa=pa, fc=fc, t1=t1: e.scalar_tensor_tensor(
                    out=t1[:], in0=pa[:, 0:TT], scalar=cw[:, fc * 3:fc * 3 + 1], in1=t1[:],
                    op0=ALU.mult, op1=ALU.add), reads=[pak, "cw", t1k], writes=[t1k])

                def fin(fc=fc, t1=t1, t1k=t1k, pb=pb, pbk=pbk):
                    ge, gek = ger.get()
                    S.op(ACT, lambda e: e.activation(out=ge[:], in_=t1[:], func=AF.Gelu_apprx_tanh, bias=cb[:, fc:fc + 1]),
                         reads=[t1k, "cb"], writes=[gek])
                    S.op(DVE, lambda e: e.tensor_tensor(out=uT[:, fc, :], in0=ge[:], in1=pb[:, 0:TT], op=ALU.mult),
                         reads=[gek, pbk], writes=["uT"])
                if deferred is not None:
                    deferred()
                deferred = fin
            deferred()
            deferred = None
            for sub in range(TT // 128):
                xt, xk = xts[sub]
                ss, sk = self.ss.get()
                for half in range(2):
                    po, pok = self.banks.get()

                    def mm_o(e, po=po, sub=sub, half=half):
                        ins = None
                        for fc in range(NFC):
                            ins = e.matmul(po[:, :], lhsT=uT[:, fc, sub * 128:(sub + 1) * 128],
                                           rhs=wdn[:, fc, half * 512:(half + 1) * 512],
                                           start=(fc == 0), stop=(fc == NFC - 1))
                        return ins
                    S.op(PE, mm_o, reads=["uT", "wdn"], writes=[pok])
                    S.op(DVE, lambda e, po=po, xt=xt, half=half: e.tensor_tensor(
                        out=xt[:, half * 512:(half + 1) * 512], in0=xt[:, half * 512:(half + 1) * 512],
                        in1=po[:, :], op=ALU.add), reads=[pok, xk], writes=[xk])
                ob, obk = osb.get()
                S.op(ACT, lambda e, xt=xt, ss=ss: e.activation(out=self.junk[:], in_=xt[:], func=AF.Square,
                                                               accum_out=ss[:]), reads=[xk], writes=[sk])
                S.op(ACT, lambda e, ss=ss: e.activation(out=ss[:], in_=ss[:], func=AF.Sqrt, scale=1.0 / D,
                                                        bias=self.epsb[:]), reads=[sk, "epsb"], writes=[sk])
                S.op(DVE, lambda e, ss=ss: e.reciprocal(out=ss[:], in_=ss[:]), reads=[sk], writes=[sk])
                S.op(DVE, lambda e, xt=xt, ss=ss, ob=ob: e.scalar_tensor_tensor(
                    out=ob[:], in0=xt[:], scalar=ss[:, 0:1], in1=gfin[:], op0=ALU.mult, op1=ALU.mult),
                    reads=[xk, sk, "gfin"], writes=[obk])
                r0 = tok0 + sub * 128
                S.dma(SP, lambda e, ob=ob, r0=r0: e.dma_start(out=self.out[r0:r0 + 128, :], in_=ob[:]),
                      reads=[obk])


def host_inputs(inp, nseq, core):
    f = np.float32
    x = np.ascontiguousarray(inp["x"][core * nseq:(core + 1) * nseq].reshape(nseq * S_LEN, D))

    def gcol(g):
        return np.ascontiguousarray(np.broadcast_to(g.reshape(8, 128).T[:, :, None], (128, 8, 128)).reshape(128, 1024))
    m = {
        "x": x,
        "w_in": np.ascontiguousarray(inp["w_in"][0]),
        "w_up": np.ascontiguousarray(inp["w_up"][0]),
        "w_down": np.ascontiguousarray(inp["w_down"][0]),
        "g_ffn": gcol(inp["norm_ffn"][0]),
        "g_mix": gcol(inp["norm_mix"][0]),
        "g_fin": np.ascontiguousarray(np.broadcast_to(inp["norm_final"][None, :], (128, D))),
        "convw": np.ascontiguousarray(inp["conv_w"][0].reshape(3, NFC, 128).transpose(2, 1, 0).reshape(128, NFC * 3)),
        "convb": np.ascontiguousarray(inp["conv_b"][0].reshape(NFC, 128).T),
    }
    m.update({
        "w_ret_o": np.ascontiguousarray(inp["w_ret_o"][0]),
        "w_nsa_o": np.ascontiguousarray(inp["w_nsa_o"][0]),
        "w_out": np.ascontiguousarray(inp["w_out"][0]),
        "gng": gcol(inp["ret_gn_g"][0]),
        "w1k": np.ascontiguousarray(inp["cmp_w1_k"][0]),
        "w1v": np.ascontiguousarray(inp["cmp_w1_v"][0]),
        "w2k": np.ascontiguousarray(inp["cmp_w2_k"][0]),
        "w2v": np.ascontiguousarray(inp["cmp_w2_v"][0]),
        "b1k": np.ascontiguousarray(inp["cmp_b1_k"][0].reshape(2, 128).T),
        "b1v": np.ascontiguousarray(inp["cmp_b1_v"][0].reshape(2, 128).T),
        "posTk": np.ascontiguousarray(np.repeat(inp["cmp_pos_k"][0].T[:, :, None], 2, axis=2).reshape(64, 64)),
        "posTv": np.ascontiguousarray(np.repeat(inp["cmp_pos_v"][0].T[:, :, None], 2, axis=2).reshape(64, 64)),
    })
    m.update(make_consts())
    return m


_CACHE = {}


def kernel(**inputs):
    inp = {k: np.asarray(v) for k, v in inputs.items()}
    ncores, nseq = 8, 2
    if "prog" not in _CACHE:
        p = Prog(nseq=nseq)
        p.build()
        _CACHE["prog"] = p
    p = _CACHE["prog"]
    in_maps = []
    for c in range(ncores):
        m = host_inputs(inp, nseq, c)
        in_maps.append({k: m[k] for k in p.in_names})
    res = run_bass_kernel_spmd(p.nc, in_maps, core_ids=list(range(ncores)))
    out = np.concatenate([r["out"] for r in res.results], axis=0)
    return out.reshape(16, S_LEN, D).astype(np.float32)
```

```python
import contextlib
import numpy as np
STAGE = 9
import ml_dtypes
import concourse.bass as bass
import concourse.mybir as mybir
from concourse.bass_utils import run_bass_kernel_spmd

F32 = mybir.dt.float32
BF16 = mybir.dt.bfloat16
ALU = mybir.AluOpType
AF = mybir.ActivationFunctionType

PE, ACT, DVE, POOL, SP = "tensor", "scalar", "vector", "gpsimd", "sync"
ENGS = (PE, ACT, DVE, POOL, SP)
NDMASEM = 8

S_LEN = 2048
D = 1024
DFF = 2816
NFC = DFF // 128
EPS = 1e-6
N_IN = 6424
O_RQ, O_RK, O_RV, O_RG, O_NQ, O_KCR, O_VCR, O_KS, O_VS, O_KW, O_VW, O_NG, O_GA, O_GB = (
    0, 512, 1024, 2048, 3072, 3584, 3712, 3840, 3968, 4096, 4224, 4352, 4376, 5400)


class Op:
    __slots__ = ("eng", "fn", "reads", "writes", "is_dma", "waits", "inc", "ticket", "dsem", "dval")

    def __init__(self, eng, fn, reads, writes, is_dma):
        self.eng = eng
        self.fn = fn
        self.reads = reads
        self.writes = writes
        self.is_dma = is_dma
        self.waits = []
        self.inc = False
        self.ticket = None
        self.dsem = None
        self.dval = None


class Sched:
    def __init__(self, nc):
        self.nc = nc
        self.ops = []
        self.last_writer = {}
        self.readers = {}
        self.dma_count = {e: 0 for e in ENGS}
        self.dma_hist = {e: [] for e in ENGS}
        self.last_op = {e: None for e in ENGS}

    def op(self, eng, fn, reads=(), writes=()):
        o = Op(eng, fn, tuple(reads), tuple(writes), False)
        self._add(o)
        return o

    def dma(self, eng, fn, reads=(), writes=()):
        o = Op(eng, fn, tuple(reads), tuple(writes), True)
        i = self.dma_count[eng]
        self.dma_count[eng] += 1
        o.dsem = (eng, i % NDMASEM)
        o.dval = 16 * (i // NDMASEM + 1)
        self.dma_hist[eng].append(o)
        if i >= NDMASEM:
            o.waits.append(self.dma_hist[eng][i - NDMASEM])
        self._add(o)
        return o

    def barrier(self):
        tails = []
        for e in ENGS:
            if self.last_op[e] is not None and not self.last_op[e].is_dma:
                tails.append(self.last_op[e])
            tails.extend(self.dma_hist[e][-NDMASEM:])
        for e in ENGS:
            o = Op(e, None, (), (), False)
            o.waits = [t for t in tails]
            self.ops.append(o)
            self.last_op[e] = o
        self.last_writer = {}
        self.readers = {}

    def _add(self, o):
        self.ops.append(o)
        deps = o.waits
        for k in o.reads:
            w = self.last_writer.get(k)
            if w is not None:
                deps.append(w)
            if k.startswith("ps"):
                for r in self.readers.get(k, ()):
                    if r.eng != o.eng:
                        deps.append(r)
        for k in o.writes:
            w = self.last_writer.get(k)
            if w is not None and (w.is_dma or o.is_dma or w.eng != o.eng):
                deps.append(w)
            for r in self.readers.get(k, ()):
                if r.is_dma or o.is_dma or r.eng != o.eng:
                    deps.append(r)
        for k in o.reads:
            self.readers.setdefault(k, []).append(o)
        for k in o.writes:
            self.last_writer[k] = o
            self.readers[k] = []
        if not o.is_dma:
            self.last_op[o.eng] = o

    def run(self):
        nc = self.nc
        for o in self.ops:
            for d in o.waits:
                if not d.is_dma:
                    d.inc = True
        cnt = {e: 0 for e in ENGS}
        for o in self.ops:
            if o.fn is None:
                o.inc = False
            if not o.is_dma and o.inc:
                cnt[o.eng] += 1
                o.ticket = cnt[o.eng]
        seen = {e: {} for e in ENGS}
        plans = {e: [] for e in ENGS}
        for o in self.ops:
            e = o.eng
            wl = {}
            for d in o.waits:
                if d.is_dma:
                    key = ("d",) + d.dsem
                    val = d.dval
                else:
                    if d.ticket is None:
                        continue
                    key = ("c", d.eng)
                    val = d.ticket
                if seen[e].get(key, 0) >= val:
                    continue
                if wl.get(key, 0) < val:
                    wl[key] = val
            for key, val in wl.items():
                seen[e][key] = val
            plans[e].append((o, list(wl.items())))
        with contextlib.ExitStack() as st:
            sems = {}
            for e in ENGS:
                sems[("c", e)] = st.enter_context(nc.semaphore(f"c_{e}"))
                for j in range(min(NDMASEM, self.dma_count[e])):
                    sems[("d", e, j)] = st.enter_context(nc.semaphore(f"d_{e}_{j}"))
            block = st.enter_context(nc.Block())

            def mk(e):
                plan = plans[e]

                def body(eng):
                    for o, wl in plan:
                        for key, val in wl:
                            eng.wait_ge(sems[key], val)
                        if o.fn is None:
                            continue
                        ins = o.fn(eng)
                        if o.is_dma:
                            ins.then_inc(sems[("d",) + o.dsem], 16)
                        elif o.inc:
                            ins.then_inc(sems[("c", e)], 1)
                    n = self.dma_count[e]
                    for j in range(min(NDMASEM, n)):
                        eng.wait_ge(sems[("d", e, j)], 16 * ((n - 1 - j) // NDMASEM + 1))
                return body

            block.tensor(mk(PE))
            block.scalar(mk(ACT))
            block.vector(mk(DVE))
            block.gpsimd(mk(POOL))
            block.sync(mk(SP))


class Ring:
    def __init__(self, items):
        self.items = items
        self.i = 0

    def get(self):
        it = self.items[self.i % len(self.items)]
        self.i += 1
        return it


def make_consts():
    c = {}
    bf = ml_dtypes.bfloat16
    f = np.float32
    c["ident"] = np.eye(128, dtype=f).astype(bf)
    lg = np.log1p(-np.exp2(-5.0 - np.arange(4, dtype=np.float64)))
    pos = np.arange(128, dtype=np.float64)
    diff = pos[None, :] - pos[:, None]
    dmt = np.where(diff >= 0, np.exp(lg[:, None, None] * np.maximum(diff, 0.0)), 0.0)
    c["c_dmt"] = np.ascontiguousarray(dmt.transpose(1, 0, 2).reshape(128, 512)).astype(f)
    xi = np.exp(lg[:, None] * (pos + 1.0))
    xi2 = np.tile(xi, (1, 2))
    c["c_xi"] = np.ascontiguousarray(np.broadcast_to(xi2.reshape(1, 4 * 256), (128, 4 * 256))).astype(f)
    zeta = np.exp(lg[:, None] * (127 - pos)) * (128 ** -0.5)
    c["c_zeta"] = np.ascontiguousarray(np.repeat(zeta.T[:, :, None], 128, axis=2).reshape(128, 512)).astype(f)
    slopes = np.exp2(-(np.arange(8, dtype=np.float64) + 1.0)).reshape(2, 4)
    t = np.arange(S_LEN)
    qaug = np.zeros((2, 4, 4, S_LEN))
    for g in range(2):
        for r in range(4):
            sl = slopes[g, r]
            qaug[g, 0, r] = 8 * sl * 128
            qaug[g, 1, r] = 8 * sl
            qaug[g, 2, r] = -8 * sl * 128 * (t // 128)
            qaug[g, 3, r] = -8 * sl * (t % 128)
    c["c_qaug"] = qaug.reshape(2 * 4, 4 * S_LEN).astype(bf)
    kaug = np.stack([t // 128, t % 128, np.ones(S_LEN), np.ones(S_LEN)]).astype(np.float64)
    c["c_kaug"] = kaug.astype(bf)
    pc = 16 * np.arange(127) + 31
    caug = np.stack([pc // 128, pc % 128, np.ones(127), np.ones(127)]).astype(np.float64)
    c["c_caug"] = caug.astype(bf)
    c["c_ehot"] = (np.arange(32)[:, None] == (t[None, :] // 64)).astype(f).astype(bf)
    tt = (128 * np.arange(16)[:, None] + np.arange(128)[None, :])
    cm = (pc[:, None, None] <= tt[None, :, :]).astype(f)
    c["c_cmask"] = np.ascontiguousarray(cm.reshape(127, 16 * 128)).astype(bf)
    kb = np.arange(128)
    c["c_caus"] = np.ascontiguousarray(np.tile((kb[:, None] <= kb[None, :]).astype(f), (1, 4))).astype(bf)
    c["c_far"] = np.ascontiguousarray(np.tile((kb[:, None] > kb[None, :]).astype(f), (1, 4))).astype(bf)
    cur = tt // 64
    jj = np.arange(32)
    force = (jj[None, None, :] == 0) | (jj[None, None, :] == cur[:, :, None]) | (jj[None, None, :] == cur[:, :, None] - 1)
    fut = jj[None, None, :] > cur[:, :, None]
    A = 1.0 - force - fut
    Bm = 1e9 * force - 1e9 * fut
    NF = 1.0 - fut
    c["c_impA"] = np.ascontiguousarray(A.transpose(1, 0, 2).reshape(128, 16 * 32)).astype(f)
    c["c_impB"] = np.ascontiguousarray(Bm.transpose(1, 0, 2).reshape(128, 16 * 32)).astype(f)
    c["c_impNF"] = np.ascontiguousarray(NF.transpose(1, 0, 2).reshape(128, 16 * 32)).astype(f)
    cs = 16 * np.arange(127)
    js = 64 * np.arange(32)
    ovl = ((cs[:, None] < js[None, :] + 64) & (cs[:, None] + 32 > js[None, :])).astype(f)
    c["c_ovl"] = np.concatenate([np.ones((127, 1), f), ovl], axis=1).astype(bf)
    return c


class Prog:
    def __init__(self, nseq=2, passes="RNF", dbg=False):
        self.nseq = nseq
        self.ntok = nseq * S_LEN
        self.passes = passes
        self.dbg = dbg
        nc = self.nc = bass.Bass("TRN2", target_bir_lowering=False)
        self.S = Sched(nc)
        self.in_names = []

    def din(self, name, shape, dt=F32):
        self.in_names.append(name)
        return self.nc.dram_tensor(name, list(shape), dt, kind="ExternalInput").ap()

    def build(self):
        nc, S = self.nc, self.S
        NT = self.ntok
        self.x = self.din("x", [NT, D])
        self.w_in = self.din("w_in", [D, N_IN])
        self.w_up = self.din("w_up", [D, 2 * DFF])
        self.w_down = self.din("w_down", [DFF, D])
        self.g_ffn = self.din("g_ffn", [128, 8 * 128])
        self.g_mix = self.din("g_mix", [128, 8 * 128])
        self.g_fin = self.din("g_fin", [128, D])
        self.convw = self.din("convw", [128, NFC * 3])
        self.convb = self.din("convb", [128, NFC])
        self.ident_d = self.din("ident", [128, 128], BF16)
        self.w_ret_o = self.din("w_ret_o", [D, D])
        self.w_nsa_o = self.din("w_nsa_o", [512, D])
        self.w_out = self.din("w_out", [D, D])
        self.gng = self.din("gng", [128, 8 * 128])
        self.w1 = [self.din("w1k", [2048, 256]), self.din("w1v", [2048, 256])]
        self.w2 = [self.din("w2k", [256, 64]), self.din("w2v", [256, 64])]
        self.b1 = [self.din("b1k", [128, 2]), self.din("b1v", [128, 2])]
        self.posT = [self.din("posTk", [64, 64]), self.din("posTv", [64, 64])]
        self.cd = {}
        for k, v in make_consts().items():
            if k != "ident":
                self.cd[k] = self.din(k, v.shape, BF16 if v.dtype == ml_dtypes.bfloat16 else F32)
        self.out = nc.dram_tensor("out", [NT, D], F32, kind="ExternalOutput").ap()
        self.x1_scr = nc.dram_tensor("x1_scr", [NT, D], F32, kind="Internal").ap()
        self.ma_scr = nc.dram_tensor("ma_scr", [128, 8 * NT], BF16,
                                     kind="ExternalOutput" if self.dbg else "Internal").ap()

        with contextlib.ExitStack() as st:
            self.st = st
            banks = []
            for i in range(7):
                t = st.enter_context(nc.psum_tensor(f"psf{i}", [128, 512], F32))
                banks.append((t, f"psf{i}"))
            self.banks = Ring(banks[0:5])
            self.accb = Ring(banks[5:7])
            self.psb = st.enter_context(nc.psum_tensor("psb", [128, 1024], BF16))
            self.ident = self.sb("ident", [128, 128], BF16)
            self.epsb = self.sb("epsb", [128, 1], F32)
            S.dma(SP, lambda e: e.dma_start(out=self.ident[:], in_=self.ident_d), writes=["ident"])
            S.op(DVE, lambda e: e.memset(self.epsb[:], EPS), writes=["epsb"])
            self.junk = self.sb("junk", [128, D], BF16)
            self.xt = Ring([(self.sb(f"xt{i}", [128, D], F32), f"xt{i}") for i in range(4)])
            self.xn = Ring([(self.sb(f"xn{i}", [128, D], BF16), f"xn{i}") for i in range(3)])
            self.ss = Ring([(self.sb(f"ss{i}", [128, 1], F32), f"ss{i}") for i in range(4)])
            if "R" in self.passes:
                with contextlib.ExitStack() as st2:
                    self.st = st2
                    self.pass_R()
                    S.barrier()
            if "N" in self.passes:
                with contextlib.ExitStack() as st2:
                    self.st = st2
                    self.pass_N_alloc()
                    with contextlib.ExitStack() as st3:
                        self.st = st3
                        self.pass_N1()
                        S.barrier()
                    with contextlib.ExitStack() as st3:
                        self.st = st3
                        self.pass_N2()
                        S.barrier()
            if "F" in self.passes:
                with contextlib.ExitStack() as st2:
                    self.st = st2
                    self.pass_F(self.x1_scr if "N" in self.passes else self.x)
                    S.barrier()
            S.run()
        return nc

    def sb(self, name, shape, dt):
        self._uid = getattr(self, "_uid", 0) + 1
        return self.st.enter_context(self.nc.sbuf_tensor(f"s{self._uid}_{name}", list(shape), dt))

    def load_w(self, dst, dkey, src_rows, c0, ncols, kcs, dcol0=0, rows=128):
        S = self.S
        for kc in range(kcs):
            for cc in range(0, ncols, 2048):
                n = min(2048, ncols - cc)
                S.dma(POOL, lambda e, kc=kc, cc=cc, n=n: e.dma_start(
                    out=dst[0:rows, kc, dcol0 + cc:dcol0 + cc + n],
                    in_=src_rows[kc * rows:(kc + 1) * rows, c0 + cc:c0 + cc + n]), writes=[dkey])

    def cload(self, name, shape, dt, src, eng=SP):
        t = self.sb(name, shape, dt)
        self.S.dma(eng, lambda e: e.dma_start(out=t[:], in_=src), writes=[name])
        return t

    def mm8(self, out, lhs_fn, rhs_fn, n=8):
        def f(e):
            ins = None
            for kc in range(n):
                ins = e.matmul(out, lhsT=lhs_fn(kc), rhs=rhs_fn(kc), start=(kc == 0), stop=(kc == n - 1))
            return ins
        return f

    def pass_R(self):
        nc, S = self.nc, self.S
        TT = 256
        NS = TT // 128
        lg = np.log1p(-np.exp2(-5.0 - np.arange(4, dtype=np.float64)))
        decay = [float(np.exp(lg[h] * 128)) for h in range(4)]
        allb = self.banks.items + self.accb.items
        gen = Ring(allb[0:2])
        (pin, pink), (po0, po0k), (po1, po1k), (pk0, pk0k), (pk1, pk1k) = allb[2:7]
        pos_ = [(po0, po0k), (po1, po1k)]
        pks_ = [(pk0, pk0k), (pk1, pk1k)]
        wr = self.sb("wr", [128, 8, 3072], BF16)
        wga = self.sb("wga", [128, 8, D], BF16)
        wro = self.sb("wro", [128, 8, D], BF16)
        gcb = self.cload("gcb", [128, 1024], F32, self.g_mix)
        gngb = self.cload("gngb", [128, 1024], F32, self.gng)
        dmt = self.cload("dmt", [128, 512], F32, self.cd["c_dmt"])
        xi = self.cload("xi", [128, 4 * 256], F32, self.cd["c_xi"])
        zeta = self.cload("zeta", [128, 512], F32, self.cd["c_zeta"])
        self.load_w(wr, "wr", self.w_in, O_RQ, 3072, 8)
        self.load_w(wga, "wga", self.w_in, O_GA, D, 8)
        self.load_w(wro, "wro", self.w_ret_o, 0, D, 8)
        hTs = [self.sb(f"hTR{b}", [128, 8, TT], BF16) for b in range(2)]
        qTs = [self.sb(f"qT{b}", [128, 4, TT], BF16) for b in range(2)]
        qxTs = [self.sb(f"qxT{b}", [128, 4, TT], BF16) for b in range(2)]
        kTs = [self.sb(f"kT{b}", [128, 4, TT], BF16) for b in range(2)]
        kzs = [self.sb(f"kz{b}", [128, NS, 512], BF16) for b in range(2)]
        vs = [self.sb(f"v{b}", [128, NS, D], BF16) for b in range(2)]
        sgs = [self.sb(f"sg{b}", [128, NS, D], BF16) for b in range(2)]
        sga = self.sb("sga", [128, 8, TT], BF16)
        roT = self.sb("roT", [128, 8, TT], BF16)
        maT = self.sb("maT", [128, 8, TT], BF16)
        R = self.sb("R", [128, 4, 256], F32)
        Rb = self.sb("Rb", [128, 4, 256], BF16)
        inT = self.sb("inT4", [128, 512], BF16)
        on = self.sb("on4", [128, D], F32)
        stt = self.sb("stt", [128, 4, 6], F32)
        mv = self.sb("mv", [128, 4, 4], F32)
        NTL = self.ntok // TT

        def proj(t):
            b = t % 2
            hT, qT, qxT, kT, kz, v, sg = hTs[b], qTs[b], qxTs[b], kTs[b], kzs[b], vs[b], sgs[b]
            hk = f"hTR{b}"
            ops = []
            for h in range(4):
                def fq(h=h):
                    pq, pqk = gen.get()
                    S.op(PE, self.mm8(pq[:, 0:TT], lambda kc: wr[:, kc, O_RQ + h * 128:O_RQ + (h + 1) * 128], lambda kc: hT[:, kc, :]),
                         reads=["wr", hk], writes=[pqk])
                    S.op(ACT, lambda e: e.copy(out=qT[:, h, :], in_=pq[:, 0:TT]), reads=[pqk], writes=[f"qT{b}"])
                    S.op(DVE, lambda e: e.tensor_tensor(out=qxT[:, h, :], in0=pq[:, 0:TT], in1=xi[:, h * 256:(h + 1) * 256], op=ALU.mult),
                         reads=[pqk, "xi"], writes=[f"qxT{b}"])

                def fk(h=h):
                    pk, pkk = gen.get()
                    S.op(PE, self.mm8(pk[:, 0:TT], lambda kc: wr[:, kc, 512 + h * 128:512 + (h + 1) * 128], lambda kc: hT[:, kc, :]),
                         reads=["wr", hk], writes=[pkk])
                    S.op(ACT, lambda e: e.mul(out=kT[:, h, :], in_=pk[:, 0:TT], mul=128 ** -0.5), reads=[pkk], writes=[f"kT{b}"])
                ops += [fq, fk]
            for c in range(NS):
                cs = slice(c * 128, (c + 1) * 128)

                def fz(c=c, cs=cs):
                    pk, pkk = gen.get()
                    S.op(PE, self.mm8(pk[:, :], lambda kc: hT[:, kc, cs], lambda kc: wr[:, kc, 512:1024]), reads=["wr", hk], writes=[pkk])
                    S.op(DVE, lambda e: e.tensor_tensor(out=kz[:, c, :], in0=pk[:, :], in1=zeta[:], op=ALU.mult),
                         reads=[pkk, "zeta"], writes=[f"kz{b}"])
                ops.append(fz)
                for half in range(2):
                    def fv(c=c, cs=cs, half=half):
                        pv, pvk = gen.get()
                        S.op(PE, self.mm8(pv[:, :], lambda kc: hT[:, kc, cs], lambda kc: wr[:, kc, 1024 + half * 512:1024 + (half + 1) * 512]),
                             reads=["wr", hk], writes=[pvk])
                        S.op(ACT, lambda e: e.copy(out=v[:, c, half * 512:(half + 1) * 512], in_=pv[:, :]), reads=[pvk], writes=[f"v{b}"])

                    def fg(c=c, cs=cs, half=half):
                        pg, pgk = gen.get()
                        S.op(PE, self.mm8(pg[:, :], lambda kc: hT[:, kc, cs], lambda kc: wr[:, kc, 2048 + half * 512:2048 + (half + 1) * 512]),
                             reads=["wr", hk], writes=[pgk])
                        S.op(ACT, lambda e: e.activation(out=sg[:, c, half * 512:(half + 1) * 512], in_=pg[:, :], func=AF.Silu),
                             reads=[pgk], writes=[f"sg{b}"])
                    ops += [fv, fg]
            return ops

        def gates(t):
            b = t % 2
            hT = hTs[b]
            ops = []
            for cc in range(8):
                def f(cc=cc):
                    pga, pgak = gen.get()
                    S.op(PE, self.mm8(pga[:, 0:TT], lambda kc: wga[:, kc, cc * 128:(cc + 1) * 128], lambda kc: hT[:, kc, :]),
                         reads=["wga", f"hTR{b}"], writes=[pgak])
                    S.op(ACT, lambda e: e.activation(out=sga[:, cc, :], in_=pga[:, 0:TT], func=AF.Sigmoid), reads=[pgak], writes=["sga"])
                ops.append(f)
            return ops

        def chunk_stages(t):
            b = t % 2
            qT, qxT, kT, kz, v, sg = qTs[b], qxTs[b], kTs[b], kzs[b], vs[b], sgs[b]
            tok0 = t * TT
            stages = []
            for c in range(NS):
                cs = slice(c * 128, (c + 1) * 128)
                first = ((tok0 + c * 128) % S_LEN) == 0

                def st0(cs=cs):
                    def mmi(e):
                        ins = None
                        for h in range(4):
                            ins = e.matmul(pin[:, h * 128:(h + 1) * 128], lhsT=kT[:, h, cs], rhs=qT[:, h, cs], start=True, stop=True,
                                           skip_group_check=True)
                        return ins
                    S.op(PE, mmi, reads=[f"kT{b}", f"qT{b}"], writes=[pink])
                    S.op(DVE, lambda e: e.tensor_tensor(out=inT[:], in0=pin[:, :], in1=dmt[:], op=ALU.mult), reads=[pink, "dmt"], writes=["inT4"])

                def st1(c=c, cs=cs, first=first):
                    for hp in range(2):
                        po, pok = pos_[hp]

                        def mmo(e, po=po, hp=hp):
                            ins = None
                            for hh in range(2):
                                h = hp * 2 + hh
                                ins = e.matmul(po[:, hh * 256:(hh + 1) * 256], lhsT=inT[:, h * 128:(h + 1) * 128], rhs=v[:, c, h * 256:(h + 1) * 256],
                                               start=True, stop=first, skip_group_check=True)
                                if not first:
                                    ins = e.matmul(po[:, hh * 256:(hh + 1) * 256], lhsT=qxT[:, h, cs], rhs=Rb[:, h, :], start=False, stop=True,
                                                   skip_group_check=True)
                            return ins
                        S.op(PE, mmo, reads=["inT4", f"v{b}", f"qxT{b}", "Rb"], writes=[pok])
                    for hp in range(2):
                        pk, pkk = pks_[hp]

                        def mmk(e, pk=pk, hp=hp):
                            ins = None
                            for hh in range(2):
                                h = hp * 2 + hh
                                ins = e.matmul(pk[:, hh * 256:(hh + 1) * 256], lhsT=kz[:, c, h * 128:(h + 1) * 128], rhs=v[:, c, h * 256:(h + 1) * 256],
                                               start=True, stop=True, skip_group_check=True)
                            return ins
                        S.op(PE, mmk, reads=[f"kz{b}", f"v{b}"], writes=[pkk])

                def st2(first=first):
                    for h in range(4):
                        pk, pkk = pks_[h // 2]
                        src = pk[:, (h % 2) * 256:(h % 2 + 1) * 256]
                        if first:
                            S.op(DVE, lambda e, h=h, src=src: e.tensor_copy(out=R[:, h, :], in_=src), reads=[pkk], writes=["R"])
                        else:
                            S.op(DVE, lambda e, h=h, src=src: e.scalar_tensor_tensor(out=R[:, h, :], in0=R[:, h, :], scalar=decay[h], in1=src,
                                                                                     op0=ALU.mult, op1=ALU.add), reads=[pkk, "R"], writes=["R"])
                    S.op(ACT, lambda e: e.copy(out=Rb[:].rearrange("p h e -> p (h e)"), in_=R[:].rearrange("p h e -> p (h e)")),
                         reads=["R"], writes=["Rb"])
                    for h in range(4):
                        po, pok = pos_[h // 2]
                        S.op(DVE, lambda e, h=h, po=po: e.bn_stats(out=stt[:, h, :], in_=po[:, (h % 2) * 256:(h % 2 + 1) * 256]),
                             reads=[pok], writes=["stt"])
                    for h in range(4):
                        S.op(DVE, lambda e, h=h: e.bn_aggr(out=mv[:, h, 0:2], in_=stt[:, h, :]), reads=["stt"], writes=["mv"])
                    S.op(ACT, lambda e: e.activation(out=mv[:, :, 2], in_=mv[:, :, 1], func=AF.Sqrt, bias=self.epsb[:]),
                         reads=["mv", "epsb"], writes=["mv"])
                    S.op(DVE, lambda e: e.reciprocal(out=mv[:, :, 2], in_=mv[:, :, 2]), reads=["mv"], writes=["mv"])
                    S.op(DVE, lambda e: e.scalar_tensor_tensor(out=mv[:, :, 3], in0=mv[:, :, 0], scalar=-1.0, in1=mv[:, :, 2],
                                                               op0=ALU.mult, op1=ALU.mult), reads=["mv"], writes=["mv"])

                def st3(c=c):
                    for h in range(4):
                        po, pok = pos_[h // 2]
                        S.op(ACT, lambda e, h=h, po=po: e.activation(out=on[:, h * 256:(h + 1) * 256], in_=po[:, (h % 2) * 256:(h % 2 + 1) * 256],
                                                                     func=AF.Identity, scale=mv[:, h, 2:3], bias=mv[:, h, 3:4]),
                             reads=[pok, "mv"], writes=["on4"])
                    S.op(DVE, lambda e: e.tensor_tensor(out=sg[:, c, :], in0=on[:], in1=sg[:, c, :], op=ALU.mult),
                         reads=["on4", f"sg{b}"], writes=[f"sg{b}"])
                stages += [st0, st1, st2, st3]
            return stages

        def tail(t):
            b = t % 2
            sg = sgs[b]
            tok0 = t * TT
            for c in range(NS):
                cs = slice(c * 128, (c + 1) * 128)

                def tr(e, c=c):
                    ins = None
                    for kc in range(8):
                        ins = e.transpose(out=self.psb[:, kc * 128:(kc + 1) * 128], in_=sg[:, c, kc * 128:(kc + 1) * 128],
                                          identity=self.ident[:])
                    return ins
                S.op(PE, tr, reads=[f"sg{b}", "ident"], writes=["psb"])
                S.op(DVE, lambda e, cs=cs: e.tensor_tensor(out=roT[:, :, cs], in0=self.psb[:].rearrange("p (k j) -> p k j", k=8),
                                                           in1=gngb[:].rearrange("p (k j) -> p k j", k=8), op=ALU.mult),
                     reads=["psb", "gngb"], writes=["roT"])
            for cc in range(8):
                ccs = slice(cc * 128, (cc + 1) * 128)
                pya, pyak = gen.get()
                S.op(PE, self.mm8(pya[:, 0:TT], lambda kc, ccs=ccs: wro[:, kc, ccs], lambda kc: roT[:, kc, :]),
                     reads=["wro", "roT"], writes=[pyak])
                S.op(DVE, lambda e, pya=pya, cc=cc: e.tensor_tensor(out=maT[:, cc, :], in0=sga[:, cc, :], in1=pya[:, 0:TT], op=ALU.mult),
                     reads=[pyak, "sga"], writes=["maT"])
            S.dma(SP, lambda e: e.dma_start(out=self.ma_scr.rearrange("p (c n) -> p c n", c=8)[:, :, tok0:tok0 + TT], in_=maT[:]),
                  reads=["maT"])

        for sub in range(NS):
            self.rmsnorm_hT(self.x, sub * 128, gcb, hTs[0], "hTR0", sub * 128)
        for f in proj(0):
            f()
        for t in range(NTL):
            stages = chunk_stages(t)
            fill = gates(t)
            if t + 1 < NTL:
                nb_ = (t + 1) % 2
                pend = {}
                for sub in range(NS):
                    def fa(sub=sub, t=t):
                        pend[sub] = self.rms_a(self.x, (t + 1) * TT + sub * 128)
                    fill.append(fa)
                for sub in range(NS):
                    def fb(sub=sub, nb_=nb_):
                        self.rms_b(pend[sub], gcb, hTs[nb_], f"hTR{nb_}", sub * 128)
                    fill.append(fb)
                fill += proj(t + 1)
            fi = 0
            for k, stg_ in enumerate(stages):
                stg_()
                tgt = ((k + 1) * len(fill) + len(stages) - 1) // len(stages)
                while fi < min(tgt, len(fill)):
                    fill[fi]()
                    fi += 1
            while fi < len(fill):
                fill[fi]()
                fi += 1
            tail(t)

    def rms_a(self, src, r0):
        S = self.S
        xt, xk = self.xt.get()
        xn, nk = self.xn.get()
        ss, sk = self.ss.get()
        S.dma(SP, lambda e: e.dma_start(out=xt[:], in_=src[r0:r0 + 128, :]), writes=[xk])
        S.op(ACT, lambda e: e.activation(out=self.junk[:], in_=xt[:], func=AF.Square, accum_out=ss[:]),
             reads=[xk], writes=[sk])
        S.op(ACT, lambda e: e.activation(out=ss[:], in_=ss[:], func=AF.Sqrt, scale=1.0 / D, bias=self.epsb[:]),
             reads=[sk, "epsb"], writes=[sk])
        S.op(DVE, lambda e: e.reciprocal(out=ss[:], in_=ss[:]), reads=[sk], writes=[sk])
        S.op(DVE, lambda e: e.tensor_scalar(out=xn[:], in0=xt[:], scalar1=ss[:, 0:1], scalar2=None, op0=ALU.mult),
             reads=[xk, sk], writes=[nk])
        return (xt, xk, xn, nk)

    def rms_b(self, state, gcb, hT, hkey, col0):
        S = self.S
        xt, xk, xn, nk = state

        def tr(e):
            ins = None
            for kc in range(8):
                ins = e.transpose(out=self.psb[:, kc * 128:(kc + 1) * 128], in_=xn[:, kc * 128:(kc + 1) * 128],
                                  identity=self.ident[:])
            return ins
        S.op(PE, tr, reads=[nk, "ident"], writes=["psb"])
        S.op(DVE, lambda e: e.tensor_tensor(
            out=hT[:, :, col0:col0 + 128], in0=self.psb[:].rearrange("p (k j) -> p k j", k=8),
            in1=gcb[:].rearrange("p (k j) -> p k j", k=8), op=ALU.mult),
            reads=["psb", "gcb"], writes=[hkey])
        return xt, xk

    def rmsnorm_hT(self, src, r0, gcb, hT, hkey, col0, keep=None):
        return self.rms_b(self.rms_a(src, r0), gcb, hT, hkey, col0)

    def pass_N_alloc(self):
        S = self.S
        ns = self.nseq
        self.KST = [[self.sb(f"KST{s}{g}", [100, S_LEN], BF16) for g in range(2)] for s in range(ns)]
        self.KWT = [[self.sb(f"KWT{s}{g}", [100, S_LEN], BF16) for g in range(2)] for s in range(ns)]
        self.VS1 = [self.sb(f"VS1{s}", [128, 16, 2, 65], BF16) for s in range(ns)]
        self.VW1 = [self.sb(f"VW1{s}", [128, 16, 2, 65], BF16) for s in range(ns)]
        self.KCM = [[self.sb(f"KCM{s}{g}", [100, 128], BF16) for g in range(2)] for s in range(ns)]
        self.VCO = [[self.sb(f"VCO{s}{g}", [128, 97], BF16) for g in range(2)] for s in range(ns)]
        for s in range(ns):
            S.op(DVE, lambda e, s=s: e.memset(self.VS1[s][:], 1.0), writes=[f"VS1{s}"])
            S.op(DVE, lambda e, s=s: e.memset(self.VW1[s][:], 1.0), writes=[f"VW1{s}"])
            for g in range(2):
                S.dma(SP, lambda e, s=s, g=g: e.dma_start(out=self.KST[s][g][64:96, :], in_=self.cd["c_ehot"]), writes=[f"KST{s}{g}"])
                S.dma(SP, lambda e, s=s, g=g: e.dma_start(out=self.KST[s][g][96:100, :], in_=self.cd["c_kaug"]), writes=[f"KST{s}{g}"])
                S.op(DVE, lambda e, s=s, g=g: e.memset(self.KWT[s][g][64:96, :], 0.0), writes=[f"KWT{s}{g}"])
                S.dma(SP, lambda e, s=s, g=g: e.dma_start(out=self.KWT[s][g][96:100, :], in_=self.cd["c_kaug"]), writes=[f"KWT{s}{g}"])
                S.op(DVE, lambda e, s=s, g=g: e.memset(self.KCM[s][g][64:96, :], 0.0), writes=[f"KCM{s}{g}"])
                S.dma(SP, lambda e, s=s, g=g: e.dma_start(out=self.KCM[s][g][96:100, 0:127], in_=self.cd["c_caug"]), writes=[f"KCM{s}{g}"])
                S.dma(SP, lambda e, s=s, g=g: e.dma_start(out=self.VCO[s][g][0:127, 64:97], in_=self.cd["c_ovl"]), writes=[f"VCO{s}{g}"])

    def pass_N1(self):
        nc, S = self.nc, self.S
        TT = 256
        saved_banks = self.banks
        self.banks = Ring(self.banks.items + self.accb.items)
        wkv = self.sb("wkv", [128, 8, 768], BF16)
        gcb = self.cload("gcb", [128, 1024], F32, self.g_mix)
        self.load_w(wkv, "wkv", self.w_in, O_KCR, 768, 8)
        w1 = [self.sb(f"w1_{k}", [64, 32, 256], BF16) for k in range(2)]
        w2 = [self.sb(f"w2_{k}", [128, 2, 64], BF16) for k in range(2)]
        cb1 = [self.sb(f"cb1_{k}", [128, 2], F32) for k in range(2)]
        for k in range(2):
            src = self.w1[k].rearrange("(p d) n -> d p n", d=64)
            for p0 in range(0, 32, 8):
                S.dma(POOL, lambda e, k=k, src=src, p0=p0: e.dma_start(out=w1[k][:, p0:p0 + 8, :], in_=src[:, p0:p0 + 8, :]),
                      writes=[f"w1_{k}"])
            self.load_w(w2[k], f"w2_{k}", self.w2[k], 0, 64, 2)
            b1 = self.cload(f"b1_{k}", [128, 2], F32, self.b1[k])
            pf = self.cload(f"posf_{k}", [64, 64], F32, self.posT[k])
            pb16 = self.sb(f"posb_{k}", [64, 64], BF16)
            S.op(DVE, lambda e, pf=pf, pb16=pb16: e.tensor_copy(out=pb16[:], in_=pf[:]), reads=[f"posf_{k}"], writes=[f"posb_{k}"])
            for nch in range(2):
                pb, pbk = self.banks.get()

                def mmc(e, pb=pb, k=k, nch=nch, pb16=pb16):
                    ins = None
                    for p in range(32):
                        ins = e.matmul(pb[:, 0:2], lhsT=w1[k][0:64, p, nch * 128:(nch + 1) * 128], rhs=pb16[0:64, 2 * p:2 * p + 2],
                                       start=(p == 0), stop=(p == 31))
                    return ins
                S.op(PE, mmc, reads=[f"w1_{k}", f"posb_{k}"], writes=[pbk])
                S.op(DVE, lambda e, pb=pb, k=k, nch=nch, b1=b1: e.tensor_tensor(
                    out=cb1[k][:, nch:nch + 1], in0=pb[:, 0:1], in1=b1[:, nch:nch + 1], op=ALU.add),
                    reads=[pbk, f"b1_{k}"], writes=[f"cb1_{k}"])
        CRT = [[self.sb(f"CRT{k}{g}", [64, S_LEN], BF16) for g in range(2)] for k in range(2)]
        hTs = [self.sb(f"hTN1_{b}", [128, 8, TT], BF16) for b in range(2)]
        hidT = self.sb("hidT", [128, 2, 128], BF16)
        NSB = TT // 128
        ntl = self.nseq * (S_LEN // TT)
        for sub in range(NSB):
            self.rmsnorm_hT(self.x, sub * 128, gcb, hTs[0], "hTN1_0", sub * 128)
        for s in range(self.nseq):
            for t in range(S_LEN // TT):
                tok0 = s * S_LEN + t * TT
                pos0 = t * TT
                tix = s * (S_LEN // TT) + t
                hT = hTs[tix % 2]
                hk = f"hTN1_{tix % 2}"
                pend = []
                if tix + 1 < ntl:
                    for sub in range(NSB):
                        pend.append(self.rms_a(self.x, tok0 + TT + sub * 128))
                dests = [(0, CRT[0], "CRT0"), (128, CRT[1], "CRT1"), (256, self.KST[s], f"KST{s}"), (512, self.KWT[s], f"KWT{s}")]
                for off, dst, dk in dests:
                    for g in range(2):
                        pb, pbk = self.banks.get()
                        S.op(PE, self.mm8(pb[0:64, 0:TT], lambda kc, off=off, g=g: wkv[:, kc, off + g * 64:off + (g + 1) * 64],
                                          lambda kc, hT=hT: hT[:, kc, :]), reads=["wkv", hk], writes=[pbk])
                        S.op(ACT, lambda e, pb=pb, dst=dst, g=g, pos0=pos0: e.copy(out=dst[g][0:64, pos0:pos0 + TT], in_=pb[0:64, 0:TT]),
                             reads=[pbk], writes=[f"{dk}{g}"])
                for sub in range(TT // 128):
                    cs = slice(sub * 128, (sub + 1) * 128)
                    kt = pos0 // 128 + sub
                    pb, pbk = self.banks.get()
                    S.op(PE, self.mm8(pb[:, 0:128], lambda kc, cs=cs, hT=hT: hT[:, kc, cs], lambda kc: wkv[:, kc, 384:512]),
                         reads=["wkv", hk], writes=[pbk])
                    S.op(PE, self.mm8(pb[:, 128:256], lambda kc, cs=cs, hT=hT: hT[:, kc, cs], lambda kc: wkv[:, kc, 640:768]),
                         reads=["wkv", hk], writes=[pbk])
                    S.op(ACT, lambda e, pb=pb, s=s, kt=kt: e.copy(out=self.VS1[s][:, kt, :, 0:64],
                                                                  in_=pb[:, 0:128].rearrange("p (g d) -> p g d", g=2)),
                         reads=[pbk], writes=[f"VS1{s}"])
                    S.op(ACT, lambda e, pb=pb, s=s, kt=kt: e.copy(out=self.VW1[s][:, kt, :, 0:64],
                                                                  in_=pb[:, 128:256].rearrange("p (g d) -> p g d", g=2)),
                         reads=[pbk], writes=[f"VW1{s}"])
                for sub, st_ in enumerate(pend):
                    nb_ = (tix + 1) % 2
                    self.rms_b(st_, gcb, hTs[nb_], f"hTN1_{nb_}", sub * 128)
            for k in range(2):
                for g in range(2):
                    for nch in range(2):
                        pb, pbk = self.banks.get()

                        def mmh(e, pb=pb, k=k, g=g, nch=nch):
                            ins = None
                            for p in range(32):
                                ins = e.matmul(pb[:, 0:127], lhsT=w1[k][0:64, p, nch * 128:(nch + 1) * 128],
                                               rhs=CRT[k][g][0:64, p:p + 2017:16], start=(p == 0), stop=(p == 31))
                            return ins
                        S.op(PE, mmh, reads=[f"w1_{k}", f"CRT{k}{g}"], writes=[pbk])
                        S.op(ACT, lambda e, pb=pb, k=k, nch=nch: e.activation(
                            out=hidT[:, nch, 0:127], in_=pb[:, 0:127], func=AF.Gelu_apprx_tanh, bias=cb1[k][:, nch:nch + 1]),
                            reads=[pbk, f"cb1_{k}"], writes=["hidT"])
                    pb, pbk = self.banks.get()
                    if k == 0:
                        S.op(PE, self.mm8(pb[0:64, 0:127], lambda nch: w2[0][:, nch, :], lambda nch: hidT[:, nch, 0:127], n=2),
                             reads=["w2_0", "hidT"], writes=[pbk])
                        S.op(ACT, lambda e, pb=pb, s=s, g=g: e.copy(out=self.KCM[s][g][0:64, 0:127], in_=pb[0:64, 0:127]),
                             reads=[pbk], writes=[f"KCM{s}{g}"])
                    else:
                        S.op(PE, self.mm8(pb[0:127, 0:64], lambda nch: hidT[:, nch, 0:127], lambda nch: w2[1][:, nch, :], n=2),
                             reads=["w2_1", "hidT"], writes=[pbk])
                        S.op(ACT, lambda e, pb=pb, s=s, g=g: e.copy(out=self.VCO[s][g][0:127, 0:64], in_=pb[0:127, 0:64]),
                             reads=[pbk], writes=[f"VCO{s}{g}"])

        self.banks = saved_banks

    def pass_N2(self):
        nc, S = self.nc, self.S
        TT = 256
        NS = TT // 128
        LA = 3
        allb = self.banks.items + self.accb.items
        gen = Ring(allb[0:3])
        scr = Ring(allb[3:5])
        accr = Ring(allb[5:7])
        wnq = self.sb("wnq", [128, 8, 512], BF16)
        wng = self.sb("wng", [128, 8, 24], BF16)
        wgb = self.sb("wgb", [128, 8, D], BF16)
        wno = self.sb("wno", [128, 4, D], BF16)
        wout = self.sb("wout", [128, 8, D], BF16)
        gcb = self.cload("gcb", [128, 1024], F32, self.g_mix)
        cmask = self.sb("cmask", [128, 2048], BF16)
        S.dma(SP, lambda e: e.dma_start(out=cmask[0:127, :], in_=self.cd["c_cmask"]), writes=["cmask"])
        caus = self.cload("caus", [128, 512], BF16, self.cd["c_caus"])
        far = self.cload("far", [128, 512], BF16, self.cd["c_far"])
        impA = self.cload("impA", [128, 512], F32, self.cd["c_impA"])
        impB = self.cload("impB", [128, 512], F32, self.cd["c_impB"])
        impNF = self.cload("impNF", [128, 512], F32, self.cd["c_impNF"])
        self.load_w(wnq, "wnq", self.w_in, O_NQ, 512, 8)
        self.load_w(wng, "wng", self.w_in, O_NG, 24, 8)
        self.load_w(wgb, "wgb", self.w_in, O_GB, D, 8)
        self.load_w(wno, "wno", self.w_nsa_o, 0, D, 4)
        self.load_w(wout, "wout", self.w_out, 0, D, 8)
        hTs = [self.sb(f"hTN2_{b}", [128, 8, TT], BF16) for b in range(2)]
        QTs = [[self.sb(f"QT{b}{g}", [100, 4, TT], BF16) for g in range(2)] for b in range(2)]
        for b in range(2):
            for g in range(2):
                S.op(DVE, lambda e, b=b, g=g: e.memset(QTs[b][g][64:96, :, :], 0.0), writes=[f"QT{b}{g}0", f"QT{b}{g}1"])
        SGs = [self.sb(f"SG{b}", [128, NS, 24], F32) for b in range(2)]
        sgbs = [self.sb(f"sgb{b}", [128, 8, TT], BF16) for b in range(2)]
        maTs = [self.sb(f"maTN{b}", [128, 8, TT], BF16) for b in range(2)]
        nso = self.sb("nso", [128, NS, 512], BF16)
        noT = self.sb("noT", [128, 4, TT], BF16)
        selr = Ring([(self.sb(f"SELB{i}", [128, 96], BF16), f"SELB{i}") for i in range(4)])
        cper = Ring([(self.sb(f"cpe{i}", [128, 512], BF16), f"cpe{i}") for i in range(4)])
        for t_, k_ in selr.items:
            S.op(DVE, lambda e, t_=t_: e.memset(t_[:], 0.0), writes=[k_])
        ONSs = [[self.sb(f"ONS{b}{sub}", [128, 8, 64], F32) for sub in range(NS)] for b in range(2)]
        per = Ring([(self.sb(f"pe{i}", [128, 512], BF16), f"pe{i}") for i in range(6)])
        rdr = Ring([(self.sb(f"rd{i}", [128, 8], F32), f"rd{i}") for i in range(8)])
        impr = Ring([(self.sb(f"imp{i}", [128, 40], F32), f"imp{i}") for i in range(4)])
        it4r = Ring([(self.sb(f"it4_{i}", [128, 4, 32], F32), f"it4_{i}") for i in range(2)])
        ftr = Ring([(self.sb(f"ft{i}", [128, 4, 64], F32), f"ft{i}") for i in range(2)])
        tmr = Ring([(self.sb(f"tmb{i}", [128, 2 * TT], F32), f"tmb{i}") for i in range(3)])
        qaug = self.cd["c_qaug"].rearrange("a (r t) -> a r t", r=4)
        mav = self.ma_scr.rearrange("p (c n) -> p c n", c=8)
        tiles = [(s, t) for s in range(self.nseq) for t in range(S_LEN // TT)]
        xts_of = {}
        LAG = 2
        IMMEDIATE = False

        def v3(ap):
            return ap.rearrange("p (r t) -> p r t", r=4)

        pending = []
        stepno = [0]
        owners = {}

        def run_due(force=False):
            while pending and (force or pending[0][0] <= stepno[0]):
                _, c = pending.pop(0)
                c2 = c()
                if c2 is not None:
                    pending.append([stepno[0] + LAG, c2])

        def emit_bg(item):
            if item == "FLUSH":
                run_due(True)
                return
            c = item()
            while IMMEDIATE and c is not None:
                c = c()
            if c is not None:
                pending.append([stepno[0] + LAG, c])

        def galloc():
            bk, bkk = gen.get()
            tok = owners.get(bkk)
            if tok is not None and not tok["done"]:
                run_due(True)
            tok = {"done": False}
            owners[bkk] = tok
            return bk, bkk, tok

        def guard(name):
            tok = owners.get(name)
            if tok is not None and not tok["done"]:
                run_due(True)
            tok = {"done": False}
            owners[name] = tok
            return tok

        def prologue(n):
            s, t = tiles[n]
            b = n % 2
            tok0 = s * S_LEN + t * TT
            pos0 = t * TT
            hT, QT, SG, sgb, maT = hTs[b], QTs[b], SGs[b], sgbs[b], maTs[b]
            hk = f"hTN2_{b}"
            items = []
            xts_of[n] = [None] * NS
            stt_ = {}
            for sub in range(NS):
                def fa(sub=sub):
                    xt, xk = self.xt.get()
                    xn, nk = self.xn.get()
                    ss, sk = self.ss.get()
                    stt_[sub] = (xt, xk, xn, nk, ss, sk)
                    S.dma(SP, lambda e: e.dma_start(out=xt[:], in_=self.x[tok0 + sub * 128:tok0 + (sub + 1) * 128, :]), writes=[xk])
                items.append(fa)
            items.append(lambda: S.dma(SP, lambda e: e.dma_start(out=maT[:], in_=mav[:, :, tok0:tok0 + TT]), writes=[f"maTN{b}"]) and None)
            for g in range(2):
                items.append(lambda g=g: S.dma(SP, lambda e: e.dma_start(out=QT[g][96:100, :, :], in_=qaug[g * 4:(g + 1) * 4, :, pos0:pos0 + TT]),
                                               writes=[f"QT{b}{g}0", f"QT{b}{g}1"]) and None)
            for sub in range(NS):
                def fb(sub=sub):
                    xt, xk, xn, nk, ss, sk = stt_[sub]
                    S.op(ACT, lambda e: e.activation(out=self.junk[:], in_=xt[:], func=AF.Square, accum_out=ss[:]), reads=[xk], writes=[sk])
                    S.op(ACT, lambda e: e.activation(out=ss[:], in_=ss[:], func=AF.Sqrt, scale=1.0 / D, bias=self.epsb[:]),
                         reads=[sk, "epsb"], writes=[sk])

                    def c1():
                        S.op(DVE, lambda e: e.reciprocal(out=ss[:], in_=ss[:]), reads=[sk], writes=[sk])
                        S.op(DVE, lambda e: e.tensor_scalar(out=xn[:], in0=xt[:], scalar1=ss[:, 0:1], scalar2=None, op0=ALU.mult),
                             reads=[xk, sk], writes=[nk])
                        stt_[("done", sub)] = True
                    return c1
                items.append(fb)
            for sub in range(NS):
                def fc(sub=sub):
                    if not stt_.get(("done", sub)):
                        run_due(True)
                    xt, xk, xn, nk, ss, sk = stt_[sub]
                    tok = guard("psb")

                    def tr(e):
                        ins = None
                        for kc in range(8):
                            ins = e.transpose(out=self.psb[:, kc * 128:(kc + 1) * 128], in_=xn[:, kc * 128:(kc + 1) * 128], identity=self.ident[:])
                        return ins
                    S.op(PE, tr, reads=[nk, "ident"], writes=["psb"])

                    def c1():
                        S.op(DVE, lambda e: e.tensor_tensor(out=hT[:, :, sub * 128:(sub + 1) * 128],
                                                            in0=self.psb[:].rearrange("p (k j) -> p k j", k=8),
                                                            in1=gcb[:].rearrange("p (k j) -> p k j", k=8), op=ALU.mult),
                             reads=["psb", "gcb"], writes=[hk])
                        tok["done"] = True
                    xts_of[n][sub] = (xt, xk)
                    return c1
                items.append(fc)
            items.append("FLUSH")
            for g in range(2):
                for r in (0, 2):
                    def fq(g=g, r=r):
                        pb, pbk, tok = galloc()
                        for dr in range(2):
                            hh = g * 4 + r + dr
                            S.op(PE, self.mm8(pb[0:64, dr * 256:dr * 256 + TT], lambda kc, hh=hh: wnq[:, kc, hh * 64:(hh + 1) * 64],
                                              lambda kc: hT[:, kc, :]), reads=["wnq", hk], writes=[pbk])

                        def c1():
                            S.op(DVE, lambda e: e.tensor_copy(out=QT[g][0:64, r:r + 2, :], in_=pb[0:64, :].rearrange("p (a n) -> p a n", a=2)),
                                 reads=[pbk], writes=[f"QT{b}{g}0", f"QT{b}{g}1"])
                            tok["done"] = True
                        return c1
                    items.append(fq)
            def fg():
                pb, pbk, tok = galloc()
                for sub in range(NS):
                    cs = slice(sub * 128, (sub + 1) * 128)
                    S.op(PE, self.mm8(pb[:, sub * 32:sub * 32 + 24], lambda kc, cs=cs: hT[:, kc, cs], lambda kc: wng[:, kc, :]),
                         reads=["wng", hk], writes=[pbk])

                def c1():
                    S.op(ACT, lambda e: e.activation(out=SG[:, :, :], in_=pb[:, 0:64].rearrange("p (a n) -> p a n", a=2)[:, :, 0:24],
                                                     func=AF.Sigmoid), reads=[pbk], writes=[f"SG{b}"])
                    tok["done"] = True
                return c1
            items.append(fg)
            items.append("FLUSH")
            chains = []
            for sub in range(NS):
                for g in range(2):
                    chains.append(cmp_chain(n, s, t, b, sub, g))
            nst = max([len(c) for c in chains] + [0])
            for st_ in range(nst):
                for c in chains:
                    if st_ < len(c):
                        items.append(c[st_])
            for cc in range(0, 8, 2):
                def fgb(cc=cc):
                    pb, pbk, tok = galloc()
                    for dc in range(2):
                        S.op(PE, self.mm8(pb[:, dc * 256:dc * 256 + TT], lambda kc, c_=cc + dc: wgb[:, kc, c_ * 128:(c_ + 1) * 128],
                                          lambda kc: hT[:, kc, :]), reads=["wgb", hk], writes=[pbk])

                    def c1():
                        S.op(ACT, lambda e: e.activation(out=sgb[:, cc:cc + 2, :], in_=pb[:, :].rearrange("p (a n) -> p a n", a=2),
                                                         func=AF.Sigmoid), reads=[pbk], writes=[f"sgb{b}"])
                        tok["done"] = True
                    return c1
                items.append(fgb)
            items.append("FLUSH")
            return items

        def cmp_chain(n, s, t, b, sub, g):
            i = t * NS + sub
            qs = slice(sub * 128, (sub + 1) * 128)
            QT, SG = QTs[b], SGs[b]
            ONS = ONSs[b][sub]
            onk = f"ONS{b}{sub}{g}"
            qk = f"QT{b}{g}{sub}"
            rhsQ = QT[g][0:100, :, qs]
            nb = min(127, 8 * i + 8)
            isl = slice(i * 32, (i + 1) * 32)
            st = {}

            def s0():
                psc, psck, tok = galloc()
                st["pe"], st["pek"] = cper.get()
                pe = st["pe"]
                S.op(PE, lambda e: e.matmul(v3(psc[0:nb, :]), lhsT=self.KCM[s][g][0:100, 0:nb], rhs=rhsQ, start=True, stop=True),
                     reads=[f"KCM{s}{g}", qk], writes=[psck])

                def c1():
                    S.op(ACT, lambda e: e.activation(out=pe[0:nb, :], in_=psc[0:nb, :], func=AF.Exp, scale=0.125),
                         reads=[psck], writes=[st["pek"]])
                    tok["done"] = True
                return c1

            def s1():
                pe, pek = st["pe"], st["pek"]
                S.op(DVE, lambda e: e.tensor_tensor(out=v3(pe[0:nb, :]), in0=v3(pe[0:nb, :]),
                                                    in1=cmask[0:nb, i * 128:(i + 1) * 128].unsqueeze(1).to_broadcast([nb, 4, 128]), op=ALU.mult),
                     reads=[pek, "cmask"], writes=[pek])

            def s2():
                pe, pek = st["pe"], st["pek"]
                pcv, pcvk, tok = galloc()

                def pvc(e):
                    ins = None
                    for r in range(4):
                        ins = e.matmul(pcv[:, r * 97:(r + 1) * 97], lhsT=pe[0:nb, r * 128:(r + 1) * 128],
                                       rhs=self.VCO[s][g][0:nb, 0:97], start=True, stop=True, skip_group_check=True)
                    return ins
                S.op(PE, pvc, reads=[pek, f"VCO{s}{g}"], writes=[pcvk])

                def c1():
                    rd, rdk = rdr.get()
                    imp, impk = impr.get()
                    st["imp"], st["impk"] = imp, impk
                    pcv3 = pcv[:, 0:388].rearrange("p (r c) -> p r c", r=4)
                    S.op(DVE, lambda e: e.tensor_scalar(out=rd[:, 0:4], in0=pcv3[:, :, 64], scalar1=1e-30, scalar2=None, op0=ALU.add),
                         reads=[pcvk], writes=[rdk])
                    S.op(DVE, lambda e: e.reciprocal(out=rd[:, 0:4], in_=rd[:, 0:4]), reads=[rdk], writes=[rdk])
                    S.op(DVE, lambda e: e.tensor_tensor(out=rd[:, 4:8], in0=rd[:, 0:4], in1=SG[:, sub, g * 4:g * 4 + 4], op=ALU.mult),
                         reads=[rdk, f"SG{b}"], writes=[rdk])
                    S.op(DVE, lambda e: e.tensor_tensor(out=ONS[:, g * 4:(g + 1) * 4, :], in0=pcv3[:, :, 0:64],
                                                        in1=rd[:, 4:8].unsqueeze(2).to_broadcast([128, 4, 64]), op=ALU.mult),
                         reads=[pcvk, rdk], writes=[onk])
                    it4, it4k = it4r.get()
                    S.op(DVE, lambda e: e.tensor_tensor(out=it4[:], in0=pcv3[:, :, 65:97],
                                                        in1=rd[:, 0:4].unsqueeze(2).to_broadcast([128, 4, 32]), op=ALU.mult),
                         reads=[pcvk, rdk], writes=[it4k])
                    S.op(DVE, lambda e: e.tensor_reduce(out=imp[:, 0:32], in_=it4[:].rearrange("p r j -> p j r"), axis=mybir.AxisListType.X,
                                                        op=ALU.add), reads=[it4k], writes=[impk])
                    tok["done"] = True
                return c1

            def s3():
                if "imp" not in st:
                    run_due(True)
                imp, impk = st["imp"], st["impk"]
                S.op(DVE, lambda e: e.tensor_tensor(out=imp[:, 0:32], in0=imp[:, 0:32], in1=impA[:, isl], op=ALU.mult),
                     reads=[impk, "impA"], writes=[impk])
                S.op(DVE, lambda e: e.tensor_tensor(out=imp[:, 0:32], in0=imp[:, 0:32], in1=impB[:, isl], op=ALU.add),
                     reads=[impk, "impB"], writes=[impk])
                S.op(DVE, lambda e: e.max(out=imp[:, 32:40], in_=imp[:, 0:32]), reads=[impk], writes=[impk])
                S.op(DVE, lambda e: e.scalar_tensor_tensor(out=imp[:, 0:32], in0=imp[:, 0:32], scalar=imp[:, 39:40], in1=impNF[:, isl],
                                                           op0=ALU.is_ge, op1=ALU.mult), reads=[impk, "impNF"], writes=[impk])
                SELB, selk = selr.get()
                S.op(DVE, lambda e: e.tensor_scalar(out=SELB[:, 64:96], in0=imp[:, 0:32], scalar1=-1.0, scalar2=30000.0,
                                                    op0=ALU.add, op1=ALU.mult), reads=[impk], writes=[selk])
                st["SELB"], st["selk"] = SELB, selk

            def s4():
                SELB, selk = st["SELB"], st["selk"]
                pst, pstk, tok = galloc()
                S.op(PE, lambda e: e.matmul(pst[0:96, 0:128], lhsT=SELB[:, 0:96], rhs=self.ident[:, :], start=True, stop=True),
                     reads=[selk, "ident"], writes=[pstk])

                def c1():
                    S.op(DVE, lambda e: e.tensor_copy(out=QT[g][64:96, :, qs], in_=pst[64:96, 0:128].unsqueeze(1).to_broadcast([32, 4, 128])),
                         reads=[pstk], writes=[qk])
                    tok["done"] = True
                return c1

            def w(f, need_flush):
                def g_():
                    if need_flush:
                        run_due(True)
                    return f()
                return g_
            return [s0, w(s1, True), s2, s3, s4]

        def pair_tasks(n):
            s, t = tiles[n]
            b = n % 2
            QT, SG = QTs[b], SGs[b]
            tasks = []
            for sub in range(NS):
                i = t * NS + sub
                qs = slice(sub * 128, (sub + 1) * 128)
                ONS = ONSs[b][sub]
                for g in range(2):
                    onk = f"ONS{b}{sub}{g}"
                    qk = f"QT{b}{g}{sub}"
                    rhsQ = QT[g][0:100, :, qs]
                    for (KT, kkey, V1, vkey, j0, goff, isw) in (
                            (self.KWT[s][g], f"KWT{s}{g}", self.VW1[s], f"VW1{s}", max(0, i - 4), 16, True),
                            (self.KST[s][g], f"KST{s}{g}", self.VS1[s], f"VS1{s}", 0, 8, False)):
                        acc = {}
                        for j in range(j0, i + 1):
                            tk = {}

                            def A(tk=tk, j=j, KT=KT, kkey=kkey, rhsQ=rhsQ, qk=qk):
                                tk["pss"], tk["pssk"] = scr.get()
                                pss = tk["pss"]
                                S.op(PE, lambda e: e.matmul(v3(pss[:, :]), lhsT=KT[0:100, j * 128:(j + 1) * 128], rhs=rhsQ,
                                                            start=True, stop=True), reads=[kkey, qk], writes=[tk["pssk"]])
                                tk["pe"], tk["pek"] = per.get()
                                pe, pek = tk["pe"], tk["pek"]
                                S.op(ACT, lambda e: e.activation(out=pe[:], in_=pss[:, :], func=AF.Exp, scale=0.125),
                                     reads=[tk["pssk"]], writes=[pek])

                            def B(tk=tk, j=j, i=i, isw=isw):
                                pe, pek = tk["pe"], tk["pek"]
                                if j == i:
                                    S.op(DVE, lambda e: e.tensor_tensor(out=pe[:], in0=pe[:], in1=caus[:], op=ALU.mult),
                                         reads=[pek, "caus"], writes=[pek])
                                if isw and i >= 4 and j == i - 4:
                                    S.op(DVE, lambda e: e.tensor_tensor(out=pe[:], in0=pe[:], in1=far[:], op=ALU.mult),
                                         reads=[pek, "far"], writes=[pek])

                            def C(tk=tk, j=j, i=i, j0=j0, acc=acc, V1=V1, vkey=vkey, g=g):
                                if j == j0:
                                    acc["psv"], acc["psvk"] = accr.get()
                                psv, psvk, pe = acc["psv"], acc["psvk"], tk["pe"]

                                def pv(e):
                                    ins = None
                                    for r in range(4):
                                        ins = e.matmul(psv[:, r * 65:(r + 1) * 65], lhsT=pe[:, r * 128:(r + 1) * 128], rhs=V1[:, j, g, :],
                                                       start=(j == j0 and r == 0), stop=(j == i), skip_group_check=True)
                                    return ins
                                S.op(PE, pv, reads=[tk["pek"], vkey], writes=[psvk])

                            def Dn(j=j, i=i, acc=acc, g=g, goff=goff, ONS=ONS, onk=onk, SG=SG, sub=sub, b=b):
                                if j != i:
                                    return
                                psv, psvk = acc["psv"], acc["psvk"]
                                rd, rdk = rdr.get()
                                psv3 = psv[:, 0:260].rearrange("p (r c) -> p r c", r=4)
                                S.op(DVE, lambda e: e.reciprocal(out=rd[:, 0:4], in_=psv3[:, :, 64]), reads=[psvk], writes=[rdk])
                                S.op(DVE, lambda e: e.tensor_tensor(out=rd[:, 4:8], in0=rd[:, 0:4],
                                                                    in1=SG[:, sub, goff + g * 4:goff + g * 4 + 4], op=ALU.mult),
                                     reads=[rdk, f"SG{b}"], writes=[rdk])
                                ft, ftk = ftr.get()
                                S.op(DVE, lambda e: e.tensor_tensor(out=ft[:], in0=psv3[:, :, 0:64],
                                                                    in1=rd[:, 4:8].unsqueeze(2).to_broadcast([128, 4, 64]), op=ALU.mult),
                                     reads=[psvk, rdk], writes=[ftk])
                                S.op(POOL, lambda e: e.tensor_tensor(out=ONS[:, g * 4:(g + 1) * 4, :], in0=ONS[:, g * 4:(g + 1) * 4, :],
                                                                     in1=ft[:], op=ALU.add), reads=[ftk, onk], writes=[onk])
                            tasks.append((A, B, C, Dn))
            return tasks

        def epilogue(n):
            s, t = tiles[n]
            b = n % 2
            tok0 = s * S_LEN + t * TT
            sgb, maT = sgbs[b], maTs[b]
            items = []
            for sub in range(NS):
                qs = slice(sub * 128, (sub + 1) * 128)
                ONS = ONSs[b][sub]

                def f1(sub=sub, ONS=ONS, qs=qs):
                    S.op(POOL, lambda e: e.tensor_copy(out=nso[:, sub, :], in_=ONS[:].rearrange("p h d -> p (h d)")),
                         reads=[f"ONS{b}{sub}0", f"ONS{b}{sub}1"], writes=["nso"])

                    def c1():
                        tok = guard("psb")

                        def tr(e):
                            ins = None
                            for k4 in range(4):
                                ins = e.transpose(out=self.psb[:, k4 * 128:(k4 + 1) * 128], in_=nso[:, sub, k4 * 128:(k4 + 1) * 128],
                                                  identity=self.ident[:])
                            return ins
                        S.op(PE, tr, reads=["nso", "ident"], writes=["psb"])

                        def c2():
                            S.op(DVE, lambda e: e.tensor_copy(out=noT[:, :, qs], in_=self.psb[:, 0:512].rearrange("p (k j) -> p k j", k=4)),
                                 reads=["psb"], writes=["noT"])
                            tok["done"] = True
                        return c2
                    return c1
                items.append(f1)
            items.append("FLUSH")
            for cc in range(0, 8, 2):
                def f2(cc=cc):
                    pb, pbk, tok = galloc()
                    for dc in range(2):
                        S.op(PE, self.mm8(pb[:, dc * 256:dc * 256 + TT], lambda k4, c_=cc + dc: wno[:, k4, c_ * 128:(c_ + 1) * 128],
                                          lambda k4: noT[:, k4, :], n=4), reads=["wno", "noT"], writes=[pbk])

                    def c1():
                        tm, tmk = tmr.get()
                        S.op(DVE, lambda e: e.tensor_tensor(out=tm[:].rearrange("p (a n) -> p a n", a=2),
                                                            in0=pb[:, :].rearrange("p (a n) -> p a n", a=2), in1=sgb[:, cc:cc + 2, :], op=ALU.mult),
                             reads=[pbk, f"sgb{b}"], writes=[tmk])
                        tok["done"] = True
                        S.op(POOL, lambda e: e.tensor_tensor(out=maT[:, cc:cc + 2, :], in0=tm[:].rearrange("p (a n) -> p a n", a=2),
                                                             in1=maT[:, cc:cc + 2, :], op=ALU.add),
                             reads=[tmk, f"maTN{b}"], writes=[f"maTN{b}"])
                    return c1
                items.append(f2)
            items.append("FLUSH")
            for sub in range(NS):
                cs = slice(sub * 128, (sub + 1) * 128)
                for half in range(2):
                    def f3(sub=sub, cs=cs, half=half):
                        xt, xk = xts_of[n][sub]
                        pb, pbk, tok = galloc()
                        S.op(PE, self.mm8(pb[:, :], lambda cc: maT[:, cc, cs], lambda cc: wout[:, cc, half * 512:(half + 1) * 512]),
                             reads=[f"maTN{b}", "wout"], writes=[pbk])

                        def c1():
                            S.op(DVE, lambda e: e.tensor_tensor(out=xt[:, half * 512:(half + 1) * 512], in0=xt[:, half * 512:(half + 1) * 512],
                                                                in1=pb[:, :], op=ALU.add), reads=[pbk, xk], writes=[xk])
                            tok["done"] = True
                            if half == 1:
                                r0 = tok0 + sub * 128
                                S.dma(SP, lambda e: e.dma_start(out=self.x1_scr[r0:r0 + 128, :], in_=xt[:]), reads=[xk])
                        return c1
                    items.append(f3)
            items.append("FLUSH")
            return items

        for it in prologue(0):
            stepno[0] += 1
            run_due()
            emit_bg(it)
        run_due(True)
        for n in range(len(tiles)):
            bg = []
            if n >= 1:
                bg += epilogue(n - 1)
            if n + 1 < len(tiles):
                bg += prologue(n + 1)
            tasks = pair_tasks(n)
            steps = len(tasks) + LA + 2
            pi = 0
            for k in range(steps):
                stepno[0] += 1
                run_due()
                if k < len(tasks):
                    tasks[k][0]()
                if 0 <= k - 1 < len(tasks):
                    tasks[k - 1][1]()
                if 0 <= k - LA < len(tasks):
                    tasks[k - LA][2]()
                if 0 <= k - LA - 1 < len(tasks):
                    tasks[k - LA - 1][3]()
                tgt = ((k + 1) * len(bg) + steps - 1) // steps
                while pi < min(tgt, len(bg)):
                    emit_bg(bg[pi])
                    pi += 1
            while pi < len(bg):
                stepno[0] += 1
                run_due()
                emit_bg(bg[pi])
                pi += 1
            run_due(True)
        for it in epilogue(len(tiles) - 1):
            stepno[0] += 1
            run_due()
            emit_bg(it)
        run_due(True)

    def pass_F(self, src):
        nc, S = self.nc, self.S
        TT = 256
        self.banks = Ring(self.banks.items + self.accb.items)
        wup = self.sb("wup", [128, 8, 2 * DFF], BF16)
        wdn = self.sb("wdn", [128, NFC, D], BF16)
        gcb = self.sb("gcbF", [128, 8 * 128], F32)
        gfin = self.sb("gfin", [128, D], F32)
        cw = self.sb("cw", [128, NFC * 3], F32)
        cb = self.sb("cb", [128, NFC], F32)
        halo = self.sb("halo", [128, NFC, 2], F32)
        hTs = [self.sb(f"hTF{b}", [128, 8, TT], BF16) for b in range(2)]
        uT = self.sb("uT", [128, NFC, TT], BF16)
        t1r = Ring([(self.sb(f"t1_{i}", [128, TT], F32), f"t1_{i}") for i in range(3)])
        ger = Ring([(self.sb(f"ge_{i}", [128, TT], F32), f"ge_{i}") for i in range(2)])
        osb = Ring([(self.sb(f"osb{i}", [128, D], F32), f"osb{i}") for i in range(2)])
        S.dma(SP, lambda e: e.dma_start(out=gcb[:], in_=self.g_ffn), writes=["gcb"])
        S.dma(SP, lambda e: e.dma_start(out=gfin[:], in_=self.g_fin), writes=["gfin"])
        S.dma(SP, lambda e: e.dma_start(out=cw[:], in_=self.convw), writes=["cw"])
        S.dma(SP, lambda e: e.dma_start(out=cb[:], in_=self.convb), writes=["cb"])
        self.load_w(wup, "wup", self.w_up, 0, 2 * DFF, 8)
        self.load_w(wdn, "wdn", self.w_down, 0, D, NFC)
        NTL = self.ntok // TT
        NSB = TT // 128
        xts_next = [self.rmsnorm_hT(src, sub * 128, gcb, hTs[0], "hTF0", sub * 128) for sub in range(NSB)]
        for t in range(NTL):
            tok0 = t * TT
            first = (tok0 % S_LEN) == 0
            hT = hTs[t % 2]
            hk = f"hTF{t % 2}"
            xts = xts_next
            xts_next = []
            pend = []
            deferred = None
            for fc in range(NFC):
                pa, pak = self.banks.get()
                pb, pbk = self.banks.get()

                def mm_a(e, pa=pa, fc=fc, hT=hT):
                    ins = None
                    for kc in range(8):
                        ins = e.matmul(pa[:, 2:2 + TT], lhsT=wup[:, kc, fc * 128:(fc + 1) * 128], rhs=hT[:, kc, :],
                                       start=(kc == 0), stop=(kc == 7))
                    return ins

                def mm_b(e, pb=pb, fc=fc, hT=hT):
                    ins = None
                    for kc in range(8):
                        ins = e.matmul(pb[:, 0:TT], lhsT=wup[:, kc, DFF + fc * 128:DFF + (fc + 1) * 128],
                                       rhs=hT[:, kc, :], start=(kc == 0), stop=(kc == 7))
                    return ins
                S.op(PE, mm_a, reads=["wup", hk], writes=[pak])
                S.op(PE, mm_b, reads=["wup", hk], writes=[pbk])
                if t + 1 < NTL:
                    nb_ = (t + 1) % 2
                    if fc in (2, 6):
                        pend.append(self.rms_a(src, tok0 + TT + (fc // 4) * 128))
                    if fc in (10, 14):
                        sub_ = (fc - 10) // 4
                        xts_next.append(self.rms_b(pend[sub_], gcb, hTs[nb_], f"hTF{nb_}", sub_ * 128))
                if first:
                    S.op(ACT, lambda e, pa=pa: e.memzero(pa[:, 0:2]), reads=[pak], writes=[pak])
                else:
                    S.op(ACT, lambda e, pa=pa, fc=fc: e.copy(out=pa[:, 0:2], in_=halo[:, fc, :]),
                         reads=[pak, "halo"], writes=[pak])
                S.op(ACT, lambda e, pa=pa, fc=fc: e.copy(out=halo[:, fc, :], in_=pa[:, TT:TT + 2]),
                     reads=[pak], writes=["halo"])
                t1, t1k = t1r.get()
                S.op(ACT, lambda e, pa=pa, fc=fc, t1=t1: e.activation(
                    out=t1[:], in_=pa[:, 2:2 + TT], func=AF.Copy, scale=cw[:, fc * 3 + 2:fc * 3 + 3]),
                    reads=[pak, "cw"], writes=[t1k])
                S.op(DVE, lambda e, pa=pa, fc=fc, t1=t1: e.scalar_tensor_tensor(
                    out=t1[:], in0=pa[:, 1:1 + TT], scalar=cw[:, fc * 3 + 1:fc * 3 + 2], in1=t1[:],
                    op0=ALU.mult, op1=ALU.add), reads=[pak, "cw", t1k], writes=[t1k])
                S.op(DVE, lambda e, pa=pa, fc=fc, t1=t1: e.scalar_tensor_tensor(
                    out=t1[:], in0=pa[:, 0:TT], scalar=cw[:, fc * 3:fc * 3 + 1], in1=t1[:],
                    op0=ALU.mult, op1=ALU.add), reads=[pak, "cw", t1k], writes=[t1k])

                def fin(fc=fc, t1=t1, t1k=t1k, pb=pb, pbk=pbk):
                    ge, gek = ger.get()
                    S.op(ACT, lambda e: e.activation(out=ge[:], in_=t1[:], func=AF.Gelu_apprx_tanh, bias=cb[:, fc:fc + 1]),
                         reads=[t1k, "cb"], writes=[gek])
                    S.op(DVE, lambda e: e.tensor_tensor(out=uT[:, fc, :], in0=ge[:], in1=pb[:, 0:TT], op=ALU.mult),
                         reads=[gek, pbk], writes=["uT"])
                if deferred is not None:
                    deferred()
                deferred = fin
            deferred()
            deferred = None
            for sub in range(TT // 128):
                xt, xk = xts[sub]
                ss, sk = self.ss.get()
                for half in range(2):
                    po, pok = self.banks.get()

                    def mm_o(e, po=po, sub=sub, half=half):
                        ins = None
                        for fc in range(NFC):
                            ins = e.matmul(po[:, :], lhsT=uT[:, fc, sub * 128:(sub + 1) * 128],
                                           rhs=wdn[:, fc, half * 512:(half + 1) * 512],
                                           start=(fc == 0), stop=(fc == NFC - 1))
                        return ins
                    S.op(PE, mm_o, reads=["uT", "wdn"], writes=[pok])
                    S.op(DVE, lambda e, po=po, xt=xt, half=half: e.tensor_tensor(
                        out=xt[:, half * 512:(half + 1) * 512], in0=xt[:, half * 512:(half + 1) * 512],
                        in1=po[:, :], op=ALU.add), reads=[pok, xk], writes=[xk])
                ob, obk = osb.get()
                S.op(ACT, lambda e, xt=xt, ss=ss: e.activation(out=self.junk[:], in_=xt[:], func=AF.Square,
                                                               accum_out=ss[:]), reads=[xk], writes=[sk])
                S.op(ACT, lambda e, ss=ss: e.activation(out=ss[:], in_=ss[:], func=AF.Sqrt, scale=1.0 / D,
                                                        bias=self.epsb[:]), reads=[sk, "epsb"], writes=[sk])
                S.op(DVE, lambda e, ss=ss: e.reciprocal(out=ss[:], in_=ss[:]), reads=[sk], writes=[sk])
                S.op(DVE, lambda e, xt=xt, ss=ss, ob=ob: e.scalar_tensor_tensor(
                    out=ob[:], in0=xt[:], scalar=ss[:, 0:1], in1=gfin[:], op0=ALU.mult, op1=ALU.mult),
                    reads=[xk, sk, "gfin"], writes=[obk])
                r0 = tok0 + sub * 128
                S.dma(SP, lambda e, ob=ob, r0=r0: e.dma_start(out=self.out[r0:r0 + 128, :], in_=ob[:]),
                      reads=[obk])


def host_inputs(inp, nseq, core):
    f = np.float32
    x = np.ascontiguousarray(inp["x"][core * nseq:(core + 1) * nseq].reshape(nseq * S_LEN, D))

    def gcol(g):
        return np.ascontiguousarray(np.broadcast_to(g.reshape(8, 128).T[:, :, None], (128, 8, 128)).reshape(128, 1024))
    m = {
        "x": x,
        "w_in": np.ascontiguousarray(inp["w_in"][0]),
        "w_up": np.ascontiguousarray(inp["w_up"][0]),
        "w_down": np.ascontiguousarray(inp["w_down"][0]),
        "g_ffn": gcol(inp["norm_ffn"][0]),
        "g_mix": gcol(inp["norm_mix"][0]),
        "g_fin": np.ascontiguousarray(np.broadcast_to(inp["norm_final"][None, :], (128, D))),
        "convw": np.ascontiguousarray(inp["conv_w"][0].reshape(3, NFC, 128).transpose(2, 1, 0).reshape(128, NFC * 3)),
        "convb": np.ascontiguousarray(inp["conv_b"][0].reshape(NFC, 128).T),
    }
    m.update({
        "w_ret_o": np.ascontiguousarray(inp["w_ret_o"][0]),
        "w_nsa_o": np.ascontiguousarray(inp["w_nsa_o"][0]),
        "w_out": np.ascontiguousarray(inp["w_out"][0]),
        "gng": gcol(inp["ret_gn_g"][0]),
        "w1k": np.ascontiguousarray(inp["cmp_w1_k"][0]),
        "w1v": np.ascontiguousarray(inp["cmp_w1_v"][0]),
        "w2k": np.ascontiguousarray(inp["cmp_w2_k"][0]),
        "w2v": np.ascontiguousarray(inp["cmp_w2_v"][0]),
        "b1k": np.ascontiguousarray(inp["cmp_b1_k"][0].reshape(2, 128).T),
        "b1v": np.ascontiguousarray(inp["cmp_b1_v"][0].reshape(2, 128).T),
        "posTk": np.ascontiguousarray(np.repeat(inp["cmp_pos_k"][0].T[:, :, None], 2, axis=2).reshape(64, 64)),
        "posTv": np.ascontiguousarray(np.repeat(inp["cmp_pos_v"][0].T[:, :, None], 2, axis=2).reshape(64, 64)),
    })
    m.update(make_consts())
    return m


_CACHE = {}


def kernel(**inputs):
    inp = {k: np.asarray(v) for k, v in inputs.items()}
    ncores, nseq = 8, 2
    if "prog" not in _CACHE:
        p = Prog(nseq=nseq)
        p.build()
        _CACHE["prog"] = p
    p = _CACHE["prog"]
    in_maps = []
    for c in range(ncores):
        m = host_inputs(inp, nseq, c)
        in_maps.append({k: m[k] for k in p.in_names})
    res = run_bass_kernel_spmd(p.nc, in_maps, core_ids=list(range(ncores)))
    out = np.concatenate([r["out"] for r in res.results], axis=0)
    return out.reshape(16, S_LEN, D).astype(np.float32)
```

```python
import contextlib
import numpy as np
STAGE = 9
import ml_dtypes
import concourse.bass as bass
import concourse.mybir as mybir
from concourse.bass_utils import run_bass_kernel_spmd

F32 = mybir.dt.float32
BF16 = mybir.dt.bfloat16
ALU = mybir.AluOpType
AF = mybir.ActivationFunctionType

PE, ACT, DVE, POOL, SP = "tensor", "scalar", "vector", "gpsimd", "sync"
ENGS = (PE, ACT, DVE, POOL, SP)
NDMASEM = 8

S_LEN = 2048
D = 1024
DFF = 2816
NFC = DFF // 128
EPS = 1e-6
N_IN = 6424
O_RQ, O_RK, O_RV, O_RG, O_NQ, O_KCR, O_VCR, O_KS, O_VS, O_KW, O_VW, O_NG, O_GA, O_GB = (
    0, 512, 1024, 2048, 3072, 3584, 3712, 3840, 3968, 4096, 4224, 4352, 4376, 5400)


class Op:
    __slots__ = ("eng", "fn", "reads", "writes", "is_dma", "waits", "inc", "ticket", "dsem", "dval")

    def __init__(self, eng, fn, reads, writes, is_dma):
        self.eng = eng
        self.fn = fn
        self.reads = reads
        self.writes = writes
        self.is_dma = is_dma
        self.waits = []
        self.inc = False
        self.ticket = None
        self.dsem = None
        self.dval = None


class Sched:
    def __init__(self, nc):
        self.nc = nc
        self.ops = []
        self.last_writer = {}
        self.readers = {}
        self.dma_count = {e: 0 for e in ENGS}
        self.dma_hist = {e: [] for e in ENGS}
        self.last_op = {e: None for e in ENGS}

    def op(self, eng, fn, reads=(), writes=()):
        o = Op(eng, fn, tuple(reads), tuple(writes), False)
        self._add(o)
        return o

    def dma(self, eng, fn, reads=(), writes=()):
        o = Op(eng, fn, tuple(reads), tuple(writes), True)
        i = self.dma_count[eng]
        self.dma_count[eng] += 1
        o.dsem = (eng, i % NDMASEM)
        o.dval = 16 * (i // NDMASEM + 1)
        self.dma_hist[eng].append(o)
        if i >= NDMASEM:
            o.waits.append(self.dma_hist[eng][i - NDMASEM])
        self._add(o)
        return o

    def barrier(self):
        tails = []
        for e in ENGS:
            if self.last_op[e] is not None and not self.last_op[e].is_dma:
                tails.append(self.last_op[e])
            tails.extend(self.dma_hist[e][-NDMASEM:])
        for e in ENGS:
            o = Op(e, None, (), (), False)
            o.waits = [t for t in tails]
            self.ops.append(o)
            self.last_op[e] = o
        self.last_writer = {}
        self.readers = {}

    def _add(self, o):
        self.ops.append(o)
        deps = o.waits
        for k in o.reads:
            w = self.last_writer.get(k)
            if w is not None:
                deps.append(w)
            if k.startswith("ps"):
                for r in self.readers.get(k, ()):
                    if r.eng != o.eng:
                        deps.append(r)
        for k in o.writes:
            w = self.last_writer.get(k)
            if w is not None and (w.is_dma or o.is_dma or w.eng != o.eng):
                deps.append(w)
            for r in self.readers.get(k, ()):
                if r.is_dma or o.is_dma or r.eng != o.eng:
                    deps.append(r)
        for k in o.reads:
            self.readers.setdefault(k, []).append(o)
        for k in o.writes:
            self.last_writer[k] = o
            self.readers[k] = []
        if not o.is_dma:
            self.last_op[o.eng] = o

    def run(self):
        nc = self.nc
        for o in self.ops:
            for d in o.waits:
                if not d.is_dma:
                    d.inc = True
        cnt = {e: 0 for e in ENGS}
        for o in self.ops:
            if o.fn is None:
                o.inc = False
            if not o.is_dma and o.inc:
                cnt[o.eng] += 1
                o.ticket = cnt[o.eng]
        seen = {e: {} for e in ENGS}
        plans = {e: [] for e in ENGS}
        for o in self.ops:
            e = o.eng
            wl = {}
            for d in o.waits:
                if d.is_dma:
                    key = ("d",) + d.dsem
                    val = d.dval
                else:
                    if d.ticket is None:
                        continue
                    key = ("c", d.eng)
                    val = d.ticket
                if seen[e].get(key, 0) >= val:
                    continue
                if wl.get(key, 0) < val:
                    wl[key] = val
            for key, val in wl.items():
                seen[e][key] = val
            plans[e].append((o, list(wl.items())))
        with contextlib.ExitStack() as st:
            sems = {}
            for e in ENGS:
                sems[("c", e)] = st.enter_context(nc.semaphore(f"c_{e}"))
                for j in range(min(NDMASEM, self.dma_count[e])):
                    sems[("d", e, j)] = st.enter_context(nc.semaphore(f"d_{e}_{j}"))
            block = st.enter_context(nc.Block())

            def mk(e):
                plan = plans[e]

                def body(eng):
                    for o, wl in plan:
                        for key, val in wl:
                            eng.wait_ge(sems[key], val)
                        if o.fn is None:
                            continue
                        ins = o.fn(eng)
                        if o.is_dma:
                            ins.then_inc(sems[("d",) + o.dsem], 16)
                        elif o.inc:
                            ins.then_inc(sems[("c", e)], 1)
                    n = self.dma_count[e]
                    for j in range(min(NDMASEM, n)):
                        eng.wait_ge(sems[("d", e, j)], 16 * ((n - 1 - j) // NDMASEM + 1))
                return body

            block.tensor(mk(PE))
            block.scalar(mk(ACT))
            block.vector(mk(DVE))
            block.gpsimd(mk(POOL))
            block.sync(mk(SP))


class Ring:
    def __init__(self, items):
        self.items = items
        self.i = 0

    def get(self):
        it = self.items[self.i % len(self.items)]
        self.i += 1
        return it


def make_consts():
    c = {}
    bf = ml_dtypes.bfloat16
    f = np.float32
    c["ident"] = np.eye(128, dtype=f).astype(bf)
    lg = np.log1p(-np.exp2(-5.0 - np.arange(4, dtype=np.float64)))
    pos = np.arange(128, dtype=np.float64)
    diff = pos[None, :] - pos[:, None]
    dmt = np.where(diff >= 0, np.exp(lg[:, None, None] * np.maximum(diff, 0.0)), 0.0)
    c["c_dmt"] = np.ascontiguousarray(dmt.transpose(1, 0, 2).reshape(128, 512)).astype(f)
    xi = np.exp(lg[:, None] * (pos + 1.0))
    xi2 = np.tile(xi, (1, 2))
    c["c_xi"] = np.ascontiguousarray(np.broadcast_to(xi2.reshape(1, 4 * 256), (128, 4 * 256))).astype(f)
    zeta = np.exp(lg[:, None] * (127 - pos)) * (128 ** -0.5)
    c["c_zeta"] = np.ascontiguousarray(np.repeat(zeta.T[:, :, None], 128, axis=2).reshape(128, 512)).astype(f)
    slopes = np.exp2(-(np.arange(8, dtype=np.float64) + 1.0)).reshape(2, 4)
    t = np.arange(S_LEN)
    qaug = np.zeros((2, 4, 4, S_LEN))
    for g in range(2):
        for r in range(4):
            sl = slopes[g, r]
            qaug[g, 0, r] = 8 * sl * 128
            qaug[g, 1, r] = 8 * sl
            qaug[g, 2, r] = -8 * sl * 128 * (t // 128)
            qaug[g, 3, r] = -8 * sl * (t % 128)
    c["c_qaug"] = qaug.reshape(2 * 4, 4 * S_LEN).astype(bf)
    kaug = np.stack([t // 128, t % 128, np.ones(S_LEN), np.ones(S_LEN)]).astype(np.float64)
    c["c_kaug"] = kaug.astype(bf)
    pc = 16 * np.arange(127) + 31
    caug = np.stack([pc // 128, pc % 128, np.ones(127), np.ones(127)]).astype(np.float64)
    c["c_caug"] = caug.astype(bf)
    c["c_ehot"] = (np.arange(32)[:, None] == (t[None, :] // 64)).astype(f).astype(bf)
    tt = (128 * np.arange(16)[:, None] + np.arange(128)[None, :])
    cm = (pc[:, None, None] <= tt[None, :, :]).astype(f)
    c["c_cmask"] = np.ascontiguousarray(cm.reshape(127, 16 * 128)).astype(bf)
    kb = np.arange(128)
    c["c_caus"] = np.ascontiguousarray(np.tile((kb[:, None] <= kb[None, :]).astype(f), (1, 4))).astype(bf)
    c["c_far"] = np.ascontiguousarray(np.tile((kb[:, None] > kb[None, :]).astype(f), (1, 4))).astype(bf)
    cur = tt // 64
    jj = np.arange(32)
    force = (jj[None, None, :] == 0) | (jj[None, None, :] == cur[:, :, None]) | (jj[None, None, :] == cur[:, :, None] - 1)
    fut = jj[None, None, :] > cur[:, :, None]
    A = 1.0 - force - fut
    Bm = 1e9 * force - 1e9 * fut
    NF = 1.0 - fut
    c["c_impA"] = np.ascontiguousarray(A.transpose(1, 0, 2).reshape(128, 16 * 32)).astype(f)
    c["c_impB"] = np.ascontiguousarray(Bm.transpose(1, 0, 2).reshape(128, 16 * 32)).astype(f)
    c["c_impNF"] = np.ascontiguousarray(NF.transpose(1, 0, 2).reshape(128, 16 * 32)).astype(f)
    cs = 16 * np.arange(127)
    js = 64 * np.arange(32)
    ovl = ((cs[:, None] < js[None, :] + 64) & (cs[:, None] + 32 > js[None, :])).astype(f)
    c["c_ovl"] = np.concatenate([np.ones((127, 1), f), ovl], axis=1).astype(bf)
    return c


class Prog:
    def __init__(self, nseq=2, passes="RNF", dbg=False):
        self.nseq = nseq
        self.ntok = nseq * S_LEN
        self.passes = passes
        self.dbg = dbg
        nc = self.nc = bass.Bass("TRN2", target_bir_lowering=False)
        self.S = Sched(nc)
        self.in_names = []

    def din(self, name, shape, dt=F32):
        self.in_names.append(name)
        return self.nc.dram_tensor(name, list(shape), dt, kind="ExternalInput").ap()

    def build(self):
        nc, S = self.nc, self.S
        NT = self.ntok
        self.x = self.din("x", [NT, D])
        self.w_in = self.din("w_in", [D, N_IN])
        self.w_up = self.din("w_up", [D, 2 * DFF])
        self.w_down = self.din("w_down", [DFF, D])
        self.g_ffn = self.din("g_ffn", [128, 8 * 128])
        self.g_mix = self.din("g_mix", [128, 8 * 128])
        self.g_fin = self.din("g_fin", [128, D])
        self.convw = self.din("convw", [128, NFC * 3])
        self.convb = self.din("convb", [128, NFC])
        self.ident_d = self.din("ident", [128, 128], BF16)
        self.w_ret_o = self.din("w_ret_o", [D, D])
        self.w_nsa_o = self.din("w_nsa_o", [512, D])
        self.w_out = self.din("w_out", [D, D])
        self.gng = self.din("gng", [128, 8 * 128])
        self.w1 = [self.din("w1k", [2048, 256]), self.din("w1v", [2048, 256])]
        self.w2 = [self.din("w2k", [256, 64]), self.din("w2v", [256, 64])]
        self.b1 = [self.din("b1k", [128, 2]), self.din("b1v", [128, 2])]
        self.posT = [self.din("posTk", [64, 64]), self.din("posTv", [64, 64])]
        self.cd = {}
        for k, v in make_consts().items():
            if k != "ident":
                self.cd[k] = self.din(k, v.shape, BF16 if v.dtype == ml_dtypes.bfloat16 else F32)
        self.out = nc.dram_tensor("out", [NT, D], F32, kind="ExternalOutput").ap()
        self.x1_scr = nc.dram_tensor("x1_scr", [NT, D], F32, kind="Internal").ap()
        self.ma_scr = nc.dram_tensor("ma_scr", [128, 8 * NT], BF16,
                                     kind="ExternalOutput" if self.dbg else "Internal").ap()

        with contextlib.ExitStack() as st:
            self.st = st
            banks = []
            for i in range(7):
                t = st.enter_context(nc.psum_tensor(f"psf{i}", [128, 512], F32))
                banks.append((t, f"psf{i}"))
            self.banks = Ring(banks[0:5])
            self.accb = Ring(banks[5:7])
            self.psb = st.enter_context(nc.psum_tensor("psb", [128, 1024], BF16))
            self.ident = self.sb("ident", [128, 128], BF16)
            self.epsb = self.sb("epsb", [128, 1], F32)
            S.dma(SP, lambda e: e.dma_start(out=self.ident[:], in_=self.ident_d), writes=["ident"])
            S.op(DVE, lambda e: e.memset(self.epsb[:], EPS), writes=["epsb"])
            self.junk = self.sb("junk", [128, D], BF16)
            self.xt = Ring([(self.sb(f"xt{i}", [128, D], F32), f"xt{i}") for i in range(4)])
            self.xn = Ring([(self.sb(f"xn{i}", [128, D], BF16), f"xn{i}") for i in range(3)])
            self.ss = Ring([(self.sb(f"ss{i}", [128, 1], F32), f"ss{i}") for i in range(4)])
            if "R" in self.passes:
                with contextlib.ExitStack() as st2:
                    self.st = st2
                    self.pass_R()
                    S.barrier()
            if "N" in self.passes:
                with contextlib.ExitStack() as st2:
                    self.st = st2
                    self.pass_N_alloc()
                    with contextlib.ExitStack() as st3:
                        self.st = st3
                        self.pass_N1()
                        S.barrier()
                    with contextlib.ExitStack() as st3:
                        self.st = st3
                        self.pass_N2()
                        S.barrier()
            if "F" in self.passes:
                with contextlib.ExitStack() as st2:
                    self.st = st2
                    self.pass_F(self.x1_scr if "N" in self.passes else self.x)
                    S.barrier()
            S.run()
        return nc

    def sb(self, name, shape, dt):
        self._uid = getattr(self, "_uid", 0) + 1
        return self.st.enter_context(self.nc.sbuf_tensor(f"s{self._uid}_{name}", list(shape), dt))

    def load_w(self, dst, dkey, src_rows, c0, ncols, kcs, dcol0=0, rows=128):
        S = self.S
        for kc in range(kcs):
            for cc in range(0, ncols, 2048):
                n = min(2048, ncols - cc)
                S.dma(POOL, lambda e, kc=kc, cc=cc, n=n: e.dma_start(
                    out=dst[0:rows, kc, dcol0 + cc:dcol0 + cc + n],
                    in_=src_rows[kc * rows:(kc + 1) * rows, c0 + cc:c0 + cc + n]), writes=[dkey])

    def load_blk(self, dst, key, src_rows, c0, ncols, kcs, dcol0):
        srcv = src_rows.rearrange("(kc p) n -> p kc n", p=128)
        self.S.dma(POOL, lambda e: e.dma_start(out=dst[:, 0:kcs, dcol0:dcol0 + ncols], in_=srcv[:, 0:kcs, c0:c0 + ncols]), writes=[key])

    def cload(self, name, shape, dt, src, eng=SP):
        t = self.sb(name, shape, dt)
        self.S.dma(eng, lambda e: e.dma_start(out=t[:], in_=src), writes=[name])
        return t

    def mm8(self, out, lhs_fn, rhs_fn, n=8):
        def f(e):
            ins = None
            for kc in range(n):
                ins = e.matmul(out, lhsT=lhs_fn(kc), rhs=rhs_fn(kc), start=(kc == 0), stop=(kc == n - 1))
            return ins
        return f

    def pass_R(self):
        nc, S = self.nc, self.S
        TT = 256
        NS = TT // 128
        lg = np.log1p(-np.exp2(-5.0 - np.arange(4, dtype=np.float64)))
        decay = [float(np.exp(lg[h] * 128)) for h in range(4)]
        allb = self.banks.items + self.accb.items
        gen = Ring(allb[0:3])
        (pin, pink), (po0, po0k), (po1, po1k), (pk1, pk1k) = allb[3:7]
        pk0, pk0k = pin, pink
        pos_ = [(po0, po0k), (po1, po1k)]
        pks_ = [(pk0, pk0k), (pk1, pk1k)]
        wr = self.sb("wr", [128, 8, 3072], BF16)
        wga = self.sb("wga", [128, 8, D], BF16)
        wro = self.sb("wro", [128, 8, D], BF16)
        gcb = self.cload("gcb", [128, 1024], F32, self.g_mix)
        gngb = self.cload("gngb", [128, 1024], F32, self.gng)
        dmt = self.cload("dmt", [128, 512], F32, self.cd["c_dmt"])
        xi = self.cload("xi", [128, 4 * 256], F32, self.cd["c_xi"])
        zeta = self.cload("zeta", [128, 512], F32, self.cd["c_zeta"])
        self.load_blk(wr, "wr_q", self.w_in, O_RQ, 512, 8, 0)
        self.load_blk(wr, "wr_k", self.w_in, O_RK, 512, 8, 512)
        self.load_blk(wr, "wr_v", self.w_in, O_RV, 1024, 8, 1024)
        self.load_blk(wr, "wr_g", self.w_in, O_RG, 1024, 8, 2048)
        self.load_blk(wga, "wga", self.w_in, O_GA, D, 8, 0)
        self.load_blk(wro, "wro", self.w_ret_o, 0, D, 8, 0)
        hTs = [self.sb(f"hTR{b}", [128, 8, TT], BF16) for b in range(2)]
        qTs = [self.sb(f"qT{b}", [128, 4, TT], BF16) for b in range(2)]
        qxTs = [self.sb(f"qxT{b}", [128, 4, TT], BF16) for b in range(2)]
        kTs = [self.sb(f"kT{b}", [128, 4, TT], BF16) for b in range(2)]
        kzs = [self.sb(f"kz{b}", [128, NS, 512], BF16) for b in range(2)]
        vs = [self.sb(f"v{b}", [128, NS, D], BF16) for b in range(2)]
        sgs = [self.sb(f"sg{b}", [128, NS, D], BF16) for b in range(2)]
        sga = self.sb("sga", [128, 8, TT], BF16)
        roT = self.sb("roT", [128, 8, TT], BF16)
        maT = self.sb("maT", [128, 8, TT], BF16)
        R = self.sb("R", [128, 4, 256], F32)
        Rb = self.sb("Rb", [128, 4, 256], BF16)
        inT = self.sb("inT4", [128, 512], BF16)
        on = self.sb("on4", [128, D], F32)
        stt = self.sb("stt", [128, 4, 6], F32)
        mv = self.sb("mv", [128, 4, 4], F32)
        NTL = self.ntok // TT

        def proj(t):
            b = t % 2
            hT, qT, qxT, kT, kz, v, sg = hTs[b], qTs[b], qxTs[b], kTs[b], kzs[b], vs[b], sgs[b]
            hk = f"hTR{b}"
            ops = []
            for h in range(4):
                def fq(h=h):
                    pq, pqk = gen.get()
                    S.op(PE, self.mm8(pq[:, 0:TT], lambda kc: wr[:, kc, O_RQ + h * 128:O_RQ + (h + 1) * 128], lambda kc: hT[:, kc, :]),
                         reads=["wr_q", hk], writes=[pqk])
                    S.op(ACT, lambda e: e.copy(out=qT[:, h, :], in_=pq[:, 0:TT]), reads=[pqk], writes=[f"qT{b}"])
                    S.op(DVE, lambda e: e.tensor_tensor(out=qxT[:, h, :], in0=pq[:, 0:TT], in1=xi[:, h * 256:(h + 1) * 256], op=ALU.mult),
                         reads=[pqk, "xi"], writes=[f"qxT{b}"])

                def fk(h=h):
                    pk, pkk = gen.get()
                    S.op(PE, self.mm8(pk[:, 0:TT], lambda kc: wr[:, kc, 512 + h * 128:512 + (h + 1) * 128], lambda kc: hT[:, kc, :]),
                         reads=["wr_k", hk], writes=[pkk])
                    S.op(ACT, lambda e: e.mul(out=kT[:, h, :], in_=pk[:, 0:TT], mul=128 ** -0.5), reads=[pkk], writes=[f"kT{b}"])
                ops += [fq, fk]
            for c in range(NS):
                cs = slice(c * 128, (c + 1) * 128)

                def fz(c=c, cs=cs):
                    pk, pkk = gen.get()
                    S.op(PE, self.mm8(pk[:, :], lambda kc: hT[:, kc, cs], lambda kc: wr[:, kc, 512:1024]), reads=["wr_k", hk], writes=[pkk])
                    S.op(DVE, lambda e: e.tensor_tensor(out=kz[:, c, :], in0=pk[:, :], in1=zeta[:], op=ALU.mult),
                         reads=[pkk, "zeta"], writes=[f"kz{b}"])
                ops.append(fz)
                for half in range(2):
                    def fv(c=c, cs=cs, half=half):
                        pv, pvk = gen.get()
                        S.op(PE, self.mm8(pv[:, :], lambda kc: hT[:, kc, cs], lambda kc: wr[:, kc, 1024 + half * 512:1024 + (half + 1) * 512]),
                             reads=["wr_v", hk], writes=[pvk])
                        S.op(ACT, lambda e: e.copy(out=v[:, c, half * 512:(half + 1) * 512], in_=pv[:, :]), reads=[pvk], writes=[f"v{b}"])

                    def fg(c=c, cs=cs, half=half):
                        pg, pgk = gen.get()
                        S.op(PE, self.mm8(pg[:, :], lambda kc: hT[:, kc, cs], lambda kc: wr[:, kc, 2048 + half * 512:2048 + (half + 1) * 512]),
                             reads=["wr_g", hk], writes=[pgk])
                        S.op(ACT, lambda e: e.activation(out=sg[:, c, half * 512:(half + 1) * 512], in_=pg[:, :], func=AF.Silu),
                             reads=[pgk], writes=[f"sg{b}"])
                    ops += [fv, fg]
            return ops

        def gates(t):
            b = t % 2
            hT = hTs[b]
            ops = []
            for cc in range(8):
                def f(cc=cc):
                    pga, pgak = gen.get()
                    S.op(PE, self.mm8(pga[:, 0:TT], lambda kc: wga[:, kc, cc * 128:(cc + 1) * 128], lambda kc: hT[:, kc, :]),
                         reads=["wga", f"hTR{b}"], writes=[pgak])
                    S.op(ACT, lambda e: e.activation(out=sga[:, cc, :], in_=pga[:, 0:TT], func=AF.Sigmoid), reads=[pgak], writes=["sga"])
                ops.append(f)
            return ops

        def chunk_stages(t):
            b = t % 2
            qT, qxT, kT, kz, v, sg = qTs[b], qxTs[b], kTs[b], kzs[b], vs[b], sgs[b]
            tok0 = t * TT
            stages = []
            for c in range(NS):
                cs = slice(c * 128, (c + 1) * 128)
                first = ((tok0 + c * 128) % S_LEN) == 0

                def st0(cs=cs):
                    def mmi(e):
                        ins = None
                        for h in range(4):
                            ins = e.matmul(pin[:, h * 128:(h + 1) * 128], lhsT=kT[:, h, cs], rhs=qT[:, h, cs], start=True, stop=True,
                                           skip_group_check=True)
                        return ins
                    S.op(PE, mmi, reads=[f"kT{b}", f"qT{b}"], writes=[pink])
                    S.op(DVE, lambda e: e.tensor_tensor(out=inT[:], in0=pin[:, :], in1=dmt[:], op=ALU.mult), reads=[pink, "dmt"], writes=["inT4"])

                def st1(c=c, cs=cs, first=first):
                    for hp in range(2):
                        po, pok = pos_[hp]

                        def mmo(e, po=po, hp=hp):
                            ins = None
                            for hh in range(2):
                                h = hp * 2 + hh
                                ins = e.matmul(po[:, hh * 256:(hh + 1) * 256], lhsT=inT[:, h * 128:(h + 1) * 128], rhs=v[:, c, h * 256:(h + 1) * 256],
                                               start=True, stop=first, skip_group_check=True)
                                if not first:
                                    ins = e.matmul(po[:, hh * 256:(hh + 1) * 256], lhsT=qxT[:, h, cs], rhs=Rb[:, h, :], start=False, stop=True,
                                                   skip_group_check=True)
                            return ins
                        S.op(PE, mmo, reads=["inT4", f"v{b}", f"qxT{b}", "Rb"], writes=[pok])
                    for hp in range(2):
                        pk, pkk = pks_[hp]

                        def mmk(e, pk=pk, hp=hp):
                            ins = None
                            for hh in range(2):
                                h = hp * 2 + hh
                                ins = e.matmul(pk[:, hh * 256:(hh + 1) * 256], lhsT=kz[:, c, h * 128:(h + 1) * 128], rhs=v[:, c, h * 256:(h + 1) * 256],
                                               start=True, stop=True, skip_group_check=True)
                            return ins
                        S.op(PE, mmk, reads=[f"kz{b}", f"v{b}"], writes=[pkk])

                def st2(first=first):
                    for h in range(4):
                        pk, pkk = pks_[h // 2]
                        src = pk[:, (h % 2) * 256:(h % 2 + 1) * 256]
                        if first:
                            S.op(DVE, lambda e, h=h, src=src: e.tensor_copy(out=R[:, h, :], in_=src), reads=[pkk], writes=["R"])
                        else:
                            S.op(DVE, lambda e, h=h, src=src: e.scalar_tensor_tensor(out=R[:, h, :], in0=R[:, h, :], scalar=decay[h], in1=src,
                                                                                     op0=ALU.mult, op1=ALU.add), reads=[pkk, "R"], writes=["R"])
                    S.op(ACT, lambda e: e.copy(out=Rb[:].rearrange("p h e -> p (h e)"), in_=R[:].rearrange("p h e -> p (h e)")),
                         reads=["R"], writes=["Rb"])
                    for h in range(4):
                        po, pok = pos_[h // 2]
                        S.op(DVE, lambda e, h=h, po=po: e.bn_stats(out=stt[:, h, :], in_=po[:, (h % 2) * 256:(h % 2 + 1) * 256]),
                             reads=[pok], writes=["stt"])
                    for h in range(4):
                        S.op(DVE, lambda e, h=h: e.bn_aggr(out=mv[:, h, 0:2], in_=stt[:, h, :]), reads=["stt"], writes=["mv"])
                    S.op(ACT, lambda e: e.activation(out=mv[:, :, 2], in_=mv[:, :, 1], func=AF.Sqrt, bias=self.epsb[:]),
                         reads=["mv", "epsb"], writes=["mv"])
                    S.op(DVE, lambda e: e.reciprocal(out=mv[:, :, 2], in_=mv[:, :, 2]), reads=["mv"], writes=["mv"])
                    S.op(DVE, lambda e: e.scalar_tensor_tensor(out=mv[:, :, 3], in0=mv[:, :, 0], scalar=-1.0, in1=mv[:, :, 2],
                                                               op0=ALU.mult, op1=ALU.mult), reads=["mv"], writes=["mv"])

                def st3(c=c):
                    for h in range(4):
                        po, pok = pos_[h // 2]
                        S.op(ACT, lambda e, h=h, po=po: e.activation(out=on[:, h * 256:(h + 1) * 256], in_=po[:, (h % 2) * 256:(h % 2 + 1) * 256],
                                                                     func=AF.Identity, scale=mv[:, h, 2:3], bias=mv[:, h, 3:4]),
                             reads=[pok, "mv"], writes=["on4"])
                    S.op(DVE, lambda e: e.tensor_tensor(out=sg[:, c, :], in0=on[:], in1=sg[:, c, :], op=ALU.mult),
                         reads=["on4", f"sg{b}"], writes=[f"sg{b}"])
                stages += [st0, st1, st2, st3]
            return stages

        def tail(t):
            b = t % 2
            sg = sgs[b]
            tok0 = t * TT
            for c in range(NS):
                cs = slice(c * 128, (c + 1) * 128)

                def tr(e, c=c):
                    ins = None
                    for kc in range(8):
                        ins = e.transpose(out=self.psb[:, kc * 128:(kc + 1) * 128], in_=sg[:, c, kc * 128:(kc + 1) * 128],
                                          identity=self.ident[:])
                    return ins
                S.op(PE, tr, reads=[f"sg{b}", "ident"], writes=["psb"])
                S.op(DVE, lambda e, cs=cs: e.tensor_tensor(out=roT[:, :, cs], in0=self.psb[:].rearrange("p (k j) -> p k j", k=8),
                                                           in1=gngb[:].rearrange("p (k j) -> p k j", k=8), op=ALU.mult),
                     reads=["psb", "gngb"], writes=["roT"])
            for cc in range(8):
                ccs = slice(cc * 128, (cc + 1) * 128)
                pya, pyak = gen.get()
                S.op(PE, self.mm8(pya[:, 0:TT], lambda kc, ccs=ccs: wro[:, kc, ccs], lambda kc: roT[:, kc, :]),
                     reads=["wro", "roT"], writes=[pyak])
                S.op(DVE, lambda e, pya=pya, cc=cc: e.tensor_tensor(out=maT[:, cc, :], in0=sga[:, cc, :], in1=pya[:, 0:TT], op=ALU.mult),
                     reads=[pyak, "sga"], writes=["maT"])
            S.dma(SP, lambda e: e.dma_start(out=self.ma_scr.rearrange("p (c n) -> p c n", c=8)[:, :, tok0:tok0 + TT], in_=maT[:]),
                  reads=["maT"])

        for sub in range(NS):
            self.rmsnorm_hT(self.x, sub * 128, gcb, hTs[0], "hTR0", sub * 128)
        for f in proj(0):
            f()
        for t in range(NTL):
            stages = chunk_stages(t)
            fill = gates(t)
            if t + 1 < NTL:
                nb_ = (t + 1) % 2
                pend = {}
                for sub in range(NS):
                    def fa(sub=sub, t=t):
                        pend[sub] = self.rms_a(self.x, (t + 1) * TT + sub * 128)
                    fill.append(fa)
                for sub in range(NS):
                    def fb(sub=sub, nb_=nb_):
                        self.rms_b(pend[sub], gcb, hTs[nb_], f"hTR{nb_}", sub * 128)
                    fill.append(fb)
                fill += proj(t + 1)
            fi = 0
            for k, stg_ in enumerate(stages):
                stg_()
                tgt = ((k + 1) * len(fill) + len(stages) - 1) // len(stages)
                while fi < min(tgt, len(fill)):
                    fill[fi]()
                    fi += 1
            while fi < len(fill):
                fill[fi]()
                fi += 1
            tail(t)

    def rms_a(self, src, r0):
        S = self.S
        xt, xk = self.xt.get()
        xn, nk = self.xn.get()
        ss, sk = self.ss.get()
        S.dma(SP, lambda e: e.dma_start(out=xt[:], in_=src[r0:r0 + 128, :]), writes=[xk])
        S.op(ACT, lambda e: e.activation(out=self.junk[:], in_=xt[:], func=AF.Square, accum_out=ss[:]),
             reads=[xk], writes=[sk])
        S.op(ACT, lambda e: e.activation(out=ss[:], in_=ss[:], func=AF.Sqrt, scale=1.0 / D, bias=self.epsb[:]),
             reads=[sk, "epsb"], writes=[sk])
        S.op(DVE, lambda e: e.reciprocal(out=ss[:], in_=ss[:]), reads=[sk], writes=[sk])
        S.op(DVE, lambda e: e.tensor_scalar(out=xn[:], in0=xt[:], scalar1=ss[:, 0:1], scalar2=None, op0=ALU.mult),
             reads=[xk, sk], writes=[nk])
        return (xt, xk, xn, nk)

    def rms_b(self, state, gcb, hT, hkey, col0):
        S = self.S
        xt, xk, xn, nk = state

        def tr(e):
            ins = None
            for kc in range(8):
                ins = e.transpose(out=self.psb[:, kc * 128:(kc + 1) * 128], in_=xn[:, kc * 128:(kc + 1) * 128],
                                  identity=self.ident[:])
            return ins
        S.op(PE, tr, reads=[nk, "ident"], writes=["psb"])
        S.op(DVE, lambda e: e.tensor_tensor(
            out=hT[:, :, col0:col0 + 128], in0=self.psb[:].rearrange("p (k j) -> p k j", k=8),
            in1=gcb[:].rearrange("p (k j) -> p k j", k=8), op=ALU.mult),
            reads=["psb", "gcb"], writes=[hkey])
        return xt, xk

    def rmsnorm_hT(self, src, r0, gcb, hT, hkey, col0, keep=None):
        return self.rms_b(self.rms_a(src, r0), gcb, hT, hkey, col0)

    def pass_N_alloc(self):
        S = self.S
        ns = self.nseq
        self.KST = [[self.sb(f"KST{s}{g}", [100, S_LEN], BF16) for g in range(2)] for s in range(ns)]
        self.KWT = [[self.sb(f"KWT{s}{g}", [100, S_LEN], BF16) for g in range(2)] for s in range(ns)]
        self.VS1 = [self.sb(f"VS1{s}", [128, 16, 2, 65], BF16) for s in range(ns)]
        self.VW1 = [self.sb(f"VW1{s}", [128, 16, 2, 65], BF16) for s in range(ns)]
        self.KCM = [[self.sb(f"KCM{s}{g}", [100, 128], BF16) for g in range(2)] for s in range(ns)]
        self.VCO = [[self.sb(f"VCO{s}{g}", [128, 97], BF16) for g in range(2)] for s in range(ns)]
        for s in range(ns):
            S.op(DVE, lambda e, s=s: e.memset(self.VS1[s][:], 1.0), writes=[f"VS1{s}"])
            S.op(DVE, lambda e, s=s: e.memset(self.VW1[s][:], 1.0), writes=[f"VW1{s}"])
            for g in range(2):
                S.dma(SP, lambda e, s=s, g=g: e.dma_start(out=self.KST[s][g][64:96, :], in_=self.cd["c_ehot"]), writes=[f"KST{s}{g}"])
                S.dma(SP, lambda e, s=s, g=g: e.dma_start(out=self.KST[s][g][96:100, :], in_=self.cd["c_kaug"]), writes=[f"KST{s}{g}"])
                S.op(DVE, lambda e, s=s, g=g: e.memset(self.KWT[s][g][64:96, :], 0.0), writes=[f"KWT{s}{g}"])
                S.dma(SP, lambda e, s=s, g=g: e.dma_start(out=self.KWT[s][g][96:100, :], in_=self.cd["c_kaug"]), writes=[f"KWT{s}{g}"])
                S.op(DVE, lambda e, s=s, g=g: e.memset(self.KCM[s][g][64:96, :], 0.0), writes=[f"KCM{s}{g}"])
                S.dma(SP, lambda e, s=s, g=g: e.dma_start(out=self.KCM[s][g][96:100, 0:127], in_=self.cd["c_caug"]), writes=[f"KCM{s}{g}"])
                S.dma(SP, lambda e, s=s, g=g: e.dma_start(out=self.VCO[s][g][0:127, 64:97], in_=self.cd["c_ovl"]), writes=[f"VCO{s}{g}"])

    def pass_N1(self):
        nc, S = self.nc, self.S
        TT = 256
        saved_banks = self.banks
        self.banks = Ring(self.banks.items + self.accb.items)
        wkv = self.sb("wkv", [128, 8, 768], BF16)
        gcb = self.cload("gcb", [128, 1024], F32, self.g_mix)
        self.load_w(wkv, "wkv", self.w_in, O_KCR, 768, 8)
        w1 = [self.sb(f"w1_{k}", [64, 32, 256], BF16) for k in range(2)]
        w2 = [self.sb(f"w2_{k}", [128, 2, 64], BF16) for k in range(2)]
        cb1 = [self.sb(f"cb1_{k}", [128, 2], F32) for k in range(2)]
        for k in range(2):
            src = self.w1[k].rearrange("(p d) n -> d p n", d=64)
            for p0 in range(0, 32, 8):
                S.dma(POOL, lambda e, k=k, src=src, p0=p0: e.dma_start(out=w1[k][:, p0:p0 + 8, :], in_=src[:, p0:p0 + 8, :]),
                      writes=[f"w1_{k}"])
            self.load_w(w2[k], f"w2_{k}", self.w2[k], 0, 64, 2)
            b1 = self.cload(f"b1_{k}", [128, 2], F32, self.b1[k])
            pf = self.cload(f"posf_{k}", [64, 64], F32, self.posT[k])
            pb16 = self.sb(f"posb_{k}", [64, 64], BF16)
            S.op(DVE, lambda e, pf=pf, pb16=pb16: e.tensor_copy(out=pb16[:], in_=pf[:]), reads=[f"posf_{k}"], writes=[f"posb_{k}"])
            for nch in range(2):
                pb, pbk = self.banks.get()

                def mmc(e, pb=pb, k=k, nch=nch, pb16=pb16):
                    ins = None
                    for p in range(32):
                        ins = e.matmul(pb[:, 0:2], lhsT=w1[k][0:64, p, nch * 128:(nch + 1) * 128], rhs=pb16[0:64, 2 * p:2 * p + 2],
                                       start=(p == 0), stop=(p == 31))
                    return ins
                S.op(PE, mmc, reads=[f"w1_{k}", f"posb_{k}"], writes=[pbk])
                S.op(DVE, lambda e, pb=pb, k=k, nch=nch, b1=b1: e.tensor_tensor(
                    out=cb1[k][:, nch:nch + 1], in0=pb[:, 0:1], in1=b1[:, nch:nch + 1], op=ALU.add),
                    reads=[pbk, f"b1_{k}"], writes=[f"cb1_{k}"])
        CRT = [[self.sb(f"CRT{k}{g}", [64, S_LEN], BF16) for g in range(2)] for k in range(2)]
        hTs = [self.sb(f"hTN1_{b}", [128, 8, TT], BF16) for b in range(2)]
        hidT = self.sb("hidT", [128, 2, 128], BF16)
        NSB = TT // 128
        ntl = self.nseq * (S_LEN // TT)
        for sub in range(NSB):
            self.rmsnorm_hT(self.x, sub * 128, gcb, hTs[0], "hTN1_0", sub * 128)
        for s in range(self.nseq):
            for t in range(S_LEN // TT):
                tok0 = s * S_LEN + t * TT
                pos0 = t * TT
                tix = s * (S_LEN // TT) + t
                hT = hTs[tix % 2]
                hk = f"hTN1_{tix % 2}"
                pend = []
                if tix + 1 < ntl:
                    for sub in range(NSB):
                        pend.append(self.rms_a(self.x, tok0 + TT + sub * 128))
                dests = [(0, CRT[0], "CRT0"), (128, CRT[1], "CRT1"), (256, self.KST[s], f"KST{s}"), (512, self.KWT[s], f"KWT{s}")]
                for off, dst, dk in dests:
                    for g in range(2):
                        pb, pbk = self.banks.get()
                        S.op(PE, self.mm8(pb[0:64, 0:TT], lambda kc, off=off, g=g: wkv[:, kc, off + g * 64:off + (g + 1) * 64],
                                          lambda kc, hT=hT: hT[:, kc, :]), reads=["wkv", hk], writes=[pbk])
                        S.op(ACT, lambda e, pb=pb, dst=dst, g=g, pos0=pos0: e.copy(out=dst[g][0:64, pos0:pos0 + TT], in_=pb[0:64, 0:TT]),
                             reads=[pbk], writes=[f"{dk}{g}"])
                for sub in range(TT // 128):
                    cs = slice(sub * 128, (sub + 1) * 128)
                    kt = pos0 // 128 + sub
                    pb, pbk = self.banks.get()
                    S.op(PE, self.mm8(pb[:, 0:128], lambda kc, cs=cs, hT=hT: hT[:, kc, cs], lambda kc: wkv[:, kc, 384:512]),
                         reads=["wkv", hk], writes=[pbk])
                    S.op(PE, self.mm8(pb[:, 128:256], lambda kc, cs=cs, hT=hT: hT[:, kc, cs], lambda kc: wkv[:, kc, 640:768]),
                         reads=["wkv", hk], writes=[pbk])
                    S.op(ACT, lambda e, pb=pb, s=s, kt=kt: e.copy(out=self.VS1[s][:, kt, :, 0:64],
                                                                  in_=pb[:, 0:128].rearrange("p (g d) -> p g d", g=2)),
                         reads=[pbk], writes=[f"VS1{s}"])
                    S.op(ACT, lambda e, pb=pb, s=s, kt=kt: e.copy(out=self.VW1[s][:, kt, :, 0:64],
                                                                  in_=pb[:, 128:256].rearrange("p (g d) -> p g d", g=2)),
                         reads=[pbk], writes=[f"VW1{s}"])
                for sub, st_ in enumerate(pend):
                    nb_ = (tix + 1) % 2
                    self.rms_b(st_, gcb, hTs[nb_], f"hTN1_{nb_}", sub * 128)
            for k in range(2):
                for g in range(2):
                    for nch in range(2):
                        pb, pbk = self.banks.get()

                        def mmh(e, pb=pb, k=k, g=g, nch=nch):
                            ins = None
                            for p in range(32):
                                ins = e.matmul(pb[:, 0:127], lhsT=w1[k][0:64, p, nch * 128:(nch + 1) * 128],
                                               rhs=CRT[k][g][0:64, p:p + 2017:16], start=(p == 0), stop=(p == 31))
                            return ins
                        S.op(PE, mmh, reads=[f"w1_{k}", f"CRT{k}{g}"], writes=[pbk])
                        S.op(ACT, lambda e, pb=pb, k=k, nch=nch: e.activation(
                            out=hidT[:, nch, 0:127], in_=pb[:, 0:127], func=AF.Gelu_apprx_tanh, bias=cb1[k][:, nch:nch + 1]),
                            reads=[pbk, f"cb1_{k}"], writes=["hidT"])
                    pb, pbk = self.banks.get()
                    if k == 0:
                        S.op(PE, self.mm8(pb[0:64, 0:127], lambda nch: w2[0][:, nch, :], lambda nch: hidT[:, nch, 0:127], n=2),
                             reads=["w2_0", "hidT"], writes=[pbk])
                        S.op(ACT, lambda e, pb=pb, s=s, g=g: e.copy(out=self.KCM[s][g][0:64, 0:127], in_=pb[0:64, 0:127]),
                             reads=[pbk], writes=[f"KCM{s}{g}"])
                    else:
                        S.op(PE, self.mm8(pb[0:127, 0:64], lambda nch: hidT[:, nch, 0:127], lambda nch: w2[1][:, nch, :], n=2),
                             reads=["w2_1", "hidT"], writes=[pbk])
                        S.op(ACT, lambda e, pb=pb, s=s, g=g: e.copy(out=self.VCO[s][g][0:127, 0:64], in_=pb[0:127, 0:64]),
                             reads=[pbk], writes=[f"VCO{s}{g}"])

        self.banks = saved_banks

    def pass_N2(self):
        nc, S = self.nc, self.S
        TT = 256
        NS = TT // 128
        LA = 3
        allb = self.banks.items + self.accb.items
        gen = Ring(allb[0:3])
        scr = Ring(allb[3:5])
        accr = Ring(allb[5:7])
        wnq = self.sb("wnq", [128, 8, 512], BF16)
        wng = self.sb("wng", [128, 8, 24], BF16)
        wgb = self.sb("wgb", [128, 8, D], BF16)
        wno = self.sb("wno", [128, 4, D], BF16)
        wout = self.sb("wout", [128, 8, D], BF16)
        gcb = self.cload("gcb", [128, 1024], F32, self.g_mix)
        cmask = self.sb("cmask", [128, 2048], BF16)
        S.dma(SP, lambda e: e.dma_start(out=cmask[0:127, :], in_=self.cd["c_cmask"]), writes=["cmask"])
        caus = self.cload("caus", [128, 512], BF16, self.cd["c_caus"])
        far = self.cload("far", [128, 512], BF16, self.cd["c_far"])
        impA = self.cload("impA", [128, 512], F32, self.cd["c_impA"])
        impB = self.cload("impB", [128, 512], F32, self.cd["c_impB"])
        impNF = self.cload("impNF", [128, 512], F32, self.cd["c_impNF"])
        self.load_w(wnq, "wnq", self.w_in, O_NQ, 512, 8)
        self.load_w(wng, "wng", self.w_in, O_NG, 24, 8)
        self.load_w(wgb, "wgb", self.w_in, O_GB, D, 8)
        self.load_w(wno, "wno", self.w_nsa_o, 0, D, 4)
        self.load_w(wout, "wout", self.w_out, 0, D, 8)
        hTs = [self.sb(f"hTN2_{b}", [128, 8, TT], BF16) for b in range(2)]
        QTs = [[self.sb(f"QT{b}{g}", [100, 4, TT], BF16) for g in range(2)] for b in range(2)]
        for b in range(2):
            for g in range(2):
                S.op(DVE, lambda e, b=b, g=g: e.memset(QTs[b][g][64:96, :, :], 0.0), writes=[f"QT{b}{g}0", f"QT{b}{g}1"])
        SGs = [self.sb(f"SG{b}", [128, NS, 24], F32) for b in range(2)]
        sgbs = [self.sb(f"sgb{b}", [128, 8, TT], BF16) for b in range(2)]
        maTs = [self.sb(f"maTN{b}", [128, 8, TT], BF16) for b in range(2)]
        nso = self.sb("nso", [128, NS, 512], BF16)
        noT = self.sb("noT", [128, 4, TT], BF16)
        selr = Ring([(self.sb(f"SELB{i}", [128, 96], BF16), f"SELB{i}") for i in range(4)])
        cper = Ring([(self.sb(f"cpe{i}", [128, 512], BF16), f"cpe{i}") for i in range(4)])
        for t_, k_ in selr.items:
            S.op(DVE, lambda e, t_=t_: e.memset(t_[:], 0.0), writes=[k_])
        ONSs = [[self.sb(f"ONS{b}{sub}", [128, 8, 64], F32) for sub in range(NS)] for b in range(2)]
        per = Ring([(self.sb(f"pe{i}", [128, 512], BF16), f"pe{i}") for i in range(6)])
        rdr = Ring([(self.sb(f"rd{i}", [128, 8], F32), f"rd{i}") for i in range(8)])
        impr = Ring([(self.sb(f"imp{i}", [128, 40], F32), f"imp{i}") for i in range(4)])
        it4r = Ring([(self.sb(f"it4_{i}", [128, 4, 32], F32), f"it4_{i}") for i in range(2)])
        ftr = Ring([(self.sb(f"ft{i}", [128, 4, 64], F32), f"ft{i}") for i in range(2)])
        tmr = Ring([(self.sb(f"tmb{i}", [128, 2 * TT], F32), f"tmb{i}") for i in range(3)])
        qaug = self.cd["c_qaug"].rearrange("a (r t) -> a r t", r=4)
        mav = self.ma_scr.rearrange("p (c n) -> p c n", c=8)
        tiles = [(s, t) for s in range(self.nseq) for t in range(S_LEN // TT)]
        xts_of = {}
        LAG = 2
        IMMEDIATE = False

        def v3(ap):
            return ap.rearrange("p (r t) -> p r t", r=4)

        pending = []
        stepno = [0]
        owners = {}

        def run_due(force=False):
            while pending and (force or pending[0][0] <= stepno[0]):
                _, c = pending.pop(0)
                c2 = c()
                if c2 is not None:
                    pending.append([stepno[0] + LAG, c2])

        def emit_bg(item):
            if item == "FLUSH":
                run_due(True)
                return
            c = item()
            while IMMEDIATE and c is not None:
                c = c()
            if c is not None:
                pending.append([stepno[0] + LAG, c])

        def galloc():
            bk, bkk = gen.get()
            tok = owners.get(bkk)
            if tok is not None and not tok["done"]:
                run_due(True)
            tok = {"done": False}
            owners[bkk] = tok
            return bk, bkk, tok

        def guard(name):
            tok = owners.get(name)
            if tok is not None and not tok["done"]:
                run_due(True)
            tok = {"done": False}
            owners[name] = tok
            return tok

        def prologue(n):
            s, t = tiles[n]
            b = n % 2
            tok0 = s * S_LEN + t * TT
            pos0 = t * TT
            hT, QT, SG, sgb, maT = hTs[b], QTs[b], SGs[b], sgbs[b], maTs[b]
            hk = f"hTN2_{b}"
            items = []
            xts_of[n] = [None] * NS
            stt_ = {}
            for sub in range(NS):
                def fa(sub=sub):
                    xt, xk = self.xt.get()
                    xn, nk = self.xn.get()
                    ss, sk = self.ss.get()
                    stt_[sub] = (xt, xk, xn, nk, ss, sk)
                    S.dma(SP, lambda e: e.dma_start(out=xt[:], in_=self.x[tok0 + sub * 128:tok0 + (sub + 1) * 128, :]), writes=[xk])
                items.append(fa)
            items.append(lambda: S.dma(SP, lambda e: e.dma_start(out=maT[:], in_=mav[:, :, tok0:tok0 + TT]), writes=[f"maTN{b}"]) and None)
            for g in range(2):
                items.append(lambda g=g: S.dma(SP, lambda e: e.dma_start(out=QT[g][96:100, :, :], in_=qaug[g * 4:(g + 1) * 4, :, pos0:pos0 + TT]),
                                               writes=[f"QT{b}{g}0", f"QT{b}{g}1"]) and None)
            for sub in range(NS):
                def fb(sub=sub):
                    xt, xk, xn, nk, ss, sk = stt_[sub]
                    S.op(ACT, lambda e: e.activation(out=self.junk[:], in_=xt[:], func=AF.Square, accum_out=ss[:]), reads=[xk], writes=[sk])
                    S.op(ACT, lambda e: e.activation(out=ss[:], in_=ss[:], func=AF.Sqrt, scale=1.0 / D, bias=self.epsb[:]),
                         reads=[sk, "epsb"], writes=[sk])

                    def c1():
                        S.op(DVE, lambda e: e.reciprocal(out=ss[:], in_=ss[:]), reads=[sk], writes=[sk])
                        S.op(DVE, lambda e: e.tensor_scalar(out=xn[:], in0=xt[:], scalar1=ss[:, 0:1], scalar2=None, op0=ALU.mult),
                             reads=[xk, sk], writes=[nk])
                        stt_[("done", sub)] = True
                    return c1
                items.append(fb)
            for sub in range(NS):
                def fc(sub=sub):
                    if not stt_.get(("done", sub)):
                        run_due(True)
                    xt, xk, xn, nk, ss, sk = stt_[sub]
                    tok = guard("psb")

                    def tr(e):
                        ins = None
                        for kc in range(8):
                            ins = e.transpose(out=self.psb[:, kc * 128:(kc + 1) * 128], in_=xn[:, kc * 128:(kc + 1) * 128], identity=self.ident[:])
                        return ins
                    S.op(PE, tr, reads=[nk, "ident"], writes=["psb"])

                    def c1():
                        S.op(DVE, lambda e: e.tensor_tensor(out=hT[:, :, sub * 128:(sub + 1) * 128],
                                                            in0=self.psb[:].rearrange("p (k j) -> p k j", k=8),
                                                            in1=gcb[:].rearrange("p (k j) -> p k j", k=8), op=ALU.mult),
                             reads=["psb", "gcb"], writes=[hk])
                        tok["done"] = True
                    xts_of[n][sub] = (xt, xk)
                    return c1
                items.append(fc)
            items.append("FLUSH")
            for g in range(2):
                for r in (0, 2):
                    def fq(g=g, r=r):
                        pb, pbk, tok = galloc()
                        for dr in range(2):
                            hh = g * 4 + r + dr
                            S.op(PE, self.mm8(pb[0:64, dr * 256:dr * 256 + TT], lambda kc, hh=hh: wnq[:, kc, hh * 64:(hh + 1) * 64],
                                              lambda kc: hT[:, kc, :]), reads=["wnq", hk], writes=[pbk])

                        def c1():
                            S.op(DVE, lambda e: e.tensor_copy(out=QT[g][0:64, r:r + 2, :], in_=pb[0:64, :].rearrange("p (a n) -> p a n", a=2)),
                                 reads=[pbk], writes=[f"QT{b}{g}0", f"QT{b}{g}1"])
                            tok["done"] = True
                        return c1
                    items.append(fq)
            def fg():
                pb, pbk, tok = galloc()
                for sub in range(NS):
                    cs = slice(sub * 128, (sub + 1) * 128)
                    S.op(PE, self.mm8(pb[:, sub * 32:sub * 32 + 24], lambda kc, cs=cs: hT[:, kc, cs], lambda kc: wng[:, kc, :]),
                         reads=["wng", hk], writes=[pbk])

                def c1():
                    S.op(ACT, lambda e: e.activation(out=SG[:, :, :], in_=pb[:, 0:64].rearrange("p (a n) -> p a n", a=2)[:, :, 0:24],
                                                     func=AF.Sigmoid), reads=[pbk], writes=[f"SG{b}"])
                    tok["done"] = True
                return c1
            items.append(fg)
            items.append("FLUSH")
            chains = []
            for sub in range(NS):
                for g in range(2):
                    chains.append(cmp_chain(n, s, t, b, sub, g))
            nst = max([len(c) for c in chains] + [0])
            for st_ in range(nst):
                for c in chains:
                    if st_ < len(c):
                        items.append(c[st_])
            for cc in range(0, 8, 2):
                def fgb(cc=cc):
                    pb, pbk, tok = galloc()
                    for dc in range(2):
                        S.op(PE, self.mm8(pb[:, dc * 256:dc * 256 + TT], lambda kc, c_=cc + dc: wgb[:, kc, c_ * 128:(c_ + 1) * 128],
                                          lambda kc: hT[:, kc, :]), reads=["wgb", hk], writes=[pbk])

                    def c1():
                        S.op(ACT, lambda e: e.activation(out=sgb[:, cc:cc + 2, :], in_=pb[:, :].rearrange("p (a n) -> p a n", a=2),
                                                         func=AF.Sigmoid), reads=[pbk], writes=[f"sgb{b}"])
                        tok["done"] = True
                    return c1
                items.append(fgb)
            items.append("FLUSH")
            return items

        def cmp_chain(n, s, t, b, sub, g):
            i = t * NS + sub
            qs = slice(sub * 128, (sub + 1) * 128)
            QT, SG = QTs[b], SGs[b]
            ONS = ONSs[b][sub]
            onk = f"ONS{b}{sub}{g}"
            qk = f"QT{b}{g}{sub}"
            rhsQ = QT[g][0:100, :, qs]
            nb = min(127, 8 * i + 8)
            isl = slice(i * 32, (i + 1) * 32)
            st = {}

            def s0():
                psc, psck, tok = galloc()
                st["pe"], st["pek"] = cper.get()
                pe = st["pe"]
                S.op(PE, lambda e: e.matmul(v3(psc[0:nb, :]), lhsT=self.KCM[s][g][0:100, 0:nb], rhs=rhsQ, start=True, stop=True),
                     reads=[f"KCM{s}{g}", qk], writes=[psck])

                def c1():
                    S.op(ACT, lambda e: e.activation(out=pe[0:nb, :], in_=psc[0:nb, :], func=AF.Exp, scale=0.125),
                         reads=[psck], writes=[st["pek"]])
                    tok["done"] = True
                return c1

            def s1():
                pe, pek = st["pe"], st["pek"]
                S.op(DVE, lambda e: e.tensor_tensor(out=v3(pe[0:nb, :]), in0=v3(pe[0:nb, :]),
                                                    in1=cmask[0:nb, i * 128:(i + 1) * 128].unsqueeze(1).to_broadcast([nb, 4, 128]), op=ALU.mult),
                     reads=[pek, "cmask"], writes=[pek])

            def s2():
                pe, pek = st["pe"], st["pek"]
                pcv, pcvk, tok = galloc()

                def pvc(e):
                    ins = None
                    for r in range(4):
                        ins = e.matmul(pcv[:, r * 97:(r + 1) * 97], lhsT=pe[0:nb, r * 128:(r + 1) * 128],
                                       rhs=self.VCO[s][g][0:nb, 0:97], start=True, stop=True, skip_group_check=True)
                    return ins
                S.op(PE, pvc, reads=[pek, f"VCO{s}{g}"], writes=[pcvk])

                def c1():
                    rd, rdk = rdr.get()
                    imp, impk = impr.get()
                    st["imp"], st["impk"] = imp, impk
                    pcv3 = pcv[:, 0:388].rearrange("p (r c) -> p r c", r=4)
                    S.op(DVE, lambda e: e.tensor_scalar(out=rd[:, 0:4], in0=pcv3[:, :, 64], scalar1=1e-30, scalar2=None, op0=ALU.add),
                         reads=[pcvk], writes=[rdk])
                    S.op(DVE, lambda e: e.reciprocal(out=rd[:, 0:4], in_=rd[:, 0:4]), reads=[rdk], writes=[rdk])
                    S.op(DVE, lambda e: e.tensor_tensor(out=rd[:, 4:8], in0=rd[:, 0:4], in1=SG[:, sub, g * 4:g * 4 + 4], op=ALU.mult),
                         reads=[rdk, f"SG{b}"], writes=[rdk])
                    S.op(DVE, lambda e: e.tensor_tensor(out=ONS[:, g * 4:(g + 1) * 4, :], in0=pcv3[:, :, 0:64],
                                                        in1=rd[:, 4:8].unsqueeze(2).to_broadcast([128, 4, 64]), op=ALU.mult),
                         reads=[pcvk, rdk], writes=[onk])
                    it4, it4k = it4r.get()
                    S.op(DVE, lambda e: e.tensor_tensor(out=it4[:], in0=pcv3[:, :, 65:97],
                                                        in1=rd[:, 0:4].unsqueeze(2).to_broadcast([128, 4, 32]), op=ALU.mult),
                         reads=[pcvk, rdk], writes=[it4k])
                    S.op(DVE, lambda e: e.tensor_reduce(out=imp[:, 0:32], in_=it4[:].rearrange("p r j -> p j r"), axis=mybir.AxisListType.X,
                                                        op=ALU.add), reads=[it4k], writes=[impk])
                    tok["done"] = True
                return c1

            def s3():
                if "imp" not in st:
                    run_due(True)
                imp, impk = st["imp"], st["impk"]
                S.op(DVE, lambda e: e.tensor_tensor(out=imp[:, 0:32], in0=imp[:, 0:32], in1=impA[:, isl], op=ALU.mult),
                     reads=[impk, "impA"], writes=[impk])
                S.op(DVE, lambda e: e.tensor_tensor(out=imp[:, 0:32], in0=imp[:, 0:32], in1=impB[:, isl], op=ALU.add),
                     reads=[impk, "impB"], writes=[impk])
                S.op(DVE, lambda e: e.max(out=imp[:, 32:40], in_=imp[:, 0:32]), reads=[impk], writes=[impk])
                S.op(DVE, lambda e: e.scalar_tensor_tensor(out=imp[:, 0:32], in0=imp[:, 0:32], scalar=imp[:, 39:40], in1=impNF[:, isl],
                                                           op0=ALU.is_ge, op1=ALU.mult), reads=[impk, "impNF"], writes=[impk])
                SELB, selk = selr.get()
                S.op(DVE, lambda e: e.tensor_scalar(out=SELB[:, 64:96], in0=imp[:, 0:32], scalar1=-1.0, scalar2=30000.0,
                                                    op0=ALU.add, op1=ALU.mult), reads=[impk], writes=[selk])
                st["SELB"], st["selk"] = SELB, selk

            def s4():
                SELB, selk = st["SELB"], st["selk"]
                pst, pstk, tok = galloc()
                S.op(PE, lambda e: e.matmul(pst[0:96, 0:128], lhsT=SELB[:, 0:96], rhs=self.ident[:, :], start=True, stop=True),
                     reads=[selk, "ident"], writes=[pstk])

                def c1():
                    S.op(DVE, lambda e: e.tensor_copy(out=QT[g][64:96, :, qs], in_=pst[64:96, 0:128].unsqueeze(1).to_broadcast([32, 4, 128])),
                         reads=[pstk], writes=[qk])
                    tok["done"] = True
                return c1

            def w(f, need_flush):
                def g_():
                    if need_flush:
                        run_due(True)
                    return f()
                return g_
            return [s0, w(s1, True), s2, s3, s4]

        def pair_tasks(n):
            s, t = tiles[n]
            b = n % 2
            QT, SG = QTs[b], SGs[b]
            tasks = []
            for sub in range(NS):
                i = t * NS + sub
                qs = slice(sub * 128, (sub + 1) * 128)
                ONS = ONSs[b][sub]
                for g in range(2):
                    onk = f"ONS{b}{sub}{g}"
                    qk = f"QT{b}{g}{sub}"
                    rhsQ = QT[g][0:100, :, qs]
                    for (KT, kkey, V1, vkey, j0, goff, isw) in (
                            (self.KWT[s][g], f"KWT{s}{g}", self.VW1[s], f"VW1{s}", max(0, i - 4), 16, True),
                            (self.KST[s][g], f"KST{s}{g}", self.VS1[s], f"VS1{s}", 0, 8, False)):
                        acc = {}
                        for j in range(j0, i + 1):
                            tk = {}

                            def A(tk=tk, j=j, KT=KT, kkey=kkey, rhsQ=rhsQ, qk=qk):
                                tk["pss"], tk["pssk"] = scr.get()
                                pss = tk["pss"]
                                S.op(PE, lambda e: e.matmul(v3(pss[:, :]), lhsT=KT[0:100, j * 128:(j + 1) * 128], rhs=rhsQ,
                                                            start=True, stop=True), reads=[kkey, qk], writes=[tk["pssk"]])
                                tk["pe"], tk["pek"] = per.get()
                                pe, pek = tk["pe"], tk["pek"]
                                S.op(ACT, lambda e: e.activation(out=pe[:], in_=pss[:, :], func=AF.Exp, scale=0.125),
                                     reads=[tk["pssk"]], writes=[pek])

                            def B(tk=tk, j=j, i=i, isw=isw):
                                pe, pek = tk["pe"], tk["pek"]
                                if j == i:
                                    S.op(DVE, lambda e: e.tensor_tensor(out=pe[:], in0=pe[:], in1=caus[:], op=ALU.mult),
                                         reads=[pek, "caus"], writes=[pek])
                                if isw and i >= 4 and j == i - 4:
                                    S.op(DVE, lambda e: e.tensor_tensor(out=pe[:], in0=pe[:], in1=far[:], op=ALU.mult),
                                         reads=[pek, "far"], writes=[pek])

                            def C(tk=tk, j=j, i=i, j0=j0, acc=acc, V1=V1, vkey=vkey, g=g):
                                if j == j0:
                                    acc["psv"], acc["psvk"] = accr.get()
                                psv, psvk, pe = acc["psv"], acc["psvk"], tk["pe"]

                                def pv(e):
                                    ins = None
                                    for r in range(4):
                                        ins = e.matmul(psv[:, r * 65:(r + 1) * 65], lhsT=pe[:, r * 128:(r + 1) * 128], rhs=V1[:, j, g, :],
                                                       start=(j == j0 and r == 0), stop=(j == i), skip_group_check=True)
                                    return ins
                                S.op(PE, pv, reads=[tk["pek"], vkey], writes=[psvk])

                            def Dn(j=j, i=i, acc=acc, g=g, goff=goff, ONS=ONS, onk=onk, SG=SG, sub=sub, b=b):
                                if j != i:
                                    return
                                psv, psvk = acc["psv"], acc["psvk"]
                                rd, rdk = rdr.get()
                                psv3 = psv[:, 0:260].rearrange("p (r c) -> p r c", r=4)
                                S.op(DVE, lambda e: e.reciprocal(out=rd[:, 0:4], in_=psv3[:, :, 64]), reads=[psvk], writes=[rdk])
                                S.op(DVE, lambda e: e.tensor_tensor(out=rd[:, 4:8], in0=rd[:, 0:4],
                                                                    in1=SG[:, sub, goff + g * 4:goff + g * 4 + 4], op=ALU.mult),
                                     reads=[rdk, f"SG{b}"], writes=[rdk])
                                ft, ftk = ftr.get()
                                S.op(DVE, lambda e: e.tensor_tensor(out=ft[:], in0=psv3[:, :, 0:64],
                                                                    in1=rd[:, 4:8].unsqueeze(2).to_broadcast([128, 4, 64]), op=ALU.mult),
                                     reads=[psvk, rdk], writes=[ftk])
                                S.op(POOL, lambda e: e.tensor_tensor(out=ONS[:, g * 4:(g + 1) * 4, :], in0=ONS[:, g * 4:(g + 1) * 4, :],
                                                                     in1=ft[:], op=ALU.add), reads=[ftk, onk], writes=[onk])
                            tasks.append((A, B, C, Dn))
            return tasks

        def epilogue(n):
            s, t = tiles[n]
            b = n % 2
            tok0 = s * S_LEN + t * TT
            sgb, maT = sgbs[b], maTs[b]
            items = []
            for sub in range(NS):
                qs = slice(sub * 128, (sub + 1) * 128)
                ONS = ONSs[b][sub]

                def f1(sub=sub, ONS=ONS, qs=qs):
                    S.op(POOL, lambda e: e.tensor_copy(out=nso[:, sub, :], in_=ONS[:].rearrange("p h d -> p (h d)")),
                         reads=[f"ONS{b}{sub}0", f"ONS{b}{sub}1"], writes=["nso"])

                    def c1():
                        tok = guard("psb")

                        def tr(e):
                            ins = None
                            for k4 in range(4):
                                ins = e.transpose(out=self.psb[:, k4 * 128:(k4 + 1) * 128], in_=nso[:, sub, k4 * 128:(k4 + 1) * 128],
                                                  identity=self.ident[:])
                            return ins
                        S.op(PE, tr, reads=["nso", "ident"], writes=["psb"])

                        def c2():
                            S.op(DVE, lambda e: e.tensor_copy(out=noT[:, :, qs], in_=self.psb[:, 0:512].rearrange("p (k j) -> p k j", k=4)),
                                 reads=["psb"], writes=["noT"])
                            tok["done"] = True
                        return c2
                    return c1
                items.append(f1)
            items.append("FLUSH")
            for cc in range(0, 8, 2):
                def f2(cc=cc):
                    pb, pbk, tok = galloc()
                    for dc in range(2):
                        S.op(PE, self.mm8(pb[:, dc * 256:dc * 256 + TT], lambda k4, c_=cc + dc: wno[:, k4, c_ * 128:(c_ + 1) * 128],
                                          lambda k4: noT[:, k4, :], n=4), reads=["wno", "noT"], writes=[pbk])

                    def c1():
                        tm, tmk = tmr.get()
                        S.op(DVE, lambda e: e.tensor_tensor(out=tm[:].rearrange("p (a n) -> p a n", a=2),
                                                            in0=pb[:, :].rearrange("p (a n) -> p a n", a=2), in1=sgb[:, cc:cc + 2, :], op=ALU.mult),
                             reads=[pbk, f"sgb{b}"], writes=[tmk])
                        tok["done"] = True
                        S.op(POOL, lambda e: e.tensor_tensor(out=maT[:, cc:cc + 2, :], in0=tm[:].rearrange("p (a n) -> p a n", a=2),
                                                             in1=maT[:, cc:cc + 2, :], op=ALU.add),
                             reads=[tmk, f"maTN{b}"], writes=[f"maTN{b}"])
                    return c1
                items.append(f2)
            items.append("FLUSH")
            for sub in range(NS):
                cs = slice(sub * 128, (sub + 1) * 128)
                for half in range(2):
                    def f3(sub=sub, cs=cs, half=half):
                        xt, xk = xts_of[n][sub]
                        pb, pbk, tok = galloc()
                        S.op(PE, self.mm8(pb[:, :], lambda cc: maT[:, cc, cs], lambda cc: wout[:, cc, half * 512:(half + 1) * 512]),
                             reads=[f"maTN{b}", "wout"], writes=[pbk])

                        def c1():
                            S.op(DVE, lambda e: e.tensor_tensor(out=xt[:, half * 512:(half + 1) * 512], in0=xt[:, half * 512:(half + 1) * 512],
                                                                in1=pb[:, :], op=ALU.add), reads=[pbk, xk], writes=[xk])
                            tok["done"] = True
                            if half == 1:
                                r0 = tok0 + sub * 128
                                S.dma(SP, lambda e: e.dma_start(out=self.x1_scr[r0:r0 + 128, :], in_=xt[:]), reads=[xk])
                        return c1
                    items.append(f3)
            items.append("FLUSH")
            return items

        for it in prologue(0):
            stepno[0] += 1
            run_due()
            emit_bg(it)
        run_due(True)
        for n in range(len(tiles)):
            bg = []
            if n >= 1:
                bg += epilogue(n - 1)
            if n + 1 < len(tiles):
                bg += prologue(n + 1)
            tasks = pair_tasks(n)
            steps = len(tasks) + LA + 2
            pi = 0
            for k in range(steps):
                stepno[0] += 1
                run_due()
                if k < len(tasks):
                    tasks[k][0]()
                if 0 <= k - 1 < len(tasks):
                    tasks[k - 1][1]()
                if 0 <= k - LA < len(tasks):
                    tasks[k - LA][2]()
                if 0 <= k - LA - 1 < len(tasks):
                    tasks[k - LA - 1][3]()
                tgt = ((k + 1) * len(bg) + steps - 1) // steps
                while pi < min(tgt, len(bg)):
                    emit_bg(bg[pi])
                    pi += 1
            while pi < len(bg):
                stepno[0] += 1
                run_due()
                emit_bg(bg[pi])
                pi += 1
            run_due(True)
        for it in epilogue(len(tiles) - 1):
            stepno[0] += 1
            run_due()
            emit_bg(it)
        run_due(True)

    def pass_F(self, src):
        nc, S = self.nc, self.S
        TT = 256
        self.banks = Ring(self.banks.items + self.accb.items)
        wup = self.sb("wup", [128, 8, 2 * DFF], BF16)
        wdn = self.sb("wdn", [128, NFC, D], BF16)
        gcb = self.sb("gcbF", [128, 8 * 128], F32)
        gfin = self.sb("gfin", [128, D], F32)
        cw = self.sb("cw", [128, NFC * 3], F32)
        cb = self.sb("cb", [128, NFC], F32)
        halo = self.sb("halo", [128, NFC, 2], F32)
        hTs = [self.sb(f"hTF{b}", [128, 8, TT], BF16) for b in range(2)]
        uT = self.sb("uT", [128, NFC, TT], BF16)
        t1r = Ring([(self.sb(f"t1_{i}", [128, TT], F32), f"t1_{i}") for i in range(3)])
        ger = Ring([(self.sb(f"ge_{i}", [128, TT], F32), f"ge_{i}") for i in range(2)])
        osb = Ring([(self.sb(f"osb{i}", [128, D], F32), f"osb{i}") for i in range(2)])
        S.dma(SP, lambda e: e.dma_start(out=gcb[:], in_=self.g_ffn), writes=["gcb"])
        S.dma(SP, lambda e: e.dma_start(out=gfin[:], in_=self.g_fin), writes=["gfin"])
        S.dma(SP, lambda e: e.dma_start(out=cw[:], in_=self.convw), writes=["cw"])
        S.dma(SP, lambda e: e.dma_start(out=cb[:], in_=self.convb), writes=["cb"])
        for blk in range(NFC // 2):
            self.load_blk(wup, f"wup{blk}", self.w_up, blk * 256, 256, 8, blk * 256)
            self.load_blk(wup, f"wup{blk}", self.w_up, DFF + blk * 256, 256, 8, DFF + blk * 256)
        wdn_keys = []
        for f0 in range(0, NFC, 2):
            S.dma(POOL, lambda e, f0=f0: e.dma_start(out=wdn[:, f0:f0 + 2, :],
                                                     in_=self.w_down.rearrange("(f p) n -> p f n", p=128)[:, f0:f0 + 2, :]),
                  writes=[f"wdn{f0}"])
            wdn_keys.append(f"wdn{f0}")
        NTL = self.ntok // TT
        NSB = TT // 128
        xts_next = [self.rmsnorm_hT(src, sub * 128, gcb, hTs[0], "hTF0", sub * 128) for sub in range(NSB)]
        for t in range(NTL):
            tok0 = t * TT
            first = (tok0 % S_LEN) == 0
            hT = hTs[t % 2]
            hk = f"hTF{t % 2}"
            xts = xts_next
            xts_next = []
            pend = []
            deferred = None
            for fc in range(NFC):
                pa, pak = self.banks.get()
                pb, pbk = self.banks.get()

                def mm_a(e, pa=pa, fc=fc, hT=hT):
                    ins = None
                    for kc in range(8):
                        ins = e.matmul(pa[:, 2:2 + TT], lhsT=wup[:, kc, fc * 128:(fc + 1) * 128], rhs=hT[:, kc, :],
                                       start=(kc == 0), stop=(kc == 7))
                    return ins

                def mm_b(e, pb=pb, fc=fc, hT=hT):
                    ins = None
                    for kc in range(8):
                        ins = e.matmul(pb[:, 0:TT], lhsT=wup[:, kc, DFF + fc * 128:DFF + (fc + 1) * 128],
                                       rhs=hT[:, kc, :], start=(kc == 0), stop=(kc == 7))
                    return ins
                S.op(PE, mm_a, reads=[f"wup{fc // 2}", hk], writes=[pak])
                S.op(PE, mm_b, reads=[f"wup{fc // 2}", hk], writes=[pbk])
                if t + 1 < NTL:
                    nb_ = (t + 1) % 2
                    if fc in (2, 6):
                        pend.append(self.rms_a(src, tok0 + TT + (fc // 4) * 128))
                    if fc in (10, 14):
                        sub_ = (fc - 10) // 4
                        xts_next.append(self.rms_b(pend[sub_], gcb, hTs[nb_], f"hTF{nb_}", sub_ * 128))
                if first:
                    S.op(ACT, lambda e, pa=pa: e.memzero(pa[:, 0:2]), reads=[pak], writes=[pak])
                else:
                    S.op(ACT, lambda e, pa=pa, fc=fc: e.copy(out=pa[:, 0:2], in_=halo[:, fc, :]),
                         reads=[pak, "halo"], writes=[pak])
                S.op(ACT, lambda e, pa=pa, fc=fc: e.copy(out=halo[:, fc, :], in_=pa[:, TT:TT + 2]),
                     reads=[pak], writes=["halo"])
                t1, t1k = t1r.get()
                S.op(ACT, lambda e, pa=pa, fc=fc, t1=t1: e.activation(
                    out=t1[:], in_=pa[:, 2:2 + TT], func=AF.Copy, scale=cw[:, fc * 3 + 2:fc * 3 + 3]),
                    reads=[pak, "cw"], writes=[t1k])
                S.op(DVE, lambda e, pa=pa, fc=fc, t1=t1: e.scalar_tensor_tensor(
                    out=t1[:], in0=pa[:, 1:1 + TT], scalar=cw[:, fc * 3 + 1:fc * 3 + 2], in1=t1[:],
                    op0=ALU.mult, op1=ALU.add), reads=[pak, "cw", t1k], writes=[t1k])
                S.op(DVE, lambda e, pa=pa, fc=fc, t1=t1: e.scalar_tensor_tensor(
                    out=t1[:], in0=pa[:, 0:TT], scalar=cw[:, fc * 3:fc * 3 + 1], in1=t1[:],
                    op0=ALU.mult, op1=ALU.add), reads=[pak, "cw", t1k], writes=[t1k])

                def fin(fc=fc, t1=t1, t1k=t1k, pb=pb, pbk=pbk):
                    ge, gek = ger.get()
                    S.op(ACT, lambda e: e.activation(out=ge[:], in_=t1[:], func=AF.Gelu_apprx_tanh, bias=cb[:, fc:fc + 1]),
                         reads=[t1k, "cb"], writes=[gek])
                    S.op(DVE, lambda e: e.tensor_tensor(out=uT[:, fc, :], in0=ge[:], in1=pb[:, 0:TT], op=ALU.mult),
                         reads=[gek, pbk], writes=["uT"])
                if deferred is not None:
                    deferred()
                deferred = fin
            deferred()
            deferred = None
            for sub in range(TT // 128):
                xt, xk = xts[sub]
                ss, sk = self.ss.get()
                for half in range(2):
                    po, pok = self.banks.get()

                    def mm_o(e, po=po, sub=sub, half=half):
                        ins = None
                        for fc in range(NFC):
                            ins = e.matmul(po[:, :], lhsT=uT[:, fc, sub * 128:(sub + 1) * 128],
                                           rhs=wdn[:, fc, half * 512:(half + 1) * 512],
                                           start=(fc == 0), stop=(fc == NFC - 1))
                        return ins
                    S.op(PE, mm_o, reads=["uT"] + wdn_keys, writes=[pok])
                    S.op(DVE, lambda e, po=po, xt=xt, half=half: e.tensor_tensor(
                        out=xt[:, half * 512:(half + 1) * 512], in0=xt[:, half * 512:(half + 1) * 512],
                        in1=po[:, :], op=ALU.add), reads=[pok, xk], writes=[xk])
                ob, obk = osb.get()
                S.op(ACT, lambda e, xt=xt, ss=ss: e.activation(out=self.junk[:], in_=xt[:], func=AF.Square,
                                                               accum_out=ss[:]), reads=[xk], writes=[sk])
                S.op(ACT, lambda e, ss=ss: e.activation(out=ss[:], in_=ss[:], func=AF.Sqrt, scale=1.0 / D,
                                                        bias=self.epsb[:]), reads=[sk, "epsb"], writes=[sk])
                S.op(DVE, lambda e, ss=ss: e.reciprocal(out=ss[:], in_=ss[:]), reads=[sk], writes=[sk])
                S.op(DVE, lambda e, xt=xt, ss=ss, ob=ob: e.scalar_tensor_tensor(
                    out=ob[:], in0=xt[:], scalar=ss[:, 0:1], in1=gfin[:], op0=ALU.mult, op1=ALU.mult),
                    reads=[xk, sk, "gfin"], writes=[obk])
                r0 = tok0 + sub * 128
                S.dma(SP, lambda e, ob=ob, r0=r0: e.dma_start(out=self.out[r0:r0 + 128, :], in_=ob[:]),
                      reads=[obk])


def host_inputs(inp, nseq, core):
    f = np.float32
    x = np.ascontiguousarray(inp["x"][core * nseq:(core + 1) * nseq].reshape(nseq * S_LEN, D))

    def gcol(g):
        return np.ascontiguousarray(np.broadcast_to(g.reshape(8, 128).T[:, :, None], (128, 8, 128)).reshape(128, 1024))
    m = {
        "x": x,
        "w_in": np.ascontiguousarray(inp["w_in"][0]),
        "w_up": np.ascontiguousarray(inp["w_up"][0]),
        "w_down": np.ascontiguousarray(inp["w_down"][0]),
        "g_ffn": gcol(inp["norm_ffn"][0]),
        "g_mix": gcol(inp["norm_mix"][0]),
        "g_fin": np.ascontiguousarray(np.broadcast_to(inp["norm_final"][None, :], (128, D))),
        "convw": np.ascontiguousarray(inp["conv_w"][0].reshape(3, NFC, 128).transpose(2, 1, 0).reshape(128, NFC * 3)),
        "convb": np.ascontiguousarray(inp["conv_b"][0].reshape(NFC, 128).T),
    }
    m.update({
        "w_ret_o": np.ascontiguousarray(inp["w_ret_o"][0]),
        "w_nsa_o": np.ascontiguousarray(inp["w_nsa_o"][0]),
        "w_out": np.ascontiguousarray(inp["w_out"][0]),
        "gng": gcol(inp["ret_gn_g"][0]),
        "w1k": np.ascontiguousarray(inp["cmp_w1_k"][0]),
        "w1v": np.ascontiguousarray(inp["cmp_w1_v"][0]),
        "w2k": np.ascontiguousarray(inp["cmp_w2_k"][0]),
        "w2v": np.ascontiguousarray(inp["cmp_w2_v"][0]),
        "b1k": np.ascontiguousarray(inp["cmp_b1_k"][0].reshape(2, 128).T),
        "b1v": np.ascontiguousarray(inp["cmp_b1_v"][0].reshape(2, 128).T),
        "posTk": np.ascontiguousarray(np.repeat(inp["cmp_pos_k"][0].T[:, :, None], 2, axis=2).reshape(64, 64)),
        "posTv": np.ascontiguousarray(np.repeat(inp["cmp_pos_v"][0].T[:, :, None], 2, axis=2).reshape(64, 64)),
    })
    m.update(make_consts())
    return m


_CACHE = {}


def kernel(**inputs):
    inp = {k: np.asarray(v) for k, v in inputs.items()}
    ncores, nseq = 8, 2
    if "prog" not in _CACHE:
        p = Prog(nseq=nseq)
        p.build()
        _CACHE["prog"] = p
    p = _CACHE["prog"]
    in_maps = []
    for c in range(ncores):
        m = host_inputs(inp, nseq, c)
        in_maps.append({k: m[k] for k in p.in_names})
    res = run_bass_kernel_spmd(p.nc, in_maps, core_ids=list(range(ncores)))
    out = np.concatenate([r["out"] for r in res.results], axis=0)
    return out.reshape(16, S_LEN, D).astype(np.float32)
```

```python
import contextlib
import numpy as np
STAGE = 9
import ml_dtypes
import concourse.bass as bass
import concourse.mybir as mybir
from concourse.bass_utils import run_bass_kernel_spmd

F32 = mybir.dt.float32
BF16 = mybir.dt.bfloat16
ALU = mybir.AluOpType
AF = mybir.ActivationFunctionType

PE, ACT, DVE, POOL, SP = "tensor", "scalar", "vector", "gpsimd", "sync"
ENGS = (PE, ACT, DVE, POOL, SP)
NDMASEM = 8

S_LEN = 2048
D = 1024
DFF = 2816
NFC = DFF // 128
EPS = 1e-6
N_IN = 6424
O_RQ, O_RK, O_RV, O_RG, O_NQ, O_KCR, O_VCR, O_KS, O_VS, O_KW, O_VW, O_NG, O_GA, O_GB = (
    0, 512, 1024, 2048, 3072, 3584, 3712, 3840, 3968, 4096, 4224, 4352, 4376, 5400)


class Op:
    __slots__ = ("eng", "fn", "reads", "writes", "is_dma", "waits", "inc", "ticket", "dsem", "dval")

    def __init__(self, eng, fn, reads, writes, is_dma):
        self.eng = eng
        self.fn = fn
        self.reads = reads
        self.writes = writes
        self.is_dma = is_dma
        self.waits = []
        self.inc = False
        self.ticket = None
        self.dsem = None
        self.dval = None


class Sched:
    def __init__(self, nc):
        self.nc = nc
        self.ops = []
        self.last_writer = {}
        self.readers = {}
        self.dma_count = {e: 0 for e in ENGS}
        self.dma_hist = {e: [] for e in ENGS}
        self.last_op = {e: None for e in ENGS}

    def op(self, eng, fn, reads=(), writes=()):
        o = Op(eng, fn, tuple(reads), tuple(writes), False)
        self._add(o)
        return o

    def dma(self, eng, fn, reads=(), writes=()):
        o = Op(eng, fn, tuple(reads), tuple(writes), True)
        i = self.dma_count[eng]
        self.dma_count[eng] += 1
        o.dsem = (eng, i % NDMASEM)
        o.dval = 16 * (i // NDMASEM + 1)
        self.dma_hist[eng].append(o)
        if i >= NDMASEM:
            o.waits.append(self.dma_hist[eng][i - NDMASEM])
        self._add(o)
        return o

    def barrier(self):
        tails = []
        for e in ENGS:
            if self.last_op[e] is not None and not self.last_op[e].is_dma:
                tails.append(self.last_op[e])
            tails.extend(self.dma_hist[e][-NDMASEM:])
        for e in ENGS:
            o = Op(e, None, (), (), False)
            o.waits = [t for t in tails]
            self.ops.append(o)
            self.last_op[e] = o
        self.last_writer = {}
        self.readers = {}

    def _add(self, o):
        self.ops.append(o)
        deps = o.waits
        for k in o.reads:
            w = self.last_writer.get(k)
            if w is not None:
                deps.append(w)
            if k.startswith("ps"):
                for r in self.readers.get(k, ()):
                    if r.eng != o.eng:
                        deps.append(r)
        for k in o.writes:
            w = self.last_writer.get(k)
            if w is not None and (w.is_dma or o.is_dma or w.eng != o.eng):
                deps.append(w)
            for r in self.readers.get(k, ()):
                if r.is_dma or o.is_dma or r.eng != o.eng:
                    deps.append(r)
        for k in o.reads:
            self.readers.setdefault(k, []).append(o)
        for k in o.writes:
            self.last_writer[k] = o
            self.readers[k] = []
        if not o.is_dma:
            self.last_op[o.eng] = o

    def run(self):
        nc = self.nc
        for o in self.ops:
            for d in o.waits:
                if not d.is_dma:
                    d.inc = True
        cnt = {e: 0 for e in ENGS}
        for o in self.ops:
            if o.fn is None:
                o.inc = False
            if not o.is_dma and o.inc:
                cnt[o.eng] += 1
                o.ticket = cnt[o.eng]
        seen = {e: {} for e in ENGS}
        plans = {e: [] for e in ENGS}
        for o in self.ops:
            e = o.eng
            wl = {}
            for d in o.waits:
                if d.is_dma:
                    key = ("d",) + d.dsem
                    val = d.dval
                else:
                    if d.ticket is None:
                        continue
                    key = ("c", d.eng)
                    val = d.ticket
                if seen[e].get(key, 0) >= val:
                    continue
                if wl.get(key, 0) < val:
                    wl[key] = val
            for key, val in wl.items():
                seen[e][key] = val
            plans[e].append((o, list(wl.items())))
        with contextlib.ExitStack() as st:
            sems = {}
            for e in ENGS:
                sems[("c", e)] = st.enter_context(nc.semaphore(f"c_{e}"))
                for j in range(min(NDMASEM, self.dma_count[e])):
                    sems[("d", e, j)] = st.enter_context(nc.semaphore(f"d_{e}_{j}"))
            block = st.enter_context(nc.Block())

            def mk(e):
                plan = plans[e]

                def body(eng):
                    for o, wl in plan:
                        for key, val in wl:
                            eng.wait_ge(sems[key], val)
                        if o.fn is None:
                            continue
                        ins = o.fn(eng)
                        if o.is_dma:
                            ins.then_inc(sems[("d",) + o.dsem], 16)
                        elif o.inc:
                            ins.then_inc(sems[("c", e)], 1)
                    n = self.dma_count[e]
                    for j in range(min(NDMASEM, n)):
                        eng.wait_ge(sems[("d", e, j)], 16 * ((n - 1 - j) // NDMASEM + 1))
                return body

            block.tensor(mk(PE))
            block.scalar(mk(ACT))
            block.vector(mk(DVE))
            block.gpsimd(mk(POOL))
            block.sync(mk(SP))


class Ring:
    def __init__(self, items):
        self.items = items
        self.i = 0

    def get(self):
        it = self.items[self.i % len(self.items)]
        self.i += 1
        return it


def make_consts():
    c = {}
    bf = ml_dtypes.bfloat16
    f = np.float32
    c["ident"] = np.eye(128, dtype=f).astype(bf)
    lg = np.log1p(-np.exp2(-5.0 - np.arange(4, dtype=np.float64)))
    pos = np.arange(128, dtype=np.float64)
    diff = pos[None, :] - pos[:, None]
    dmt = np.where(diff >= 0, np.exp(lg[:, None, None] * np.maximum(diff, 0.0)), 0.0)
    c["c_dmt"] = np.ascontiguousarray(dmt.transpose(1, 0, 2).reshape(128, 512)).astype(f)
    xi = np.exp(lg[:, None] * (pos + 1.0))
    xi2 = np.tile(xi, (1, 2))
    c["c_xi"] = np.ascontiguousarray(np.broadcast_to(xi2.reshape(1, 4 * 256), (128, 4 * 256))).astype(f)
    zeta = np.exp(lg[:, None] * (127 - pos)) * (128 ** -0.5)
    c["c_zeta"] = np.ascontiguousarray(np.repeat(zeta.T[:, :, None], 128, axis=2).reshape(128, 512)).astype(f)
    slopes = np.exp2(-(np.arange(8, dtype=np.float64) + 1.0)).reshape(2, 4)
    t = np.arange(S_LEN)
    qaug = np.zeros((2, 4, 4, S_LEN))
    for g in range(2):
        for r in range(4):
            sl = slopes[g, r]
            qaug[g, 0, r] = 8 * sl * 128
            qaug[g, 1, r] = 8 * sl
            qaug[g, 2, r] = -8 * sl * 128 * (t // 128)
            qaug[g, 3, r] = -8 * sl * (t % 128)
    c["c_qaug"] = qaug.reshape(2 * 4, 4 * S_LEN).astype(bf)
    kaug = np.stack([t // 128, t % 128, np.ones(S_LEN), np.ones(S_LEN)]).astype(np.float64)
    c["c_kaug"] = kaug.astype(bf)
    pc = 16 * np.arange(127) + 31
    caug = np.stack([pc // 128, pc % 128, np.ones(127), np.ones(127)]).astype(np.float64)
    c["c_caug"] = caug.astype(bf)
    c["c_ehot"] = (np.arange(32)[:, None] == (t[None, :] // 64)).astype(f).astype(bf)
    tt = (128 * np.arange(16)[:, None] + np.arange(128)[None, :])
    cm = (pc[:, None, None] <= tt[None, :, :]).astype(f)
    c["c_cmask"] = np.ascontiguousarray(cm.reshape(127, 16 * 128)).astype(bf)
    kb = np.arange(128)
    c["c_caus"] = np.ascontiguousarray(np.tile((kb[:, None] <= kb[None, :]).astype(f), (1, 4))).astype(bf)
    c["c_far"] = np.ascontiguousarray(np.tile((kb[:, None] > kb[None, :]).astype(f), (1, 4))).astype(bf)
    cur = tt // 64
    jj = np.arange(32)
    force = (jj[None, None, :] == 0) | (jj[None, None, :] == cur[:, :, None]) | (jj[None, None, :] == cur[:, :, None] - 1)
    fut = jj[None, None, :] > cur[:, :, None]
    A = 1.0 - force - fut
    Bm = 1e9 * force - 1e9 * fut
    NF = 1.0 - fut
    c["c_impA"] = np.ascontiguousarray(A.transpose(1, 0, 2).reshape(128, 16 * 32)).astype(f)
    c["c_impB"] = np.ascontiguousarray(Bm.transpose(1, 0, 2).reshape(128, 16 * 32)).astype(f)
    c["c_impNF"] = np.ascontiguousarray(NF.transpose(1, 0, 2).reshape(128, 16 * 32)).astype(f)
    cs = 16 * np.arange(127)
    js = 64 * np.arange(32)
    ovl = ((cs[:, None] < js[None, :] + 64) & (cs[:, None] + 32 > js[None, :])).astype(f)
    c["c_ovl"] = np.concatenate([np.ones((127, 1), f), ovl], axis=1).astype(bf)
    return c


class Prog:
    def __init__(self, nseq=2, passes="RNF", dbg=False):
        self.nseq = nseq
        self.ntok = nseq * S_LEN
        self.passes = passes
        self.dbg = dbg
        nc = self.nc = bass.Bass("TRN2", target_bir_lowering=False)
        self.S = Sched(nc)
        self.in_names = []

    def din(self, name, shape, dt=F32):
        self.in_names.append(name)
        return self.nc.dram_tensor(name, list(shape), dt, kind="ExternalInput").ap()

    def build(self):
        nc, S = self.nc, self.S
        NT = self.ntok
        self.x = self.din("x", [NT, D])
        self.w_in = self.din("w_in", [D, N_IN])
        self.w_up = self.din("w_up", [D, 2 * DFF])
        self.w_down = self.din("w_down", [DFF, D])
        self.g_ffn = self.din("g_ffn", [128, 8 * 128])
        self.g_mix = self.din("g_mix", [128, 8 * 128])
        self.g_fin = self.din("g_fin", [128, D])
        self.convw = self.din("convw", [128, NFC * 3])
        self.convb = self.din("convb", [128, NFC])
        self.ident_d = self.din("ident", [128, 128], BF16)
        self.w_ret_o = self.din("w_ret_o", [D, D])
        self.w_nsa_o = self.din("w_nsa_o", [512, D])
        self.w_out = self.din("w_out", [D, D])
        self.gng = self.din("gng", [128, 8 * 128])
        self.w1 = [self.din("w1k", [2048, 256]), self.din("w1v", [2048, 256])]
        self.w2 = [self.din("w2k", [256, 64]), self.din("w2v", [256, 64])]
        self.b1 = [self.din("b1k", [128, 2]), self.din("b1v", [128, 2])]
        self.posT = [self.din("posTk", [64, 64]), self.din("posTv", [64, 64])]
        self.cd = {}
        for k, v in make_consts().items():
            if k != "ident":
                self.cd[k] = self.din(k, v.shape, BF16 if v.dtype == ml_dtypes.bfloat16 else F32)
        self.out = nc.dram_tensor("out", [NT, D], F32, kind="ExternalOutput").ap()
        self.x1_scr = nc.dram_tensor("x1_scr", [NT, D], F32, kind="Internal").ap()
        self.ma_scr = nc.dram_tensor("ma_scr", [128, 8 * NT], BF16,
                                     kind="ExternalOutput" if self.dbg else "Internal").ap()

        with contextlib.ExitStack() as st:
            self.st = st
            banks = []
            for i in range(7):
                t = st.enter_context(nc.psum_tensor(f"psf{i}", [128, 512], F32))
                banks.append((t, f"psf{i}"))
            self.banks = Ring(banks[0:5])
            self.accb = Ring(banks[5:7])
            self.psb = st.enter_context(nc.psum_tensor("psb", [128, 1024], BF16))
            self.ident = self.sb("ident", [128, 128], BF16)
            self.epsb = self.sb("epsb", [128, 1], F32)
            S.dma(SP, lambda e: e.dma_start(out=self.ident[:], in_=self.ident_d), writes=["ident"])
            S.op(DVE, lambda e: e.memset(self.epsb[:], EPS), writes=["epsb"])
            self.junk = self.sb("junk", [128, D], BF16)
            self.xt = Ring([(self.sb(f"xt{i}", [128, D], F32), f"xt{i}") for i in range(4)])
            self.xn = Ring([(self.sb(f"xn{i}", [128, D], BF16), f"xn{i}") for i in range(3)])
            self.ss = Ring([(self.sb(f"ss{i}", [128, 1], F32), f"ss{i}") for i in range(4)])
            if "R" in self.passes:
                with contextlib.ExitStack() as st2:
                    self.st = st2
                    self.pass_R()
                    S.barrier()
            if "N" in self.passes:
                with contextlib.ExitStack() as st2:
                    self.st = st2
                    self.pass_N_alloc()
                    with contextlib.ExitStack() as st3:
                        self.st = st3
                        self.pass_N1()
                        S.barrier()
                    with contextlib.ExitStack() as st3:
                        self.st = st3
                        self.pass_N2()
                        S.barrier()
            if "F" in self.passes:
                with contextlib.ExitStack() as st2:
                    self.st = st2
                    self.pass_F(self.x1_scr if "N" in self.passes else self.x)
                    S.barrier()
            S.run()
        return nc

    def sb(self, name, shape, dt):
        self._uid = getattr(self, "_uid", 0) + 1
        return self.st.enter_context(self.nc.sbuf_tensor(f"s{self._uid}_{name}", list(shape), dt))

    def load_w(self, dst, dkey, src_rows, c0, ncols, kcs, dcol0=0, rows=128):
        S = self.S
        for kc in range(kcs):
            for cc in range(0, ncols, 2048):
                n = min(2048, ncols - cc)
                S.dma(POOL, lambda e, kc=kc, cc=cc, n=n: e.dma_start(
                    out=dst[0:rows, kc, dcol0 + cc:dcol0 + cc + n],
                    in_=src_rows[kc * rows:(kc + 1) * rows, c0 + cc:c0 + cc + n]), writes=[dkey])

    def load_blk(self, dst, key, src_rows, c0, ncols, kcs, dcol0):
        srcv = src_rows.rearrange("(kc p) n -> p kc n", p=128)
        self.S.dma(POOL, lambda e: e.dma_start(out=dst[:, 0:kcs, dcol0:dcol0 + ncols], in_=srcv[:, 0:kcs, c0:c0 + ncols]), writes=[key])

    def cload(self, name, shape, dt, src, eng=SP):
        t = self.sb(name, shape, dt)
        self.S.dma(eng, lambda e: e.dma_start(out=t[:], in_=src), writes=[name])
        return t

    def mm8(self, out, lhs_fn, rhs_fn, n=8):
        def f(e):
            ins = None
            for kc in range(n):
                ins = e.matmul(out, lhsT=lhs_fn(kc), rhs=rhs_fn(kc), start=(kc == 0), stop=(kc == n - 1))
            return ins
        return f

    def pass_R(self):
        nc, S = self.nc, self.S
        TT = 256
        NS = TT // 128
        lg = np.log1p(-np.exp2(-5.0 - np.arange(4, dtype=np.float64)))
        decay = [float(np.exp(lg[h] * 128)) for h in range(4)]
        allb = self.banks.items + self.accb.items
        gen = Ring(allb[0:3])
        (pin, pink), (po0, po0k), (po1, po1k), (pk1, pk1k) = allb[3:7]
        pk0, pk0k = pin, pink
        pos_ = [(po0, po0k), (po1, po1k)]
        pks_ = [(pk0, pk0k), (pk1, pk1k)]
        wr = self.sb("wr", [128, 8, 3072], BF16)
        wga = self.sb("wga", [128, 8, D], BF16)
        wro = self.sb("wro", [128, 8, D], BF16)
        gcb = self.cload("gcb", [128, 1024], F32, self.g_mix)
        gngb = self.cload("gngb", [128, 1024], F32, self.gng)
        dmt = self.cload("dmt", [128, 512], F32, self.cd["c_dmt"])
        xi = self.cload("xi", [128, 4 * 256], F32, self.cd["c_xi"])
        zeta = self.cload("zeta", [128, 512], F32, self.cd["c_zeta"])
        self.load_blk(wr, "wr_q", self.w_in, O_RQ, 512, 8, 0)
        self.load_blk(wr, "wr_k", self.w_in, O_RK, 512, 8, 512)
        self.load_blk(wr, "wr_v", self.w_in, O_RV, 1024, 8, 1024)
        self.load_blk(wr, "wr_g", self.w_in, O_RG, 1024, 8, 2048)
        self.load_blk(wga, "wga", self.w_in, O_GA, D, 8, 0)
        self.load_blk(wro, "wro", self.w_ret_o, 0, D, 8, 0)
        hTs = [self.sb(f"hTR{b}", [128, 8, TT], BF16) for b in range(2)]
        qTs = [self.sb(f"qT{b}", [128, 4, TT], BF16) for b in range(2)]
        qxTs = [self.sb(f"qxT{b}", [128, 4, TT], BF16) for b in range(2)]
        kTs = [self.sb(f"kT{b}", [128, 4, TT], BF16) for b in range(2)]
        kzs = [self.sb(f"kz{b}", [128, NS, 512], BF16) for b in range(2)]
        vs = [self.sb(f"v{b}", [128, NS, D], BF16) for b in range(2)]
        sgs = [self.sb(f"sg{b}", [128, NS, D], BF16) for b in range(2)]
        sga = self.sb("sga", [128, 8, TT], BF16)
        roT = self.sb("roT", [128, 8, TT], BF16)
        maT = self.sb("maT", [128, 8, TT], BF16)
        R = self.sb("R", [128, 4, 256], F32)
        Rb = self.sb("Rb", [128, 4, 256], BF16)
        inT = self.sb("inT4", [128, 512], BF16)
        on = self.sb("on4", [128, D], F32)
        stt = self.sb("stt", [128, 4, 6], F32)
        mv = self.sb("mv", [128, 4, 4], F32)
        NTL = self.ntok // TT

        def proj(t):
            b = t % 2
            hT, qT, qxT, kT, kz, v, sg = hTs[b], qTs[b], qxTs[b], kTs[b], kzs[b], vs[b], sgs[b]
            hk = f"hTR{b}"
            ops = []
            for h in range(4):
                def fq(h=h):
                    pq, pqk = gen.get()
                    S.op(PE, self.mm8(pq[:, 0:TT], lambda kc: wr[:, kc, O_RQ + h * 128:O_RQ + (h + 1) * 128], lambda kc: hT[:, kc, :]),
                         reads=["wr_q", hk], writes=[pqk])
                    S.op(ACT, lambda e: e.copy(out=qT[:, h, :], in_=pq[:, 0:TT]), reads=[pqk], writes=[f"qT{b}"])
                    S.op(DVE, lambda e: e.tensor_tensor(out=qxT[:, h, :], in0=pq[:, 0:TT], in1=xi[:, h * 256:(h + 1) * 256], op=ALU.mult),
                         reads=[pqk, "xi"], writes=[f"qxT{b}"])

                def fk(h=h):
                    pk, pkk = gen.get()
                    S.op(PE, self.mm8(pk[:, 0:TT], lambda kc: wr[:, kc, 512 + h * 128:512 + (h + 1) * 128], lambda kc: hT[:, kc, :]),
                         reads=["wr_k", hk], writes=[pkk])
                    S.op(ACT, lambda e: e.mul(out=kT[:, h, :], in_=pk[:, 0:TT], mul=128 ** -0.5), reads=[pkk], writes=[f"kT{b}"])
                ops += [fq, fk]
            for c in range(NS):
                cs = slice(c * 128, (c + 1) * 128)

                def fz(c=c, cs=cs):
                    pk, pkk = gen.get()
                    S.op(PE, self.mm8(pk[:, :], lambda kc: hT[:, kc, cs], lambda kc: wr[:, kc, 512:1024]), reads=["wr_k", hk], writes=[pkk])
                    S.op(DVE, lambda e: e.tensor_tensor(out=kz[:, c, :], in0=pk[:, :], in1=zeta[:], op=ALU.mult),
                         reads=[pkk, "zeta"], writes=[f"kz{b}"])
                ops.append(fz)
                for half in range(2):
                    def fv(c=c, cs=cs, half=half):
                        pv, pvk = gen.get()
                        S.op(PE, self.mm8(pv[:, :], lambda kc: hT[:, kc, cs], lambda kc: wr[:, kc, 1024 + half * 512:1024 + (half + 1) * 512]),
                             reads=["wr_v", hk], writes=[pvk])
                        S.op(ACT, lambda e: e.copy(out=v[:, c, half * 512:(half + 1) * 512], in_=pv[:, :]), reads=[pvk], writes=[f"v{b}"])

                    def fg(c=c, cs=cs, half=half):
                        pg, pgk = gen.get()
                        S.op(PE, self.mm8(pg[:, :], lambda kc: hT[:, kc, cs], lambda kc: wr[:, kc, 2048 + half * 512:2048 + (half + 1) * 512]),
                             reads=["wr_g", hk], writes=[pgk])
                        S.op(ACT, lambda e: e.activation(out=sg[:, c, half * 512:(half + 1) * 512], in_=pg[:, :], func=AF.Silu),
                             reads=[pgk], writes=[f"sg{b}"])
                    ops += [fv, fg]
            return ops

        def gates(t):
            b = t % 2
            hT = hTs[b]
            ops = []
            for cc in range(8):
                def f(cc=cc):
                    pga, pgak = gen.get()
                    S.op(PE, self.mm8(pga[:, 0:TT], lambda kc: wga[:, kc, cc * 128:(cc + 1) * 128], lambda kc: hT[:, kc, :]),
                         reads=["wga", f"hTR{b}"], writes=[pgak])
                    S.op(ACT, lambda e: e.activation(out=sga[:, cc, :], in_=pga[:, 0:TT], func=AF.Sigmoid), reads=[pgak], writes=["sga"])
                ops.append(f)
            return ops

        def chunk_stages(t):
            b = t % 2
            qT, qxT, kT, kz, v, sg = qTs[b], qxTs[b], kTs[b], kzs[b], vs[b], sgs[b]
            tok0 = t * TT
            stages = []
            for c in range(NS):
                cs = slice(c * 128, (c + 1) * 128)
                first = ((tok0 + c * 128) % S_LEN) == 0

                def st0(cs=cs):
                    def mmi(e):
                        ins = None
                        for h in range(4):
                            ins = e.matmul(pin[:, h * 128:(h + 1) * 128], lhsT=kT[:, h, cs], rhs=qT[:, h, cs], start=True, stop=True,
                                           skip_group_check=True)
                        return ins
                    S.op(PE, mmi, reads=[f"kT{b}", f"qT{b}"], writes=[pink])
                    S.op(DVE, lambda e: e.tensor_tensor(out=inT[:], in0=pin[:, :], in1=dmt[:], op=ALU.mult), reads=[pink, "dmt"], writes=["inT4"])

                def st1(c=c, cs=cs, first=first):
                    for hp in range(2):
                        po, pok = pos_[hp]

                        def mmo(e, po=po, hp=hp):
                            ins = None
                            for hh in range(2):
                                h = hp * 2 + hh
                                ins = e.matmul(po[:, hh * 256:(hh + 1) * 256], lhsT=inT[:, h * 128:(h + 1) * 128], rhs=v[:, c, h * 256:(h + 1) * 256],
                                               start=True, stop=first, skip_group_check=True)
                                if not first:
                                    ins = e.matmul(po[:, hh * 256:(hh + 1) * 256], lhsT=qxT[:, h, cs], rhs=Rb[:, h, :], start=False, stop=True,
                                                   skip_group_check=True)
                            return ins
                        S.op(PE, mmo, reads=["inT4", f"v{b}", f"qxT{b}", "Rb"], writes=[pok])
                    for hp in range(2):
                        pk, pkk = pks_[hp]

                        def mmk(e, pk=pk, hp=hp):
                            ins = None
                            for hh in range(2):
                                h = hp * 2 + hh
                                ins = e.matmul(pk[:, hh * 256:(hh + 1) * 256], lhsT=kz[:, c, h * 128:(h + 1) * 128], rhs=v[:, c, h * 256:(h + 1) * 256],
                                               start=True, stop=True, skip_group_check=True)
                            return ins
                        S.op(PE, mmk, reads=[f"kz{b}", f"v{b}"], writes=[pkk])

                def st2(first=first):
                    for h in range(4):
                        pk, pkk = pks_[h // 2]
                        src = pk[:, (h % 2) * 256:(h % 2 + 1) * 256]
                        if first:
                            S.op(DVE, lambda e, h=h, src=src: e.tensor_copy(out=R[:, h, :], in_=src), reads=[pkk], writes=["R"])
                        else:
                            S.op(DVE, lambda e, h=h, src=src: e.scalar_tensor_tensor(out=R[:, h, :], in0=R[:, h, :], scalar=decay[h], in1=src,
                                                                                     op0=ALU.mult, op1=ALU.add), reads=[pkk, "R"], writes=["R"])
                    S.op(ACT, lambda e: e.copy(out=Rb[:].rearrange("p h e -> p (h e)"), in_=R[:].rearrange("p h e -> p (h e)")),
                         reads=["R"], writes=["Rb"])
                    for h in range(4):
                        po, pok = pos_[h // 2]
                        S.op(DVE, lambda e, h=h, po=po: e.bn_stats(out=stt[:, h, :], in_=po[:, (h % 2) * 256:(h % 2 + 1) * 256]),
                             reads=[pok], writes=["stt"])
                    for h in range(4):
                        S.op(DVE, lambda e, h=h: e.bn_aggr(out=mv[:, h, 0:2], in_=stt[:, h, :]), reads=["stt"], writes=["mv"])
                    S.op(ACT, lambda e: e.activation(out=mv[:, :, 2], in_=mv[:, :, 1], func=AF.Sqrt, bias=self.epsb[:]),
                         reads=["mv", "epsb"], writes=["mv"])
                    S.op(DVE, lambda e: e.reciprocal(out=mv[:, :, 2], in_=mv[:, :, 2]), reads=["mv"], writes=["mv"])
                    S.op(DVE, lambda e: e.scalar_tensor_tensor(out=mv[:, :, 3], in0=mv[:, :, 0], scalar=-1.0, in1=mv[:, :, 2],
                                                               op0=ALU.mult, op1=ALU.mult), reads=["mv"], writes=["mv"])

                def st3(c=c):
                    for h in range(4):
                        po, pok = pos_[h // 2]
                        S.op(ACT, lambda e, h=h, po=po: e.activation(out=on[:, h * 256:(h + 1) * 256], in_=po[:, (h % 2) * 256:(h % 2 + 1) * 256],
                                                                     func=AF.Identity, scale=mv[:, h, 2:3], bias=mv[:, h, 3:4]),
                             reads=[pok, "mv"], writes=["on4"])
                    S.op(DVE, lambda e: e.tensor_tensor(out=sg[:, c, :], in0=on[:], in1=sg[:, c, :], op=ALU.mult),
                         reads=["on4", f"sg{b}"], writes=[f"sg{b}"])
                stages += [st0, st1, st2, st3]
            return stages

        def tail(t):
            b = t % 2
            sg = sgs[b]
            tok0 = t * TT
            for c in range(NS):
                cs = slice(c * 128, (c + 1) * 128)

                def tr(e, c=c):
                    ins = None
                    for kc in range(8):
                        ins = e.transpose(out=self.psb[:, kc * 128:(kc + 1) * 128], in_=sg[:, c, kc * 128:(kc + 1) * 128],
                                          identity=self.ident[:])
                    return ins
                S.op(PE, tr, reads=[f"sg{b}", "ident"], writes=["psb"])
                S.op(DVE, lambda e, cs=cs: e.tensor_tensor(out=roT[:, :, cs], in0=self.psb[:].rearrange("p (k j) -> p k j", k=8),
                                                           in1=gngb[:].rearrange("p (k j) -> p k j", k=8), op=ALU.mult),
                     reads=["psb", "gngb"], writes=["roT"])
            for cc in range(8):
                ccs = slice(cc * 128, (cc + 1) * 128)
                pya, pyak = gen.get()
                S.op(PE, self.mm8(pya[:, 0:TT], lambda kc, ccs=ccs: wro[:, kc, ccs], lambda kc: roT[:, kc, :]),
                     reads=["wro", "roT"], writes=[pyak])
                S.op(DVE, lambda e, pya=pya, cc=cc: e.tensor_tensor(out=maT[:, cc, :], in0=sga[:, cc, :], in1=pya[:, 0:TT], op=ALU.mult),
                     reads=[pyak, "sga"], writes=["maT"])
            S.dma(SP, lambda e: e.dma_start(out=self.ma_scr.rearrange("p (c n) -> p c n", c=8)[:, :, tok0:tok0 + TT], in_=maT[:]),
                  reads=["maT"])

        for sub in range(NS):
            self.rmsnorm_hT(self.x, sub * 128, gcb, hTs[0], "hTR0", sub * 128)
        for f in proj(0):
            f()
        for t in range(NTL):
            stages = chunk_stages(t)
            fill = gates(t)
            if t + 1 < NTL:
                nb_ = (t + 1) % 2
                pend = {}
                for sub in range(NS):
                    def fa(sub=sub, t=t):
                        pend[sub] = self.rms_a(self.x, (t + 1) * TT + sub * 128)
                    fill.append(fa)
                for sub in range(NS):
                    def fb(sub=sub, nb_=nb_):
                        self.rms_b(pend[sub], gcb, hTs[nb_], f"hTR{nb_}", sub * 128)
                    fill.append(fb)
                fill += proj(t + 1)
            fi = 0
            for k, stg_ in enumerate(stages):
                stg_()
                tgt = ((k + 1) * len(fill) + len(stages) - 1) // len(stages)
                while fi < min(tgt, len(fill)):
                    fill[fi]()
                    fi += 1
            while fi < len(fill):
                fill[fi]()
                fi += 1
            tail(t)

    def rms_a(self, src, r0):
        S = self.S
        xt, xk = self.xt.get()
        xn, nk = self.xn.get()
        ss, sk = self.ss.get()
        S.dma(SP, lambda e: e.dma_start(out=xt[:], in_=src[r0:r0 + 128, :]), writes=[xk])
        S.op(ACT, lambda e: e.activation(out=self.junk[:], in_=xt[:], func=AF.Square, accum_out=ss[:]),
             reads=[xk], writes=[sk])
        S.op(ACT, lambda e: e.activation(out=ss[:], in_=ss[:], func=AF.Sqrt, scale=1.0 / D, bias=self.epsb[:]),
             reads=[sk, "epsb"], writes=[sk])
        S.op(DVE, lambda e: e.reciprocal(out=ss[:], in_=ss[:]), reads=[sk], writes=[sk])
        S.op(DVE, lambda e: e.tensor_scalar(out=xn[:], in0=xt[:], scalar1=ss[:, 0:1], scalar2=None, op0=ALU.mult),
             reads=[xk, sk], writes=[nk])
        return (xt, xk, xn, nk)

    def rms_b(self, state, gcb, hT, hkey, col0):
        S = self.S
        xt, xk, xn, nk = state

        def tr(e):
            ins = None
            for kc in range(8):
                ins = e.transpose(out=self.psb[:, kc * 128:(kc + 1) * 128], in_=xn[:, kc * 128:(kc + 1) * 128],
                                  identity=self.ident[:])
            return ins
        S.op(PE, tr, reads=[nk, "ident"], writes=["psb"])
        S.op(DVE, lambda e: e.tensor_tensor(
            out=hT[:, :, col0:col0 + 128], in0=self.psb[:].rearrange("p (k j) -> p k j", k=8),
            in1=gcb[:].rearrange("p (k j) -> p k j", k=8), op=ALU.mult),
            reads=["psb", "gcb"], writes=[hkey])
        return xt, xk

    def rmsnorm_hT(self, src, r0, gcb, hT, hkey, col0, keep=None):
        return self.rms_b(self.rms_a(src, r0), gcb, hT, hkey, col0)

    def pass_N_alloc(self):
        S = self.S
        ns = self.nseq
        self.KST = [[self.sb(f"KST{s}{g}", [100, S_LEN], BF16) for g in range(2)] for s in range(ns)]
        self.KWT = [[self.sb(f"KWT{s}{g}", [100, S_LEN], BF16) for g in range(2)] for s in range(ns)]
        self.VS1 = [self.sb(f"VS1{s}", [128, 16, 2, 65], BF16) for s in range(ns)]
        self.VW1 = [self.sb(f"VW1{s}", [128, 16, 2, 65], BF16) for s in range(ns)]
        self.KCM = [[self.sb(f"KCM{s}{g}", [100, 128], BF16) for g in range(2)] for s in range(ns)]
        self.VCO = [[self.sb(f"VCO{s}{g}", [128, 97], BF16) for g in range(2)] for s in range(ns)]
        for s in range(ns):
            S.op(DVE, lambda e, s=s: e.memset(self.VS1[s][:], 1.0), writes=[f"VS1{s}"])
            S.op(DVE, lambda e, s=s: e.memset(self.VW1[s][:], 1.0), writes=[f"VW1{s}"])
            for g in range(2):
                S.dma(SP, lambda e, s=s, g=g: e.dma_start(out=self.KST[s][g][64:96, :], in_=self.cd["c_ehot"]), writes=[f"KST{s}{g}"])
                S.dma(SP, lambda e, s=s, g=g: e.dma_start(out=self.KST[s][g][96:100, :], in_=self.cd["c_kaug"]), writes=[f"KST{s}{g}"])
                S.op(DVE, lambda e, s=s, g=g: e.memset(self.KWT[s][g][64:96, :], 0.0), writes=[f"KWT{s}{g}"])
                S.dma(SP, lambda e, s=s, g=g: e.dma_start(out=self.KWT[s][g][96:100, :], in_=self.cd["c_kaug"]), writes=[f"KWT{s}{g}"])
                S.op(DVE, lambda e, s=s, g=g: e.memset(self.KCM[s][g][64:96, :], 0.0), writes=[f"KCM{s}{g}"])
                S.dma(SP, lambda e, s=s, g=g: e.dma_start(out=self.KCM[s][g][96:100, 0:127], in_=self.cd["c_caug"]), writes=[f"KCM{s}{g}"])
                S.dma(SP, lambda e, s=s, g=g: e.dma_start(out=self.VCO[s][g][0:127, 64:97], in_=self.cd["c_ovl"]), writes=[f"VCO{s}{g}"])

    def pass_N1(self):
        nc, S = self.nc, self.S
        TT = 256
        saved_banks = self.banks
        self.banks = Ring(self.banks.items + self.accb.items)
        wkv = self.sb("wkv", [128, 8, 768], BF16)
        gcb = self.cload("gcb", [128, 1024], F32, self.g_mix)
        self.load_w(wkv, "wkv", self.w_in, O_KCR, 768, 8)
        w1 = [self.sb(f"w1_{k}", [64, 32, 256], BF16) for k in range(2)]
        w2 = [self.sb(f"w2_{k}", [128, 2, 64], BF16) for k in range(2)]
        cb1 = [self.sb(f"cb1_{k}", [128, 2], F32) for k in range(2)]
        cb1_args = []
        for k in range(2):
            src = self.w1[k].rearrange("(p d) n -> d p n", d=64)
            for p0 in range(0, 32, 8):
                S.dma(POOL, lambda e, k=k, src=src, p0=p0: e.dma_start(out=w1[k][:, p0:p0 + 8, :], in_=src[:, p0:p0 + 8, :]),
                      writes=[f"w1_{k}"])
            self.load_w(w2[k], f"w2_{k}", self.w2[k], 0, 64, 2)
            b1 = self.cload(f"b1_{k}", [128, 2], F32, self.b1[k])
            pf = self.cload(f"posf_{k}", [64, 64], F32, self.posT[k])
            pb16 = self.sb(f"posb_{k}", [64, 64], BF16)
            S.op(DVE, lambda e, pf=pf, pb16=pb16: e.tensor_copy(out=pb16[:], in_=pf[:]), reads=[f"posf_{k}"], writes=[f"posb_{k}"])
            cb1_args.append((k, b1, pb16))

        def compute_cb1(k, b1, pb16):
            for nch in range(2):
                pb, pbk = self.banks.get()

                def mmc(e, pb=pb, k=k, nch=nch, pb16=pb16):
                    ins = None
                    for p in range(32):
                        ins = e.matmul(pb[:, 0:2], lhsT=w1[k][0:64, p, nch * 128:(nch + 1) * 128], rhs=pb16[0:64, 2 * p:2 * p + 2],
                                       start=(p == 0), stop=(p == 31))
                    return ins
                S.op(PE, mmc, reads=[f"w1_{k}", f"posb_{k}"], writes=[pbk])
                S.op(DVE, lambda e, pb=pb, k=k, nch=nch, b1=b1: e.tensor_tensor(
                    out=cb1[k][:, nch:nch + 1], in0=pb[:, 0:1], in1=b1[:, nch:nch + 1], op=ALU.add),
                    reads=[pbk, f"b1_{k}"], writes=[f"cb1_{k}"])
        CRT = [[self.sb(f"CRT{k}{g}", [64, S_LEN], BF16) for g in range(2)] for k in range(2)]
        hTs = [self.sb(f"hTN1_{b}", [128, 8, TT], BF16) for b in range(2)]
        hidT = self.sb("hidT", [128, 2, 128], BF16)
        NSB = TT // 128
        ntl = self.nseq * (S_LEN // TT)
        for sub in range(NSB):
            self.rmsnorm_hT(self.x, sub * 128, gcb, hTs[0], "hTN1_0", sub * 128)
        for s in range(self.nseq):
            for t in range(S_LEN // TT):
                tok0 = s * S_LEN + t * TT
                pos0 = t * TT
                tix = s * (S_LEN // TT) + t
                hT = hTs[tix % 2]
                hk = f"hTN1_{tix % 2}"
                pend = []
                if tix + 1 < ntl:
                    for sub in range(NSB):
                        pend.append(self.rms_a(self.x, tok0 + TT + sub * 128))
                dests = [(0, CRT[0], "CRT0"), (128, CRT[1], "CRT1"), (256, self.KST[s], f"KST{s}"), (512, self.KWT[s], f"KWT{s}")]
                for off, dst, dk in dests:
                    for g in range(2):
                        pb, pbk = self.banks.get()
                        S.op(PE, self.mm8(pb[0:64, 0:TT], lambda kc, off=off, g=g: wkv[:, kc, off + g * 64:off + (g + 1) * 64],
                                          lambda kc, hT=hT: hT[:, kc, :]), reads=["wkv", hk], writes=[pbk])
                        S.op(ACT, lambda e, pb=pb, dst=dst, g=g, pos0=pos0: e.copy(out=dst[g][0:64, pos0:pos0 + TT], in_=pb[0:64, 0:TT]),
                             reads=[pbk], writes=[f"{dk}{g}"])
                for sub in range(TT // 128):
                    cs = slice(sub * 128, (sub + 1) * 128)
                    kt = pos0 // 128 + sub
                    pb, pbk = self.banks.get()
                    S.op(PE, self.mm8(pb[:, 0:128], lambda kc, cs=cs, hT=hT: hT[:, kc, cs], lambda kc: wkv[:, kc, 384:512]),
                         reads=["wkv", hk], writes=[pbk])
                    S.op(PE, self.mm8(pb[:, 128:256], lambda kc, cs=cs, hT=hT: hT[:, kc, cs], lambda kc: wkv[:, kc, 640:768]),
                         reads=["wkv", hk], writes=[pbk])
                    S.op(ACT, lambda e, pb=pb, s=s, kt=kt: e.copy(out=self.VS1[s][:, kt, :, 0:64],
                                                                  in_=pb[:, 0:128].rearrange("p (g d) -> p g d", g=2)),
                         reads=[pbk], writes=[f"VS1{s}"])
                    S.op(ACT, lambda e, pb=pb, s=s, kt=kt: e.copy(out=self.VW1[s][:, kt, :, 0:64],
                                                                  in_=pb[:, 128:256].rearrange("p (g d) -> p g d", g=2)),
                         reads=[pbk], writes=[f"VW1{s}"])
                for sub, st_ in enumerate(pend):
                    nb_ = (tix + 1) % 2
                    self.rms_b(st_, gcb, hTs[nb_], f"hTN1_{nb_}", sub * 128)
            if s == 0:
                for args_ in cb1_args:
                    compute_cb1(*args_)
            for k in range(2):
                for g in range(2):
                    for nch in range(2):
                        pb, pbk = self.banks.get()

                        def mmh(e, pb=pb, k=k, g=g, nch=nch):
                            ins = None
                            for p in range(32):
                                ins = e.matmul(pb[:, 0:127], lhsT=w1[k][0:64, p, nch * 128:(nch + 1) * 128],
                                               rhs=CRT[k][g][0:64, p:p + 2017:16], start=(p == 0), stop=(p == 31))
                            return ins
                        S.op(PE, mmh, reads=[f"w1_{k}", f"CRT{k}{g}"], writes=[pbk])
                        S.op(ACT, lambda e, pb=pb, k=k, nch=nch: e.activation(
                            out=hidT[:, nch, 0:127], in_=pb[:, 0:127], func=AF.Gelu_apprx_tanh, bias=cb1[k][:, nch:nch + 1]),
                            reads=[pbk, f"cb1_{k}"], writes=["hidT"])
                    pb, pbk = self.banks.get()
                    if k == 0:
                        S.op(PE, self.mm8(pb[0:64, 0:127], lambda nch: w2[0][:, nch, :], lambda nch: hidT[:, nch, 0:127], n=2),
                             reads=["w2_0", "hidT"], writes=[pbk])
                        S.op(ACT, lambda e, pb=pb, s=s, g=g: e.copy(out=self.KCM[s][g][0:64, 0:127], in_=pb[0:64, 0:127]),
                             reads=[pbk], writes=[f"KCM{s}{g}"])
                    else:
                        S.op(PE, self.mm8(pb[0:127, 0:64], lambda nch: hidT[:, nch, 0:127], lambda nch: w2[1][:, nch, :], n=2),
                             reads=["w2_1", "hidT"], writes=[pbk])
                        S.op(ACT, lambda e, pb=pb, s=s, g=g: e.copy(out=self.VCO[s][g][0:127, 0:64], in_=pb[0:127, 0:64]),
                             reads=[pbk], writes=[f"VCO{s}{g}"])

        self.banks = saved_banks

    def pass_N2(self):
        nc, S = self.nc, self.S
        TT = 256
        NS = TT // 128
        LA = 3
        allb = self.banks.items + self.accb.items
        gen = Ring(allb[0:3])
        scr = Ring(allb[3:5])
        accr = Ring(allb[5:7])
        wnq = self.sb("wnq", [128, 8, 512], BF16)
        wng = self.sb("wng", [128, 8, 24], BF16)
        wgb = self.sb("wgb", [128, 8, D], BF16)
        wno = self.sb("wno", [128, 4, D], BF16)
        wout = self.sb("wout", [128, 8, D], BF16)
        gcb = self.cload("gcb", [128, 1024], F32, self.g_mix)
        cmask = self.sb("cmask", [128, 2048], BF16)
        S.dma(SP, lambda e: e.dma_start(out=cmask[0:127, :], in_=self.cd["c_cmask"]), writes=["cmask"])
        caus = self.cload("caus", [128, 512], BF16, self.cd["c_caus"])
        far = self.cload("far", [128, 512], BF16, self.cd["c_far"])
        impA = self.cload("impA", [128, 512], F32, self.cd["c_impA"])
        impB = self.cload("impB", [128, 512], F32, self.cd["c_impB"])
        impNF = self.cload("impNF", [128, 512], F32, self.cd["c_impNF"])
        self.load_w(wnq, "wnq", self.w_in, O_NQ, 512, 8)
        self.load_w(wng, "wng", self.w_in, O_NG, 24, 8)
        self.load_w(wgb, "wgb", self.w_in, O_GB, D, 8)
        self.load_w(wno, "wno", self.w_nsa_o, 0, D, 4)
        self.load_w(wout, "wout", self.w_out, 0, D, 8)
        hTs = [self.sb(f"hTN2_{b}", [128, 8, TT], BF16) for b in range(2)]
        QTs = [[self.sb(f"QT{b}{g}", [100, 4, TT], BF16) for g in range(2)] for b in range(2)]
        for b in range(2):
            for g in range(2):
                S.op(DVE, lambda e, b=b, g=g: e.memset(QTs[b][g][64:96, :, :], 0.0), writes=[f"QT{b}{g}0", f"QT{b}{g}1"])
        SGs = [self.sb(f"SG{b}", [128, NS, 24], F32) for b in range(2)]
        sgbs = [self.sb(f"sgb{b}", [128, 8, TT], BF16) for b in range(2)]
        maTs = [self.sb(f"maTN{b}", [128, 8, TT], BF16) for b in range(2)]
        nso = self.sb("nso", [128, NS, 512], BF16)
        noT = self.sb("noT", [128, 4, TT], BF16)
        selr = Ring([(self.sb(f"SELB{i}", [128, 96], BF16), f"SELB{i}") for i in range(4)])
        cper = Ring([(self.sb(f"cpe{i}", [128, 512], BF16), f"cpe{i}") for i in range(4)])
        for t_, k_ in selr.items:
            S.op(DVE, lambda e, t_=t_: e.memset(t_[:], 0.0), writes=[k_])
        ONSs = [[self.sb(f"ONS{b}{sub}", [128, 8, 64], F32) for sub in range(NS)] for b in range(2)]
        per = Ring([(self.sb(f"pe{i}", [128, 512], BF16), f"pe{i}") for i in range(6)])
        rdr = Ring([(self.sb(f"rd{i}", [128, 8], F32), f"rd{i}") for i in range(8)])
        impr = Ring([(self.sb(f"imp{i}", [128, 40], F32), f"imp{i}") for i in range(4)])
        it4r = Ring([(self.sb(f"it4_{i}", [128, 4, 32], F32), f"it4_{i}") for i in range(2)])
        ftr = Ring([(self.sb(f"ft{i}", [128, 4, 64], F32), f"ft{i}") for i in range(2)])
        tmr = Ring([(self.sb(f"tmb{i}", [128, 2 * TT], F32), f"tmb{i}") for i in range(3)])
        qaug = self.cd["c_qaug"].rearrange("a (r t) -> a r t", r=4)
        mav = self.ma_scr.rearrange("p (c n) -> p c n", c=8)
        tiles = [(s, t) for s in range(self.nseq) for t in range(S_LEN // TT)]
        xts_of = {}
        LAG = 2
        IMMEDIATE = False

        def v3(ap):
            return ap.rearrange("p (r t) -> p r t", r=4)

        pending = []
        stepno = [0]
        owners = {}

        def run_due(force=False):
            while pending and (force or pending[0][0] <= stepno[0]):
                _, c = pending.pop(0)
                c2 = c()
                if c2 is not None:
                    pending.append([stepno[0] + LAG, c2])

        def emit_bg(item):
            if item == "FLUSH":
                run_due(True)
                return
            c = item()
            while IMMEDIATE and c is not None:
                c = c()
            if c is not None:
                pending.append([stepno[0] + LAG, c])

        def galloc():
            bk, bkk = gen.get()
            tok = owners.get(bkk)
            if tok is not None and not tok["done"]:
                run_due(True)
            tok = {"done": False}
            owners[bkk] = tok
            return bk, bkk, tok

        def guard(name):
            tok = owners.get(name)
            if tok is not None and not tok["done"]:
                run_due(True)
            tok = {"done": False}
            owners[name] = tok
            return tok

        def prologue(n):
            s, t = tiles[n]
            b = n % 2
            tok0 = s * S_LEN + t * TT
            pos0 = t * TT
            hT, QT, SG, sgb, maT = hTs[b], QTs[b], SGs[b], sgbs[b], maTs[b]
            hk = f"hTN2_{b}"
            items = []
            xts_of[n] = [None] * NS
            stt_ = {}
            for sub in range(NS):
                def fa(sub=sub):
                    xt, xk = self.xt.get()
                    xn, nk = self.xn.get()
                    ss, sk = self.ss.get()
                    stt_[sub] = (xt, xk, xn, nk, ss, sk)
                    S.dma(SP, lambda e: e.dma_start(out=xt[:], in_=self.x[tok0 + sub * 128:tok0 + (sub + 1) * 128, :]), writes=[xk])
                items.append(fa)
            items.append(lambda: S.dma(SP, lambda e: e.dma_start(out=maT[:], in_=mav[:, :, tok0:tok0 + TT]), writes=[f"maTN{b}"]) and None)
            for g in range(2):
                items.append(lambda g=g: S.dma(SP, lambda e: e.dma_start(out=QT[g][96:100, :, :], in_=qaug[g * 4:(g + 1) * 4, :, pos0:pos0 + TT]),
                                               writes=[f"QT{b}{g}0", f"QT{b}{g}1"]) and None)
            for sub in range(NS):
                def fb(sub=sub):
                    xt, xk, xn, nk, ss, sk = stt_[sub]
                    S.op(ACT, lambda e: e.activation(out=self.junk[:], in_=xt[:], func=AF.Square, accum_out=ss[:]), reads=[xk], writes=[sk])
                    S.op(ACT, lambda e: e.activation(out=ss[:], in_=ss[:], func=AF.Sqrt, scale=1.0 / D, bias=self.epsb[:]),
                         reads=[sk, "epsb"], writes=[sk])

                    def c1():
                        S.op(DVE, lambda e: e.reciprocal(out=ss[:], in_=ss[:]), reads=[sk], writes=[sk])
                        S.op(DVE, lambda e: e.tensor_scalar(out=xn[:], in0=xt[:], scalar1=ss[:, 0:1], scalar2=None, op0=ALU.mult),
                             reads=[xk, sk], writes=[nk])
                        stt_[("done", sub)] = True
                    return c1
                items.append(fb)
            for sub in range(NS):
                def fc(sub=sub):
                    if not stt_.get(("done", sub)):
                        run_due(True)
                    xt, xk, xn, nk, ss, sk = stt_[sub]
                    tok = guard("psb")

                    def tr(e):
                        ins = None
                        for kc in range(8):
                            ins = e.transpose(out=self.psb[:, kc * 128:(kc + 1) * 128], in_=xn[:, kc * 128:(kc + 1) * 128], identity=self.ident[:])
                        return ins
                    S.op(PE, tr, reads=[nk, "ident"], writes=["psb"])

                    def c1():
                        S.op(DVE, lambda e: e.tensor_tensor(out=hT[:, :, sub * 128:(sub + 1) * 128],
                                                            in0=self.psb[:].rearrange("p (k j) -> p k j", k=8),
                                                            in1=gcb[:].rearrange("p (k j) -> p k j", k=8), op=ALU.mult),
                             reads=["psb", "gcb"], writes=[hk])
                        tok["done"] = True
                    xts_of[n][sub] = (xt, xk)
                    return c1
                items.append(fc)
            items.append("FLUSH")
            for g in range(2):
                for r in (0, 2):
                    def fq(g=g, r=r):
                        pb, pbk, tok = galloc()
                        for dr in range(2):
                            hh = g * 4 + r + dr
                            S.op(PE, self.mm8(pb[0:64, dr * 256:dr * 256 + TT], lambda kc, hh=hh: wnq[:, kc, hh * 64:(hh + 1) * 64],
                                              lambda kc: hT[:, kc, :]), reads=["wnq", hk], writes=[pbk])

                        def c1():
                            S.op(DVE, lambda e: e.tensor_copy(out=QT[g][0:64, r:r + 2, :], in_=pb[0:64, :].rearrange("p (a n) -> p a n", a=2)),
                                 reads=[pbk], writes=[f"QT{b}{g}0", f"QT{b}{g}1"])
                            tok["done"] = True
                        return c1
                    items.append(fq)
            def fg():
                pb, pbk, tok = galloc()
                for sub in range(NS):
                    cs = slice(sub * 128, (sub + 1) * 128)
                    S.op(PE, self.mm8(pb[:, sub * 32:sub * 32 + 24], lambda kc, cs=cs: hT[:, kc, cs], lambda kc: wng[:, kc, :]),
                         reads=["wng", hk], writes=[pbk])

                def c1():
                    S.op(ACT, lambda e: e.activation(out=SG[:, :, :], in_=pb[:, 0:64].rearrange("p (a n) -> p a n", a=2)[:, :, 0:24],
                                                     func=AF.Sigmoid), reads=[pbk], writes=[f"SG{b}"])
                    tok["done"] = True
                return c1
            items.append(fg)
            items.append("FLUSH")
            chains = []
            for sub in range(NS):
                for g in range(2):
                    chains.append(cmp_chain(n, s, t, b, sub, g))
            nst = max([len(c) for c in chains] + [0])
            for st_ in range(nst):
                for c in chains:
                    if st_ < len(c):
                        items.append(c[st_])
            for cc in range(0, 8, 2):
                def fgb(cc=cc):
                    pb, pbk, tok = galloc()
                    for dc in range(2):
                        S.op(PE, self.mm8(pb[:, dc * 256:dc * 256 + TT], lambda kc, c_=cc + dc: wgb[:, kc, c_ * 128:(c_ + 1) * 128],
                                          lambda kc: hT[:, kc, :]), reads=["wgb", hk], writes=[pbk])

                    def c1():
                        S.op(ACT, lambda e: e.activation(out=sgb[:, cc:cc + 2, :], in_=pb[:, :].rearrange("p (a n) -> p a n", a=2),
                                                         func=AF.Sigmoid), reads=[pbk], writes=[f"sgb{b}"])
                        tok["done"] = True
                    return c1
                items.append(fgb)
            items.append("FLUSH")
            return items

        def cmp_chain(n, s, t, b, sub, g):
            i = t * NS + sub
            qs = slice(sub * 128, (sub + 1) * 128)
            QT, SG = QTs[b], SGs[b]
            ONS = ONSs[b][sub]
            onk = f"ONS{b}{sub}{g}"
            qk = f"QT{b}{g}{sub}"
            rhsQ = QT[g][0:100, :, qs]
            nb = min(127, 8 * i + 8)
            isl = slice(i * 32, (i + 1) * 32)
            st = {}

            def s0():
                psc, psck, tok = galloc()
                st["pe"], st["pek"] = cper.get()
                pe = st["pe"]
                S.op(PE, lambda e: e.matmul(v3(psc[0:nb, :]), lhsT=self.KCM[s][g][0:100, 0:nb], rhs=rhsQ, start=True, stop=True),
                     reads=[f"KCM{s}{g}", qk], writes=[psck])

                def c1():
                    S.op(ACT, lambda e: e.activation(out=pe[0:nb, :], in_=psc[0:nb, :], func=AF.Exp, scale=0.125),
                         reads=[psck], writes=[st["pek"]])
                    tok["done"] = True
                return c1

            def s1():
                pe, pek = st["pe"], st["pek"]
                S.op(DVE, lambda e: e.tensor_tensor(out=v3(pe[0:nb, :]), in0=v3(pe[0:nb, :]),
                                                    in1=cmask[0:nb, i * 128:(i + 1) * 128].unsqueeze(1).to_broadcast([nb, 4, 128]), op=ALU.mult),
                     reads=[pek, "cmask"], writes=[pek])

            def s2():
                pe, pek = st["pe"], st["pek"]
                pcv, pcvk, tok = galloc()

                def pvc(e):
                    ins = None
                    for r in range(4):
                        ins = e.matmul(pcv[:, r * 97:(r + 1) * 97], lhsT=pe[0:nb, r * 128:(r + 1) * 128],
                                       rhs=self.VCO[s][g][0:nb, 0:97], start=True, stop=True, skip_group_check=True)
                    return ins
                S.op(PE, pvc, reads=[pek, f"VCO{s}{g}"], writes=[pcvk])

                def c1():
                    rd, rdk = rdr.get()
                    imp, impk = impr.get()
                    st["imp"], st["impk"] = imp, impk
                    pcv3 = pcv[:, 0:388].rearrange("p (r c) -> p r c", r=4)
                    S.op(DVE, lambda e: e.tensor_scalar(out=rd[:, 0:4], in0=pcv3[:, :, 64], scalar1=1e-30, scalar2=None, op0=ALU.add),
                         reads=[pcvk], writes=[rdk])
                    S.op(DVE, lambda e: e.reciprocal(out=rd[:, 0:4], in_=rd[:, 0:4]), reads=[rdk], writes=[rdk])
                    S.op(DVE, lambda e: e.tensor_tensor(out=rd[:, 4:8], in0=rd[:, 0:4], in1=SG[:, sub, g * 4:g * 4 + 4], op=ALU.mult),
                         reads=[rdk, f"SG{b}"], writes=[rdk])
                    S.op(DVE, lambda e: e.tensor_tensor(out=ONS[:, g * 4:(g + 1) * 4, :], in0=pcv3[:, :, 0:64],
                                                        in1=rd[:, 4:8].unsqueeze(2).to_broadcast([128, 4, 64]), op=ALU.mult),
                         reads=[pcvk, rdk], writes=[onk])
                    it4, it4k = it4r.get()
                    S.op(DVE, lambda e: e.tensor_tensor(out=it4[:], in0=pcv3[:, :, 65:97],
                                                        in1=rd[:, 0:4].unsqueeze(2).to_broadcast([128, 4, 32]), op=ALU.mult),
                         reads=[pcvk, rdk], writes=[it4k])
                    S.op(DVE, lambda e: e.tensor_reduce(out=imp[:, 0:32], in_=it4[:].rearrange("p r j -> p j r"), axis=mybir.AxisListType.X,
                                                        op=ALU.add), reads=[it4k], writes=[impk])
                    tok["done"] = True
                return c1

            def s3():
                if "imp" not in st:
                    run_due(True)
                imp, impk = st["imp"], st["impk"]
                S.op(DVE, lambda e: e.tensor_tensor(out=imp[:, 0:32], in0=imp[:, 0:32], in1=impA[:, isl], op=ALU.mult),
                     reads=[impk, "impA"], writes=[impk])
                S.op(DVE, lambda e: e.tensor_tensor(out=imp[:, 0:32], in0=imp[:, 0:32], in1=impB[:, isl], op=ALU.add),
                     reads=[impk, "impB"], writes=[impk])
                S.op(DVE, lambda e: e.max(out=imp[:, 32:40], in_=imp[:, 0:32]), reads=[impk], writes=[impk])
                S.op(DVE, lambda e: e.scalar_tensor_tensor(out=imp[:, 0:32], in0=imp[:, 0:32], scalar=imp[:, 39:40], in1=impNF[:, isl],
                                                           op0=ALU.is_ge, op1=ALU.mult), reads=[impk, "impNF"], writes=[impk])
                SELB, selk = selr.get()
                S.op(DVE, lambda e: e.tensor_scalar(out=SELB[:, 64:96], in0=imp[:, 0:32], scalar1=-1.0, scalar2=30000.0,
                                                    op0=ALU.add, op1=ALU.mult), reads=[impk], writes=[selk])
                st["SELB"], st["selk"] = SELB, selk

            def s4():
                SELB, selk = st["SELB"], st["selk"]
                pst, pstk, tok = galloc()
                S.op(PE, lambda e: e.matmul(pst[0:96, 0:128], lhsT=SELB[:, 0:96], rhs=self.ident[:, :], start=True, stop=True),
                     reads=[selk, "ident"], writes=[pstk])

                def c1():
                    S.op(DVE, lambda e: e.tensor_copy(out=QT[g][64:96, :, qs], in_=pst[64:96, 0:128].unsqueeze(1).to_broadcast([32, 4, 128])),
                         reads=[pstk], writes=[qk])
                    tok["done"] = True
                return c1

            def w(f, need_flush):
                def g_():
                    if need_flush:
                        run_due(True)
                    return f()
                return g_
            return [s0, w(s1, True), s2, s3, s4]

        def pair_tasks(n):
            s, t = tiles[n]
            b = n % 2
            QT, SG = QTs[b], SGs[b]
            tasks = []
            for sub in range(NS):
                i = t * NS + sub
                qs = slice(sub * 128, (sub + 1) * 128)
                ONS = ONSs[b][sub]
                for g in range(2):
                    onk = f"ONS{b}{sub}{g}"
                    qk = f"QT{b}{g}{sub}"
                    rhsQ = QT[g][0:100, :, qs]
                    for (KT, kkey, V1, vkey, j0, goff, isw) in (
                            (self.KWT[s][g], f"KWT{s}{g}", self.VW1[s], f"VW1{s}", max(0, i - 4), 16, True),
                            (self.KST[s][g], f"KST{s}{g}", self.VS1[s], f"VS1{s}", 0, 8, False)):
                        acc = {}
                        for j in range(j0, i + 1):
                            tk = {}

                            def A(tk=tk, j=j, KT=KT, kkey=kkey, rhsQ=rhsQ, qk=qk):
                                tk["pss"], tk["pssk"] = scr.get()
                                pss = tk["pss"]
                                S.op(PE, lambda e: e.matmul(v3(pss[:, :]), lhsT=KT[0:100, j * 128:(j + 1) * 128], rhs=rhsQ,
                                                            start=True, stop=True), reads=[kkey, qk], writes=[tk["pssk"]])
                                tk["pe"], tk["pek"] = per.get()
                                pe, pek = tk["pe"], tk["pek"]
                                S.op(ACT, lambda e: e.activation(out=pe[:], in_=pss[:, :], func=AF.Exp, scale=0.125),
                                     reads=[tk["pssk"]], writes=[pek])

                            def B(tk=tk, j=j, i=i, isw=isw):
                                pe, pek = tk["pe"], tk["pek"]
                                if j == i:
                                    S.op(DVE, lambda e: e.tensor_tensor(out=pe[:], in0=pe[:], in1=caus[:], op=ALU.mult),
                                         reads=[pek, "caus"], writes=[pek])
                                if isw and i >= 4 and j == i - 4:
                                    S.op(DVE, lambda e: e.tensor_tensor(out=pe[:], in0=pe[:], in1=far[:], op=ALU.mult),
                                         reads=[pek, "far"], writes=[pek])

                            def C(tk=tk, j=j, i=i, j0=j0, acc=acc, V1=V1, vkey=vkey, g=g):
                                if j == j0:
                                    acc["psv"], acc["psvk"] = accr.get()
                                psv, psvk, pe = acc["psv"], acc["psvk"], tk["pe"]

                                def pv(e):
                                    ins = None
                                    for r in range(4):
                                        ins = e.matmul(psv[:, r * 65:(r + 1) * 65], lhsT=pe[:, r * 128:(r + 1) * 128], rhs=V1[:, j, g, :],
                                                       start=(j == j0 and r == 0), stop=(j == i), skip_group_check=True)
                                    return ins
                                S.op(PE, pv, reads=[tk["pek"], vkey], writes=[psvk])

                            def Dn(j=j, i=i, acc=acc, g=g, goff=goff, ONS=ONS, onk=onk, SG=SG, sub=sub, b=b):
                                if j != i:
                                    return
                                psv, psvk = acc["psv"], acc["psvk"]
                                rd, rdk = rdr.get()
                                psv3 = psv[:, 0:260].rearrange("p (r c) -> p r c", r=4)
                                S.op(DVE, lambda e: e.reciprocal(out=rd[:, 0:4], in_=psv3[:, :, 64]), reads=[psvk], writes=[rdk])
                                S.op(DVE, lambda e: e.tensor_tensor(out=rd[:, 4:8], in0=rd[:, 0:4],
                                                                    in1=SG[:, sub, goff + g * 4:goff + g * 4 + 4], op=ALU.mult),
                                     reads=[rdk, f"SG{b}"], writes=[rdk])
                                ft, ftk = ftr.get()
                                S.op(DVE, lambda e: e.tensor_tensor(out=ft[:], in0=psv3[:, :, 0:64],
                                                                    in1=rd[:, 4:8].unsqueeze(2).to_broadcast([128, 4, 64]), op=ALU.mult),
                                     reads=[psvk, rdk], writes=[ftk])
                                S.op(POOL, lambda e: e.tensor_tensor(out=ONS[:, g * 4:(g + 1) * 4, :], in0=ONS[:, g * 4:(g + 1) * 4, :],
                                                                     in1=ft[:], op=ALU.add), reads=[ftk, onk], writes=[onk])
                            tasks.append((A, B, C, Dn))
            return tasks

        def epilogue(n):
            s, t = tiles[n]
            b = n % 2
            tok0 = s * S_LEN + t * TT
            sgb, maT = sgbs[b], maTs[b]
            items = []
            for sub in range(NS):
                qs = slice(sub * 128, (sub + 1) * 128)
                ONS = ONSs[b][sub]

                def f1(sub=sub, ONS=ONS, qs=qs):
                    S.op(POOL, lambda e: e.tensor_copy(out=nso[:, sub, :], in_=ONS[:].rearrange("p h d -> p (h d)")),
                         reads=[f"ONS{b}{sub}0", f"ONS{b}{sub}1"], writes=["nso"])

                    def c1():
                        tok = guard("psb")

                        def tr(e):
                            ins = None
                            for k4 in range(4):
                                ins = e.transpose(out=self.psb[:, k4 * 128:(k4 + 1) * 128], in_=nso[:, sub, k4 * 128:(k4 + 1) * 128],
                                                  identity=self.ident[:])
                            return ins
                        S.op(PE, tr, reads=["nso", "ident"], writes=["psb"])

                        def c2():
                            S.op(DVE, lambda e: e.tensor_copy(out=noT[:, :, qs], in_=self.psb[:, 0:512].rearrange("p (k j) -> p k j", k=4)),
                                 reads=["psb"], writes=["noT"])
                            tok["done"] = True
                        return c2
                    return c1
                items.append(f1)
            items.append("FLUSH")
            for cc in range(0, 8, 2):
                def f2(cc=cc):
                    pb, pbk, tok = galloc()
                    for dc in range(2):
                        S.op(PE, self.mm8(pb[:, dc * 256:dc * 256 + TT], lambda k4, c_=cc + dc: wno[:, k4, c_ * 128:(c_ + 1) * 128],
                                          lambda k4: noT[:, k4, :], n=4), reads=["wno", "noT"], writes=[pbk])

                    def c1():
                        tm, tmk = tmr.get()
                        S.op(DVE, lambda e: e.tensor_tensor(out=tm[:].rearrange("p (a n) -> p a n", a=2),
                                                            in0=pb[:, :].rearrange("p (a n) -> p a n", a=2), in1=sgb[:, cc:cc + 2, :], op=ALU.mult),
                             reads=[pbk, f"sgb{b}"], writes=[tmk])
                        tok["done"] = True
                        S.op(POOL, lambda e: e.tensor_tensor(out=maT[:, cc:cc + 2, :], in0=tm[:].rearrange("p (a n) -> p a n", a=2),
                                                             in1=maT[:, cc:cc + 2, :], op=ALU.add),
                             reads=[tmk, f"maTN{b}"], writes=[f"maTN{b}"])
                    return c1
                items.append(f2)
            items.append("FLUSH")
            for sub in range(NS):
                cs = slice(sub * 128, (sub + 1) * 128)
                for half in range(2):
                    def f3(sub=sub, cs=cs, half=half):
                        xt, xk = xts_of[n][sub]
                        pb, pbk, tok = galloc()
                        S.op(PE, self.mm8(pb[:, :], lambda cc: maT[:, cc, cs], lambda cc: wout[:, cc, half * 512:(half + 1) * 512]),
                             reads=[f"maTN{b}", "wout"], writes=[pbk])

                        def c1():
                            S.op(DVE, lambda e: e.tensor_tensor(out=xt[:, half * 512:(half + 1) * 512], in0=xt[:, half * 512:(half + 1) * 512],
                                                                in1=pb[:, :], op=ALU.add), reads=[pbk, xk], writes=[xk])
                            tok["done"] = True
                            if half == 1:
                                r0 = tok0 + sub * 128
                                S.dma(SP, lambda e: e.dma_start(out=self.x1_scr[r0:r0 + 128, :], in_=xt[:]), reads=[xk])
                        return c1
                    items.append(f3)
            items.append("FLUSH")
            return items

        for it in prologue(0):
            stepno[0] += 1
            run_due()
            emit_bg(it)
        run_due(True)
        for n in range(len(tiles)):
            bg = []
            if n >= 1:
                bg += epilogue(n - 1)
            if n + 1 < len(tiles):
                bg += prologue(n + 1)
            tasks = pair_tasks(n)
            steps = len(tasks) + LA + 2
            pi = 0
            for k in range(steps):
                stepno[0] += 1
                run_due()
                if k < len(tasks):
                    tasks[k][0]()
                if 0 <= k - 1 < len(tasks):
                    tasks[k - 1][1]()
                if 0 <= k - LA < len(tasks):
                    tasks[k - LA][2]()
                if 0 <= k - LA - 1 < len(tasks):
                    tasks[k - LA - 1][3]()
                tgt = ((k + 1) * len(bg) + steps - 1) // steps
                while pi < min(tgt, len(bg)):
                    emit_bg(bg[pi])
                    pi += 1
            while pi < len(bg):
                stepno[0] += 1
                run_due()
                emit_bg(bg[pi])
                pi += 1
            run_due(True)
        for it in epilogue(len(tiles) - 1):
            stepno[0] += 1
            run_due()
            emit_bg(it)
        run_due(True)

    def pass_F(self, src):
        nc, S = self.nc, self.S
        TT = 256
        self.banks = Ring(self.banks.items + self.accb.items)
        wup = self.sb("wup", [128, 8, 2 * DFF], BF16)
        wdn = self.sb("wdn", [128, NFC, D], BF16)
        gcb = self.sb("gcbF", [128, 8 * 128], F32)
        gfin = self.sb("gfin", [128, D], F32)
        cw = self.sb("cw", [128, NFC * 3], F32)
        cb = self.sb("cb", [128, NFC], F32)
        halo = self.sb("halo", [128, NFC, 2], F32)
        hTs = [self.sb(f"hTF{b}", [128, 8, TT], BF16) for b in range(2)]
        uT = self.sb("uT", [128, NFC, TT], BF16)
        t1r = Ring([(self.sb(f"t1_{i}", [128, TT], F32), f"t1_{i}") for i in range(3)])
        ger = Ring([(self.sb(f"ge_{i}", [128, TT], F32), f"ge_{i}") for i in range(2)])
        osb = Ring([(self.sb(f"osb{i}", [128, D], F32), f"osb{i}") for i in range(2)])
        S.dma(SP, lambda e: e.dma_start(out=gcb[:], in_=self.g_ffn), writes=["gcb"])
        S.dma(SP, lambda e: e.dma_start(out=gfin[:], in_=self.g_fin), writes=["gfin"])
        S.dma(SP, lambda e: e.dma_start(out=cw[:], in_=self.convw), writes=["cw"])
        S.dma(SP, lambda e: e.dma_start(out=cb[:], in_=self.convb), writes=["cb"])
        for blk in range(NFC // 2):
            self.load_blk(wup, f"wup{blk}", self.w_up, blk * 256, 256, 8, blk * 256)
            self.load_blk(wup, f"wup{blk}", self.w_up, DFF + blk * 256, 256, 8, DFF + blk * 256)
        wdn_keys = []
        for f0 in range(0, NFC, 2):
            S.dma(POOL, lambda e, f0=f0: e.dma_start(out=wdn[:, f0:f0 + 2, :],
                                                     in_=self.w_down.rearrange("(f p) n -> p f n", p=128)[:, f0:f0 + 2, :]),
                  writes=[f"wdn{f0}"])
            wdn_keys.append(f"wdn{f0}")
        NTL = self.ntok // TT
        NSB = TT // 128
        xts_next = [self.rmsnorm_hT(src, sub * 128, gcb, hTs[0], "hTF0", sub * 128) for sub in range(NSB)]
        for t in range(NTL):
            tok0 = t * TT
            first = (tok0 % S_LEN) == 0
            hT = hTs[t % 2]
            hk = f"hTF{t % 2}"
            xts = xts_next
            xts_next = []
            pend = []
            deferred = None
            for fc in range(NFC):
                pa, pak = self.banks.get()
                pb, pbk = self.banks.get()

                def mm_a(e, pa=pa, fc=fc, hT=hT):
                    ins = None
                    for kc in range(8):
                        ins = e.matmul(pa[:, 2:2 + TT], lhsT=wup[:, kc, fc * 128:(fc + 1) * 128], rhs=hT[:, kc, :],
                                       start=(kc == 0), stop=(kc == 7))
                    return ins

                def mm_b(e, pb=pb, fc=fc, hT=hT):
                    ins = None
                    for kc in range(8):
                        ins = e.matmul(pb[:, 0:TT], lhsT=wup[:, kc, DFF + fc * 128:DFF + (fc + 1) * 128],
                                       rhs=hT[:, kc, :], start=(kc == 0), stop=(kc == 7))
                    return ins
                S.op(PE, mm_a, reads=[f"wup{fc // 2}", hk], writes=[pak])
                S.op(PE, mm_b, reads=[f"wup{fc // 2}", hk], writes=[pbk])
                if t + 1 < NTL:
                    nb_ = (t + 1) % 2
                    if fc in (2, 6):
                        pend.append(self.rms_a(src, tok0 + TT + (fc // 4) * 128))
                    if fc in (10, 14):
                        sub_ = (fc - 10) // 4
                        xts_next.append(self.rms_b(pend[sub_], gcb, hTs[nb_], f"hTF{nb_}", sub_ * 128))
                if first:
                    S.op(ACT, lambda e, pa=pa: e.memzero(pa[:, 0:2]), reads=[pak], writes=[pak])
                else:
                    S.op(ACT, lambda e, pa=pa, fc=fc: e.copy(out=pa[:, 0:2], in_=halo[:, fc, :]),
                         reads=[pak, "halo"], writes=[pak])
                S.op(ACT, lambda e, pa=pa, fc=fc: e.copy(out=halo[:, fc, :], in_=pa[:, TT:TT + 2]),
                     reads=[pak], writes=["halo"])
                t1, t1k = t1r.get()
                S.op(ACT, lambda e, pa=pa, fc=fc, t1=t1: e.activation(
                    out=t1[:], in_=pa[:, 2:2 + TT], func=AF.Copy, scale=cw[:, fc * 3 + 2:fc * 3 + 3]),
                    reads=[pak, "cw"], writes=[t1k])
                S.op(DVE, lambda e, pa=pa, fc=fc, t1=t1: e.scalar_tensor_tensor(
                    out=t1[:], in0=pa[:, 1:1 + TT], scalar=cw[:, fc * 3 + 1:fc * 3 + 2], in1=t1[:],
                    op0=ALU.mult, op1=ALU.add), reads=[pak, "cw", t1k], writes=[t1k])
                S.op(DVE, lambda e, pa=pa, fc=fc, t1=t1: e.scalar_tensor_tensor(
                    out=t1[:], in0=pa[:, 0:TT], scalar=cw[:, fc * 3:fc * 3 + 1], in1=t1[:],
                    op0=ALU.mult, op1=ALU.add), reads=[pak, "cw", t1k], writes=[t1k])

                def fin(fc=fc, t1=t1, t1k=t1k, pb=pb, pbk=pbk):
                    ge, gek = ger.get()
                    S.op(ACT, lambda e: e.activation(out=ge[:], in_=t1[:], func=AF.Gelu_apprx_tanh, bias=cb[:, fc:fc + 1]),
                         reads=[t1k, "cb"], writes=[gek])
                    S.op(DVE, lambda e: e.tensor_tensor(out=uT[:, fc, :], in0=ge[:], in1=pb[:, 0:TT], op=ALU.mult),
                         reads=[gek, pbk], writes=["uT"])
                if deferred is not None:
                    deferred()
                deferred = fin
            deferred()
            deferred = None
            for sub in range(TT // 128):
                xt, xk = xts[sub]
                ss, sk = self.ss.get()
                for half in range(2):
                    po, pok = self.banks.get()

                    def mm_o(e, po=po, sub=sub, half=half):
                        ins = None
                        for fc in range(NFC):
                            ins = e.matmul(po[:, :], lhsT=uT[:, fc, sub * 128:(sub + 1) * 128],
                                           rhs=wdn[:, fc, half * 512:(half + 1) * 512],
                                           start=(fc == 0), stop=(fc == NFC - 1))
                        return ins
                    S.op(PE, mm_o, reads=["uT"] + wdn_keys, writes=[pok])
                    S.op(DVE, lambda e, po=po, xt=xt, half=half: e.tensor_tensor(
                        out=xt[:, half * 512:(half + 1) * 512], in0=xt[:, half * 512:(half + 1) * 512],
                        in1=po[:, :], op=ALU.add), reads=[pok, xk], writes=[xk])
                ob, obk = osb.get()
                S.op(ACT, lambda e, xt=xt, ss=ss: e.activation(out=self.junk[:], in_=xt[:], func=AF.Square,
                                                               accum_out=ss[:]), reads=[xk], writes=[sk])
                S.op(ACT, lambda e, ss=ss: e.activation(out=ss[:], in_=ss[:], func=AF.Sqrt, scale=1.0 / D,
                                                        bias=self.epsb[:]), reads=[sk, "epsb"], writes=[sk])
                S.op(DVE, lambda e, ss=ss: e.reciprocal(out=ss[:], in_=ss[:]), reads=[sk], writes=[sk])
                S.op(DVE, lambda e, xt=xt, ss=ss, ob=ob: e.scalar_tensor_tensor(
                    out=ob[:], in0=xt[:], scalar=ss[:, 0:1], in1=gfin[:], op0=ALU.mult, op1=ALU.mult),
                    reads=[xk, sk, "gfin"], writes=[obk])
                r0 = tok0 + sub * 128
                S.dma(SP, lambda e, ob=ob, r0=r0: e.dma_start(out=self.out[r0:r0 + 128, :], in_=ob[:]),
                      reads=[obk])


def host_inputs(inp, nseq, core):
    f = np.float32
    x = np.ascontiguousarray(inp["x"][core * nseq:(core + 1) * nseq].reshape(nseq * S_LEN, D))

    def gcol(g):
        return np.ascontiguousarray(np.broadcast_to(g.reshape(8, 128).T[:, :, None], (128, 8, 128)).reshape(128, 1024))
    m = {
        "x": x,
        "w_in": np.ascontiguousarray(inp["w_in"][0]),
        "w_up": np.ascontiguousarray(inp["w_up"][0]),
        "w_down": np.ascontiguousarray(inp["w_down"][0]),
        "g_ffn": gcol(inp["norm_ffn"][0]),
        "g_mix": gcol(inp["norm_mix"][0]),
        "g_fin": np.ascontiguousarray(np.broadcast_to(inp["norm_final"][None, :], (128, D))),
        "convw": np.ascontiguousarray(inp["conv_w"][0].reshape(3, NFC, 128).transpose(2, 1, 0).reshape(128, NFC * 3)),
        "convb": np.ascontiguousarray(inp["conv_b"][0].reshape(NFC, 128).T),
    }
    m.update({
        "w_ret_o": np.ascontiguousarray(inp["w_ret_o"][0]),
        "w_nsa_o": np.ascontiguousarray(inp["w_nsa_o"][0]),
        "w_out": np.ascontiguousarray(inp["w_out"][0]),
        "gng": gcol(inp["ret_gn_g"][0]),
        "w1k": np.ascontiguousarray(inp["cmp_w1_k"][0]),
        "w1v": np.ascontiguousarray(inp["cmp_w1_v"][0]),
        "w2k": np.ascontiguousarray(inp["cmp_w2_k"][0]),
        "w2v": np.ascontiguousarray(inp["cmp_w2_v"][0]),
        "b1k": np.ascontiguousarray(inp["cmp_b1_k"][0].reshape(2, 128).T),
        "b1v": np.ascontiguousarray(inp["cmp_b1_v"][0].reshape(2, 128).T),
        "posTk": np.ascontiguousarray(np.repeat(inp["cmp_pos_k"][0].T[:, :, None], 2, axis=2).reshape(64, 64)),
        "posTv": np.ascontiguousarray(np.repeat(inp["cmp_pos_v"][0].T[:, :, None], 2, axis=2).reshape(64, 64)),
    })
    m.update(make_consts())
    return m


_CACHE = {}


def kernel(**inputs):
    inp = {k: np.asarray(v) for k, v in inputs.items()}
    ncores, nseq = 8, 2
    if "prog" not in _CACHE:
        p = Prog(nseq=nseq)
        p.build()
        _CACHE["prog"] = p
    p = _CACHE["prog"]
    in_maps = []
    for c in range(ncores):
        m = host_inputs(inp, nseq, c)
        in_maps.append({k: m[k] for k in p.in_names})
    res = run_bass_kernel_spmd(p.nc, in_maps, core_ids=list(range(ncores)))
    out = np.concatenate([r["out"] for r in res.results], axis=0)
    return out.reshape(16, S_LEN, D).astype(np.float32)
```
